# Optimizing a Trainium2 kernel written in Bass

```python
import math
import jax, jax.numpy as jnp
from jax import lax
import numpy as np

D_MODEL = 1024
BATCH = 8
SEQ = 2048
DEPTH = 4
DEC_BATCH = 128
DEC_SEQ = 1
PAST_LEN = 16384
PAGE_SIZE = 128

D_RNN = D_MODEL
RG_BLOCKS = 16
RG_BLOCK = D_RNN // RG_BLOCKS
RG_C = 8.0
CONV_W = 4
GDN_HEADS = 8
GDN_DK = 128
GDN_DV = 128
GDN_QK = GDN_HEADS * GDN_DK
GDN_VW = GDN_HEADS * GDN_DV
GDN_CONV_CH = 2 * GDN_QK + GDN_VW
CHUNK = 64
D_FF = 2816
ALPHA = (2 * DEPTH) ** 0.25
BETA_INIT = (8 * DEPTH) ** -0.25
LN_EPS = 1e-5
RMS_EPS = 1e-6
SPLITS = (D_RNN, D_RNN, GDN_QK, GDN_QK, GDN_VW, GDN_VW, GDN_HEADS, GDN_HEADS, D_MODEL, D_MODEL)
D_IN = 2 * D_RNN + 2 * GDN_QK + 2 * GDN_VW + 2 * GDN_HEADS + 2 * D_MODEL

kernel_name = "hybrid_rglru_gdn_macaron_deepnorm_step"


def layer_norm(x, g, b):
    xf = x.astype(jnp.float32)
    mu = jnp.mean(xf, -1, keepdims=True)
    var = jnp.mean(jnp.square(xf - mu), -1, keepdims=True)
    return ((xf - mu) * lax.rsqrt(var + LN_EPS) * g.astype(jnp.float32) + b.astype(jnp.float32)).astype(x.dtype)


def swiglu(x, w1, w3, w2):
    return (jax.nn.silu(x @ w1) * (x @ w3)) @ w2


def causal_conv(x, buf, w):
    T = x.shape[1]
    xx = jnp.concatenate([buf.astype(x.dtype), x], axis=1)
    y = xx[:, 0:T] * w[0]
    for j in range(1, CONV_W):
        y = y + xx[:, j:j + T] * w[j]
    return y, xx[:, -(CONV_W - 1):]


def rg_lru(x, h0, w_r, b_r, w_i, b_i, lam):
    B, T, _ = x.shape
    xb = x.reshape(B, T, RG_BLOCKS, RG_BLOCK)
    r = jax.nn.sigmoid(jnp.einsum('btnc,ncd->btnd', xb, w_r).reshape(B, T, D_RNN) + b_r).astype(jnp.float32)
    i = jax.nn.sigmoid(jnp.einsum('btnc,ncd->btnd', xb, w_i).reshape(B, T, D_RNN) + b_i)
    log_a = -RG_C * r * jax.nn.softplus(-lam.astype(jnp.float32))
    a = jnp.exp(log_a)
    bterm = jnp.sqrt(-jnp.expm1(2.0 * log_a)) * (i * x).astype(jnp.float32)
    bterm = bterm.at[:, 0].add(a[:, 0] * h0.astype(jnp.float32))

    def comb(e1, e2):
        a1, b1 = e1
        a2, b2 = e2
        return a1 * a2, a2 * b1 + b2

    _, h = lax.associative_scan(comb, (a, bterm), axis=1)
    return h.astype(x.dtype), h[:, -1]


def gated_delta_chunked(q, k, v, g, beta, S0):
    B, T, H, _ = q.shape
    C = min(CHUNK, T)
    N = -(-T // C)
    pad = N * C - T

    def prep(t):
        t = jnp.pad(t, [(0, 0), (0, pad)] + [(0, 0)] * (t.ndim - 2))
        t = jnp.moveaxis(t, 2, 1)
        return t.reshape((B, H, N, C) + t.shape[3:])

    q, k, v, g, beta = prep(q), prep(k), prep(v), prep(g), prep(beta)
    gc = jnp.cumsum(g, axis=-1)
    causal = jnp.tril(jnp.ones((C, C), bool))
    strict = jnp.tril(jnp.ones((C, C), bool), -1)
    diff = gc[..., :, None] - gc[..., None, :]
    decay = jnp.where(causal, jnp.exp(jnp.where(causal, diff, 0.0)), 0.0)
    kb = k * beta[..., None]
    vb = v * beta[..., None]
    L = jnp.where(strict, jnp.einsum('bhncd,bhnsd->bhncs', kb, k) * decay, 0.0)
    eye = jnp.eye(C, dtype=jnp.float32)
    rhs = jnp.concatenate([vb, kb * jnp.exp(gc)[..., None]], axis=-1)
    sol = lax.linalg.triangular_solve(eye + L, rhs, left_side=True, lower=True)
    u, w = sol[..., :GDN_DV], sol[..., GDN_DV:]
    a_intra = jnp.where(causal, jnp.einsum('bhncd,bhnsd->bhncs', q, k) * decay, 0.0)
    g_last = gc[..., -1]
    k_tail = k * jnp.exp(g_last[..., None] - gc)[..., None]
    q_dec = q * jnp.exp(gc)[..., None]
    xs = tuple(jnp.moveaxis(t, 2, 0) for t in (q_dec, k_tail, u, w, a_intra, g_last))

    def step(S, inp):
        qd, kt, ui, wi, ai, gl = inp
        v_new = ui - jnp.einsum('bhck,bhkv->bhcv', wi, S)
        o = jnp.einsum('bhck,bhkv->bhcv', qd, S) + jnp.einsum('bhcs,bhsv->bhcv', ai, v_new)
        S = S * jnp.exp(gl)[..., None, None] + jnp.einsum('bhck,bhcv->bhkv', kt, v_new)
        return S, o

    S, o = lax.scan(step, S0, xs)
    o = jnp.moveaxis(o, 0, 2).reshape(B, H, N * C, GDN_DV)[:, :, :T]
    return jnp.moveaxis(o, 1, 2), S


def l2norm(t):
    return t * lax.rsqrt(jnp.sum(jnp.square(t), -1, keepdims=True) + RMS_EPS)


def hybrid_mixer(x, h0, rc0, S0, gc0, w_in, b_in, rg_conv_w, rg_conv_b, rg_wr, rg_br, rg_wi, rg_bi,
                 rg_lambda, gdn_conv_w, gdn_a_log, gdn_dt_bias, gdn_norm_w, w_o):
    B, T, _ = x.shape
    proj = x @ w_in + b_in
    idx = [int(s) for s in np.cumsum(SPLITS)[:-1]]
    rg_x, rg_y, q, k, v, z, gdn_a, gdn_b, gate_a, gate_b = jnp.split(proj, idx, axis=-1)
    xc, rc_new = causal_conv(rg_x, rc0, rg_conv_w)
    xc = xc + rg_conv_b
    hseq, h_last = rg_lru(xc, h0, rg_wr, rg_br, rg_wi, rg_bi, rg_lambda)
    out_a = hseq * jax.nn.gelu(rg_y)
    qkv, gc_new = causal_conv(jnp.concatenate([q, k, v], axis=-1), gc0, gdn_conv_w)
    qkv = jax.nn.silu(qkv)
    q, k, v = jnp.split(qkv, [GDN_QK, 2 * GDN_QK], axis=-1)
    q = l2norm(q.reshape(B, T, GDN_HEADS, GDN_DK).astype(jnp.float32)) * (GDN_DK ** -0.5)
    k = l2norm(k.reshape(B, T, GDN_HEADS, GDN_DK).astype(jnp.float32))
    v = v.reshape(B, T, GDN_HEADS, GDN_DV).astype(jnp.float32)
    beta = jax.nn.sigmoid(gdn_b.astype(jnp.float32))
    g = -jnp.exp(gdn_a_log.astype(jnp.float32)) * jax.nn.softplus(
        gdn_a.astype(jnp.float32) + gdn_dt_bias.astype(jnp.float32))
    o, S_new = gated_delta_chunked(q, k, v, g, beta, S0.astype(jnp.float32))
    o = o * lax.rsqrt(jnp.mean(jnp.square(o), -1, keepdims=True) + RMS_EPS) * gdn_norm_w.astype(jnp.float32)
    out_b = o.reshape(B, T, GDN_VW).astype(x.dtype) * jax.nn.silu(z)
    merged = jax.nn.sigmoid(gate_a) * out_a + jax.nn.sigmoid(gate_b) * out_b
    new_state = (h_last.astype(h0.dtype), rc_new.astype(rc0.dtype), S_new.astype(S0.dtype), gc_new.astype(gc0.dtype))
    return merged @ w_o, new_state


def decoder_layer(x, h0, rc0, S0, gc0, ln1_g, ln1_b, ffn1_w1, ffn1_w3, ffn1_w2, w_in, b_in,
                  rg_conv_w, rg_conv_b, rg_wr, rg_br, rg_wi, rg_bi, rg_lambda, gdn_conv_w, gdn_a_log,
                  gdn_dt_bias, gdn_norm_w, w_o, ln2_g, ln2_b, ffn2_w1, ffn2_w3, ffn2_w2, ln3_g, ln3_b):
    x = layer_norm(ALPHA * x + 0.5 * swiglu(x, ffn1_w1, ffn1_w3, ffn1_w2), ln1_g, ln1_b)
    m, st = hybrid_mixer(x, h0, rc0, S0, gc0, w_in, b_in, rg_conv_w, rg_conv_b, rg_wr, rg_br, rg_wi,
                         rg_bi, rg_lambda, gdn_conv_w, gdn_a_log, gdn_dt_bias, gdn_norm_w, w_o)
    x = layer_norm(ALPHA * x + m, ln2_g, ln2_b)
    x = layer_norm(ALPHA * x + 0.5 * swiglu(x, ffn2_w1, ffn2_w3, ffn2_w2), ln3_g, ln3_b)
    return x, st


def trunk(x, h0, rc0, S0, gc0, params):
    hs, rcs, Ss, gcs = [], [], [], []
    for l in range(DEPTH):
        x, (h, rc, S, gc) = decoder_layer(x, h0[l], rc0[l], S0[l], gc0[l], *[p[l] for p in params])
        hs.append(h)
        rcs.append(rc)
        Ss.append(S)
        gcs.append(gc)
    return x, jnp.stack(hs), jnp.stack(rcs), jnp.stack(Ss), jnp.stack(gcs)


def setup_inputs(seed: int = 0) -> dict:
    key = jax.random.key(seed)
    ks = iter(jax.random.split(key, 48))

    def nrm(shape, scale):
        return jax.random.normal(next(ks), shape, jnp.float32) * scale

    L, D = DEPTH, D_MODEL
    x_prompt = nrm((BATCH, SEQ, D), 1.0)
    x_sample = nrm((DEC_BATCH, DEC_SEQ, D), 1.0)
    state_rglru_h = nrm((L, DEC_BATCH, D_RNN), 0.5)
    state_rglru_conv = nrm((L, DEC_BATCH, CONV_W - 1, D_RNN), 1.0)
    state_gdn_S = nrm((L, DEC_BATCH, GDN_HEADS, GDN_DK, GDN_DV), 0.1)
    state_gdn_conv = nrm((L, DEC_BATCH, CONV_W - 1, GDN_CONV_CH), 1.0)

    ln1_g = 1.0 + nrm((L, D), 0.02)
    ln1_b = nrm((L, D), 0.02)
    ffn1_w1 = nrm((L, D, D_FF), D ** -0.5)
    ffn1_w3 = nrm((L, D, D_FF), D ** -0.5)
    ffn1_w2 = nrm((L, D_FF, D), BETA_INIT * D_FF ** -0.5)
    w_in = nrm((L, D, D_IN), D ** -0.5)
    v_start = 2 * D_RNN + 2 * GDN_QK
    w_in = w_in.at[:, :, v_start:v_start + GDN_VW].multiply(BETA_INIT)
    b_in = nrm((L, D_IN), 0.01)
    rg_conv_w = nrm((L, CONV_W, D_RNN), CONV_W ** -0.5)
    rg_conv_b = nrm((L, D_RNN), 0.01)
    rg_wr = nrm((L, RG_BLOCKS, RG_BLOCK, RG_BLOCK), RG_BLOCK ** -0.5)
    rg_br = nrm((L, D_RNN), 0.01)
    rg_wi = nrm((L, RG_BLOCKS, RG_BLOCK, RG_BLOCK), RG_BLOCK ** -0.5)
    rg_bi = nrm((L, D_RNN), 0.01)
    u = jax.random.uniform(next(ks), (L, D_RNN), jnp.float32, 0.9, 0.999)
    a0 = u ** (1.0 / RG_C)
    rg_lambda = jnp.log(a0) - jnp.log1p(-a0)
    gdn_conv_w = nrm((L, CONV_W, GDN_CONV_CH), CONV_W ** -0.5)
    gdn_a_log = jnp.log(jax.random.uniform(next(ks), (L, GDN_HEADS), jnp.float32, 1.0, 16.0))
    dt = jnp.exp(jax.random.uniform(next(ks), (L, GDN_HEADS), jnp.float32, math.log(1e-3), math.log(1e-1)))
    gdn_dt_bias = dt + jnp.log(-jnp.expm1(-dt))
    gdn_norm_w = 1.0 + nrm((L, GDN_DV), 0.02)
    w_o = nrm((L, D, D), BETA_INIT * D ** -0.5)
    ln2_g = 1.0 + nrm((L, D), 0.02)
    ln2_b = nrm((L, D), 0.02)
    ffn2_w1 = nrm((L, D, D_FF), D ** -0.5)
    ffn2_w3 = nrm((L, D, D_FF), D ** -0.5)
    ffn2_w2 = nrm((L, D_FF, D), BETA_INIT * D_FF ** -0.5)
    ln3_g = 1.0 + nrm((L, D), 0.02)
    ln3_b = nrm((L, D), 0.02)
    return {"x_prompt": x_prompt, "x_sample": x_sample,
            "state_rglru_h": state_rglru_h, "state_rglru_conv": state_rglru_conv,
            "state_gdn_S": state_gdn_S, "state_gdn_conv": state_gdn_conv,
            "ln1_g": ln1_g, "ln1_b": ln1_b, "ffn1_w1": ffn1_w1, "ffn1_w3": ffn1_w3, "ffn1_w2": ffn1_w2,
            "w_in": w_in, "b_in": b_in, "rg_conv_w": rg_conv_w, "rg_conv_b": rg_conv_b,
            "rg_wr": rg_wr, "rg_br": rg_br, "rg_wi": rg_wi, "rg_bi": rg_bi, "rg_lambda": rg_lambda,
            "gdn_conv_w": gdn_conv_w, "gdn_a_log": gdn_a_log, "gdn_dt_bias": gdn_dt_bias,
            "gdn_norm_w": gdn_norm_w, "w_o": w_o, "ln2_g": ln2_g, "ln2_b": ln2_b,
            "ffn2_w1": ffn2_w1, "ffn2_w3": ffn2_w3, "ffn2_w2": ffn2_w2, "ln3_g": ln3_g, "ln3_b": ln3_b}


def reference(x_prompt, x_sample, state_rglru_h, state_rglru_conv, state_gdn_S, state_gdn_conv,
              ln1_g, ln1_b, ffn1_w1, ffn1_w3, ffn1_w2, w_in, b_in, rg_conv_w, rg_conv_b, rg_wr, rg_br,
              rg_wi, rg_bi, rg_lambda, gdn_conv_w, gdn_a_log, gdn_dt_bias, gdn_norm_w, w_o, ln2_g, ln2_b,
              ffn2_w1, ffn2_w3, ffn2_w2, ln3_g, ln3_b):
    params = (ln1_g, ln1_b, ffn1_w1, ffn1_w3, ffn1_w2, w_in, b_in, rg_conv_w, rg_conv_b, rg_wr, rg_br,
              rg_wi, rg_bi, rg_lambda, gdn_conv_w, gdn_a_log, gdn_dt_bias, gdn_norm_w, w_o, ln2_g, ln2_b,
              ffn2_w1, ffn2_w3, ffn2_w2, ln3_g, ln3_b)
    Bp = x_prompt.shape[0]
    dt = x_prompt.dtype
    y_prompt, p_h, p_rc, p_S, p_gc = trunk(
        x_prompt,
        jnp.zeros((DEPTH, Bp, D_RNN), dt),
        jnp.zeros((DEPTH, Bp, CONV_W - 1, D_RNN), dt),
        jnp.zeros((DEPTH, Bp, GDN_HEADS, GDN_DK, GDN_DV), dt),
        jnp.zeros((DEPTH, Bp, CONV_W - 1, GDN_CONV_CH), dt),
        params)
    y_sample, s_h, s_rc, s_S, s_gc = trunk(
        x_sample, state_rglru_h, state_rglru_conv, state_gdn_S, state_gdn_conv, params)
    return (y_prompt, y_sample, p_h, p_rc, p_S, p_gc, s_h, s_rc, s_S, s_gc)
```

```python
import math
from contextlib import ExitStack
import numpy as np
import concourse.bass as bass
import concourse.mybir as mybir
from concourse.bass_utils import run_bass_kernel_spmd

F32 = mybir.dt.float32
BF16 = mybir.dt.bfloat16
AF = mybir.ActivationFunctionType
ALU = mybir.AluOpType
AX = mybir.AxisListType

L = 4
D = 1024
NP = 2048
NS = 16
TOK = NP + NS
DFF = 2816
NJ = DFF // 128
ALPHA = (2 * L) ** 0.25
LN_EPS = 1e-5
RMS_EPS = 1e-6
TILES = [(0, 512), (512, 512), (1024, 512), (1536, 512), (2048, 16)]
NPV = 273
ENGS = ("pe", "act", "dve", "pool", "sp")


class Buf:
    __slots__ = ("w", "r", "excl")

    def __init__(self, excl=False):
        self.w = None
        self.r = []
        self.excl = excl


class Prog:
    def __init__(self, nc):
        self.nc = nc
        self.all = []
        self.n_ch = {"sp": 28, "pool": 12}

    def _collect(self, eng, reads, writes, is_dma):
        deps = {}
        for b in reads:
            ev = b.w
            if ev is None:
                continue
            ee = self.all[ev]
            need = not (eng == "pe" and ee[0] == "pe" and ee[1] == "c")
            deps[ev] = deps.get(ev, False) or need
        for b in writes:
            evs = list(b.r)
            if b.w is not None:
                evs.append(b.w)
            for ev in evs:
                ee = self.all[ev]
                same = (ee[0] == eng and ee[1] == "c" and not is_dma)
                deps[ev] = deps.get(ev, False) or (not same)
        return list(deps.items())

    def _add(self, eng, kind, fn, reads, writes, dur, fam):
        ex = [b for b in reads if b.excl and b not in writes]
        if ex:
            writes = list(writes) + ex
        deps = self._collect(eng, reads, writes, kind == "d")
        idx = len(self.all)
        self.all.append([eng, kind, fn, deps, dur, fam])
        for b in reads:
            b.r.append(idx)
        for b in writes:
            b.w = idx
            b.r = []
        return idx

    def op(self, eng, fn, reads=(), writes=(), dur=300.0, fam=None):
        return self._add(eng, "c", fn, reads, writes, dur, fam)

    def dma(self, eng, fn, reads=(), writes=(), dur=3000.0):
        return self._add(eng, "d", fn, reads, writes, dur, None)

    def schedule(self):
        import heapq
        ops = self.all
        n = len(ops)
        succ = [[] for _ in range(n)]
        npred = [0] * n
        last_q = {}
        for i, o in enumerate(ops):
            if o[0] in QUEUE_INORDER:
                p_ = last_q.get(o[0])
                if p_ is not None and all(d != p_ for d, _ in o[3]):
                    o[3].append((p_, False) if ops[p_][1] == "c" and o[1] == "c" else (p_, None))
                last_q[o[0]] = i
        for i, o in enumerate(ops):
            npred[i] = len(o[3])
            for d, _ in o[3]:
                succ[d].append(i)
        issue_only = {}
        for j_, o in enumerate(ops):
            for d, need in o[3]:
                if need is None:
                    issue_only[(d, j_)] = True
        ready_t = [0.0] * n
        fin = [0.0] * n
        eng_free = {e: 0.0 for e in ENGS}
        eng_fam = {e: None for e in ENGS}
        order = {e: [] for e in ENGS}
        future = {e: [] for e in ENGS}
        avail = {e: [] for e in ENGS}
        for i in range(n):
            if npred[i] == 0:
                heapq.heappush(future[ops[i][0]], (0.0, i))
        done = 0
        XLAT = 250.0
        while done < n:
            best = None
            for e in ENGS:
                t = eng_free[e]
                fut, av = future[e], avail[e]
                while fut and fut[0][0] <= t:
                    heapq.heappush(av, heapq.heappop(fut)[1])
                if av:
                    c_ = (t, av[0], e, True)
                elif fut:
                    c_ = (fut[0][0], fut[0][1], e, False)
                else:
                    continue
                if best is None or (c_[0], c_[1]) < (best[0], best[1]):
                    best = c_
            st, i, e, from_av = best
            if from_av:
                av = avail[e]
                if e == "act" and eng_fam[e] is not None and len(av) > 1 and ops[i][5] not in (None, eng_fam[e]):
                    cands = heapq.nsmallest(12, av)
                    alt = [c for c in cands if ops[c][5] in (None, eng_fam[e]) and c - i < 400]
                    if alt:
                        i = alt[0]
                        av.remove(i)
                        heapq.heapify(av)
                    else:
                        heapq.heappop(av)
                else:
                    heapq.heappop(av)
            else:
                heapq.heappop(future[e])
            o = ops[i]
            dur = o[4]
            if o[1] == "c":
                if o[5] is not None and eng_fam[e] is not None and eng_fam[e] != o[5]:
                    st += 1300.0
                if o[5] is not None:
                    eng_fam[e] = o[5]
                eng_free[e] = st + dur
                fin[i] = st + dur
            else:
                eng_free[e] = st + 120.0
                fin[i] = st + dur
            order[e].append(i)
            done += 1
            for j in succ[i]:
                oj = ops[j]
                lat = XLAT if (oj[0] != e or o[1] == "d") else 60.0
                t = fin[i] + lat
                if issue_only.get((i, j)):
                    t = st
                if t > ready_t[j]:
                    ready_t[j] = t
                npred[j] -= 1
                if npred[j] == 0:
                    heapq.heappush(future[oj[0]], (ready_t[j], j))
        self.order = order
        self.est_total = max(eng_free.values())

    def emit(self, scheduled=True):
        nc = self.nc
        ops = self.all
        if scheduled:
            self.schedule()
            order = self.order
        else:
            order = {e: [i for i, o in enumerate(ops) if o[0] == e] for e in ENGS}
        pos = {}
        for e in ENGS:
            for p_, i in enumerate(order[e]):
                pos[i] = p_
        ch_of, ch_val = {}, {}
        ch_next = {e: 0 for e in self.n_ch}
        ch_count = {}
        ch_prev = {}
        for e in self.n_ch:
            for i in order[e]:
                if ops[i][1] != "d":
                    continue
                c = ch_next[e]
                ch_next[e] = (c + 1) % self.n_ch[e]
                prev = ch_count.get((e, c), 0)
                ch_prev[i] = (c, prev)
                ch_count[(e, c)] = prev + 16
                ch_of[i] = c
                ch_val[i] = prev + 16
        marked = {e: set() for e in ENGS}
        waits_of = {}
        for e in ENGS:
            waited = {}
            for i in order[e]:
                o = ops[i]
                wl = []
                for d, need in o[3]:
                    od = ops[d]
                    if need is None:
                        assert od[0] == e and pos[d] < pos[i]
                        continue
                    if od[1] == "d":
                        key, val = ("d", od[0], ch_of[d]), ch_val[d]
                    else:
                        if not need:
                            assert od[0] == e and pos[d] < pos[i]
                            continue
                        key, val = ("e", od[0]), pos[d]
                    if val <= waited.get(key, -1):
                        continue
                    waited[key] = val
                    wl.append((key, val, d))
                    if key[0] == "e":
                        marked[od[0]].add(d)
                if o[1] == "d":
                    c, prev = ch_prev[i]
                    key = ("d", e, c)
                    if prev and prev > waited.get(key, -1):
                        waited[key] = prev
                        wl.append((key, prev, None))
                waits_of[i] = wl
        cnt = {}
        for e in ENGS:
            m = sorted(marked[e], key=lambda i: pos[i])
            cnt[e] = {i: k + 1 for k, i in enumerate(m)}
        with ExitStack() as st:
            esem = {e: st.enter_context(nc.semaphore("s_" + e)) for e in ENGS}
            dsem = {}
            for e, n_ in self.n_ch.items():
                for c in range(n_):
                    dsem[(e, c)] = st.enter_context(nc.semaphore("d_%s%d" % (e, c)))
            block = st.enter_context(nc.Block())
            n_ch = self.n_ch

            def run(e, h):
                for i in order[e]:
                    o = ops[i]
                    for key, val, d in waits_of[i]:
                        if key[0] == "e":
                            h.wait_ge(esem[key[1]], cnt[key[1]][d])
                        else:
                            h.wait_ge(dsem[(key[1], key[2])], val)
                    ins = o[2](h)
                    if o[1] == "d":
                        ins.then_inc(dsem[(e, ch_of[i])], 16)
                    elif i in cnt[e]:
                        ins.then_inc(esem[e], 1)
                if e in n_ch:
                    for c in range(n_ch[e]):
                        v = ch_count.get((e, c), 0)
                        if v:
                            h.wait_ge(dsem[(e, c)], v)

            @block.tensor
            def _(h):
                run("pe", h)

            @block.scalar
            def _(h):
                run("act", h)

            @block.vector
            def _(h):
                run("dve", h)

            @block.gpsimd
            def _(h):
                run("pool", h)

            @block.sync
            def _(h):
                run("sp", h)


SCHED = True
QUEUE_INORDER = ()


def build_program(n_layers=L, debug=None):
    nc = bass.Bass("TRN2", target_bir_lowering=False)
    st = ExitStack()

    def din(name, shape):
        return nc.dram_tensor(name, list(shape), F32, kind="ExternalInput").ap()

    def dout(name, shape):
        return nc.dram_tensor(name, list(shape), F32, kind="ExternalOutput").ap()

    xin = din("xin", [128, 8, TOK])
    w1d = din("w1", [2 * L, D, DFF])
    w3d = din("w3", [2 * L, D, DFF])
    w2d = din("w2", [2 * L, DFF, D])
    wind = din("win", [L, 8, D, 1024])
    wabd = din("wab", [L, D, 16])
    wod = din("wo", [L, D, D])
    gwd = din("gatew", [L, 2, 8, 128, 128])
    pvd = din("pvec", [L, 128, NPV])
    prd = din("prow", [L, 1, 32])
    sh0d = din("sh0", [L, 8, 128, NS])
    src0d = din("src0", [L, 8, 128, 3, NS])
    sgc0d = din("sgc0", [L, 24, 128, 3, NS])
    sS0d = din("sS0", [L, NS, 8, 128, 128])
    yout = dout("yout", [128, 8, TOK])
    o_ph = dout("o_ph", [L, 128, 8])
    o_prc = dout("o_prc", [L, 8, 128, 3])
    o_pS = dout("o_pS", [L, 8, 128, 128])
    o_pgc = dout("o_pgc", [L, 24, 128, 3])
    o_sh = dout("o_sh", [L, 8, 128, NS])
    o_src = dout("o_src", [L, 8, 128, 3, NS])
    o_sS = dout("o_sS", [L, NS, 8, 128, 128])
    o_sgc = dout("o_sgc", [L, 24, 128, 3, NS])
    dbg = {}
    if debug:
        for nm, shp in debug.items():
            dbg[nm] = dout("dbg_" + nm, shp)

    def sb(name, shape, dt):
        return st.enter_context(nc.sbuf_tensor(name, list(shape), dt))

    xres = sb("xres", [128, 8, TOK], F32)
    xbf = sb("xbf", [128, 8, TOK], BF16)
    mbuf = sb("mbuf", [128, 8, TOK], BF16)
    wsl = [sb("wsl%d" % i, [128, 4096], BF16) for i in range(4)]
    NSF, NSB = 6, 13
    SF = [sb("sf%d" % i, [128, 512], F32) for i in range(NSF)]
    SBf = [sb("sbf%d" % i, [128, 520 if i < 2 else 512], BF16) for i in range(NSB)]
    wo_c0 = sb("wo_c0", [128, 1024], BF16)
    ident_f = sb("ident_f", [128, 128], F32)
    ident_b = sb("ident_b", [128, 128], BF16)
    ones_f = sb("ones_f", [128, 128], F32)
    ones_b = sb("ones_b", [128, 128], BF16)
    maskU = sb("maskU", [128, 128], F32)
    maskSL = sb("maskSL", [128, 128], F32)
    sel_last = sb("sel_last", [128, 128], F32)
    BD16 = sb("BD16", [128, 128], BF16)
    OFFM = {m_: sb("OFF%d" % m_, [128, 128], BF16) for m_ in (32, 64, 128)}
    pv = sb("pv_t", [128, NPV], F32)
    pv2 = sb("pv2", [128, 48], F32)
    nsp = sb("nsp", [128, 16], F32)
    prow = sb("prow_t", [128, 32], F32)
    aexp = sb("aexp", [128, 8], F32)
    wab_b = sb("wab_b", [128, 8, 16], BF16)
    gw_b = sb("gw_b", [128, 2, 128], BF16)
    TB = {nm: sb("tb_" + nm, [128, 17, 8], F32) for nm in
          ("beta", "g", "gc", "egc", "bge", "nbeta")}
    ksq_t = sb("ksq_t", [128, 2 * NS], F32)
    glb = sb("glb", [128, 128], F32)
    egl = sb("egl", [128, 128], F32)
    ekt = sb("ekt", [128, 128], F32)
    S_f = sb("S_f", [128, 128], F32)
    S_b = sb("S_b", [128, 128], BF16)
    hcar = sb("hcar", [128, 8], F32)
    hist = sb("hist", [128, 4, 4], BF16)
    pco = sb("pco", [128, 4, 3], F32)
    dgw = sb("dgw", [128, 4, 128], BF16)
    sst = sb("sst", [128, 4, 3, NS], F32)
    sstb = sb("sstb", [128, 4, 3, NS], BF16)
    sso = sb("sso", [128, 4, 3, NS], F32)
    sh0 = sb("sh0t", [128, NS], F32)
    sho = sb("shot", [128, NS], F32)
    Sin = [sb("Sin%d" % i, [128, 128], F32) for i in range(2)]
    Sout = [sb("Sout%d" % i, [128, 128], F32) for i in range(2)]
    srow = {nm: sb("srow_" + nm, [128, 128], F32) for nm in ("eg", "beta", "nbe")}
    srow["qk"] = sb("srow_qk", [128, NS], F32)
    skq = sb("skq", [128, 2, NS], F32)
    PS = [st.enter_context(nc.psum_tensor("ps%d" % i, [128, 512], F32)) for i in range(8)]
    print("sbuf bytes remaining/partition:", nc.sbuf_bytes_remaining)

    P = Prog(nc)
    B_ = Buf
    xres_b = [[B_() for _ in TILES] for _ in range(8)]
    xbf_b = [[B_() for _ in TILES] for _ in range(8)]
    mbuf_b = [[B_() for _ in TILES] for _ in range(8)]
    wsl_b = [B_() for _ in range(4)]
    SF_b = [B_() for _ in range(NSF)]
    SB_b = [B_() for _ in range(NSB)]
    PS_b = [B_(True) for _ in range(8)]
    cst = B_()
    pvb = B_()
    tbb = B_()
    gwb = B_()
    wabb = B_()
    dgwb = B_()
    ksqb = B_()
    smb = B_()
    Sfb, Sbb = B_(), B_()
    sstb_b, ssob = B_(), B_()
    Sin_b = [B_() for _ in range(2)]
    Sout_b = [B_() for _ in range(2)]
    srb = B_()

    def fsz(ap):
        sh = ap.shape
        n_ = 1
        for d_ in sh[1:]:
            n_ *= d_
        return n_

    ACT_FAM = {AF.Exp: "el", AF.Ln: "el", AF.Sigmoid: "sg", AF.Silu: "si", AF.Sqrt: "sq"}

    def A(e, fn, r=(), w=(), dur=None, fam=None):
        if dur is None:
            dur = {"pe": 150.0, "act": 400.0, "dve": 400.0, "pool": 600.0}.get(e, 300.0)
        P.op(e, fn, reads=r, writes=w, dur=dur, fam=fam)

    def act(out, in_, func, r, w, bias=None, scale=None):
        kw = {}
        if bias is not None:
            kw["bias"] = bias
        if scale is not None:
            kw["scale"] = scale
        A("act", lambda h: h.activation(out=out, in_=in_, func=func, **kw), r, w,
          dur=200.0 + 0.65 * fsz(out), fam=ACT_FAM.get(func))

    def mm(out, lhsT, rhs, start, stop, r, w):
        n_ = fsz(rhs)
        f32 = 4.0 if rhs.dtype == F32 else 1.0
        A("pe", lambda h: h.matmul(out, lhsT=lhsT, rhs=rhs, start=start, stop=stop), r, w,
          dur=(45.0 + max(64, n_) * 0.5 * f32) if n_ > 128 else (100.0 * f32))

    def tt(out, in0, in1, op, r, w, eng="dve"):
        A(eng, lambda h: h.tensor_tensor(out=out, in0=in0, in1=in1, op=op), r, w,
          dur=(100.0 + 0.9 * fsz(out)) if eng == "dve" else (300.0 + 1.4 * fsz(out)))

    def ts(out, in0, s1, s2, op0, op1, r, w, eng="dve"):
        if s2 is None:
            A(eng, lambda h: h.tensor_scalar(out=out, in0=in0, scalar1=s1, scalar2=None, op0=op0), r, w,
              dur=(100.0 + 0.85 * fsz(out)) if eng == "dve" else (1500.0 + 1.5 * fsz(out)))
        else:
            A(eng, lambda h: h.tensor_scalar(out=out, in0=in0, scalar1=s1, scalar2=s2, op0=op0, op1=op1), r, w,
              dur=100.0 + 0.85 * fsz(out))

    def stt(out, in0, scalar, in1, op0, op1, r, w):
        A("dve", lambda h: h.scalar_tensor_tensor(out=out, in0=in0, scalar=scalar, in1=in1, op0=op0, op1=op1), r, w,
          dur=100.0 + 0.9 * fsz(out))

    def dma(eng, out, in_, r=(), w=()):
        P.dma(eng, lambda h: h.dma_start(out=out, in_=in_), reads=r, writes=w,
              dur=2200.0 + fsz(out) * out.shape[0] * 4 / 150.0)

    def cp(out, in_, r, w, eng="dve"):
        A(eng, lambda h: h.tensor_copy(out=out, in_=in_), r, w,
          dur=(100.0 + 1.0 * fsz(out)) if eng == "dve" else (300.0 + 0.9 * fsz(out)))

    def mset(ap, val, w, eng="dve"):
        A(eng, lambda h: h.memset(ap, val), (), w)

    def tr(out, in_, idn, r, w):
        A("pe", lambda h: h.transpose(out, in_, idn), r, w, dur=130.0)

    def scan(out, d0, d1, init, r, w):
        A("dve", lambda h: h.tensor_tensor_scan(out=out, data0=d0, data1=d1, initial=init,
                                                op0=ALU.mult, op1=ALU.add), r, w, dur=110.0 + 2.1 * fsz(out))

    def dbg_dump(nm, ap, r):
        if nm in dbg:
            dma("sp", dbg[nm], ap, r)

    A("pool", lambda h: h.memset(ident_f[:], 1.0), (), [cst])
    A("pool", lambda h: h.affine_select(out=ident_f[:], in_=ident_f[:], pattern=[[1, 128]],
                                        compare_op=ALU.is_equal, fill=0.0, base=0, channel_multiplier=-1), [cst], [cst])
    A("pool", lambda h: h.memset(ones_f[:], 1.0), (), [cst])
    A("pool", lambda h: h.memset(ones_b[:], 1.0), (), [cst])
    A("pool", lambda h: h.memset(maskU[:], 1.0), (), [cst])
    A("pool", lambda h: h.affine_select(out=maskU[:], in_=maskU[:], pattern=[[1, 128]],
                                        compare_op=ALU.is_ge, fill=0.0, base=0, channel_multiplier=-1), [cst], [cst])
    A("pool", lambda h: h.memset(maskSL[:], 1.0), (), [cst])
    A("pool", lambda h: h.affine_select(out=maskSL[:], in_=maskSL[:], pattern=[[-1, 128]],
                                        compare_op=ALU.is_gt, fill=0.0, base=0, channel_multiplier=1), [cst], [cst])
    A("pool", lambda h: h.memset(sel_last[:], 1.0), (), [cst])
    A("pool", lambda h: h.affine_select(out=sel_last[:], in_=sel_last[:], pattern=[[0, 128]],
                                        compare_op=ALU.is_equal, fill=0.0, base=-127, channel_multiplier=1), [cst], [cst])
    A("dve", lambda h: h.tensor_copy(out=ident_b[:], in_=ident_f[:]), [cst], [cst])
    def mk_bd(b_, dst_tile, col0):
        nb_ = 128 // b_
        E = SF[5][0:nb_, 0:128]
        A("pool", lambda h: h.memset(E, 1.0), (), [SF_b[5]])
        A("pool", lambda h: h.affine_select(out=E, in_=E, pattern=[[1, 128]], compare_op=ALU.is_ge, fill=0.0,
                                            base=0, channel_multiplier=-b_), [SF_b[5]], [SF_b[5]])
        A("pool", lambda h: h.affine_select(out=E, in_=E, pattern=[[-1, 128]], compare_op=ALU.is_ge, fill=0.0,
                                            base=b_ - 1, channel_multiplier=b_), [SF_b[5]], [SF_b[5]])
        mm(PS[7][:, col0:col0 + 128], E, E, True, True, [SF_b[5]], [PS_b[7]])
    mk_bd(16, None, 0)
    mk_bd(32, None, 128)
    mk_bd(64, None, 256)
    act(BD16[:], PS[7][:, 0:128], AF.Copy, [PS_b[7]], [cst])
    bdf = SF[4]
    act(bdf[:, 0:384], PS[7][:, 0:384], AF.Copy, [PS_b[7]], [SF_b[4]])
    tt(OFFM[32][:], bdf[:, 128:256], bdf[:, 0:128], ALU.subtract, [SF_b[4]], [cst])
    tt(OFFM[64][:], bdf[:, 256:384], bdf[:, 128:256], ALU.subtract, [SF_b[4]], [cst])
    ts(OFFM[128][:], bdf[:, 256:384], -1.0, 1.0, ALU.mult, ALU.add, [SF_b[4]], [cst])

    for k in range(8):
        for ti, (c0, n) in enumerate(TILES):
            dma("sp", xres[:, k, c0:c0 + n], xin[:, k, c0:c0 + n], (), [xres_b[k][ti]])
            act(xbf[:, k, c0:c0 + n], xres[:, k, c0:c0 + n], AF.Copy, [xres_b[k][ti]], [xbf_b[k][ti]])
            ts(xres[:, k, c0:c0 + n], xres[:, k, c0:c0 + n], ALPHA, None, ALU.mult, None,
               [xres_b[k][ti]], [xres_b[k][ti]])

    wctr = [0]

    def wslot():
        i = wctr[0] % 4
        wctr[0] += 1
        return i

    def ffn(fi):
        w1v = w1d[fi].rearrange("(k p) n -> p k n", p=128)
        w3v = w3d[fi].rearrange("(k p) n -> p k n", p=128)
        w2v = w2d[fi].rearrange("(j p) n -> p j n", p=128)
        ftmp_b = SF_b[0:5]
        subs = [(0, 4), (4, 4), (8, 4), (12, 4), (16, 4), (20, 2)]
        groups = [(0, 1), (2, 3), (4, 5)]

        def load13(su):
            j0, nj = subs[su]
            s1, s3 = wslot(), wslot()
            for s, wv in ((s1, w1v), (s3, w3v)):
                dst = wsl[s][:].rearrange("p (k n) -> p k n", k=8)[:, :, 0:nj * 128]
                dma("pool", dst, wv[:, :, j0 * 128:(j0 + nj) * 128], (), [wsl_b[s]])
            return s1, s3

        def load2(su):
            j0, nj = subs[su]
            s2 = wslot()
            dst = wsl[s2][:].rearrange("p (j n) -> p j n", j=4)[:, 0:nj, :]
            dma("pool", dst, w2v[:, j0:j0 + nj, :], (), [wsl_b[s2]])
            return s2

        for (sa, sbb) in groups:
            l13 = {}
            l13[sa] = load13(sa)
            l13[sbb] = load13(sbb)
            for su in (sa, sbb):
                j0, nj = subs[su]
                s1, s3 = l13[su]
                hj0 = 0 if su == sa else 4
                for j in range(nj):
                    for (s, which) in ((s1, 0), (s3, 1)):
                        wv = wsl[s][:].rearrange("p (k n) -> p k n", k=8)
                        for k in range(8):
                            for ti, (c0, n) in enumerate(TILES):
                                mm(PS[ti][:, 0:n], wv[:, k, j * 128:(j + 1) * 128], xbf[:, k, c0:c0 + n],
                                   k == 0, k == 7, [wsl_b[s], xbf_b[k][ti]], [PS_b[ti]])
                        for ti, (c0, n) in enumerate(TILES):
                            fsl = SF[ti][:, 0:n]
                            if which == 0:
                                act(fsl, PS[ti][:, 0:n], AF.Silu, [PS_b[ti]], [ftmp_b[ti]])
                            else:
                                tt(mbuf[:, hj0 + j, c0:c0 + n], fsl, PS[ti][:, 0:n], ALU.mult,
                                   [ftmp_b[ti], PS_b[ti]], [mbuf_b[hj0 + j][ti]])
            s2a = load2(sa)
            s2b = load2(sbb)
            parts = [(s2a, subs[sa][1], 0), (s2b, subs[sbb][1], 4)]
            tot = subs[sa][1] + subs[sbb][1]
            last_group = (sa, sbb) == groups[-1]
            tile_sets = [[t_] for t_ in range(len(TILES))] if last_group else [list(range(len(TILES)))]
            for tset in tile_sets:
                for m in range(8):
                    cnt = 0
                    for (s2, nj, hj0) in parts:
                        wv = wsl[s2][:].rearrange("p (j n) -> p j n", j=4)
                        for j in range(nj):
                            for ti in tset:
                                c0, n = TILES[ti]
                                pbk = ti if not last_group else (ti + m) % 5
                                mm(PS[pbk][:, 0:n], wv[:, j, m * 128:(m + 1) * 128], mbuf[:, hj0 + j, c0:c0 + n],
                                   cnt == 0, cnt == tot - 1, [wsl_b[s2], mbuf_b[hj0 + j][ti]], [PS_b[pbk]])
                            cnt += 1
                    for ti in tset:
                        c0, n = TILES[ti]
                        pbk = ti if not last_group else (ti + m) % 5
                        stt(xres[:, m, c0:c0 + n], PS[pbk][:, 0:n], 0.5, xres[:, m, c0:c0 + n], ALU.mult, ALU.add,
                            [PS_b[pbk], xres_b[m][ti]], [xres_b[m][ti]])

    def layernorm(i_ln, final):
        for ti, (c0, n) in enumerate(TILES):
            pm, pq = 5, 6
            for k in range(8):
                xb_, sq_ = SBf[0 + (k % 2)], SBf[2 + (k % 2)]
                xbb, sqb = SB_b[0 + (k % 2)], SB_b[2 + (k % 2)]
                cp(xb_[:, 0:n], xres[:, k, c0:c0 + n], [xres_b[k][ti]], [xbb], eng="pool")
                if k % 2 == 0:
                    tt(sq_[:, 0:n], xres[:, k, c0:c0 + n], xres[:, k, c0:c0 + n], ALU.mult, [xres_b[k][ti]], [sqb], eng="pool")
                else:
                    act(sq_[:, 0:n], xres[:, k, c0:c0 + n], AF.Square, [xres_b[k][ti]], [sqb])
                mm(PS[pm][:, 0:n], ones_b[:], xb_[:, 0:n], k == 0, k == 7, [cst, xbb], [PS_b[pm]])
                mm(PS[pq][:, 0:n], ones_b[:], sq_[:, 0:n], k == 0, k == 7, [cst, sqb], [PS_b[pq]])
            mean, msq, var, Aa, Bb = SF[0], SF[1], SF[2], SF[3], SF[4]
            act(mean[:, 0:n], PS[pm][:, 0:n], AF.Copy, [PS_b[pm]], [SF_b[0]], scale=1.0 / D)
            act(msq[:, 0:n], PS[pm][:, 0:n], AF.Square, [PS_b[pm]], [SF_b[1]], scale=1.0 / D)
            stt(var[:, 0:n], PS[pq][:, 0:n], 1.0 / D, msq[:, 0:n], ALU.mult, ALU.subtract,
                [PS_b[pq], SF_b[1]], [SF_b[2]])
            ts(var[:, 0:n], var[:, 0:n], LN_EPS, None, ALU.add, None, [SF_b[2]], [SF_b[2]])
            act(var[:, 0:n], var[:, 0:n], AF.Ln, [SF_b[2]], [SF_b[2]])
            act(Aa[:, 0:n], var[:, 0:n], AF.Exp, [SF_b[2]], [SF_b[3]], scale=-0.5)
            stt(Bb[:, 0:n], mean[:, 0:n], -1.0, Aa[:, 0:n], ALU.mult, ALU.mult, [SF_b[0], SF_b[3]], [SF_b[4]])
            for k in range(8):
                z, zb = SF[k % 2], SF_b[k % 2]
                tt(z[:, 0:n], xres[:, k, c0:c0 + n], Aa[:, 0:n], ALU.mult, [xres_b[k][ti], SF_b[3]], [zb])
                tt(z[:, 0:n], z[:, 0:n], Bb[:, 0:n], ALU.add, [zb, SF_b[4]], [zb])
                gi, bi = 16 * i_ln + k, 16 * i_ln + 8 + k
                if final:
                    ts(xres[:, k, c0:c0 + n], z[:, 0:n], pv[:, gi:gi + 1], pv[:, bi:bi + 1], ALU.mult, ALU.add,
                       [zb, pvb], [xres_b[k][ti]])
                    dma("sp", yout[:, k, c0:c0 + n], xres[:, k, c0:c0 + n], [xres_b[k][ti]])
                else:
                    act(xbf[:, k, c0:c0 + n], z[:, 0:n], AF.Identity, [zb, pvb], [xbf_b[k][ti]],
                        bias=pv[:, bi:bi + 1], scale=pv[:, gi:gi + 1])
                    ts(xres[:, k, c0:c0 + n], z[:, 0:n], pv2[:, gi:gi + 1], pv2[:, bi:bi + 1], ALU.mult, ALU.add,
                       [zb, pvb], [xres_b[k][ti]])

    def softplus(dst, src, tmps, r, w):
        t0, t1, t2 = tmps
        act(t0, src, AF.Abs, r, w)
        act(t0, t0, AF.Exp, w, w, scale=-1.0)
        ts(t1, t0, 2.0, None, ALU.add, None, w, w)
        A("dve", lambda h: h.reciprocal(out=t1, in_=t1), w, w)
        tt(t1, t1, t0, ALU.mult, w, w)
        tt(t2, t1, t1, ALU.mult, w, w)
        ts(t0, t2, 1.0 / 11, 1.0 / 9, ALU.mult, ALU.add, w, w)
        for cf in (1.0 / 7, 1.0 / 5, 1.0 / 3, 1.0):
            tt(t0, t0, t2, ALU.mult, w, w)
            ts(t0, t0, cf, None, ALU.add, None, w, w)
        tt(t0, t0, t1, ALU.mult, w, w)
        ts(t1, src, 0.0, None, ALU.max, None, list(r) + list(w), w)
        stt(dst, t0, 2.0, t1, ALU.mult, ALU.add, w, w)

    def layer_params(l):
        dma("sp", pv[:], pvd[l], (), [pvb])
        dma("sp", prow[:], prd[l].partition_broadcast(128), (), [pvb])
        dma("pool", wab_b[:], wabd[l].rearrange("(k p) n -> p k n", p=128), (), [wabb])
        ts(pv2[:], pv[:, 0:48], ALPHA, None, ALU.mult, None, [pvb], [pvb])
        wl = [pvb] + SF_b[0:5]
        tn = [SF[i][:, 0:8] for i in range(5)]
        ts(tn[3], pv[:, 168:176], -1.0, None, ALU.mult, None, [pvb], wl)
        softplus(tn[4], tn[3], (tn[0], tn[1], tn[2]), wl, wl)
        ts(nsp[:, 0:8], tn[4], -8.0, None, ALU.mult, None, wl, wl)
        ts(nsp[:, 8:16], tn[4], -16.0, None, ALU.mult, None, wl, wl)
        act(aexp[:], prow[:, 16:24], AF.Exp, [pvb], [pvb])

    def gate_tables(l):
        pab = 7
        for nb in range(17):
            c0 = nb * 128
            nt = 128 if nb < 16 else NS
            ti = min(nb // 4, 4)
            for k in range(8):
                mm(PS[pab][0:nt, nb * 16:(nb + 1) * 16], xbf[:, k, c0:c0 + nt], wab_b[:, k, :], k == 0, k == 7,
                   [xbf_b[k][ti], wabb], [PS_b[pab]])
        ab_f = SF[4][:, 0:272].rearrange("p (b n) -> p b n", n=16)
        abv = ab_f
        psv = PS[pab][:, 0:272].rearrange("p (b n) -> p b n", n=16)
        brow = prow[:, 0:16].unsqueeze(1).to_broadcast([128, 17, 16])
        tt(abv, psv, brow, ALU.add, [PS_b[pab], pvb], [tbb, SF_b[4]])
        a_v, b_v = ab_f[:, :, 0:8], ab_f[:, :, 8:16]
        T = {k_: v_[:] for k_, v_ in TB.items()}
        for i_ in range(4):
            T["t%d" % i_] = SF[i_][:, 0:136].rearrange("p (b n) -> p b n", n=8)
        wl = [tbb] + SF_b[0:5]
        act(T["beta"], b_v, AF.Sigmoid, wl, [tbb])
        ts(T["nbeta"], T["beta"], -1.0, None, ALU.mult, None, [tbb], [tbb])
        dtb = prow[:, 24:32].unsqueeze(1).to_broadcast([128, 17, 8])
        tt(T["t3"], a_v, dtb, ALU.add, [tbb, pvb], wl)
        softplus(T["g"], T["t3"], (T["t0"], T["t1"], T["t2"]), wl, wl)
        aex = aexp[:].unsqueeze(1).to_broadcast([128, 17, 8])
        stt(T["g"], T["g"], -1.0, aex, ALU.mult, ALU.mult, [tbb, pvb], [tbb])
        gflat = TB["g"][:].rearrange("p b n -> p (b n)")
        gcflat = TB["gc"][:].rearrange("p b n -> p (b n)")
        mm(PS[pab][:, 0:128], maskU[:], gflat[:, 0:128], True, True, [cst, tbb], [PS_b[pab]])
        act(gcflat[:, 0:128], PS[pab][:, 0:128], AF.Copy, [PS_b[pab]], [tbb])
        act(gcflat[:, 128:136], gflat[:, 128:136], AF.Copy, [tbb], [tbb])
        mm(PS[pab][:, 128:256], sel_last[:], gcflat[:, 0:128], True, True, [cst, tbb], [PS_b[pab]])
        act(glb[:], PS[pab][:, 128:256], AF.Copy, [PS_b[pab]], [tbb])
        act(egl[:], glb[:], AF.Exp, [tbb], [tbb])
        tt(ekt[:], glb[:], gcflat[:, 0:128], ALU.subtract, [tbb], [tbb])
        act(ekt[:], ekt[:], AF.Exp, [tbb], [tbb])
        act(T["egc"], T["gc"], AF.Exp, [tbb], [tbb])
        tt(T["bge"], T["beta"], T["egc"], ALU.mult, [tbb], [tbb])
        for nm, src in (("eg", TB["egc"]), ("beta", TB["beta"])):
            sv = src[0:NS, 16, :].unsqueeze(2).to_broadcast([NS, 8, NS])
            iv = ident_f[0:NS, 0:NS].unsqueeze(1).to_broadcast([NS, 8, NS])
            sM = SF[5][0:NS, 0:128]
            tt(sM.rearrange("p (h s) -> p h s", h=8), sv, iv, ALU.mult, [tbb, cst], [srb, SF_b[5]])
            mm(PS[pab][:, 256:384], ones_f[0:NS, :], sM, True, True, [cst, srb, SF_b[5]], [PS_b[pab]])
            act(srow[nm][:], PS[pab][:, 256:384], AF.Copy, [PS_b[pab]], [srb])
        stt(srow["nbe"][:], srow["beta"][:], -1.0, srow["eg"][:], ALU.mult, ALU.mult, [srb], [srb])

    def rbank(X):
        for _ in range(8):
            i = X.banks[X.rot % len(X.banks)]
            X.rot += 1
            if i not in X.held:
                return i
        raise RuntimeError("no free psum bank")

    class Ctx:
        pass

    def mk_ctx(tid):
        X = Ctx()
        X.tid = tid
        X.banks = [0, 1, 2, 3] if tid == 0 else [4, 5, 6, 7]
        X.held = set()
        X.rot = 0
        X.slots = (0, 1) if tid == 0 else (2, 3)
        for nm in ("smb", "gwb", "dgwb", "sstb_b", "ssob", "Sfb", "Sbb", "ksqb", "srq", "wob"):
            setattr(X, nm, Buf())
        X.SF_b = [Buf() for _ in range(6)] if tid else SF_b
        X.SB_b = [Buf() for _ in range(13)] if tid else SB_b
        if tid == 0:
            X.SF = [t[:] for t in SF]
            X.SBf = [t[:] for t in SBf]
            X.hist, X.pco, X.dgw, X.gw_b = hist[:], pco[:], dgw[:], gw_b[:]
            X.sst, X.sstb, X.sso, X.sh0, X.sho = sst[:], sstb[:], sso[:], sh0[:], sho[:]
            X.S_f, X.S_b, X.skq, X.ksq_t, X.srow_qk = S_f[:], S_b[:], skq[:], ksq_t[:], srow["qk"][:]
            X.wo_c = wo_c0[:]
        else:
            mflat = mbuf[:].rearrange("p a b -> p (a b)")
            off = [0]

            def cv(n_, dt=BF16):
                ne = n_ if dt == BF16 else 2 * n_
                a = off[0]
                off[0] += ne + (ne % 2)
                v = mflat[:, a:a + ne]
                return v if dt == BF16 else v.bitcast(F32)
            X.SF = [cv(512, F32) for _ in range(6)]
            X.SBf = [cv(520 if i < 2 else 512) for i in range(13)]
            X.hist = cv(16).rearrange("p (a b) -> p a b", a=4)
            X.pco = cv(12, F32).rearrange("p (a b) -> p a b", a=4)
            X.dgw = cv(512).rearrange("p (a b) -> p a b", a=4)
            X.gw_b = cv(256).rearrange("p (a b) -> p a b", a=2)
            X.sst = cv(192, F32).rearrange("p (a b c) -> p a b c", a=4, b=3)
            X.sstb = cv(192).rearrange("p (a b c) -> p a b c", a=4, b=3)
            X.sso = cv(192, F32).rearrange("p (a b c) -> p a b c", a=4, b=3)
            X.sh0, X.sho = cv(NS, F32), cv(NS, F32)
            X.S_f, X.S_b = cv(128, F32), cv(128)
            X.skq = cv(2 * NS, F32).rearrange("p (a b) -> p a b", a=2)
            X.ksq_t, X.srow_qk = cv(2 * NS, F32), cv(NS, F32)
            X.wo_c = cv(1024)
            assert off[0] <= 8 * TOK, off[0]
        return X

    CTX = [mk_ctx(0), mk_ctx(1)]
    t1_bufs = CTX[1].SF_b + CTX[1].SB_b + [getattr(CTX[1], nm) for nm in
                                           ("smb", "gwb", "dgwb", "sstb_b", "ssob", "Sfb", "Sbb", "ksqb", "srq", "wob")]

    def chunk_body(l, c, X):
        T = TB
        sA, sB = X.slots
        wv_d = wind[l, c].rearrange("(k p) n -> p k n", p=128)
        for s, b0 in ((sA, 0), (sB, 4)):
            dst = wsl[s][:].rearrange("p (k n) -> p k n", k=8)
            dma("pool", dst, wv_d[:, :, b0 * 128:(b0 + 4) * 128], (), [wsl_b[s]])
        dma("pool", X.gw_b[:], gwd[l, :, c].rearrange("g p n -> p g n"), (), [X.gwb])
        dma("pool", X.wo_c[:], wod[l, c * 128:(c + 1) * 128, :], (), [X.wob])
        wA = wsl[sA][:].rearrange("p (k n) -> p k n", k=8)
        wB = wsl[sB][:].rearrange("p (k n) -> p k n", k=8)

        def wchunk(k, blk):
            if blk < 4:
                return wA[:, k, blk * 128:(blk + 1) * 128], wsl_b[sA]
            return wB[:, k, (blk - 4) * 128:(blk - 3) * 128], wsl_b[sB]

        cw_base = (112 + c, 176 + c, 176 + 8 + c, 176 + 16 + c)
        dma("sp", X.sst[:, 0], src0d[l, c], (), [X.sstb_b])
        for q_ in range(3):
            dma("sp", X.sst[:, 1 + q_], sgc0d[l, q_ * 8 + c], (), [X.sstb_b])
        dma("sp", X.sh0[:], sh0d[l, c], (), [X.sstb_b])
        act(X.sstb[:], X.sst[:], AF.Copy, [X.sstb_b], [X.sstb_b])
        mset(hcar[:, c:c + 1], 0.0, [X.smb])
        mset(X.hist[:], 0.0, [X.smb])
        mset(X.S_f[:], 0.0, [X.Sfb])
        mset(X.S_b[:], 0.0, [X.Sbb])

        for ti, (c0, n) in enumerate(TILES):
            samp = ti == 4

            if samp:
                sbl = []
                for tix in range(7, 13):
                    for half in (0, 1):
                        sbl.append((X.SBf[tix][:, half * 256:(half + 1) * 256].bitcast(F32), X.SB_b[tix]))
                for tix in range(0, 7):
                    sbl.append((X.SBf[tix][:, 256:512].bitcast(F32), X.SB_b[tix]))
                for q_ in range(1, 4):
                    sbl.append((X.SF[2][0:NS, q_ * 128:(q_ + 1) * 128], X.SF_b[2]))
                sbb_ = []
                for (_, par) in sbl:
                    cb = Buf()
                    cb.r = list(par.r) + ([par.w] if par.w is not None else [])
                    sbb_.append(cb)
                for s_ in range(NS):
                    dma("sp", sbl[s_][0], sS0d[l, s_, c], (), [sbb_[s_]])

            def proj(blk):
                pb = rbank(X)
                for k in range(8):
                    wv_, wb_ = wchunk(k, blk)
                    mm(PS[pb][:, 0:n], wv_, xbf[:, k, c0:c0 + n], k == 0, k == 7, [wb_, xbf_b[k][ti]], [PS_b[pb]])
                return pb

            def bias(blk):
                col = 48 + c * 8 + blk
                return pv[:, col:col + 1]

            def conv(blk, hb, prex_i):
                pb = proj(blk)
                prex, pxb = X.SBf[prex_i], X.SB_b[prex_i]
                for j in range(4):
                    col = cw_base[hb] + j * (8 if hb == 0 else 24)
                    ts(X.dgw[:, j, :], ident_f[:], pv[:, col:col + 1], None, ALU.mult, None, [cst, pvb], [X.dgwb])
                if not samp:
                    cp(prex[:, 0:3], X.hist[:, hb, 0:3], [X.smb], [pxb])
                    act(prex[:, 3:3 + n], PS[pb][:, 0:n], AF.Identity, [PS_b[pb], pvb], [pxb], bias=bias(blk))
                    if ti < 3:
                        cp(X.hist[:, hb, 0:3], prex[:, n:n + 3], [pxb], [X.smb])
                    else:
                        act(X.pco[:, hb, :], PS[pb][:, n - 3:n], AF.Identity, [PS_b[pb], pvb], [X.smb], bias=bias(blk))
                    pc = rbank(X)
                    for j in range(4):
                        mm(PS[pc][:, 0:n], X.dgw[:, j, :], prex[:, j:j + n], j == 0, j == 3, [X.dgwb, pxb], [PS_b[pc]])
                else:
                    act(prex[:, 0:n], PS[pb][:, 0:n], AF.Identity, [PS_b[pb], pvb], [pxb], bias=bias(blk))
                    act(X.sso[:, hb, 2, :], PS[pb][:, 0:n], AF.Identity, [PS_b[pb], pvb], [X.ssob], bias=bias(blk))
                    cp(X.sso[:, hb, 0:2, :], X.sst[:, hb, 1:3, :], [X.sstb_b], [X.ssob])
                    pc = rbank(X)
                    for j in range(3):
                        mm(PS[pc][:, 0:n], X.dgw[:, j, :], X.sstb[:, hb, j, :], j == 0, False, [X.dgwb, X.sstb_b], [PS_b[pc]])
                    mm(PS[pc][:, 0:n], X.dgw[:, 3, :], prex[:, 0:n], False, True, [X.dgwb, pxb], [PS_b[pc]])
                return pc

            pc = conv(0, 0, 0)
            xc, xcb = X.SF[0], X.SF_b[0]
            act(xc[:, 0:n], PS[pc][:, 0:n], AF.Identity, [PS_b[pc], pvb], [xcb], bias=pv[:, 144 + c:145 + c])
            xcbf, xcbfb = X.SBf[1], X.SB_b[1]
            act(xcbf[:, 0:n], PS[pc][:, 0:n], AF.Identity, [PS_b[pc], pvb], [xcbfb], bias=pv[:, 144 + c:145 + c])
            pr_, pi_ = rbank(X), rbank(X)
            mm(PS[pr_][:, 0:n], X.gw_b[:, 0, :], xcbf[:, 0:n], True, True, [X.gwb, xcbfb], [PS_b[pr_]])
            mm(PS[pi_][:, 0:n], X.gw_b[:, 1, :], xcbf[:, 0:n], True, True, [X.gwb, xcbfb], [PS_b[pi_]])
            rr, rrb = X.SF[1], X.SF_b[1]
            ig, igb = X.SF[2], X.SF_b[2]
            act(rr[:, 0:n], PS[pr_][:, 0:n], AF.Sigmoid, [PS_b[pr_], pvb], [rrb], bias=pv[:, 152 + c:153 + c])
            act(ig[:, 0:n], PS[pi_][:, 0:n], AF.Sigmoid, [PS_b[pi_], pvb], [igb], bias=pv[:, 160 + c:161 + c])
            aa, aab = X.SF[3], X.SF_b[3]
            act(aa[:, 0:n], rr[:, 0:n], AF.Exp, [rrb, pvb], [aab], scale=nsp[:, c:c + 1])
            act(rr[:, 0:n], rr[:, 0:n], AF.Exp, [rrb, pvb], [rrb], scale=nsp[:, 8 + c:9 + c])
            act(rr[:, 0:n], rr[:, 0:n], AF.Ln, [rrb], [rrb], bias=1.0, scale=-1.0)
            act(rr[:, 0:n], rr[:, 0:n], AF.Exp, [rrb], [rrb], scale=0.5)
            tt(ig[:, 0:n], ig[:, 0:n], xc[:, 0:n], ALU.mult, [igb, xcb], [igb])
            tt(ig[:, 0:n], ig[:, 0:n], rr[:, 0:n], ALU.mult, [igb, rrb], [igb])
            hs, hsb = X.SF[0], X.SF_b[0]
            if not samp:
                scan(hs[:, 0:n], aa[:, 0:n], ig[:, 0:n], hcar[:, c:c + 1], [aab, igb, X.smb], [hsb])
                cp(hcar[:, c:c + 1], hs[:, n - 1:n], [hsb], [X.smb])
                if ti == 3:
                    dma("sp", o_prc[l, c], X.pco[:, 0, :], [X.smb])
            else:
                tt(hs[:, 0:n], aa[:, 0:n], X.sh0[:], ALU.mult, [aab, X.sstb_b], [hsb])
                tt(hs[:, 0:n], hs[:, 0:n], ig[:, 0:n], ALU.add, [hsb, igb], [hsb])
                cp(X.sho[:], hs[:, 0:n], [hsb], [X.ssob])
                dma("sp", o_sh[l, c], X.sho[:], [X.ssob])
                dma("sp", o_src[l, c], X.sso[:, 0], [X.ssob])
            pb = proj(1)
            yy, yyb = X.SF[1], X.SF_b[1]
            act(yy[:, 0:n], PS[pb][:, 0:n], AF.Identity, [PS_b[pb], pvb], [yyb], bias=bias(1))
            y2, y2b = X.SF[2], X.SF_b[2]
            tt(y2[:, 0:n], yy[:, 0:n], yy[:, 0:n], ALU.mult, [yyb], [y2b])
            ts(y2[:, 0:n], y2[:, 0:n], 0.044715, 1.0, ALU.mult, ALU.add, [y2b], [y2b])
            tt(y2[:, 0:n], y2[:, 0:n], yy[:, 0:n], ALU.mult, [y2b, yyb], [y2b])
            act(y2[:, 0:n], y2[:, 0:n], AF.Sigmoid, [y2b], [y2b], scale=1.5957691216057308)
            tt(yy[:, 0:n], yy[:, 0:n], y2[:, 0:n], ALU.mult, [yyb, y2b], [yyb])
            tt(hs[:, 0:n], hs[:, 0:n], yy[:, 0:n], ALU.mult, [hsb, yyb], [hsb])
            pb = proj(2)
            act(y2[:, 0:n], PS[pb][:, 0:n], AF.Sigmoid, [PS_b[pb], pvb], [y2b], bias=bias(2))
            ma, mab = X.SF[5], X.SF_b[5]
            tt(ma[:, 0:n], hs[:, 0:n], y2[:, 0:n], ALU.mult, [hsb, y2b], [mab])

            h_ = c
            pc = conv(3, 1, 0)
            qf, qfb = X.SF[0], X.SF_b[0]
            act(qf[:, 0:n], PS[pc][:, 0:n], AF.Sigmoid, [PS_b[pc]], [qfb])
            tt(qf[:, 0:n], qf[:, 0:n], PS[pc][:, 0:n], ALU.mult, [qfb, PS_b[pc]], [qfb])
            pc = conv(4, 2, 1)
            kf, kfb = X.SF[1], X.SF_b[1]
            act(kf[:, 0:n], PS[pc][:, 0:n], AF.Sigmoid, [PS_b[pc]], [kfb])
            tt(kf[:, 0:n], kf[:, 0:n], PS[pc][:, 0:n], ALU.mult, [kfb, PS_b[pc]], [kfb])
            pc = conv(5, 3, 0)
            vT, vTb = X.SBf[2], X.SB_b[2]
            vsg, vsgb = X.SF[2], X.SF_b[2]
            act(vsg[:, 0:n], PS[pc][:, 0:n], AF.Sigmoid, [PS_b[pc]], [vsgb])
            tt(vT[:, 0:n], vsg[:, 0:n], PS[pc][:, 0:n], ALU.mult, [vsgb, PS_b[pc]], [vTb])
            if samp:
                for q_ in range(3):
                    dma("sp", o_sgc[l, q_ * 8 + c], X.sso[:, 1 + q_], [X.ssob])
            elif ti == 3:
                for q_ in range(3):
                    dma("sp", o_pgc[l, q_ * 8 + c], X.pco[:, 1 + q_, :], [X.smb])
            sqs, sqsb = X.SBf[3], X.SB_b[3]
            rs, rsb = X.SF[2], X.SF_b[2]
            qT, qTb = X.SBf[4], X.SB_b[4]
            kT, kTb = X.SBf[5], X.SB_b[5]
            for (src, srcb, dstT, dstTb, scl) in ((qf, qfb, qT, qTb, 128 ** -0.5), (kf, kfb, kT, kTb, 1.0)):
                act(sqs[:, 0:n], src[:, 0:n], AF.Square, [srcb], [sqsb])
                pq = rbank(X)
                mm(PS[pq][:, 0:n], ones_b[:], sqs[:, 0:n], True, True, [cst, sqsb], [PS_b[pq]])
                ts(rs[:, 0:n], PS[pq][:, 0:n], RMS_EPS, None, ALU.add, None, [PS_b[pq]], [rsb])
                act(rs[:, 0:n], rs[:, 0:n], AF.Ln, [rsb], [rsb])
                act(rs[:, 0:n], rs[:, 0:n], AF.Exp, [rsb], [rsb], scale=-0.5)
                stt(src[:, 0:n], src[:, 0:n], scl, rs[:, 0:n], ALU.mult, ALU.mult, [srcb, rsb], [srcb])
                act(dstT[:, 0:n], src[:, 0:n], AF.Copy, [srcb], [dstTb])

            if samp:
                cols = slice(h_ * NS, (h_ + 1) * NS)
                cp(X.skq[:, 0, :], kf[:, 0:n], [kfb], [X.srq])
                cp(X.skq[:, 1, :], qf[:, 0:n], [qfb], [X.srq])
                qk_, qkb = X.SF[3], X.SF_b[3]
                tt(qk_[:, 0:n], qf[:, 0:n], kf[:, 0:n], ALU.mult, [qfb, kfb], [qkb])
                pq = rbank(X)
                mm(PS[pq][:, 0:n], ones_f[:], qk_[:, 0:n], True, True, [cst, qkb], [PS_b[pq]])
                act(X.srow_qk[:], PS[pq][:, 0:n], AF.Copy, [PS_b[pq]], [X.srq])
                pt = rbank(X)
                tr(PS[pt][0:NS, 0:128], kf[:, 0:n], ident_f[:], [kfb, cst], [PS_b[pt]])
                kpm, kpmb = X.SF[4], X.SF_b[4]
                act(kpm[0:NS, 0:128], PS[pt][0:NS, 0:128], AF.Copy, [PS_b[pt]], [kpmb])
                pks = X.banks[0]
                X.held = {pks}
                vfs, vfsb = X.SF[1], X.SF_b[1]
                act(vfs[:, 0:n], vT[:, 0:n], AF.Copy, [vTb], [vfsb])
                for s_ in range(NS):
                    mm(PS[pks][:, 2 * s_:2 * s_ + 2], sbl[s_][0], X.skq[:, :, s_], True, True,
                       [sbb_[s_], X.srq], [PS_b[pks]])
                act(X.ksq_t[:], PS[pks][:, 0:2 * NS], AF.Copy, [PS_b[pks]], [X.ksqb])
                ksv = X.ksq_t[:].rearrange("p (s t) -> p t s", t=2)
                X.held = set()
                vn, vnb = X.SF[3], X.SF_b[3]
                tt(vn[:, 0:n], vfs[:, 0:n], srow["beta"][:, cols], ALU.mult, [vfsb, srb], [vnb])
                tt(vfs[:, 0:n], ksv[:, 0, :], srow["nbe"][:, cols], ALU.mult, [X.ksqb, srb], [vfsb])
                tt(vn[:, 0:n], vn[:, 0:n], vfs[:, 0:n], ALU.add, [vnb, vfsb], [vnb])
                oo, oob = X.SF[0], X.SF_b[0]
                tt(oo[:, 0:n], ksv[:, 1, :], srow["eg"][:, cols], ALU.mult, [X.ksqb, srb], [oob])
                tt(vfs[:, 0:n], vn[:, 0:n], X.srow_qk[:], ALU.mult, [vnb, X.srq], [vfsb])
                tt(oo[:, 0:n], oo[:, 0:n], vfs[:, 0:n], ALU.add, [oob, vfsb], [oob])
                pt = rbank(X)
                tr(PS[pt][0:NS, 0:128], vn[:, 0:n], ident_f[:], [vnb, cst], [PS_b[pt]])
                vpm, vpmb = X.SF[1], X.SF_b[1]
                act(vpm[0:NS, 0:128], PS[pt][0:NS, 0:128], AF.Copy, [PS_b[pt]], [vpmb])
                for s_ in range(NS):
                    p_ = h_ * NS + s_
                    vm, vmb = sbl[19 + s_ % 3][0], sbb_[19 + s_ % 3]
                    ts(vm, vpm[0:NS, 0:128], ident_f[0:NS, s_:s_ + 1], None, ALU.mult, None, [vpmb, cst], [vmb])
                    po = rbank(X)
                    mm(PS[po][:, 0:128], kpm[0:NS, 0:128], vm, True, True, [kpmb, vmb], [PS_b[po]])
                    so_, sob_ = sbl[16 + s_ % 3][0], sbb_[16 + s_ % 3]
                    stt(so_, sbl[s_][0], srow["eg"][:, p_:p_ + 1], PS[po][:, 0:128], ALU.mult, ALU.add,
                        [sbb_[s_], srb, PS_b[po]], [sob_])
                    dma("sp", o_sS[l, s_, h_], so_, [sob_])
                for (_, par), cb in zip(sbl, sbb_):
                    par.r.extend(cb.r)
                    if cb.w is not None:
                        par.r.append(cb.w)
                po_hold = None
                of, ofb = oo, oob
            else:
                nb0 = ti * 4
                pg, pO, pA, pB = X.banks[0], X.banks[0], X.banks[1], X.banks[2]
                X.held = {pg}
                for nn in range(4):
                    dg_, dgb = X.SF[3], X.SF_b[3]
                    ts(dg_[:, 0:128], ident_f[:], T["gc"][:, nb0 + nn, h_:h_ + 1], None, ALU.mult, None,
                       [cst, tbb], [dgb])
                    mm(PS[pg][:, nn * 128:(nn + 1) * 128], ones_f[:], dg_[:, 0:128], True, True, [cst, dgb], [PS_b[pg]])
                egr, egrb = X.SF[3], X.SF_b[3]
                act(egr[:, 0:n], PS[pg][:, 0:n], AF.Exp, [PS_b[pg]], [egrb])
                qdT, qdTb = X.SBf[6], X.SB_b[6]
                tt(qdT[:, 0:n], qf[:, 0:n], egr[:, 0:n], ALU.mult, [qfb, egrb], [qdTb])
                pk_ = rbank(X)
                pkb = PS[pk_][:].bitcast(BF16)
                for nn in range(4):
                    tr(pkb[:, nn * 128:(nn + 1) * 128], kT[:, nn * 128:(nn + 1) * 128], ident_b[:], [kTb, cst], [PS_b[pk_]])
                kbg, kbgb = X.SBf[7], X.SB_b[7]
                ktm, ktmb = X.SBf[8], X.SB_b[8]
                pk3 = pkb[:, 0:512].rearrange("p (a d) -> p a d", a=4)
                bge_v = T["bge"][:, nb0:nb0 + 4, h_:h_ + 1].to_broadcast([128, 4, 128])
                ekt_v = ekt[:].rearrange("p (b e) -> p b e", e=8)[:, nb0:nb0 + 4, h_:h_ + 1].to_broadcast([128, 4, 128])
                tt(kbg[:, 0:512].rearrange("p (a d) -> p a d", a=4), pk3, bge_v, ALU.mult, [PS_b[pk_], tbb], [kbgb])
                tt(ktm[:, 0:512].rearrange("p (a d) -> p a d", a=4), pk3, ekt_v, ALU.mult, [PS_b[pk_], tbb], [ktmb])
                pv_ = rbank(X)
                pvb_ = PS[pv_][:].bitcast(BF16)
                for nn in range(4):
                    tr(pvb_[:, nn * 128:(nn + 1) * 128], vT[:, nn * 128:(nn + 1) * 128], ident_b[:], [vTb, cst], [PS_b[pv_]])
                vbt, vbtb = X.SBf[9], X.SB_b[9]
                be_v = T["beta"][:, nb0:nb0 + 4, h_:h_ + 1].to_broadcast([128, 4, 128])
                tt(vbt[:, 0:512].rearrange("p (a d) -> p a d", a=4), pvb_[:, 0:512].rearrange("p (a d) -> p a d", a=4),
                   be_v, ALU.mult, [PS_b[pv_], tbb], [vbtb])
                X.held = {pg, pA, pB}
                for nn in range(4):
                    sl = slice(nn * 128, (nn + 1) * 128)
                    mm(PS[pA][:, sl], kT[:, sl], kT[:, sl], True, True, [kTb], [PS_b[pA]])
                    mm(PS[pB][:, sl], kT[:, sl], qT[:, sl], True, True, [kTb, qTb], [PS_b[pB]])
                Dm, Dmb = X.SF[2], X.SF_b[2]
                DTm, DTmb = X.SF[3], X.SF_b[3]
                for nn in range(4):
                    sl = slice(nn * 128, (nn + 1) * 128)
                    gcc = T["gc"][:, nb0 + nn, h_:h_ + 1]
                    ts(Dm[:, sl], PS[pg][:, sl], gcc, 0.0, ALU.subtract, ALU.max, [PS_b[pg], tbb], [Dmb])
                    ts(DTm[:, sl], PS[pg][:, sl], gcc, 0.0, ALU.subtract, ALU.min, [PS_b[pg], tbb], [DTmb])
                act(Dm[:, 0:n], Dm[:, 0:n], AF.Exp, [Dmb], [Dmb], scale=-1.0)
                act(DTm[:, 0:n], DTm[:, 0:n], AF.Exp, [DTmb], [DTmb])
                m3 = lambda t_: t_[:, 0:512].rearrange("p (a d) -> p a d", a=4)
                tt(m3(Dm), m3(Dm), maskSL[:].unsqueeze(1).to_broadcast([128, 4, 128]), ALU.mult, [Dmb, cst], [Dmb])
                tt(m3(DTm), m3(DTm), maskU[:].unsqueeze(1).to_broadcast([128, 4, 128]), ALU.mult, [DTmb, cst], [DTmb])
                X0, X0b = X.SBf[10], X.SB_b[10]
                for nn in range(4):
                    sl = slice(nn * 128, (nn + 1) * 128)
                    stt(X0[:, sl], PS[pA][:, sl], T["nbeta"][:, nb0 + nn, h_:h_ + 1], Dm[:, sl], ALU.mult, ALU.mult,
                        [PS_b[pA], tbb, Dmb], [X0b])
                AIT, AITb = X.SBf[4], X.SB_b[4]
                tt(AIT[:, 0:n], PS[pB][:, 0:n], DTm[:, 0:n], ALU.mult, [PS_b[pB], DTmb], [AITb])
                X.held = set()
                px = rbank(X)
                pxb_ = PS[px][:].bitcast(BF16)
                for nn in range(4):
                    tr(pxb_[:, nn * 128:(nn + 1) * 128], X0[:, nn * 128:(nn + 1) * 128], ident_b[:], [X0b, cst], [PS_b[px]])
                XT0, XT0b = X.SBf[11], X.SB_b[11]
                act(XT0[:, 0:n], pxb_[:, 0:n], AF.Copy, [PS_b[px]], [XT0b])
                RT, RTb = X.SBf[12], X.SB_b[12]
                Rm, Rmb = X.SBf[5], X.SB_b[5]
                U16 = mybir.dt.uint16
                bd3 = BD16[:].unsqueeze(1).to_broadcast([128, 4, 128])
                id3 = ident_b[:].unsqueeze(1).to_broadcast([128, 4, 128])
                Pt = [(X.SBf[0], X.SB_b[0]), (X.SBf[1], X.SB_b[1]), (X.SBf[2], X.SB_b[2]), (X.SBf[3], X.SB_b[3])]
                (Pc, Pcb), (PTc, PTcb) = Pt[0], Pt[1]
                tt(m3(Pc), m3(X0), bd3, ALU.mult, [X0b, cst], [Pcb])
                tt(m3(PTc), m3(XT0), bd3, ALU.mult, [XT0b, cst], [PTcb])
                tt(m3(Rm), m3(Pc), id3, ALU.add, [Pcb, cst], [Rmb])
                tt(m3(RT), m3(PTc), id3, ALU.add, [PTcb, cst], [RTb])

                def mm4(pbank, lhs, rhs, r_):
                    for nn in range(4):
                        sl = slice(nn * 128, (nn + 1) * 128)
                        mm(PS[pbank][:, sl], lhs[:, sl], rhs[:, sl], True, True, r_, [PS_b[pbank]])

                cur = 0
                for lvl in range(1, 4):
                    (Pn, Pnb), (PTn, PTnb) = Pt[2 - cur], Pt[3 - cur]
                    p1, p2 = rbank(X), rbank(X)
                    mm4(p1, PTc, Pc, [PTcb, Pcb])
                    mm4(p2, Pc, PTc, [PTcb, Pcb])
                    act(Pn[:, 0:n], PS[p1][:, 0:n], AF.Copy, [PS_b[p1]], [Pnb])
                    act(PTn[:, 0:n], PS[p2][:, 0:n], AF.Copy, [PS_b[p2]], [PTnb])
                    p3, p4 = rbank(X), rbank(X)
                    mm4(p3, PTn, Rm, [PTnb, Rmb])
                    mm4(p4, Pn, RT, [Pnb, RTb])
                    tt(Rm[:, 0:n], Rm[:, 0:n], PS[p3][:, 0:n], ALU.add, [Rmb, PS_b[p3]], [Rmb])
                    tt(RT[:, 0:n], RT[:, 0:n], PS[p4][:, 0:n], ALU.add, [RTb, PS_b[p4]], [RTb])
                    Pc, Pcb, PTc, PTcb = Pn, Pnb, PTn, PTnb
                    cur = 2 - cur
                (Wt, Wtb), (Ut, Utb) = Pt[0], Pt[1]
                for m_ in (32, 64, 128):
                    om = OFFM[m_][:].bitcast(U16).unsqueeze(1).to_broadcast([128, 4, 128])
                    pu = rbank(X)
                    mm4(pu, X0, RT, [X0b, RTb])
                    if m_ < 128:
                        pw_ = rbank(X)
                        mm4(pw_, XT0, Rm, [XT0b, Rmb])
                        act(Wt[:, 0:n], PS[pw_][:, 0:n], AF.Copy, [PS_b[pw_]], [Wtb])
                    act(Ut[:, 0:n], PS[pu][:, 0:n], AF.Copy, [PS_b[pu]], [Utb])
                    pv2_ = rbank(X)
                    mm4(pv2_, Rm, Ut, [Rmb, Utb])
                    if m_ < 128:
                        pz = rbank(X)
                        mm4(pz, RT, Wt, [RTb, Wtb])
                        A("dve", lambda h, om=om, pz=pz: h.copy_predicated(out=m3(Rm), mask=om, data=m3(PS[pz])),
                          [PS_b[pz], cst], [Rmb], dur=700.0)
                    A("dve", lambda h, om=om, pv2_=pv2_: h.copy_predicated(out=m3(RT), mask=om, data=m3(PS[pv2_])),
                      [PS_b[pv2_], cst], [RTb], dur=700.0)
                pw = rbank(X)
                for nn in range(4):
                    sl = slice(nn * 128, (nn + 1) * 128)
                    mm(PS[pw][:, sl], kbg[:, sl], RT[:, sl], True, True, [kbgb, RTb], [PS_b[pw]])
                wnT, wnTb = X.SBf[3], X.SB_b[3]
                act(wnT[:, 0:n], PS[pw][:, 0:n], AF.Copy, [PS_b[pw]], [wnTb], scale=-1.0)
                X.held = {pO}
                for nn in range(4):
                    sl = slice(nn * 128, (nn + 1) * 128)
                    pn_ = rbank(X)
                    mm(PS[pn_][:, 0:128], RT[:, sl], vbt[:, sl], True, False, [RTb, vbtb], [PS_b[pn_]])
                    mm(PS[pn_][:, 0:128], wnT[:, sl], X.S_b[:], False, True, [wnTb, X.Sbb], [PS_b[pn_]])
                    vnw, vnwb = X.SBf[2][:, (nn % 2) * 128:(nn % 2) * 128 + 128], X.SB_b[2]
                    act(vnw[:, 0:128], PS[pn_][:, 0:128], AF.Copy, [PS_b[pn_]], [vnwb])
                    mm(PS[pO][:, sl], X.S_b[:], qdT[:, sl], True, False, [X.Sbb, qdTb], [PS_b[pO]])
                    mm(PS[pO][:, sl], vnw[:, 0:128], AIT[:, sl], False, True, [vnwb, AITb], [PS_b[pO]])
                    ps_ = rbank(X)
                    mm(PS[ps_][:, 0:128], ktm[:, sl], vnw[:, 0:128], True, True, [ktmb, vnwb], [PS_b[ps_]])
                    eglc = egl[:, (nb0 + nn) * 8 + h_:(nb0 + nn) * 8 + h_ + 1]
                    stt(X.S_b[:], X.S_f[:], eglc, PS[ps_][:, 0:128], ALU.mult, ALU.add, [X.Sfb, tbb, PS_b[ps_]], [X.Sbb])
                    stt(X.S_f[:], X.S_f[:], eglc, PS[ps_][:, 0:128], ALU.mult, ALU.add, [X.Sfb, tbb, PS_b[ps_]], [X.Sfb])
                if ti == 3:
                    dma("sp", o_pS[l, h_], X.S_f[:], [X.Sfb])
                of, ofb = X.SF[0], X.SF_b[0]
                act(of[:, 0:n], PS[pO][:, 0:n], AF.Copy, [PS_b[pO]], [ofb])
                X.held = set()

            sq2, sq2b = X.SBf[3], X.SB_b[3]
            act(sq2[:, 0:n], of[:, 0:n], AF.Square, [ofb], [sq2b])
            pq = rbank(X)
            mm(PS[pq][:, 0:n], ones_b[:], sq2[:, 0:n], True, True, [cst, sq2b], [PS_b[pq]])
            rs2, rs2b = X.SF[1], X.SF_b[1]
            ts(rs2[:, 0:n], PS[pq][:, 0:n], 1.0 / 128, RMS_EPS, ALU.mult, ALU.add, [PS_b[pq]], [rs2b])
            act(rs2[:, 0:n], rs2[:, 0:n], AF.Ln, [rs2b], [rs2b])
            act(rs2[:, 0:n], rs2[:, 0:n], AF.Exp, [rs2b], [rs2b], scale=-0.5)
            stt(of[:, 0:n], of[:, 0:n], pv[:, 272:273], rs2[:, 0:n], ALU.mult, ALU.mult, [ofb, pvb, rs2b], [ofb])
            pb = proj(6)
            zs, zsb = X.SF[2], X.SF_b[2]
            act(zs[:, 0:n], PS[pb][:, 0:n], AF.Sigmoid, [PS_b[pb], pvb], [zsb], bias=bias(6))
            stt(zs[:, 0:n], PS[pb][:, 0:n], bias(6), zs[:, 0:n], ALU.add, ALU.mult, [PS_b[pb], pvb, zsb], [zsb])
            tt(of[:, 0:n], of[:, 0:n], zs[:, 0:n], ALU.mult, [ofb, zsb], [ofb])
            pb = proj(7)
            act(zs[:, 0:n], PS[pb][:, 0:n], AF.Sigmoid, [PS_b[pb], pvb], [zsb], bias=bias(7))
            tt(of[:, 0:n], of[:, 0:n], zs[:, 0:n], ALU.mult, [ofb, zsb], [ofb])
            mt, mtb = X.SBf[6], X.SB_b[6]
            tt(mt[:, 0:n], of[:, 0:n], ma[:, 0:n], ALU.add, [ofb, mab], [mtb])
            for m in range(8):
                pb = rbank(X)
                mm(PS[pb][:, 0:n], X.wo_c[:, m * 128:(m + 1) * 128], mt[:, 0:n], True, True, [X.wob, mtb], [PS_b[pb]])
                tt(xres[:, m, c0:c0 + n], PS[pb][:, 0:n], xres[:, m, c0:c0 + n], ALU.add,
                   [PS_b[pb], xres_b[m][ti]], [xres_b[m][ti]])


    def mixer(l):
        gate_tables(l)
        nops0 = len(P.all)
        prior = []
        for row in mbuf_b:
            for b in row:
                prior.extend(b.r)
                if b.w is not None:
                    prior.append(b.w)
        for b in t1_bufs:
            b.w = None
            b.r = list(prior)
        for pair in range(4):
            chunk_body(l, 2 * pair, CTX[0])
            chunk_body(l, 2 * pair + 1, CTX[1])
        dma("sp", o_ph[l], hcar[:], [CTX[0].smb, CTX[1].smb])
        if l == 0:
            print("mixer ops per (chunk,tile):", (len(P.all) - nops0) / 40.0)
        post = []
        for b in t1_bufs:
            post.extend(b.r)
            if b.w is not None:
                post.append(b.w)
        for row in mbuf_b:
            for b in row:
                b.r = list(b.r) + post

    for l in range(n_layers):
        layer_params(l)
        ffn(2 * l)
        layernorm(0, False)
        mixer(l)
        layernorm(1, False)
        ffn(2 * l + 1)
        layernorm(2, l == n_layers - 1)
    P.emit(scheduled=SCHED)
    print("ops:", len(P.all), "est total us:", getattr(P, "est_total", 0) / 1e3)
    st.close()
    return nc


def _prep_shared(inp):
    f = lambda a: np.ascontiguousarray(np.asarray(a, dtype=np.float32))
    w1 = f(np.stack([inp["ffn1_w1"], inp["ffn2_w1"]], 1).reshape(2 * L, D, DFF))
    w3 = f(np.stack([inp["ffn1_w3"], inp["ffn2_w3"]], 1).reshape(2 * L, D, DFF))
    w2 = f(np.stack([inp["ffn1_w2"], inp["ffn2_w2"]], 1).reshape(2 * L, DFF, D))
    w_in = np.asarray(inp["w_in"], np.float32)
    b_in = np.asarray(inp["b_in"], np.float32)
    offs = [0, 1024, 6160, 2048, 3072, 4096, 5120, 7184]
    win = np.empty((L, 8, D, 1024), np.float32)
    for c in range(8):
        for bi, o in enumerate(offs):
            win[:, c, :, bi * 128:(bi + 1) * 128] = w_in[:, :, o + c * 128:o + (c + 1) * 128]
    wab = f(w_in[:, :, 6144:6160])
    gatew = np.zeros((L, 2, 8, 128, 128), np.float32)
    for gi, nm in enumerate(("rg_wr", "rg_wi")):
        w = np.asarray(inp[nm], np.float32)
        for c in range(8):
            gatew[:, gi, c, 0:64, 0:64] = w[:, 2 * c]
            gatew[:, gi, c, 64:128, 64:128] = w[:, 2 * c + 1]
    pvec = np.empty((L, 128, NPV), np.float32)
    fm = lambda v: np.asarray(v, np.float32).reshape(L, -1, 128).transpose(0, 2, 1)
    for i, (g, b) in enumerate((("ln1_g", "ln1_b"), ("ln2_g", "ln2_b"), ("ln3_g", "ln3_b"))):
        pvec[:, :, 16 * i:16 * i + 8] = fm(inp[g])
        pvec[:, :, 16 * i + 8:16 * i + 16] = fm(inp[b])
    for c in range(8):
        for bi, o in enumerate(offs):
            pvec[:, :, 48 + c * 8 + bi] = b_in[:, o + c * 128:o + (c + 1) * 128]
    rcw = np.asarray(inp["rg_conv_w"], np.float32)
    for j in range(4):
        pvec[:, :, 112 + j * 8:112 + (j + 1) * 8] = fm(rcw[:, j])
    pvec[:, :, 144:152] = fm(inp["rg_conv_b"])
    pvec[:, :, 152:160] = fm(inp["rg_br"])
    pvec[:, :, 160:168] = fm(inp["rg_bi"])
    pvec[:, :, 168:176] = fm(inp["rg_lambda"])
    gcw = np.asarray(inp["gdn_conv_w"], np.float32)
    for j in range(4):
        pvec[:, :, 176 + j * 24:176 + (j + 1) * 24] = fm(gcw[:, j])
    pvec[:, :, 272] = np.asarray(inp["gdn_norm_w"], np.float32)
    prow = np.concatenate([b_in[:, 6144:6160], np.asarray(inp["gdn_a_log"], np.float32),
                           np.asarray(inp["gdn_dt_bias"], np.float32)], axis=1).reshape(L, 1, 32)
    return {"w1": w1, "w3": w3, "w2": w2, "win": win, "wab": wab, "wo": f(inp["w_o"]), "gatew": gatew,
            "pvec": f(pvec), "prow": f(prow)}


def _prep_core(inp, i):
    xp = np.asarray(inp["x_prompt"], np.float32)[i]
    xs = np.asarray(inp["x_sample"], np.float32)[i * NS:(i + 1) * NS, 0]
    x = np.concatenate([xp, xs], 0)
    xin = np.ascontiguousarray(x.T.reshape(8, 128, TOK).transpose(1, 0, 2))
    sl = slice(i * NS, (i + 1) * NS)
    h0 = np.asarray(inp["state_rglru_h"], np.float32)[:, sl]
    sh0 = np.ascontiguousarray(h0.transpose(0, 2, 1).reshape(L, 8, 128, NS))
    rc = np.asarray(inp["state_rglru_conv"], np.float32)[:, sl]
    src0 = np.ascontiguousarray(rc.transpose(0, 3, 2, 1).reshape(L, 8, 128, 3, NS))
    gc = np.asarray(inp["state_gdn_conv"], np.float32)[:, sl]
    sgc0 = np.ascontiguousarray(gc.transpose(0, 3, 2, 1).reshape(L, 24, 128, 3, NS))
    sS0 = np.ascontiguousarray(np.asarray(inp["state_gdn_S"], np.float32)[:, sl])
    return {"xin": xin, "sh0": sh0, "src0": src0, "sgc0": sgc0, "sS0": sS0}


_NC_CACHE = {}


def kernel(**inp):
    if "nc" not in _NC_CACHE:
        _NC_CACHE["nc"] = build_program()
    nc = _NC_CACHE["nc"]
    shared = _prep_shared(inp)
    in_maps = []
    for i in range(8):
        m = dict(shared)
        m.update(_prep_core(inp, i))
        in_maps.append(m)
    res = run_bass_kernel_spmd(nc, in_maps, core_ids=list(range(8)))
    R = res.results
    Bp = 8
    y_prompt = np.empty((Bp, NP, D), np.float32)
    y_sample = np.empty((8 * NS, 1, D), np.float32)
    p_h = np.empty((L, Bp, D), np.float32)
    p_rc = np.empty((L, Bp, 3, D), np.float32)
    p_S = np.empty((L, Bp, 8, 128, 128), np.float32)
    p_gc = np.empty((L, Bp, 3, 3072), np.float32)
    s_h = np.empty((L, 8 * NS, D), np.float32)
    s_rc = np.empty((L, 8 * NS, 3, D), np.float32)
    s_S = np.empty((L, 8 * NS, 8, 128, 128), np.float32)
    s_gc = np.empty((L, 8 * NS, 3, 3072), np.float32)
    for i in range(8):
        r = R[i]
        y = np.asarray(r["yout"]).transpose(1, 0, 2).reshape(D, TOK).T
        y_prompt[i] = y[:NP]
        sl = slice(i * NS, (i + 1) * NS)
        y_sample[sl, 0] = y[NP:]
        p_h[:, i] = np.asarray(r["o_ph"]).transpose(0, 2, 1).reshape(L, D)
        p_rc[:, i] = np.asarray(r["o_prc"]).transpose(0, 3, 1, 2).reshape(L, 3, D)
        p_S[:, i] = np.asarray(r["o_pS"])
        p_gc[:, i] = np.asarray(r["o_pgc"]).transpose(0, 3, 1, 2).reshape(L, 3, 3072)
        s_h[:, sl] = np.asarray(r["o_sh"]).transpose(0, 3, 1, 2).reshape(L, NS, D)
        s_rc[:, sl] = np.asarray(r["o_src"]).transpose(0, 4, 3, 1, 2).reshape(L, NS, 3, D)
        s_S[:, sl] = np.asarray(r["o_sS"])
        s_gc[:, sl] = np.asarray(r["o_sgc"]).transpose(0, 4, 3, 1, 2).reshape(L, NS, 3, 3072)
    return (y_prompt, y_sample, p_h, p_rc, p_S, p_gc, s_h, s_rc, s_S, s_gc)
```

```python
import math
from contextlib import ExitStack
import numpy as np
import concourse.bass as bass
import concourse.mybir as mybir
from concourse.bass_utils import run_bass_kernel_spmd

F32 = mybir.dt.float32
BF16 = mybir.dt.bfloat16
AF = mybir.ActivationFunctionType
ALU = mybir.AluOpType
AX = mybir.AxisListType

L = 4
D = 1024
NP = 2048
NS = 16
TOK = NP + NS
DFF = 2816
NJ = DFF // 128
ALPHA = (2 * L) ** 0.25
LN_EPS = 1e-5
RMS_EPS = 1e-6
TILES = [(0, 512), (512, 512), (1024, 512), (1536, 512), (2048, 16)]
NPV = 273
ENGS = ("pe", "act", "dve", "pool", "sp")


class Buf:
    __slots__ = ("w", "r", "excl")

    def __init__(self, excl=False):
        self.w = None
        self.r = []
        self.excl = excl


class Prog:
    def __init__(self, nc):
        self.nc = nc
        self.all = []
        self.n_ch = {"sp": 28, "pool": 12}

    def _collect(self, eng, reads, writes, is_dma):
        deps = {}
        for b in reads:
            ev = b.w
            if ev is None:
                continue
            ee = self.all[ev]
            need = not (eng == "pe" and ee[0] == "pe" and ee[1] == "c")
            deps[ev] = deps.get(ev, False) or need
        for b in writes:
            evs = list(b.r)
            if b.w is not None:
                evs.append(b.w)
            for ev in evs:
                ee = self.all[ev]
                same = (ee[0] == eng and ee[1] == "c" and not is_dma)
                deps[ev] = deps.get(ev, False) or (not same)
        return list(deps.items())

    def _add(self, eng, kind, fn, reads, writes, dur, fam):
        ex = [b for b in reads if b.excl and b not in writes]
        if ex:
            writes = list(writes) + ex
        deps = self._collect(eng, reads, writes, kind == "d")
        idx = len(self.all)
        self.all.append([eng, kind, fn, deps, dur, fam])
        for b in reads:
            b.r.append(idx)
        for b in writes:
            b.w = idx
            b.r = []
        return idx

    def op(self, eng, fn, reads=(), writes=(), dur=300.0, fam=None):
        return self._add(eng, "c", fn, reads, writes, dur, fam)

    def dma(self, eng, fn, reads=(), writes=(), dur=3000.0):
        return self._add(eng, "d", fn, reads, writes, dur, None)

    def schedule(self):
        import heapq
        ops = self.all
        n = len(ops)
        succ = [[] for _ in range(n)]
        npred = [0] * n
        last_q = {}
        for i, o in enumerate(ops):
            if o[0] in QUEUE_INORDER:
                p_ = last_q.get(o[0])
                if p_ is not None and all(d != p_ for d, _ in o[3]):
                    o[3].append((p_, False) if ops[p_][1] == "c" and o[1] == "c" else (p_, None))
                last_q[o[0]] = i
        for i, o in enumerate(ops):
            npred[i] = len(o[3])
            for d, _ in o[3]:
                succ[d].append(i)
        issue_only = {}
        for j_, o in enumerate(ops):
            for d, need in o[3]:
                if need is None:
                    issue_only[(d, j_)] = True
        ready_t = [0.0] * n
        fin = [0.0] * n
        eng_free = {e: 0.0 for e in ENGS}
        eng_fam = {e: None for e in ENGS}
        order = {e: [] for e in ENGS}
        future = {e: [] for e in ENGS}
        avail = {e: [] for e in ENGS}
        for i in range(n):
            if npred[i] == 0:
                heapq.heappush(future[ops[i][0]], (0.0, i))
        done = 0
        XLAT = 250.0
        while done < n:
            best = None
            for e in ENGS:
                t = eng_free[e]
                fut, av = future[e], avail[e]
                while fut and fut[0][0] <= t:
                    heapq.heappush(av, heapq.heappop(fut)[1])
                if av:
                    c_ = (t, av[0], e, True)
                elif fut:
                    c_ = (fut[0][0], fut[0][1], e, False)
                else:
                    continue
                if best is None or (c_[0], c_[1]) < (best[0], best[1]):
                    best = c_
            st, i, e, from_av = best
            if from_av:
                av = avail[e]
                if e == "act" and eng_fam[e] is not None and len(av) > 1 and ops[i][5] not in (None, eng_fam[e]):
                    cands = heapq.nsmallest(12, av)
                    alt = [c for c in cands if ops[c][5] in (None, eng_fam[e]) and c - i < 400]
                    if alt:
                        i = alt[0]
                        av.remove(i)
                        heapq.heapify(av)
                    else:
                        heapq.heappop(av)
                else:
                    heapq.heappop(av)
            else:
                heapq.heappop(future[e])
            o = ops[i]
            dur = o[4]
            if o[1] == "c":
                if o[5] is not None and eng_fam[e] is not None and eng_fam[e] != o[5]:
                    st += 1300.0
                if o[5] is not None:
                    eng_fam[e] = o[5]
                eng_free[e] = st + dur
                fin[i] = st + dur
            else:
                eng_free[e] = st + 120.0
                fin[i] = st + dur
            order[e].append(i)
            done += 1
            for j in succ[i]:
                oj = ops[j]
                lat = XLAT if (oj[0] != e or o[1] == "d") else 60.0
                t = fin[i] + lat
                if issue_only.get((i, j)):
                    t = st
                if t > ready_t[j]:
                    ready_t[j] = t
                npred[j] -= 1
                if npred[j] == 0:
                    heapq.heappush(future[oj[0]], (ready_t[j], j))
        self.order = order
        self.est_total = max(eng_free.values())

    def emit(self, scheduled=True):
        nc = self.nc
        ops = self.all
        if scheduled:
            self.schedule()
            order = self.order
        else:
            order = {e: [i for i, o in enumerate(ops) if o[0] == e] for e in ENGS}
        pos = {}
        for e in ENGS:
            for p_, i in enumerate(order[e]):
                pos[i] = p_
        ch_of, ch_val = {}, {}
        ch_next = {e: 0 for e in self.n_ch}
        ch_count = {}
        ch_prev = {}
        for e in self.n_ch:
            for i in order[e]:
                if ops[i][1] != "d":
                    continue
                c = ch_next[e]
                ch_next[e] = (c + 1) % self.n_ch[e]
                prev = ch_count.get((e, c), 0)
                ch_prev[i] = (c, prev)
                ch_count[(e, c)] = prev + 16
                ch_of[i] = c
                ch_val[i] = prev + 16
        marked = {e: set() for e in ENGS}
        waits_of = {}
        for e in ENGS:
            waited = {}
            for i in order[e]:
                o = ops[i]
                wl = []
                for d, need in o[3]:
                    od = ops[d]
                    if need is None:
                        assert od[0] == e and pos[d] < pos[i]
                        continue
                    if od[1] == "d":
                        key, val = ("d", od[0], ch_of[d]), ch_val[d]
                    else:
                        if not need:
                            assert od[0] == e and pos[d] < pos[i]
                            continue
                        key, val = ("e", od[0]), pos[d]
                    if val <= waited.get(key, -1):
                        continue
                    waited[key] = val
                    wl.append((key, val, d))
                    if key[0] == "e":
                        marked[od[0]].add(d)
                if o[1] == "d":
                    c, prev = ch_prev[i]
                    key = ("d", e, c)
                    if prev and prev > waited.get(key, -1):
                        waited[key] = prev
                        wl.append((key, prev, None))
                waits_of[i] = wl
        cnt = {}
        for e in ENGS:
            m = sorted(marked[e], key=lambda i: pos[i])
            cnt[e] = {i: k + 1 for k, i in enumerate(m)}
        with ExitStack() as st:
            esem = {e: st.enter_context(nc.semaphore("s_" + e)) for e in ENGS}
            dsem = {}
            for e, n_ in self.n_ch.items():
                for c in range(n_):
                    dsem[(e, c)] = st.enter_context(nc.semaphore("d_%s%d" % (e, c)))
            block = st.enter_context(nc.Block())
            n_ch = self.n_ch

            def run(e, h):
                for i in order[e]:
                    o = ops[i]
                    for key, val, d in waits_of[i]:
                        if key[0] == "e":
                            h.wait_ge(esem[key[1]], cnt[key[1]][d])
                        else:
                            h.wait_ge(dsem[(key[1], key[2])], val)
                    ins = o[2](h)
                    if o[1] == "d":
                        ins.then_inc(dsem[(e, ch_of[i])], 16)
                    elif i in cnt[e]:
                        ins.then_inc(esem[e], 1)
                if e in n_ch:
                    for c in range(n_ch[e]):
                        v = ch_count.get((e, c), 0)
                        if v:
                            h.wait_ge(dsem[(e, c)], v)

            @block.tensor
            def _(h):
                run("pe", h)

            @block.scalar
            def _(h):
                run("act", h)

            @block.vector
            def _(h):
                run("dve", h)

            @block.gpsimd
            def _(h):
                run("pool", h)

            @block.sync
            def _(h):
                run("sp", h)


SCHED = True
QUEUE_INORDER = ()


def build_program(n_layers=L, debug=None):
    nc = bass.Bass("TRN2", target_bir_lowering=False)
    st = ExitStack()

    def din(name, shape):
        return nc.dram_tensor(name, list(shape), F32, kind="ExternalInput").ap()

    def dout(name, shape):
        return nc.dram_tensor(name, list(shape), F32, kind="ExternalOutput").ap()

    xin = din("xin", [128, 8, TOK])
    w1d = din("w1", [2 * L, D, DFF])
    w3d = din("w3", [2 * L, D, DFF])
    w2d = din("w2", [2 * L, DFF, D])
    wind = din("win", [L, 8, D, 1024])
    wabd = din("wab", [L, D, 16])
    wod = din("wo", [L, D, D])
    gwd = din("gatew", [L, 2, 8, 128, 128])
    pvd = din("pvec", [L, 128, NPV])
    prd = din("prow", [L, 1, 32])
    sh0d = din("sh0", [L, 8, 128, NS])
    src0d = din("src0", [L, 8, 128, 3, NS])
    sgc0d = din("sgc0", [L, 24, 128, 3, NS])
    sS0d = din("sS0", [L, NS, 8, 128, 128])
    yout = dout("yout", [128, 8, TOK])
    o_ph = dout("o_ph", [L, 128, 8])
    o_prc = dout("o_prc", [L, 8, 128, 3])
    o_pS = dout("o_pS", [L, 8, 128, 128])
    o_pgc = dout("o_pgc", [L, 24, 128, 3])
    o_sh = dout("o_sh", [L, 8, 128, NS])
    o_src = dout("o_src", [L, 8, 128, 3, NS])
    o_sS = dout("o_sS", [L, NS, 8, 128, 128])
    o_sgc = dout("o_sgc", [L, 24, 128, 3, NS])
    dbg = {}
    if debug:
        for nm, shp in debug.items():
            dbg[nm] = dout("dbg_" + nm, shp)

    def sb(name, shape, dt):
        return st.enter_context(nc.sbuf_tensor(name, list(shape), dt))

    xres = sb("xres", [128, 8, TOK], F32)
    xbf = sb("xbf", [128, 8, TOK], BF16)
    mbuf = sb("mbuf", [128, 8, TOK], BF16)
    wsl = [sb("wsl%d" % i, [128, 4096], BF16) for i in range(4)]
    NSF, NSB = 6, 13
    SF = [sb("sf%d" % i, [128, 512], F32) for i in range(NSF)]
    SBf = [sb("sbf%d" % i, [128, 520 if i < 2 else 512], BF16) for i in range(NSB)]
    wo_c0 = sb("wo_c0", [128, 1024], BF16)
    ident_f = sb("ident_f", [128, 128], F32)
    ident_b = sb("ident_b", [128, 128], BF16)
    ones_f = sb("ones_f", [128, 128], F32)
    ones_b = sb("ones_b", [128, 128], BF16)
    maskU = sb("maskU", [128, 128], F32)
    maskSL = sb("maskSL", [128, 128], F32)
    sel_last = sb("sel_last", [128, 128], F32)
    BD16 = sb("BD16", [128, 128], BF16)
    OFFM = {m_: sb("OFF%d" % m_, [128, 128], BF16) for m_ in (32, 64, 128)}
    pv = sb("pv_t", [128, NPV], F32)
    pv2 = sb("pv2", [128, 48], F32)
    nsp = sb("nsp", [128, 16], F32)
    prow = sb("prow_t", [128, 32], F32)
    aexp = sb("aexp", [128, 8], F32)
    wab_b = sb("wab_b", [128, 8, 16], BF16)
    gw_b = sb("gw_b", [128, 2, 128], BF16)
    TB = {nm: sb("tb_" + nm, [128, 17, 8], F32) for nm in
          ("beta", "g", "gc", "egc", "bge", "nbeta")}
    ksq_t = sb("ksq_t", [128, 2 * NS], F32)
    glb = sb("glb", [128, 128], F32)
    egl = sb("egl", [128, 128], F32)
    ekt = sb("ekt", [128, 128], F32)
    S_f = sb("S_f", [128, 128], F32)
    S_b = sb("S_b", [128, 128], BF16)
    hcar = sb("hcar", [128, 8], F32)
    hist = sb("hist", [128, 4, 4], BF16)
    pco = sb("pco", [128, 4, 3], F32)
    dgw = sb("dgw", [128, 4, 128], BF16)
    sst = sb("sst", [128, 4, 3, NS], F32)
    sstb = sb("sstb", [128, 4, 3, NS], BF16)
    sso = sb("sso", [128, 4, 3, NS], F32)
    sh0 = sb("sh0t", [128, NS], F32)
    sho = sb("shot", [128, NS], F32)
    Sin = [sb("Sin%d" % i, [128, 128], F32) for i in range(2)]
    Sout = [sb("Sout%d" % i, [128, 128], F32) for i in range(2)]
    srow = {nm: sb("srow_" + nm, [128, 128], F32) for nm in ("eg", "beta", "nbe")}
    srow["qk"] = sb("srow_qk", [128, NS], F32)
    skq = sb("skq", [128, 2, NS], F32)
    PS = [st.enter_context(nc.psum_tensor("ps%d" % i, [128, 512], F32)) for i in range(8)]
    print("sbuf bytes remaining/partition:", nc.sbuf_bytes_remaining)

    P = Prog(nc)
    B_ = Buf
    xres_b = [[B_() for _ in TILES] for _ in range(8)]
    xbf_b = [[B_() for _ in TILES] for _ in range(8)]
    mbuf_b = [[B_() for _ in TILES] for _ in range(8)]
    wsl_b = [B_() for _ in range(4)]
    SF_b = [B_() for _ in range(NSF)]
    SB_b = [B_() for _ in range(NSB)]
    PS_b = [B_(True) for _ in range(8)]
    cst = B_()
    pvb = B_()
    tbb = B_()
    gwb = B_()
    wabb = B_()
    dgwb = B_()
    ksqb = B_()
    smb = B_()
    Sfb, Sbb = B_(), B_()
    sstb_b, ssob = B_(), B_()
    Sin_b = [B_() for _ in range(2)]
    Sout_b = [B_() for _ in range(2)]
    srb = B_()

    def fsz(ap):
        sh = ap.shape
        n_ = 1
        for d_ in sh[1:]:
            n_ *= d_
        return n_

    ACT_FAM = {AF.Exp: "el", AF.Ln: "el", AF.Sigmoid: "sg", AF.Silu: "si", AF.Sqrt: "sq"}

    def A(e, fn, r=(), w=(), dur=None, fam=None):
        if dur is None:
            dur = {"pe": 150.0, "act": 400.0, "dve": 400.0, "pool": 600.0}.get(e, 300.0)
        P.op(e, fn, reads=r, writes=w, dur=dur, fam=fam)

    def act(out, in_, func, r, w, bias=None, scale=None):
        kw = {}
        if bias is not None:
            kw["bias"] = bias
        if scale is not None:
            kw["scale"] = scale
        A("act", lambda h: h.activation(out=out, in_=in_, func=func, **kw), r, w,
          dur=220.0 + 0.8 * fsz(out), fam=ACT_FAM.get(func))

    def mm(out, lhsT, rhs, start, stop, r, w):
        n_ = fsz(rhs)
        f32 = 4.0 if rhs.dtype == F32 else 1.0
        A("pe", lambda h: h.matmul(out, lhsT=lhsT, rhs=rhs, start=start, stop=stop), r, w,
          dur=(45.0 + max(64, n_) * 0.5 * f32) if n_ > 128 else (100.0 * f32))

    def tt(out, in0, in1, op, r, w, eng="dve"):
        A(eng, lambda h: h.tensor_tensor(out=out, in0=in0, in1=in1, op=op), r, w,
          dur=(110.0 + 1.1 * fsz(out)) if eng == "dve" else (300.0 + 1.7 * fsz(out)))

    def ts(out, in0, s1, s2, op0, op1, r, w, eng="dve"):
        if s2 is None:
            A(eng, lambda h: h.tensor_scalar(out=out, in0=in0, scalar1=s1, scalar2=None, op0=op0), r, w,
              dur=110.0 + 1.0 * fsz(out))
        else:
            A(eng, lambda h: h.tensor_scalar(out=out, in0=in0, scalar1=s1, scalar2=s2, op0=op0, op1=op1), r, w,
              dur=110.0 + 1.0 * fsz(out))

    def stt(out, in0, scalar, in1, op0, op1, r, w):
        A("dve", lambda h: h.scalar_tensor_tensor(out=out, in0=in0, scalar=scalar, in1=in1, op0=op0, op1=op1), r, w,
          dur=110.0 + 1.1 * fsz(out))

    def dma(eng, out, in_, r=(), w=()):
        P.dma(eng, lambda h: h.dma_start(out=out, in_=in_), reads=r, writes=w,
              dur=2200.0 + fsz(out) * out.shape[0] * 4 / 150.0)

    def cp(out, in_, r, w, eng="dve"):
        A(eng, lambda h: h.tensor_copy(out=out, in_=in_), r, w,
          dur=(100.0 + 1.0 * fsz(out)) if eng == "dve" else (300.0 + 0.9 * fsz(out)))

    def mset(ap, val, w, eng="dve"):
        A(eng, lambda h: h.memset(ap, val), (), w)

    def tr(out, in_, idn, r, w):
        A("pe", lambda h: h.transpose(out, in_, idn), r, w, dur=130.0)

    def scan(out, d0, d1, init, r, w):
        A("dve", lambda h: h.tensor_tensor_scan(out=out, data0=d0, data1=d1, initial=init,
                                                op0=ALU.mult, op1=ALU.add), r, w, dur=110.0 + 2.1 * fsz(out))

    def dbg_dump(nm, ap, r):
        if nm in dbg:
            dma("sp", dbg[nm], ap, r)

    A("pool", lambda h: h.memset(ident_f[:], 1.0), (), [cst])
    A("pool", lambda h: h.affine_select(out=ident_f[:], in_=ident_f[:], pattern=[[1, 128]],
                                        compare_op=ALU.is_equal, fill=0.0, base=0, channel_multiplier=-1), [cst], [cst])
    A("pool", lambda h: h.memset(ones_f[:], 1.0), (), [cst])
    A("pool", lambda h: h.memset(ones_b[:], 1.0), (), [cst])
    A("pool", lambda h: h.memset(maskU[:], 1.0), (), [cst])
    A("pool", lambda h: h.affine_select(out=maskU[:], in_=maskU[:], pattern=[[1, 128]],
                                        compare_op=ALU.is_ge, fill=0.0, base=0, channel_multiplier=-1), [cst], [cst])
    A("pool", lambda h: h.memset(maskSL[:], 1.0), (), [cst])
    A("pool", lambda h: h.affine_select(out=maskSL[:], in_=maskSL[:], pattern=[[-1, 128]],
                                        compare_op=ALU.is_gt, fill=0.0, base=0, channel_multiplier=1), [cst], [cst])
    A("pool", lambda h: h.memset(sel_last[:], 1.0), (), [cst])
    A("pool", lambda h: h.affine_select(out=sel_last[:], in_=sel_last[:], pattern=[[0, 128]],
                                        compare_op=ALU.is_equal, fill=0.0, base=-127, channel_multiplier=1), [cst], [cst])
    A("dve", lambda h: h.tensor_copy(out=ident_b[:], in_=ident_f[:]), [cst], [cst])
    def mk_bd(b_, dst_tile, col0):
        nb_ = 128 // b_
        E = SF[5][0:nb_, 0:128]
        A("pool", lambda h: h.memset(E, 1.0), (), [SF_b[5]])
        A("pool", lambda h: h.affine_select(out=E, in_=E, pattern=[[1, 128]], compare_op=ALU.is_ge, fill=0.0,
                                            base=0, channel_multiplier=-b_), [SF_b[5]], [SF_b[5]])
        A("pool", lambda h: h.affine_select(out=E, in_=E, pattern=[[-1, 128]], compare_op=ALU.is_ge, fill=0.0,
                                            base=b_ - 1, channel_multiplier=b_), [SF_b[5]], [SF_b[5]])
        mm(PS[7][:, col0:col0 + 128], E, E, True, True, [SF_b[5]], [PS_b[7]])
    mk_bd(16, None, 0)
    mk_bd(32, None, 128)
    mk_bd(64, None, 256)
    act(BD16[:], PS[7][:, 0:128], AF.Copy, [PS_b[7]], [cst])
    bdf = SF[4]
    act(bdf[:, 0:384], PS[7][:, 0:384], AF.Copy, [PS_b[7]], [SF_b[4]])
    tt(OFFM[32][:], bdf[:, 128:256], bdf[:, 0:128], ALU.subtract, [SF_b[4]], [cst])
    tt(OFFM[64][:], bdf[:, 256:384], bdf[:, 128:256], ALU.subtract, [SF_b[4]], [cst])
    ts(OFFM[128][:], bdf[:, 256:384], -1.0, 1.0, ALU.mult, ALU.add, [SF_b[4]], [cst])

    for k in range(8):
        for ti, (c0, n) in enumerate(TILES):
            dma("sp", xres[:, k, c0:c0 + n], xin[:, k, c0:c0 + n], (), [xres_b[k][ti]])
            act(xbf[:, k, c0:c0 + n], xres[:, k, c0:c0 + n], AF.Copy, [xres_b[k][ti]], [xbf_b[k][ti]])
            ts(xres[:, k, c0:c0 + n], xres[:, k, c0:c0 + n], ALPHA, None, ALU.mult, None,
               [xres_b[k][ti]], [xres_b[k][ti]])

    wctr = [0]

    def wslot():
        i = wctr[0] % 4
        wctr[0] += 1
        return i

    def ffn(fi):
        w1v = w1d[fi].rearrange("(k p) n -> p k n", p=128)
        w3v = w3d[fi].rearrange("(k p) n -> p k n", p=128)
        w2v = w2d[fi].rearrange("(j p) n -> p j n", p=128)
        ftmp_b = SF_b[0:5]
        subs = [(0, 4), (4, 4), (8, 4), (12, 4), (16, 4), (20, 2)]
        groups = [(0, 1), (2, 3), (4, 5)]

        def load13(su):
            j0, nj = subs[su]
            s1, s3 = wslot(), wslot()
            for s, wv in ((s1, w1v), (s3, w3v)):
                dst = wsl[s][:].rearrange("p (k n) -> p k n", k=8)[:, :, 0:nj * 128]
                dma("pool", dst, wv[:, :, j0 * 128:(j0 + nj) * 128], (), [wsl_b[s]])
            return s1, s3

        def load2(su):
            j0, nj = subs[su]
            s2 = wslot()
            dst = wsl[s2][:].rearrange("p (j n) -> p j n", j=4)[:, 0:nj, :]
            dma("pool", dst, w2v[:, j0:j0 + nj, :], (), [wsl_b[s2]])
            return s2

        for (sa, sbb) in groups:
            l13 = {}
            l13[sa] = load13(sa)
            l13[sbb] = load13(sbb)
            for su in (sa, sbb):
                j0, nj = subs[su]
                s1, s3 = l13[su]
                hj0 = 0 if su == sa else 4
                for j in range(nj):
                    for (s, which) in ((s1, 0), (s3, 1)):
                        wv = wsl[s][:].rearrange("p (k n) -> p k n", k=8)
                        for k in range(8):
                            for ti, (c0, n) in enumerate(TILES):
                                mm(PS[ti][:, 0:n], wv[:, k, j * 128:(j + 1) * 128], xbf[:, k, c0:c0 + n],
                                   k == 0, k == 7, [wsl_b[s], xbf_b[k][ti]], [PS_b[ti]])
                        for ti, (c0, n) in enumerate(TILES):
                            fsl = SF[ti][:, 0:n]
                            if which == 0:
                                act(fsl, PS[ti][:, 0:n], AF.Silu, [PS_b[ti]], [ftmp_b[ti]])
                            else:
                                tt(mbuf[:, hj0 + j, c0:c0 + n], fsl, PS[ti][:, 0:n], ALU.mult,
                                   [ftmp_b[ti], PS_b[ti]], [mbuf_b[hj0 + j][ti]])
            s2a = load2(sa)
            s2b = load2(sbb)
            parts = [(s2a, subs[sa][1], 0), (s2b, subs[sbb][1], 4)]
            tot = subs[sa][1] + subs[sbb][1]
            for m in range(8):
                cnt = 0
                for (s2, nj, hj0) in parts:
                    wv = wsl[s2][:].rearrange("p (j n) -> p j n", j=4)
                    for j in range(nj):
                        for ti, (c0, n) in enumerate(TILES):
                            mm(PS[ti][:, 0:n], wv[:, j, m * 128:(m + 1) * 128], mbuf[:, hj0 + j, c0:c0 + n],
                               cnt == 0, cnt == tot - 1, [wsl_b[s2], mbuf_b[hj0 + j][ti]], [PS_b[ti]])
                        cnt += 1
                for ti, (c0, n) in enumerate(TILES):
                    stt(xres[:, m, c0:c0 + n], PS[ti][:, 0:n], 0.5, xres[:, m, c0:c0 + n], ALU.mult, ALU.add,
                        [PS_b[ti], xres_b[m][ti]], [xres_b[m][ti]])

    def layernorm(i_ln, final):
        for ti, (c0, n) in enumerate(TILES):
            pm, pq = 5, 6
            for k in range(8):
                sq_, sqb = SBf[k % 4], SB_b[k % 4]
                act(sq_[:, 0:n], xres[:, k, c0:c0 + n], AF.Square, [xres_b[k][ti]], [sqb])
                mm(PS[pm][:, 0:n], ones_f[:], xres[:, k, c0:c0 + n], k == 0, k == 7, [cst, xres_b[k][ti]], [PS_b[pm]])
                mm(PS[pq][:, 0:n], ones_b[:], sq_[:, 0:n], k == 0, k == 7, [cst, sqb], [PS_b[pq]])
            mean, msq, var, Aa, Bb = SF[0], SF[1], SF[2], SF[3], SF[4]
            act(mean[:, 0:n], PS[pm][:, 0:n], AF.Copy, [PS_b[pm]], [SF_b[0]], scale=1.0 / D)
            act(msq[:, 0:n], PS[pm][:, 0:n], AF.Square, [PS_b[pm]], [SF_b[1]], scale=1.0 / D)
            stt(var[:, 0:n], PS[pq][:, 0:n], 1.0 / D, msq[:, 0:n], ALU.mult, ALU.subtract,
                [PS_b[pq], SF_b[1]], [SF_b[2]])
            act(var[:, 0:n], var[:, 0:n], AF.Ln, [SF_b[2]], [SF_b[2]], bias=LN_EPS)
            act(Aa[:, 0:n], var[:, 0:n], AF.Exp, [SF_b[2]], [SF_b[3]], scale=-0.5)
            stt(Bb[:, 0:n], mean[:, 0:n], -1.0, Aa[:, 0:n], ALU.mult, ALU.mult, [SF_b[0], SF_b[3]], [SF_b[4]])
            for k in range(8):
                z, zb = SF[k % 2], SF_b[k % 2]
                tt(z[:, 0:n], xres[:, k, c0:c0 + n], Aa[:, 0:n], ALU.mult, [xres_b[k][ti], SF_b[3]], [zb])
                tt(z[:, 0:n], z[:, 0:n], Bb[:, 0:n], ALU.add, [zb, SF_b[4]], [zb])
                gi, bi = 16 * i_ln + k, 16 * i_ln + 8 + k
                if final:
                    ts(xres[:, k, c0:c0 + n], z[:, 0:n], pv[:, gi:gi + 1], pv[:, bi:bi + 1], ALU.mult, ALU.add,
                       [zb, pvb], [xres_b[k][ti]])
                    dma("sp", yout[:, k, c0:c0 + n], xres[:, k, c0:c0 + n], [xres_b[k][ti]])
                else:
                    act(xbf[:, k, c0:c0 + n], z[:, 0:n], AF.Identity, [zb, pvb], [xbf_b[k][ti]],
                        bias=pv[:, bi:bi + 1], scale=pv[:, gi:gi + 1])
                    if k % 2 == 0:
                        ts(xres[:, k, c0:c0 + n], z[:, 0:n], pv2[:, gi:gi + 1], pv2[:, bi:bi + 1], ALU.mult, ALU.add,
                           [zb, pvb], [xres_b[k][ti]])
                    else:
                        act(xres[:, k, c0:c0 + n], z[:, 0:n], AF.Identity, [zb, pvb], [xres_b[k][ti]],
                            bias=pv2[:, bi:bi + 1], scale=pv2[:, gi:gi + 1])

    def softplus(dst, src, tmps, r, w):
        t0, t1, t2 = tmps
        act(t0, src, AF.Abs, r, w)
        act(t0, t0, AF.Exp, w, w, scale=-1.0)
        ts(t1, t0, 2.0, None, ALU.add, None, w, w)
        A("dve", lambda h: h.reciprocal(out=t1, in_=t1), w, w)
        tt(t1, t1, t0, ALU.mult, w, w)
        tt(t2, t1, t1, ALU.mult, w, w)
        ts(t0, t2, 1.0 / 11, 1.0 / 9, ALU.mult, ALU.add, w, w)
        for cf in (1.0 / 7, 1.0 / 5, 1.0 / 3, 1.0):
            tt(t0, t0, t2, ALU.mult, w, w)
            ts(t0, t0, cf, None, ALU.add, None, w, w)
        tt(t0, t0, t1, ALU.mult, w, w)
        ts(t1, src, 0.0, None, ALU.max, None, list(r) + list(w), w)
        stt(dst, t0, 2.0, t1, ALU.mult, ALU.add, w, w)

    def layer_params(l):
        dma("sp", pv[:], pvd[l], (), [pvb])
        dma("sp", prow[:], prd[l].partition_broadcast(128), (), [pvb])
        dma("pool", wab_b[:], wabd[l].rearrange("(k p) n -> p k n", p=128), (), [wabb])
        ts(pv2[:], pv[:, 0:48], ALPHA, None, ALU.mult, None, [pvb], [pvb])
        wl = [pvb] + SF_b[0:5]
        tn = [SF[i][:, 0:8] for i in range(5)]
        ts(tn[3], pv[:, 168:176], -1.0, None, ALU.mult, None, [pvb], wl)
        softplus(tn[4], tn[3], (tn[0], tn[1], tn[2]), wl, wl)
        ts(nsp[:, 0:8], tn[4], -8.0, None, ALU.mult, None, wl, wl)
        ts(nsp[:, 8:16], tn[4], -16.0, None, ALU.mult, None, wl, wl)
        act(aexp[:], prow[:, 16:24], AF.Exp, [pvb], [pvb])

    def gate_tables(l):
        pab = 7
        for nb in range(17):
            c0 = nb * 128
            nt = 128 if nb < 16 else NS
            ti = min(nb // 4, 4)
            for k in range(8):
                mm(PS[pab][0:nt, nb * 16:(nb + 1) * 16], xbf[:, k, c0:c0 + nt], wab_b[:, k, :], k == 0, k == 7,
                   [xbf_b[k][ti], wabb], [PS_b[pab]])
        ab_f = SF[4][:, 0:272].rearrange("p (b n) -> p b n", n=16)
        abv = ab_f
        psv = PS[pab][:, 0:272].rearrange("p (b n) -> p b n", n=16)
        brow = prow[:, 0:16].unsqueeze(1).to_broadcast([128, 17, 16])
        tt(abv, psv, brow, ALU.add, [PS_b[pab], pvb], [tbb, SF_b[4]])
        a_v, b_v = ab_f[:, :, 0:8], ab_f[:, :, 8:16]
        T = {k_: v_[:] for k_, v_ in TB.items()}
        for i_ in range(4):
            T["t%d" % i_] = SF[i_][:, 0:136].rearrange("p (b n) -> p b n", n=8)
        wl = [tbb] + SF_b[0:5]
        act(T["beta"], b_v, AF.Sigmoid, wl, [tbb])
        ts(T["nbeta"], T["beta"], -1.0, None, ALU.mult, None, [tbb], [tbb])
        dtb = prow[:, 24:32].unsqueeze(1).to_broadcast([128, 17, 8])
        tt(T["t3"], a_v, dtb, ALU.add, [tbb, pvb], wl)
        softplus(T["g"], T["t3"], (T["t0"], T["t1"], T["t2"]), wl, wl)
        aex = aexp[:].unsqueeze(1).to_broadcast([128, 17, 8])
        stt(T["g"], T["g"], -1.0, aex, ALU.mult, ALU.mult, [tbb, pvb], [tbb])
        gflat = TB["g"][:].rearrange("p b n -> p (b n)")
        gcflat = TB["gc"][:].rearrange("p b n -> p (b n)")
        mm(PS[pab][:, 0:128], maskU[:], gflat[:, 0:128], True, True, [cst, tbb], [PS_b[pab]])
        act(gcflat[:, 0:128], PS[pab][:, 0:128], AF.Copy, [PS_b[pab]], [tbb])
        act(gcflat[:, 128:136], gflat[:, 128:136], AF.Copy, [tbb], [tbb])
        mm(PS[pab][:, 128:256], sel_last[:], gcflat[:, 0:128], True, True, [cst, tbb], [PS_b[pab]])
        act(glb[:], PS[pab][:, 128:256], AF.Copy, [PS_b[pab]], [tbb])
        act(egl[:], glb[:], AF.Exp, [tbb], [tbb])
        tt(ekt[:], glb[:], gcflat[:, 0:128], ALU.subtract, [tbb], [tbb])
        act(ekt[:], ekt[:], AF.Exp, [tbb], [tbb])
        act(T["egc"], T["gc"], AF.Exp, [tbb], [tbb])
        tt(T["bge"], T["beta"], T["egc"], ALU.mult, [tbb], [tbb])
        for nm, src in (("eg", TB["egc"]), ("beta", TB["beta"])):
            sv = src[0:NS, 16, :].unsqueeze(2).to_broadcast([NS, 8, NS])
            iv = ident_f[0:NS, 0:NS].unsqueeze(1).to_broadcast([NS, 8, NS])
            sM = SF[5][0:NS, 0:128]
            tt(sM.rearrange("p (h s) -> p h s", h=8), sv, iv, ALU.mult, [tbb, cst], [srb, SF_b[5]])
            mm(PS[pab][:, 256:384], ones_f[0:NS, :], sM, True, True, [cst, srb, SF_b[5]], [PS_b[pab]])
            act(srow[nm][:], PS[pab][:, 256:384], AF.Copy, [PS_b[pab]], [srb])
        stt(srow["nbe"][:], srow["beta"][:], -1.0, srow["eg"][:], ALU.mult, ALU.mult, [srb], [srb])

    def rbank(X):
        for _ in range(8):
            i = X.banks[X.rot % len(X.banks)]
            X.rot += 1
            if i not in X.held:
                return i
        raise RuntimeError("no free psum bank")

    class Ctx:
        pass

    def mk_ctx(tid):
        X = Ctx()
        X.tid = tid
        X.banks = [0, 1, 2, 3] if tid == 0 else [4, 5, 6, 7]
        X.held = set()
        X.rot = 0
        X.slots = (0, 1) if tid == 0 else (2, 3)
        for nm in ("smb", "gwb", "dgwb", "sstb_b", "ssob", "Sfb", "Sbb", "ksqb", "srq", "wob"):
            setattr(X, nm, Buf())
        X.SF_b = [Buf() for _ in range(6)] if tid else SF_b
        X.SB_b = [Buf() for _ in range(13)] if tid else SB_b
        if tid == 0:
            X.SF = [t[:] for t in SF]
            X.SBf = [t[:] for t in SBf]
            X.hist, X.pco, X.dgw, X.gw_b = hist[:], pco[:], dgw[:], gw_b[:]
            X.sst, X.sstb, X.sso, X.sh0, X.sho = sst[:], sstb[:], sso[:], sh0[:], sho[:]
            X.S_f, X.S_b, X.skq, X.ksq_t, X.srow_qk = S_f[:], S_b[:], skq[:], ksq_t[:], srow["qk"][:]
            X.wo_c = wo_c0[:]
        else:
            mflat = mbuf[:].rearrange("p a b -> p (a b)")
            off = [0]

            def cv(n_, dt=BF16):
                ne = n_ if dt == BF16 else 2 * n_
                a = off[0]
                off[0] += ne + (ne % 2)
                v = mflat[:, a:a + ne]
                return v if dt == BF16 else v.bitcast(F32)
            X.SF = [cv(512, F32) for _ in range(6)]
            X.SBf = [cv(520 if i < 2 else 512) for i in range(13)]
            X.hist = cv(16).rearrange("p (a b) -> p a b", a=4)
            X.pco = cv(12, F32).rearrange("p (a b) -> p a b", a=4)
            X.dgw = cv(512).rearrange("p (a b) -> p a b", a=4)
            X.gw_b = cv(256).rearrange("p (a b) -> p a b", a=2)
            X.sst = cv(192, F32).rearrange("p (a b c) -> p a b c", a=4, b=3)
            X.sstb = cv(192).rearrange("p (a b c) -> p a b c", a=4, b=3)
            X.sso = cv(192, F32).rearrange("p (a b c) -> p a b c", a=4, b=3)
            X.sh0, X.sho = cv(NS, F32), cv(NS, F32)
            X.S_f, X.S_b = cv(128, F32), cv(128)
            X.skq = cv(2 * NS, F32).rearrange("p (a b) -> p a b", a=2)
            X.ksq_t, X.srow_qk = cv(2 * NS, F32), cv(NS, F32)
            X.wo_c = cv(1024)
            assert off[0] <= 8 * TOK, off[0]
        return X

    CTX = [mk_ctx(0), mk_ctx(1)]
    t1_bufs = CTX[1].SF_b + CTX[1].SB_b + [getattr(CTX[1], nm) for nm in
                                           ("smb", "gwb", "dgwb", "sstb_b", "ssob", "Sfb", "Sbb", "ksqb", "srq", "wob")]

    def chunk_body(l, c, X):
        T = TB
        sA, sB = X.slots
        wv_d = wind[l, c].rearrange("(k p) n -> p k n", p=128)
        for s, b0 in ((sA, 0), (sB, 4)):
            dst = wsl[s][:].rearrange("p (k n) -> p k n", k=8)
            dma("pool", dst, wv_d[:, :, b0 * 128:(b0 + 4) * 128], (), [wsl_b[s]])
        dma("pool", X.gw_b[:], gwd[l, :, c].rearrange("g p n -> p g n"), (), [X.gwb])
        dma("pool", X.wo_c[:], wod[l, c * 128:(c + 1) * 128, :], (), [X.wob])
        wA = wsl[sA][:].rearrange("p (k n) -> p k n", k=8)
        wB = wsl[sB][:].rearrange("p (k n) -> p k n", k=8)

        def wchunk(k, blk):
            if blk < 4:
                return wA[:, k, blk * 128:(blk + 1) * 128], wsl_b[sA]
            return wB[:, k, (blk - 4) * 128:(blk - 3) * 128], wsl_b[sB]

        cw_base = (112 + c, 176 + c, 176 + 8 + c, 176 + 16 + c)
        dma("sp", X.sst[:, 0], src0d[l, c], (), [X.sstb_b])
        for q_ in range(3):
            dma("sp", X.sst[:, 1 + q_], sgc0d[l, q_ * 8 + c], (), [X.sstb_b])
        dma("sp", X.sh0[:], sh0d[l, c], (), [X.sstb_b])
        act(X.sstb[:], X.sst[:], AF.Copy, [X.sstb_b], [X.sstb_b])
        mset(hcar[:, c:c + 1], 0.0, [X.smb])
        mset(X.hist[:], 0.0, [X.smb])
        mset(X.S_f[:], 0.0, [X.Sfb])
        mset(X.S_b[:], 0.0, [X.Sbb])

        for ti, (c0, n) in enumerate(TILES):
            samp = ti == 4

            if samp:
                sbl = []
                for tix in range(7, 13):
                    for half in (0, 1):
                        sbl.append((X.SBf[tix][:, half * 256:(half + 1) * 256].bitcast(F32), X.SB_b[tix]))
                for tix in range(0, 7):
                    sbl.append((X.SBf[tix][:, 256:512].bitcast(F32), X.SB_b[tix]))
                for q_ in range(1, 4):
                    sbl.append((X.SF[2][0:NS, q_ * 128:(q_ + 1) * 128], X.SF_b[2]))
                sbb_ = []
                for (_, par) in sbl:
                    cb = Buf()
                    cb.r = list(par.r) + ([par.w] if par.w is not None else [])
                    sbb_.append(cb)
                for s_ in range(NS):
                    dma("sp", sbl[s_][0], sS0d[l, s_, c], (), [sbb_[s_]])

            def proj(blk):
                pb = rbank(X)
                for k in range(8):
                    wv_, wb_ = wchunk(k, blk)
                    mm(PS[pb][:, 0:n], wv_, xbf[:, k, c0:c0 + n], k == 0, k == 7, [wb_, xbf_b[k][ti]], [PS_b[pb]])
                return pb

            def bias(blk):
                col = 48 + c * 8 + blk
                return pv[:, col:col + 1]

            def conv(blk, hb, prex_i):
                pb = proj(blk)
                prex, pxb = X.SBf[prex_i], X.SB_b[prex_i]
                for j in range(4):
                    col = cw_base[hb] + j * (8 if hb == 0 else 24)
                    ts(X.dgw[:, j, :], ident_f[:], pv[:, col:col + 1], None, ALU.mult, None, [cst, pvb], [X.dgwb])
                if not samp:
                    cp(prex[:, 0:3], X.hist[:, hb, 0:3], [X.smb], [pxb])
                    act(prex[:, 3:3 + n], PS[pb][:, 0:n], AF.Identity, [PS_b[pb], pvb], [pxb], bias=bias(blk))
                    if ti < 3:
                        cp(X.hist[:, hb, 0:3], prex[:, n:n + 3], [pxb], [X.smb])
                    else:
                        act(X.pco[:, hb, :], PS[pb][:, n - 3:n], AF.Identity, [PS_b[pb], pvb], [X.smb], bias=bias(blk))
                    pc = rbank(X)
                    for j in range(4):
                        mm(PS[pc][:, 0:n], X.dgw[:, j, :], prex[:, j:j + n], j == 0, j == 3, [X.dgwb, pxb], [PS_b[pc]])
                else:
                    act(prex[:, 0:n], PS[pb][:, 0:n], AF.Identity, [PS_b[pb], pvb], [pxb], bias=bias(blk))
                    act(X.sso[:, hb, 2, :], PS[pb][:, 0:n], AF.Identity, [PS_b[pb], pvb], [X.ssob], bias=bias(blk))
                    cp(X.sso[:, hb, 0:2, :], X.sst[:, hb, 1:3, :], [X.sstb_b], [X.ssob])
                    pc = rbank(X)
                    for j in range(3):
                        mm(PS[pc][:, 0:n], X.dgw[:, j, :], X.sstb[:, hb, j, :], j == 0, False, [X.dgwb, X.sstb_b], [PS_b[pc]])
                    mm(PS[pc][:, 0:n], X.dgw[:, 3, :], prex[:, 0:n], False, True, [X.dgwb, pxb], [PS_b[pc]])
                return pc

            pc = conv(0, 0, 0)
            xc, xcb = X.SF[0], X.SF_b[0]
            act(xc[:, 0:n], PS[pc][:, 0:n], AF.Identity, [PS_b[pc], pvb], [xcb], bias=pv[:, 144 + c:145 + c])
            xcbf, xcbfb = X.SBf[1], X.SB_b[1]
            act(xcbf[:, 0:n], PS[pc][:, 0:n], AF.Identity, [PS_b[pc], pvb], [xcbfb], bias=pv[:, 144 + c:145 + c])
            pr_, pi_ = rbank(X), rbank(X)
            mm(PS[pr_][:, 0:n], X.gw_b[:, 0, :], xcbf[:, 0:n], True, True, [X.gwb, xcbfb], [PS_b[pr_]])
            mm(PS[pi_][:, 0:n], X.gw_b[:, 1, :], xcbf[:, 0:n], True, True, [X.gwb, xcbfb], [PS_b[pi_]])
            rr, rrb = X.SF[1], X.SF_b[1]
            ig, igb = X.SF[2], X.SF_b[2]
            act(rr[:, 0:n], PS[pr_][:, 0:n], AF.Sigmoid, [PS_b[pr_], pvb], [rrb], bias=pv[:, 152 + c:153 + c])
            act(ig[:, 0:n], PS[pi_][:, 0:n], AF.Sigmoid, [PS_b[pi_], pvb], [igb], bias=pv[:, 160 + c:161 + c])
            aa, aab = X.SF[3], X.SF_b[3]
            act(aa[:, 0:n], rr[:, 0:n], AF.Exp, [rrb, pvb], [aab], scale=nsp[:, c:c + 1])
            act(rr[:, 0:n], rr[:, 0:n], AF.Exp, [rrb, pvb], [rrb], scale=nsp[:, 8 + c:9 + c])
            act(rr[:, 0:n], rr[:, 0:n], AF.Ln, [rrb], [rrb], bias=1.0, scale=-1.0)
            act(rr[:, 0:n], rr[:, 0:n], AF.Exp, [rrb], [rrb], scale=0.5)
            tt(ig[:, 0:n], ig[:, 0:n], xc[:, 0:n], ALU.mult, [igb, xcb], [igb])
            tt(ig[:, 0:n], ig[:, 0:n], rr[:, 0:n], ALU.mult, [igb, rrb], [igb])
            hs, hsb = X.SF[0], X.SF_b[0]
            if not samp:
                scan(hs[:, 0:n], aa[:, 0:n], ig[:, 0:n], hcar[:, c:c + 1], [aab, igb, X.smb], [hsb])
                cp(hcar[:, c:c + 1], hs[:, n - 1:n], [hsb], [X.smb])
                if ti == 3:
                    dma("sp", o_prc[l, c], X.pco[:, 0, :], [X.smb])
            else:
                tt(hs[:, 0:n], aa[:, 0:n], X.sh0[:], ALU.mult, [aab, X.sstb_b], [hsb])
                tt(hs[:, 0:n], hs[:, 0:n], ig[:, 0:n], ALU.add, [hsb, igb], [hsb])
                cp(X.sho[:], hs[:, 0:n], [hsb], [X.ssob])
                dma("sp", o_sh[l, c], X.sho[:], [X.ssob])
                dma("sp", o_src[l, c], X.sso[:, 0], [X.ssob])
            pb = proj(1)
            yy, yyb = X.SF[1], X.SF_b[1]
            act(yy[:, 0:n], PS[pb][:, 0:n], AF.Identity, [PS_b[pb], pvb], [yyb], bias=bias(1))
            y2, y2b = X.SF[2], X.SF_b[2]
            tt(y2[:, 0:n], yy[:, 0:n], yy[:, 0:n], ALU.mult, [yyb], [y2b])
            ts(y2[:, 0:n], y2[:, 0:n], 0.044715, 1.0, ALU.mult, ALU.add, [y2b], [y2b])
            tt(y2[:, 0:n], y2[:, 0:n], yy[:, 0:n], ALU.mult, [y2b, yyb], [y2b])
            act(y2[:, 0:n], y2[:, 0:n], AF.Sigmoid, [y2b], [y2b], scale=1.5957691216057308)
            tt(yy[:, 0:n], yy[:, 0:n], y2[:, 0:n], ALU.mult, [yyb, y2b], [yyb])
            tt(hs[:, 0:n], hs[:, 0:n], yy[:, 0:n], ALU.mult, [hsb, yyb], [hsb])
            pb = proj(2)
            act(y2[:, 0:n], PS[pb][:, 0:n], AF.Sigmoid, [PS_b[pb], pvb], [y2b], bias=bias(2))
            ma, mab = X.SF[5], X.SF_b[5]
            tt(ma[:, 0:n], hs[:, 0:n], y2[:, 0:n], ALU.mult, [hsb, y2b], [mab])

            h_ = c
            pc = conv(3, 1, 0)
            qf, qfb = X.SF[0], X.SF_b[0]
            act(qf[:, 0:n], PS[pc][:, 0:n], AF.Sigmoid, [PS_b[pc]], [qfb])
            tt(qf[:, 0:n], qf[:, 0:n], PS[pc][:, 0:n], ALU.mult, [qfb, PS_b[pc]], [qfb])
            pc = conv(4, 2, 1)
            kf, kfb = X.SF[1], X.SF_b[1]
            act(kf[:, 0:n], PS[pc][:, 0:n], AF.Sigmoid, [PS_b[pc]], [kfb])
            tt(kf[:, 0:n], kf[:, 0:n], PS[pc][:, 0:n], ALU.mult, [kfb, PS_b[pc]], [kfb])
            pc = conv(5, 3, 0)
            vT, vTb = X.SBf[2], X.SB_b[2]
            vsg, vsgb = X.SF[2], X.SF_b[2]
            act(vsg[:, 0:n], PS[pc][:, 0:n], AF.Sigmoid, [PS_b[pc]], [vsgb])
            tt(vT[:, 0:n], vsg[:, 0:n], PS[pc][:, 0:n], ALU.mult, [vsgb, PS_b[pc]], [vTb])
            if samp:
                for q_ in range(3):
                    dma("sp", o_sgc[l, q_ * 8 + c], X.sso[:, 1 + q_], [X.ssob])
            elif ti == 3:
                for q_ in range(3):
                    dma("sp", o_pgc[l, q_ * 8 + c], X.pco[:, 1 + q_, :], [X.smb])
            sqs, sqsb = X.SBf[3], X.SB_b[3]
            rs, rsb = X.SF[2], X.SF_b[2]
            qT, qTb = X.SBf[4], X.SB_b[4]
            kT, kTb = X.SBf[5], X.SB_b[5]
            for (src, srcb, dstT, dstTb, scl) in ((qf, qfb, qT, qTb, 128 ** -0.5), (kf, kfb, kT, kTb, 1.0)):
                act(sqs[:, 0:n], src[:, 0:n], AF.Square, [srcb], [sqsb])
                pq = rbank(X)
                mm(PS[pq][:, 0:n], ones_b[:], sqs[:, 0:n], True, True, [cst, sqsb], [PS_b[pq]])
                ts(rs[:, 0:n], PS[pq][:, 0:n], RMS_EPS, None, ALU.add, None, [PS_b[pq]], [rsb])
                act(rs[:, 0:n], rs[:, 0:n], AF.Ln, [rsb], [rsb])
                act(rs[:, 0:n], rs[:, 0:n], AF.Exp, [rsb], [rsb], scale=-0.5)
                stt(src[:, 0:n], src[:, 0:n], scl, rs[:, 0:n], ALU.mult, ALU.mult, [srcb, rsb], [srcb])
                act(dstT[:, 0:n], src[:, 0:n], AF.Copy, [srcb], [dstTb])

            if samp:
                cols = slice(h_ * NS, (h_ + 1) * NS)
                cp(X.skq[:, 0, :], kf[:, 0:n], [kfb], [X.srq])
                cp(X.skq[:, 1, :], qf[:, 0:n], [qfb], [X.srq])
                qk_, qkb = X.SF[3], X.SF_b[3]
                tt(qk_[:, 0:n], qf[:, 0:n], kf[:, 0:n], ALU.mult, [qfb, kfb], [qkb])
                pq = rbank(X)
                mm(PS[pq][:, 0:n], ones_f[:], qk_[:, 0:n], True, True, [cst, qkb], [PS_b[pq]])
                act(X.srow_qk[:], PS[pq][:, 0:n], AF.Copy, [PS_b[pq]], [X.srq])
                pt = rbank(X)
                tr(PS[pt][0:NS, 0:128], kf[:, 0:n], ident_f[:], [kfb, cst], [PS_b[pt]])
                kpm, kpmb = X.SF[4], X.SF_b[4]
                act(kpm[0:NS, 0:128], PS[pt][0:NS, 0:128], AF.Copy, [PS_b[pt]], [kpmb])
                pks = X.banks[0]
                X.held = {pks}
                vfs, vfsb = X.SF[1], X.SF_b[1]
                act(vfs[:, 0:n], vT[:, 0:n], AF.Copy, [vTb], [vfsb])
                for s_ in range(NS):
                    mm(PS[pks][:, 2 * s_:2 * s_ + 2], sbl[s_][0], X.skq[:, :, s_], True, True,
                       [sbb_[s_], X.srq], [PS_b[pks]])
                act(X.ksq_t[:], PS[pks][:, 0:2 * NS], AF.Copy, [PS_b[pks]], [X.ksqb])
                ksv = X.ksq_t[:].rearrange("p (s t) -> p t s", t=2)
                X.held = set()
                vn, vnb = X.SF[3], X.SF_b[3]
                tt(vn[:, 0:n], vfs[:, 0:n], srow["beta"][:, cols], ALU.mult, [vfsb, srb], [vnb])
                tt(vfs[:, 0:n], ksv[:, 0, :], srow["nbe"][:, cols], ALU.mult, [X.ksqb, srb], [vfsb])
                tt(vn[:, 0:n], vn[:, 0:n], vfs[:, 0:n], ALU.add, [vnb, vfsb], [vnb])
                oo, oob = X.SF[0], X.SF_b[0]
                tt(oo[:, 0:n], ksv[:, 1, :], srow["eg"][:, cols], ALU.mult, [X.ksqb, srb], [oob])
                tt(vfs[:, 0:n], vn[:, 0:n], X.srow_qk[:], ALU.mult, [vnb, X.srq], [vfsb])
                tt(oo[:, 0:n], oo[:, 0:n], vfs[:, 0:n], ALU.add, [oob, vfsb], [oob])
                pt = rbank(X)
                tr(PS[pt][0:NS, 0:128], vn[:, 0:n], ident_f[:], [vnb, cst], [PS_b[pt]])
                vpm, vpmb = X.SF[1], X.SF_b[1]
                act(vpm[0:NS, 0:128], PS[pt][0:NS, 0:128], AF.Copy, [PS_b[pt]], [vpmb])
                for s_ in range(NS):
                    p_ = h_ * NS + s_
                    vm, vmb = sbl[19 + s_ % 3][0], sbb_[19 + s_ % 3]
                    ts(vm, vpm[0:NS, 0:128], ident_f[0:NS, s_:s_ + 1], None, ALU.mult, None, [vpmb, cst], [vmb])
                    po = rbank(X)
                    mm(PS[po][:, 0:128], kpm[0:NS, 0:128], vm, True, True, [kpmb, vmb], [PS_b[po]])
                    so_, sob_ = sbl[16 + s_ % 3][0], sbb_[16 + s_ % 3]
                    stt(so_, sbl[s_][0], srow["eg"][:, p_:p_ + 1], PS[po][:, 0:128], ALU.mult, ALU.add,
                        [sbb_[s_], srb, PS_b[po]], [sob_])
                    dma("sp", o_sS[l, s_, h_], so_, [sob_])
                for (_, par), cb in zip(sbl, sbb_):
                    par.r.extend(cb.r)
                    if cb.w is not None:
                        par.r.append(cb.w)
                po_hold = None
                of, ofb = oo, oob
            else:
                nb0 = ti * 4
                pg, pO, pA, pB = X.banks[0], X.banks[0], X.banks[1], X.banks[2]
                X.held = {pg}
                for nn in range(4):
                    dg_, dgb = X.SF[3], X.SF_b[3]
                    ts(dg_[:, 0:128], ident_f[:], T["gc"][:, nb0 + nn, h_:h_ + 1], None, ALU.mult, None,
                       [cst, tbb], [dgb])
                    mm(PS[pg][:, nn * 128:(nn + 1) * 128], ones_f[:], dg_[:, 0:128], True, True, [cst, dgb], [PS_b[pg]])
                egr, egrb = X.SF[3], X.SF_b[3]
                act(egr[:, 0:n], PS[pg][:, 0:n], AF.Exp, [PS_b[pg]], [egrb])
                qdT, qdTb = X.SBf[6], X.SB_b[6]
                tt(qdT[:, 0:n], qf[:, 0:n], egr[:, 0:n], ALU.mult, [qfb, egrb], [qdTb])
                pk_ = rbank(X)
                pkb = PS[pk_][:].bitcast(BF16)
                for nn in range(4):
                    tr(pkb[:, nn * 128:(nn + 1) * 128], kT[:, nn * 128:(nn + 1) * 128], ident_b[:], [kTb, cst], [PS_b[pk_]])
                kbg, kbgb = X.SBf[7], X.SB_b[7]
                ktm, ktmb = X.SBf[8], X.SB_b[8]
                pk3 = pkb[:, 0:512].rearrange("p (a d) -> p a d", a=4)
                bge_v = T["bge"][:, nb0:nb0 + 4, h_:h_ + 1].to_broadcast([128, 4, 128])
                ekt_v = ekt[:].rearrange("p (b e) -> p b e", e=8)[:, nb0:nb0 + 4, h_:h_ + 1].to_broadcast([128, 4, 128])
                tt(kbg[:, 0:512].rearrange("p (a d) -> p a d", a=4), pk3, bge_v, ALU.mult, [PS_b[pk_], tbb], [kbgb])
                tt(ktm[:, 0:512].rearrange("p (a d) -> p a d", a=4), pk3, ekt_v, ALU.mult, [PS_b[pk_], tbb], [ktmb])
                pv_ = rbank(X)
                pvb_ = PS[pv_][:].bitcast(BF16)
                for nn in range(4):
                    tr(pvb_[:, nn * 128:(nn + 1) * 128], vT[:, nn * 128:(nn + 1) * 128], ident_b[:], [vTb, cst], [PS_b[pv_]])
                vbt, vbtb = X.SBf[9], X.SB_b[9]
                be_v = T["beta"][:, nb0:nb0 + 4, h_:h_ + 1].to_broadcast([128, 4, 128])
                tt(vbt[:, 0:512].rearrange("p (a d) -> p a d", a=4), pvb_[:, 0:512].rearrange("p (a d) -> p a d", a=4),
                   be_v, ALU.mult, [PS_b[pv_], tbb], [vbtb])
                X.held = {pg, pA, pB}
                for nn in range(4):
                    sl = slice(nn * 128, (nn + 1) * 128)
                    mm(PS[pA][:, sl], kT[:, sl], kT[:, sl], True, True, [kTb], [PS_b[pA]])
                    mm(PS[pB][:, sl], kT[:, sl], qT[:, sl], True, True, [kTb, qTb], [PS_b[pB]])
                Dm, Dmb = X.SF[2], X.SF_b[2]
                DTm, DTmb = X.SF[3], X.SF_b[3]
                for nn in range(4):
                    sl = slice(nn * 128, (nn + 1) * 128)
                    gcc = T["gc"][:, nb0 + nn, h_:h_ + 1]
                    ts(Dm[:, sl], PS[pg][:, sl], gcc, 0.0, ALU.subtract, ALU.max, [PS_b[pg], tbb], [Dmb])
                    ts(DTm[:, sl], PS[pg][:, sl], gcc, 0.0, ALU.subtract, ALU.min, [PS_b[pg], tbb], [DTmb])
                act(Dm[:, 0:n], Dm[:, 0:n], AF.Exp, [Dmb], [Dmb], scale=-1.0)
                act(DTm[:, 0:n], DTm[:, 0:n], AF.Exp, [DTmb], [DTmb])
                m3 = lambda t_: t_[:, 0:512].rearrange("p (a d) -> p a d", a=4)
                tt(m3(Dm), m3(Dm), maskSL[:].unsqueeze(1).to_broadcast([128, 4, 128]), ALU.mult, [Dmb, cst], [Dmb])
                tt(m3(DTm), m3(DTm), maskU[:].unsqueeze(1).to_broadcast([128, 4, 128]), ALU.mult, [DTmb, cst], [DTmb])
                X0, X0b = X.SBf[10], X.SB_b[10]
                for nn in range(4):
                    sl = slice(nn * 128, (nn + 1) * 128)
                    stt(X0[:, sl], PS[pA][:, sl], T["nbeta"][:, nb0 + nn, h_:h_ + 1], Dm[:, sl], ALU.mult, ALU.mult,
                        [PS_b[pA], tbb, Dmb], [X0b])
                AIT, AITb = X.SBf[4], X.SB_b[4]
                tt(AIT[:, 0:n], PS[pB][:, 0:n], DTm[:, 0:n], ALU.mult, [PS_b[pB], DTmb], [AITb])
                X.held = set()
                px = rbank(X)
                pxb_ = PS[px][:].bitcast(BF16)
                for nn in range(4):
                    tr(pxb_[:, nn * 128:(nn + 1) * 128], X0[:, nn * 128:(nn + 1) * 128], ident_b[:], [X0b, cst], [PS_b[px]])
                XT0, XT0b = X.SBf[11], X.SB_b[11]
                act(XT0[:, 0:n], pxb_[:, 0:n], AF.Copy, [PS_b[px]], [XT0b])
                RT, RTb = X.SBf[12], X.SB_b[12]
                Rm, Rmb = X.SBf[5], X.SB_b[5]
                U16 = mybir.dt.uint16
                bd3 = BD16[:].unsqueeze(1).to_broadcast([128, 4, 128])
                id3 = ident_b[:].unsqueeze(1).to_broadcast([128, 4, 128])
                Pt = [(X.SBf[0], X.SB_b[0]), (X.SBf[1], X.SB_b[1]), (X.SBf[2], X.SB_b[2]), (X.SBf[3], X.SB_b[3])]
                (Pc, Pcb), (PTc, PTcb) = Pt[0], Pt[1]
                tt(m3(Pc), m3(X0), bd3, ALU.mult, [X0b, cst], [Pcb])
                tt(m3(PTc), m3(XT0), bd3, ALU.mult, [XT0b, cst], [PTcb])
                tt(m3(Rm), m3(Pc), id3, ALU.add, [Pcb, cst], [Rmb])
                tt(m3(RT), m3(PTc), id3, ALU.add, [PTcb, cst], [RTb])

                def mm4(pbank, lhs, rhs, r_):
                    for nn in range(4):
                        sl = slice(nn * 128, (nn + 1) * 128)
                        mm(PS[pbank][:, sl], lhs[:, sl], rhs[:, sl], True, True, r_, [PS_b[pbank]])

                cur = 0
                for lvl in range(1, 4):
                    (Pn, Pnb), (PTn, PTnb) = Pt[2 - cur], Pt[3 - cur]
                    p1, p2 = rbank(X), rbank(X)
                    mm4(p1, PTc, Pc, [PTcb, Pcb])
                    mm4(p2, Pc, PTc, [PTcb, Pcb])
                    act(Pn[:, 0:n], PS[p1][:, 0:n], AF.Copy, [PS_b[p1]], [Pnb])
                    act(PTn[:, 0:n], PS[p2][:, 0:n], AF.Copy, [PS_b[p2]], [PTnb])
                    p3, p4 = rbank(X), rbank(X)
                    mm4(p3, PTn, Rm, [PTnb, Rmb])
                    mm4(p4, Pn, RT, [Pnb, RTb])
                    tt(Rm[:, 0:n], Rm[:, 0:n], PS[p3][:, 0:n], ALU.add, [Rmb, PS_b[p3]], [Rmb])
                    tt(RT[:, 0:n], RT[:, 0:n], PS[p4][:, 0:n], ALU.add, [RTb, PS_b[p4]], [RTb])
                    Pc, Pcb, PTc, PTcb = Pn, Pnb, PTn, PTnb
                    cur = 2 - cur
                (Wt, Wtb), (Ut, Utb) = Pt[0], Pt[1]
                for m_ in (32, 64, 128):
                    om = OFFM[m_][:].bitcast(U16).unsqueeze(1).to_broadcast([128, 4, 128])
                    pu = rbank(X)
                    mm4(pu, X0, RT, [X0b, RTb])
                    if m_ < 128:
                        pw_ = rbank(X)
                        mm4(pw_, XT0, Rm, [XT0b, Rmb])
                        act(Wt[:, 0:n], PS[pw_][:, 0:n], AF.Copy, [PS_b[pw_]], [Wtb])
                    act(Ut[:, 0:n], PS[pu][:, 0:n], AF.Copy, [PS_b[pu]], [Utb])
                    pv2_ = rbank(X)
                    mm4(pv2_, Rm, Ut, [Rmb, Utb])
                    if m_ < 128:
                        pz = rbank(X)
                        mm4(pz, RT, Wt, [RTb, Wtb])
                        A("dve", lambda h, om=om, pz=pz: h.copy_predicated(out=m3(Rm), mask=om, data=m3(PS[pz])),
                          [PS_b[pz], cst], [Rmb], dur=700.0)
                    A("dve", lambda h, om=om, pv2_=pv2_: h.copy_predicated(out=m3(RT), mask=om, data=m3(PS[pv2_])),
                      [PS_b[pv2_], cst], [RTb], dur=700.0)
                pw = rbank(X)
                for nn in range(4):
                    sl = slice(nn * 128, (nn + 1) * 128)
                    mm(PS[pw][:, sl], kbg[:, sl], RT[:, sl], True, True, [kbgb, RTb], [PS_b[pw]])
                wnT, wnTb = X.SBf[3], X.SB_b[3]
                act(wnT[:, 0:n], PS[pw][:, 0:n], AF.Copy, [PS_b[pw]], [wnTb], scale=-1.0)
                X.held = {pO}
                for nn in range(4):
                    sl = slice(nn * 128, (nn + 1) * 128)
                    pn_ = rbank(X)
                    mm(PS[pn_][:, 0:128], RT[:, sl], vbt[:, sl], True, False, [RTb, vbtb], [PS_b[pn_]])
                    mm(PS[pn_][:, 0:128], wnT[:, sl], X.S_b[:], False, True, [wnTb, X.Sbb], [PS_b[pn_]])
                    vnw, vnwb = X.SBf[2][:, (nn % 2) * 128:(nn % 2) * 128 + 128], X.SB_b[2]
                    act(vnw[:, 0:128], PS[pn_][:, 0:128], AF.Copy, [PS_b[pn_]], [vnwb])
                    mm(PS[pO][:, sl], X.S_b[:], qdT[:, sl], True, False, [X.Sbb, qdTb], [PS_b[pO]])
                    mm(PS[pO][:, sl], vnw[:, 0:128], AIT[:, sl], False, True, [vnwb, AITb], [PS_b[pO]])
                    ps_ = rbank(X)
                    mm(PS[ps_][:, 0:128], ktm[:, sl], vnw[:, 0:128], True, True, [ktmb, vnwb], [PS_b[ps_]])
                    eglc = egl[:, (nb0 + nn) * 8 + h_:(nb0 + nn) * 8 + h_ + 1]
                    stt(X.S_f[:], X.S_f[:], eglc, PS[ps_][:, 0:128], ALU.mult, ALU.add, [X.Sfb, tbb, PS_b[ps_]], [X.Sfb])
                    act(X.S_b[:], X.S_f[:], AF.Copy, [X.Sfb], [X.Sbb])
                if ti == 3:
                    dma("sp", o_pS[l, h_], X.S_f[:], [X.Sfb])
                of, ofb = X.SF[0], X.SF_b[0]
                act(of[:, 0:n], PS[pO][:, 0:n], AF.Copy, [PS_b[pO]], [ofb])
                X.held = set()

            sq2, sq2b = X.SBf[3], X.SB_b[3]
            act(sq2[:, 0:n], of[:, 0:n], AF.Square, [ofb], [sq2b])
            pq = rbank(X)
            mm(PS[pq][:, 0:n], ones_b[:], sq2[:, 0:n], True, True, [cst, sq2b], [PS_b[pq]])
            rs2, rs2b = X.SF[1], X.SF_b[1]
            ts(rs2[:, 0:n], PS[pq][:, 0:n], 1.0 / 128, RMS_EPS, ALU.mult, ALU.add, [PS_b[pq]], [rs2b])
            act(rs2[:, 0:n], rs2[:, 0:n], AF.Ln, [rs2b], [rs2b])
            act(rs2[:, 0:n], rs2[:, 0:n], AF.Exp, [rs2b], [rs2b], scale=-0.5)
            stt(of[:, 0:n], of[:, 0:n], pv[:, 272:273], rs2[:, 0:n], ALU.mult, ALU.mult, [ofb, pvb, rs2b], [ofb])
            pb = proj(6)
            zs, zsb = X.SF[2], X.SF_b[2]
            act(zs[:, 0:n], PS[pb][:, 0:n], AF.Sigmoid, [PS_b[pb], pvb], [zsb], bias=bias(6))
            stt(zs[:, 0:n], PS[pb][:, 0:n], bias(6), zs[:, 0:n], ALU.add, ALU.mult, [PS_b[pb], pvb, zsb], [zsb])
            tt(of[:, 0:n], of[:, 0:n], zs[:, 0:n], ALU.mult, [ofb, zsb], [ofb])
            pb = proj(7)
            act(zs[:, 0:n], PS[pb][:, 0:n], AF.Sigmoid, [PS_b[pb], pvb], [zsb], bias=bias(7))
            tt(of[:, 0:n], of[:, 0:n], zs[:, 0:n], ALU.mult, [ofb, zsb], [ofb])
            mt, mtb = X.SBf[6], X.SB_b[6]
            tt(mt[:, 0:n], of[:, 0:n], ma[:, 0:n], ALU.add, [ofb, mab], [mtb])
            for m in range(8):
                pb = rbank(X)
                mm(PS[pb][:, 0:n], X.wo_c[:, m * 128:(m + 1) * 128], mt[:, 0:n], True, True, [X.wob, mtb], [PS_b[pb]])
                tt(xres[:, m, c0:c0 + n], PS[pb][:, 0:n], xres[:, m, c0:c0 + n], ALU.add,
                   [PS_b[pb], xres_b[m][ti]], [xres_b[m][ti]])


    def mixer(l):
        gate_tables(l)
        nops0 = len(P.all)
        prior = []
        for row in mbuf_b:
            for b in row:
                prior.extend(b.r)
                if b.w is not None:
                    prior.append(b.w)
        for b in t1_bufs:
            b.w = None
            b.r = list(prior)
        for pair in range(4):
            chunk_body(l, 2 * pair, CTX[0])
            chunk_body(l, 2 * pair + 1, CTX[1])
        dma("sp", o_ph[l], hcar[:], [CTX[0].smb, CTX[1].smb])
        if l == 0:
            print("mixer ops per (chunk,tile):", (len(P.all) - nops0) / 40.0)
        post = []
        for b in t1_bufs:
            post.extend(b.r)
            if b.w is not None:
                post.append(b.w)
        for row in mbuf_b:
            for b in row:
                b.r = list(b.r) + post

    for l in range(n_layers):
        layer_params(l)
        ffn(2 * l)
        layernorm(0, False)
        mixer(l)
        layernorm(1, False)
        ffn(2 * l + 1)
        layernorm(2, l == n_layers - 1)
    P.emit(scheduled=SCHED)
    print("ops:", len(P.all), "est total us:", getattr(P, "est_total", 0) / 1e3)
    st.close()
    return nc


def _prep_shared(inp):
    f = lambda a: np.ascontiguousarray(np.asarray(a, dtype=np.float32))
    w1 = f(np.stack([inp["ffn1_w1"], inp["ffn2_w1"]], 1).reshape(2 * L, D, DFF))
    w3 = f(np.stack([inp["ffn1_w3"], inp["ffn2_w3"]], 1).reshape(2 * L, D, DFF))
    w2 = f(np.stack([inp["ffn1_w2"], inp["ffn2_w2"]], 1).reshape(2 * L, DFF, D))
    w_in = np.asarray(inp["w_in"], np.float32)
    b_in = np.asarray(inp["b_in"], np.float32)
    offs = [0, 1024, 6160, 2048, 3072, 4096, 5120, 7184]
    win = np.empty((L, 8, D, 1024), np.float32)
    for c in range(8):
        for bi, o in enumerate(offs):
            win[:, c, :, bi * 128:(bi + 1) * 128] = w_in[:, :, o + c * 128:o + (c + 1) * 128]
    wab = f(w_in[:, :, 6144:6160])
    gatew = np.zeros((L, 2, 8, 128, 128), np.float32)
    for gi, nm in enumerate(("rg_wr", "rg_wi")):
        w = np.asarray(inp[nm], np.float32)
        for c in range(8):
            gatew[:, gi, c, 0:64, 0:64] = w[:, 2 * c]
            gatew[:, gi, c, 64:128, 64:128] = w[:, 2 * c + 1]
    pvec = np.empty((L, 128, NPV), np.float32)
    fm = lambda v: np.asarray(v, np.float32).reshape(L, -1, 128).transpose(0, 2, 1)
    for i, (g, b) in enumerate((("ln1_g", "ln1_b"), ("ln2_g", "ln2_b"), ("ln3_g", "ln3_b"))):
        pvec[:, :, 16 * i:16 * i + 8] = fm(inp[g])
        pvec[:, :, 16 * i + 8:16 * i + 16] = fm(inp[b])
    for c in range(8):
        for bi, o in enumerate(offs):
            pvec[:, :, 48 + c * 8 + bi] = b_in[:, o + c * 128:o + (c + 1) * 128]
    rcw = np.asarray(inp["rg_conv_w"], np.float32)
    for j in range(4):
        pvec[:, :, 112 + j * 8:112 + (j + 1) * 8] = fm(rcw[:, j])
    pvec[:, :, 144:152] = fm(inp["rg_conv_b"])
    pvec[:, :, 152:160] = fm(inp["rg_br"])
    pvec[:, :, 160:168] = fm(inp["rg_bi"])
    pvec[:, :, 168:176] = fm(inp["rg_lambda"])
    gcw = np.asarray(inp["gdn_conv_w"], np.float32)
    for j in range(4):
        pvec[:, :, 176 + j * 24:176 + (j + 1) * 24] = fm(gcw[:, j])
    pvec[:, :, 272] = np.asarray(inp["gdn_norm_w"], np.float32)
    prow = np.concatenate([b_in[:, 6144:6160], np.asarray(inp["gdn_a_log"], np.float32),
                           np.asarray(inp["gdn_dt_bias"], np.float32)], axis=1).reshape(L, 1, 32)
    return {"w1": w1, "w3": w3, "w2": w2, "win": win, "wab": wab, "wo": f(inp["w_o"]), "gatew": gatew,
            "pvec": f(pvec), "prow": f(prow)}


def _prep_core(inp, i):
    xp = np.asarray(inp["x_prompt"], np.float32)[i]
    xs = np.asarray(inp["x_sample"], np.float32)[i * NS:(i + 1) * NS, 0]
    x = np.concatenate([xp, xs], 0)
    xin = np.ascontiguousarray(x.T.reshape(8, 128, TOK).transpose(1, 0, 2))
    sl = slice(i * NS, (i + 1) * NS)
    h0 = np.asarray(inp["state_rglru_h"], np.float32)[:, sl]
    sh0 = np.ascontiguousarray(h0.transpose(0, 2, 1).reshape(L, 8, 128, NS))
    rc = np.asarray(inp["state_rglru_conv"], np.float32)[:, sl]
    src0 = np.ascontiguousarray(rc.transpose(0, 3, 2, 1).reshape(L, 8, 128, 3, NS))
    gc = np.asarray(inp["state_gdn_conv"], np.float32)[:, sl]
    sgc0 = np.ascontiguousarray(gc.transpose(0, 3, 2, 1).reshape(L, 24, 128, 3, NS))
    sS0 = np.ascontiguousarray(np.asarray(inp["state_gdn_S"], np.float32)[:, sl])
    return {"xin": xin, "sh0": sh0, "src0": src0, "sgc0": sgc0, "sS0": sS0}


_NC_CACHE = {}


def kernel(**inp):
    if "nc" not in _NC_CACHE:
        _NC_CACHE["nc"] = build_program()
    nc = _NC_CACHE["nc"]
    shared = _prep_shared(inp)
    in_maps = []
    for i in range(8):
        m = dict(shared)
        m.update(_prep_core(inp, i))
        in_maps.append(m)
    res = run_bass_kernel_spmd(nc, in_maps, core_ids=list(range(8)))
    R = res.results
    Bp = 8
    y_prompt = np.empty((Bp, NP, D), np.float32)
    y_sample = np.empty((8 * NS, 1, D), np.float32)
    p_h = np.empty((L, Bp, D), np.float32)
    p_rc = np.empty((L, Bp, 3, D), np.float32)
    p_S = np.empty((L, Bp, 8, 128, 128), np.float32)
    p_gc = np.empty((L, Bp, 3, 3072), np.float32)
    s_h = np.empty((L, 8 * NS, D), np.float32)
    s_rc = np.empty((L, 8 * NS, 3, D), np.float32)
    s_S = np.empty((L, 8 * NS, 8, 128, 128), np.float32)
    s_gc = np.empty((L, 8 * NS, 3, 3072), np.float32)
    for i in range(8):
        r = R[i]
        y = np.asarray(r["yout"]).transpose(1, 0, 2).reshape(D, TOK).T
        y_prompt[i] = y[:NP]
        sl = slice(i * NS, (i + 1) * NS)
        y_sample[sl, 0] = y[NP:]
        p_h[:, i] = np.asarray(r["o_ph"]).transpose(0, 2, 1).reshape(L, D)
        p_rc[:, i] = np.asarray(r["o_prc"]).transpose(0, 3, 1, 2).reshape(L, 3, D)
        p_S[:, i] = np.asarray(r["o_pS"])
        p_gc[:, i] = np.asarray(r["o_pgc"]).transpose(0, 3, 1, 2).reshape(L, 3, 3072)
        s_h[:, sl] = np.asarray(r["o_sh"]).transpose(0, 3, 1, 2).reshape(L, NS, D)
        s_rc[:, sl] = np.asarray(r["o_src"]).transpose(0, 4, 3, 1, 2).reshape(L, NS, 3, D)
        s_S[:, sl] = np.asarray(r["o_sS"])
        s_gc[:, sl] = np.asarray(r["o_sgc"]).transpose(0, 4, 3, 1, 2).reshape(L, NS, 3, 3072)
    return (y_prompt, y_sample, p_h, p_rc, p_S, p_gc, s_h, s_rc, s_S, s_gc)
```

```python
import math
from contextlib import ExitStack
import numpy as np
import concourse.bass as bass
import concourse.mybir as mybir
from concourse.bass_utils import run_bass_kernel_spmd

F32 = mybir.dt.float32
BF16 = mybir.dt.bfloat16
AF = mybir.ActivationFunctionType
ALU = mybir.AluOpType
AX = mybir.AxisListType

L = 4
D = 1024
NP = 2048
NS = 16
TOK = NP + NS
DFF = 2816
NJ = DFF // 128
ALPHA = (2 * L) ** 0.25
LN_EPS = 1e-5
RMS_EPS = 1e-6
TILES = [(0, 512), (512, 512), (1024, 512), (1536, 512), (2048, 16)]
NPV = 273
ENGS = ("pe", "act", "dve", "pool", "sp")


class Buf:
    __slots__ = ("w", "r", "excl")

    def __init__(self, excl=False):
        self.w = None
        self.r = []
        self.excl = excl


class Prog:
    def __init__(self, nc):
        self.nc = nc
        self.all = []
        self.n_ch = {"sp": 28, "pool": 12}

    def _collect(self, eng, reads, writes, is_dma):
        deps = {}
        for b in reads:
            ev = b.w
            if ev is None:
                continue
            ee = self.all[ev]
            need = not (eng == "pe" and ee[0] == "pe" and ee[1] == "c")
            deps[ev] = deps.get(ev, False) or need
        for b in writes:
            evs = list(b.r)
            if b.w is not None:
                evs.append(b.w)
            for ev in evs:
                ee = self.all[ev]
                same = (ee[0] == eng and ee[1] == "c" and not is_dma)
                deps[ev] = deps.get(ev, False) or (not same)
        return list(deps.items())

    def _add(self, eng, kind, fn, reads, writes, dur, fam):
        ex = [b for b in reads if b.excl and b not in writes]
        if ex:
            writes = list(writes) + ex
        deps = self._collect(eng, reads, writes, kind == "d")
        idx = len(self.all)
        self.all.append([eng, kind, fn, deps, dur, fam])
        for b in reads:
            b.r.append(idx)
        for b in writes:
            b.w = idx
            b.r = []
        return idx

    def op(self, eng, fn, reads=(), writes=(), dur=300.0, fam=None):
        return self._add(eng, "c", fn, reads, writes, dur, fam)

    def dma(self, eng, fn, reads=(), writes=(), dur=3000.0):
        return self._add(eng, "d", fn, reads, writes, dur, None)

    def schedule(self):
        import heapq
        ops = self.all
        n = len(ops)
        succ = [[] for _ in range(n)]
        npred = [0] * n
        last_q = {}
        for i, o in enumerate(ops):
            if o[0] in QUEUE_INORDER:
                p_ = last_q.get(o[0])
                if p_ is not None and all(d != p_ for d, _ in o[3]):
                    o[3].append((p_, False) if ops[p_][1] == "c" and o[1] == "c" else (p_, None))
                last_q[o[0]] = i
        for i, o in enumerate(ops):
            npred[i] = len(o[3])
            for d, _ in o[3]:
                succ[d].append(i)
        issue_only = {}
        for j_, o in enumerate(ops):
            for d, need in o[3]:
                if need is None:
                    issue_only[(d, j_)] = True
        ready_t = [0.0] * n
        fin = [0.0] * n
        eng_free = {e: 0.0 for e in ENGS}
        eng_fam = {e: None for e in ENGS}
        order = {e: [] for e in ENGS}
        future = {e: [] for e in ENGS}
        avail = {e: [] for e in ENGS}
        for i in range(n):
            if npred[i] == 0:
                heapq.heappush(future[ops[i][0]], (0.0, i))
        done = 0
        XLAT = 250.0
        while done < n:
            best = None
            for e in ENGS:
                t = eng_free[e]
                fut, av = future[e], avail[e]
                while fut and fut[0][0] <= t:
                    heapq.heappush(av, heapq.heappop(fut)[1])
                if av:
                    c_ = (t, av[0], e, True)
                elif fut:
                    c_ = (fut[0][0], fut[0][1], e, False)
                else:
                    continue
                if best is None or (c_[0], c_[1]) < (best[0], best[1]):
                    best = c_
            st, i, e, from_av = best
            if from_av:
                av = avail[e]
                if e == "act" and eng_fam[e] is not None and len(av) > 1 and ops[i][5] not in (None, eng_fam[e]):
                    cands = heapq.nsmallest(12, av)
                    alt = [c for c in cands if ops[c][5] in (None, eng_fam[e]) and c - i < 400]
                    if alt:
                        i = alt[0]
                        av.remove(i)
                        heapq.heapify(av)
                    else:
                        heapq.heappop(av)
                else:
                    heapq.heappop(av)
            else:
                heapq.heappop(future[e])
            o = ops[i]
            dur = o[4]
            if o[1] == "c":
                if o[5] is not None and eng_fam[e] is not None and eng_fam[e] != o[5]:
                    st += 1300.0
                if o[5] is not None:
                    eng_fam[e] = o[5]
                eng_free[e] = st + dur
                fin[i] = st + dur
            else:
                eng_free[e] = st + 120.0
                fin[i] = st + dur
            order[e].append(i)
            done += 1
            for j in succ[i]:
                oj = ops[j]
                lat = XLAT if (oj[0] != e or o[1] == "d") else 60.0
                t = fin[i] + lat
                if issue_only.get((i, j)):
                    t = st
                if t > ready_t[j]:
                    ready_t[j] = t
                npred[j] -= 1
                if npred[j] == 0:
                    heapq.heappush(future[oj[0]], (ready_t[j], j))
        self.order = order
        self.est_total = max(eng_free.values())

    def emit(self, scheduled=True):
        nc = self.nc
        ops = self.all
        if scheduled:
            self.schedule()
            order = self.order
        else:
            order = {e: [i for i, o in enumerate(ops) if o[0] == e] for e in ENGS}
        pos = {}
        for e in ENGS:
            for p_, i in enumerate(order[e]):
                pos[i] = p_
        ch_of, ch_val = {}, {}
        ch_next = {e: 0 for e in self.n_ch}
        ch_count = {}
        ch_prev = {}
        for e in self.n_ch:
            for i in order[e]:
                if ops[i][1] != "d":
                    continue
                c = ch_next[e]
                ch_next[e] = (c + 1) % self.n_ch[e]
                prev = ch_count.get((e, c), 0)
                ch_prev[i] = (c, prev)
                ch_count[(e, c)] = prev + 16
                ch_of[i] = c
                ch_val[i] = prev + 16
        marked = {e: set() for e in ENGS}
        waits_of = {}
        for e in ENGS:
            waited = {}
            for i in order[e]:
                o = ops[i]
                wl = []
                for d, need in o[3]:
                    od = ops[d]
                    if need is None:
                        assert od[0] == e and pos[d] < pos[i]
                        continue
                    if od[1] == "d":
                        key, val = ("d", od[0], ch_of[d]), ch_val[d]
                    else:
                        if not need:
                            assert od[0] == e and pos[d] < pos[i]
                            continue
                        key, val = ("e", od[0]), pos[d]
                    if val <= waited.get(key, -1):
                        continue
                    waited[key] = val
                    wl.append((key, val, d))
                    if key[0] == "e":
                        marked[od[0]].add(d)
                if o[1] == "d":
                    c, prev = ch_prev[i]
                    key = ("d", e, c)
                    if prev and prev > waited.get(key, -1):
                        waited[key] = prev
                        wl.append((key, prev, None))
                waits_of[i] = wl
        cnt = {}
        for e in ENGS:
            m = sorted(marked[e], key=lambda i: pos[i])
            cnt[e] = {i: k + 1 for k, i in enumerate(m)}
        with ExitStack() as st:
            esem = {e: st.enter_context(nc.semaphore("s_" + e)) for e in ENGS}
            dsem = {}
            for e, n_ in self.n_ch.items():
                for c in range(n_):
                    dsem[(e, c)] = st.enter_context(nc.semaphore("d_%s%d" % (e, c)))
            block = st.enter_context(nc.Block())
            n_ch = self.n_ch

            def run(e, h):
                for i in order[e]:
                    o = ops[i]
                    for key, val, d in waits_of[i]:
                        if key[0] == "e":
                            h.wait_ge(esem[key[1]], cnt[key[1]][d])
                        else:
                            h.wait_ge(dsem[(key[1], key[2])], val)
                    ins = o[2](h)
                    if o[1] == "d":
                        ins.then_inc(dsem[(e, ch_of[i])], 16)
                    elif i in cnt[e]:
                        ins.then_inc(esem[e], 1)
                if e in n_ch:
                    for c in range(n_ch[e]):
                        v = ch_count.get((e, c), 0)
                        if v:
                            h.wait_ge(dsem[(e, c)], v)

            @block.tensor
            def _(h):
                run("pe", h)

            @block.scalar
            def _(h):
                run("act", h)

            @block.vector
            def _(h):
                run("dve", h)

            @block.gpsimd
            def _(h):
                run("pool", h)

            @block.sync
            def _(h):
                run("sp", h)


SCHED = True
QUEUE_INORDER = ()


def build_program(n_layers=L, debug=None):
    nc = bass.Bass("TRN2", target_bir_lowering=False)
    st = ExitStack()

    def din(name, shape):
        return nc.dram_tensor(name, list(shape), F32, kind="ExternalInput").ap()

    def dout(name, shape):
        return nc.dram_tensor(name, list(shape), F32, kind="ExternalOutput").ap()

    xin = din("xin", [128, 8, TOK])
    w1d = din("w1", [2 * L, D, DFF])
    w3d = din("w3", [2 * L, D, DFF])
    w2d = din("w2", [2 * L, DFF, D])
    wind = din("win", [L, 8, D, 1024])
    wabd = din("wab", [L, D, 16])
    wod = din("wo", [L, D, D])
    gwd = din("gatew", [L, 2, 8, 128, 128])
    pvd = din("pvec", [L, 128, NPV])
    prd = din("prow", [L, 1, 32])
    sh0d = din("sh0", [L, 8, 128, NS])
    src0d = din("src0", [L, 8, 128, 3, NS])
    sgc0d = din("sgc0", [L, 24, 128, 3, NS])
    sS0d = din("sS0", [L, NS, 8, 128, 128])
    yout = dout("yout", [128, 8, TOK])
    o_ph = dout("o_ph", [L, 128, 8])
    o_prc = dout("o_prc", [L, 8, 128, 3])
    o_pS = dout("o_pS", [L, 8, 128, 128])
    o_pgc = dout("o_pgc", [L, 24, 128, 3])
    o_sh = dout("o_sh", [L, 8, 128, NS])
    o_src = dout("o_src", [L, 8, 128, 3, NS])
    o_sS = dout("o_sS", [L, NS, 8, 128, 128])
    o_sgc = dout("o_sgc", [L, 24, 128, 3, NS])
    dbg = {}
    if debug:
        for nm, shp in debug.items():
            dbg[nm] = dout("dbg_" + nm, shp)

    def sb(name, shape, dt):
        return st.enter_context(nc.sbuf_tensor(name, list(shape), dt))

    xres = sb("xres", [128, 8, TOK], F32)
    xbf = sb("xbf", [128, 8, TOK], BF16)
    mbuf = sb("mbuf", [128, 8, TOK], BF16)
    wsl = [sb("wsl%d" % i, [128, 4096], BF16) for i in range(4)]
    NSF, NSB = 6, 13
    SF = [sb("sf%d" % i, [128, 512], F32) for i in range(NSF)]
    SBf = [sb("sbf%d" % i, [128, 520 if i < 2 else 512], BF16) for i in range(NSB)]
    wo_c0 = sb("wo_c0", [128, 1024], BF16)
    ident_f = sb("ident_f", [128, 128], F32)
    ident_b = sb("ident_b", [128, 128], BF16)
    ones_f = sb("ones_f", [128, 128], F32)
    ones_b = sb("ones_b", [128, 128], BF16)
    maskU = sb("maskU", [128, 128], F32)
    maskSL = sb("maskSL", [128, 128], F32)
    maskUn = sb("maskUn", [128, 128], F32)
    sel_last = sb("sel_last", [128, 128], F32)
    BD16 = sb("BD16", [128, 128], BF16)
    OFFM = {m_: sb("OFF%d" % m_, [128, 128], BF16) for m_ in (32, 64, 128)}
    pv = sb("pv_t", [128, NPV], F32)
    pv2 = sb("pv2", [128, 48], F32)
    nsp = sb("nsp", [128, 16], F32)
    prow = sb("prow_t", [128, 32], F32)
    aexp = sb("aexp", [128, 8], F32)
    wab_b = sb("wab_b", [128, 8, 16], BF16)
    gw_b = sb("gw_b", [128, 2, 128], BF16)
    TB = {nm: sb("tb_" + nm, [128, 17, 8], F32) for nm in
          ("beta", "g", "gc", "egc", "bge", "nbeta")}
    ksq_t = sb("ksq_t", [128, 2 * NS], F32)
    glb = sb("glb", [128, 128], F32)
    egl = sb("egl", [128, 128], F32)
    ekt = sb("ekt", [128, 128], F32)
    S_f = sb("S_f", [128, 128], F32)
    S_b = sb("S_b", [128, 128], BF16)
    hcar = sb("hcar", [128, 8], F32)
    hist = sb("hist", [128, 4, 4], BF16)
    pco = sb("pco", [128, 4, 3], F32)
    dgw = sb("dgw", [128, 4, 128], BF16)
    sst = sb("sst", [128, 4, 3, NS], F32)
    sstb = sb("sstb", [128, 4, 3, NS], BF16)
    sso = sb("sso", [128, 4, 3, NS], F32)
    sh0 = sb("sh0t", [128, NS], F32)
    sho = sb("shot", [128, NS], F32)
    srow = {nm: sb("srow_" + nm, [128, 128], F32) for nm in ("eg", "beta", "nbe")}
    srow["qk"] = sb("srow_qk", [128, NS], F32)
    skq = sb("skq", [128, 2, NS], F32)
    PS = [st.enter_context(nc.psum_tensor("ps%d" % i, [128, 512], F32)) for i in range(8)]
    print("sbuf bytes remaining/partition:", nc.sbuf_bytes_remaining)

    P = Prog(nc)
    B_ = Buf
    xres_b = [[B_() for _ in TILES] for _ in range(8)]
    xbf_b = [[B_() for _ in TILES] for _ in range(8)]
    mbuf_b = [[B_() for _ in TILES] for _ in range(8)]
    wsl_b = [B_() for _ in range(4)]
    SF_b = [B_() for _ in range(NSF)]
    SB_b = [B_() for _ in range(NSB)]
    PS_b = [B_(True) for _ in range(8)]
    cst = B_()
    pvb = B_()
    tbb = B_()
    gwb = B_()
    wabb = B_()
    dgwb = B_()
    ksqb = B_()
    smb = B_()
    Sfb, Sbb = B_(), B_()
    sstb_b, ssob = B_(), B_()
    Sin_b = [B_() for _ in range(2)]
    Sout_b = [B_() for _ in range(2)]
    srb = B_()

    def fsz(ap):
        sh = ap.shape
        n_ = 1
        for d_ in sh[1:]:
            n_ *= d_
        return n_

    ACT_FAM = {AF.Exp: "el", AF.Ln: "el", AF.Sigmoid: "sg", AF.Silu: "si", AF.Sqrt: "sq"}

    def A(e, fn, r=(), w=(), dur=None, fam=None):
        if dur is None:
            dur = {"pe": 150.0, "act": 400.0, "dve": 400.0, "pool": 600.0}.get(e, 300.0)
        P.op(e, fn, reads=r, writes=w, dur=dur, fam=fam)

    def act(out, in_, func, r, w, bias=None, scale=None):
        kw = {}
        if bias is not None:
            kw["bias"] = bias
        if scale is not None:
            kw["scale"] = scale
        A("act", lambda h: h.activation(out=out, in_=in_, func=func, **kw), r, w,
          dur=220.0 + 0.8 * fsz(out), fam=ACT_FAM.get(func))

    def mm(out, lhsT, rhs, start, stop, r, w):
        n_ = fsz(rhs)
        f32 = 4.0 if rhs.dtype == F32 else 1.0
        A("pe", lambda h: h.matmul(out, lhsT=lhsT, rhs=rhs, start=start, stop=stop), r, w,
          dur=(45.0 + max(64, n_) * 0.5 * f32) if n_ > 128 else (100.0 * f32))

    def tt(out, in0, in1, op, r, w, eng="dve"):
        A(eng, lambda h: h.tensor_tensor(out=out, in0=in0, in1=in1, op=op), r, w,
          dur=(110.0 + 1.1 * fsz(out)) if eng == "dve" else (300.0 + 1.7 * fsz(out)))

    def ts(out, in0, s1, s2, op0, op1, r, w, eng="dve"):
        if s2 is None:
            A(eng, lambda h: h.tensor_scalar(out=out, in0=in0, scalar1=s1, scalar2=None, op0=op0), r, w,
              dur=110.0 + 1.0 * fsz(out))
        else:
            A(eng, lambda h: h.tensor_scalar(out=out, in0=in0, scalar1=s1, scalar2=s2, op0=op0, op1=op1), r, w,
              dur=110.0 + 1.0 * fsz(out))

    def stt(out, in0, scalar, in1, op0, op1, r, w):
        A("dve", lambda h: h.scalar_tensor_tensor(out=out, in0=in0, scalar=scalar, in1=in1, op0=op0, op1=op1), r, w,
          dur=110.0 + 1.1 * fsz(out))

    def dma(eng, out, in_, r=(), w=()):
        P.dma(eng, lambda h: h.dma_start(out=out, in_=in_), reads=r, writes=w,
              dur=2200.0 + fsz(out) * out.shape[0] * 4 / 150.0)

    def cp(out, in_, r, w, eng="dve"):
        A(eng, lambda h: h.tensor_copy(out=out, in_=in_), r, w,
          dur=(100.0 + 1.0 * fsz(out)) if eng == "dve" else (300.0 + 0.9 * fsz(out)))

    def mset(ap, val, w, eng="dve"):
        A(eng, lambda h: h.memset(ap, val), (), w)

    def tr(out, in_, idn, r, w):
        A("pe", lambda h: h.transpose(out, in_, idn), r, w, dur=130.0)

    def scan(out, d0, d1, init, r, w):
        A("dve", lambda h: h.tensor_tensor_scan(out=out, data0=d0, data1=d1, initial=init,
                                                op0=ALU.mult, op1=ALU.add), r, w, dur=110.0 + 2.1 * fsz(out))

    def dbg_dump(nm, ap, r):
        if nm in dbg:
            dma("sp", dbg[nm], ap, r)

    A("pool", lambda h: h.memset(ident_f[:], 1.0), (), [cst])
    A("pool", lambda h: h.affine_select(out=ident_f[:], in_=ident_f[:], pattern=[[1, 128]],
                                        compare_op=ALU.is_equal, fill=0.0, base=0, channel_multiplier=-1), [cst], [cst])
    A("pool", lambda h: h.memset(ones_f[:], 1.0), (), [cst])
    A("pool", lambda h: h.memset(ones_b[:], 1.0), (), [cst])
    A("pool", lambda h: h.memset(maskU[:], 1.0), (), [cst])
    A("pool", lambda h: h.affine_select(out=maskU[:], in_=maskU[:], pattern=[[1, 128]],
                                        compare_op=ALU.is_ge, fill=0.0, base=0, channel_multiplier=-1), [cst], [cst])
    A("pool", lambda h: h.memset(maskSL[:], 0.0), (), [cst])
    A("pool", lambda h: h.affine_select(out=maskSL[:], in_=maskSL[:], pattern=[[-1, 128]],
                                        compare_op=ALU.is_gt, fill=1e30, base=0, channel_multiplier=1), [cst], [cst])
    A("pool", lambda h: h.memset(maskUn[:], 0.0), (), [cst])
    A("pool", lambda h: h.affine_select(out=maskUn[:], in_=maskUn[:], pattern=[[1, 128]],
                                        compare_op=ALU.is_ge, fill=-1e30, base=0, channel_multiplier=-1), [cst], [cst])
    A("pool", lambda h: h.memset(sel_last[:], 1.0), (), [cst])
    A("pool", lambda h: h.affine_select(out=sel_last[:], in_=sel_last[:], pattern=[[0, 128]],
                                        compare_op=ALU.is_equal, fill=0.0, base=-127, channel_multiplier=1), [cst], [cst])
    A("dve", lambda h: h.tensor_copy(out=ident_b[:], in_=ident_f[:]), [cst], [cst])
    def mk_bd(b_, dst_tile, col0):
        nb_ = 128 // b_
        E = SF[5][0:nb_, 0:128]
        A("pool", lambda h: h.memset(E, 1.0), (), [SF_b[5]])
        A("pool", lambda h: h.affine_select(out=E, in_=E, pattern=[[1, 128]], compare_op=ALU.is_ge, fill=0.0,
                                            base=0, channel_multiplier=-b_), [SF_b[5]], [SF_b[5]])
        A("pool", lambda h: h.affine_select(out=E, in_=E, pattern=[[-1, 128]], compare_op=ALU.is_ge, fill=0.0,
                                            base=b_ - 1, channel_multiplier=b_), [SF_b[5]], [SF_b[5]])
        mm(PS[7][:, col0:col0 + 128], E, E, True, True, [SF_b[5]], [PS_b[7]])
    mk_bd(16, None, 0)
    mk_bd(32, None, 128)
    mk_bd(64, None, 256)
    act(BD16[:], PS[7][:, 0:128], AF.Copy, [PS_b[7]], [cst])
    bdf = SF[4]
    act(bdf[:, 0:384], PS[7][:, 0:384], AF.Copy, [PS_b[7]], [SF_b[4]])
    tt(OFFM[32][:], bdf[:, 128:256], bdf[:, 0:128], ALU.subtract, [SF_b[4]], [cst])
    tt(OFFM[64][:], bdf[:, 256:384], bdf[:, 128:256], ALU.subtract, [SF_b[4]], [cst])
    ts(OFFM[128][:], bdf[:, 256:384], -1.0, 1.0, ALU.mult, ALU.add, [SF_b[4]], [cst])

    for k in range(8):
        for ti, (c0, n) in enumerate(TILES):
            dma("sp", xres[:, k, c0:c0 + n], xin[:, k, c0:c0 + n], (), [xres_b[k][ti]])
            act(xbf[:, k, c0:c0 + n], xres[:, k, c0:c0 + n], AF.Copy, [xres_b[k][ti]], [xbf_b[k][ti]])
            ts(xres[:, k, c0:c0 + n], xres[:, k, c0:c0 + n], ALPHA, None, ALU.mult, None,
               [xres_b[k][ti]], [xres_b[k][ti]])

    wctr = [0]

    def wslot():
        i = wctr[0] % 4
        wctr[0] += 1
        return i

    def ffn(fi):
        w1v = w1d[fi].rearrange("(k p) n -> p k n", p=128)
        w3v = w3d[fi].rearrange("(k p) n -> p k n", p=128)
        w2v = w2d[fi].rearrange("(j p) n -> p j n", p=128)
        ftmp_b = SF_b[0:5]
        subs = [(0, 4), (4, 4), (8, 4), (12, 4), (16, 4), (20, 2)]
        groups = [(0, 1), (2, 3), (4, 5)]

        def load13(su):
            j0, nj = subs[su]
            s1, s3 = wslot(), wslot()
            for s, wv in ((s1, w1v), (s3, w3v)):
                dst = wsl[s][:].rearrange("p (k n) -> p k n", k=8)[:, :, 0:nj * 128]
                dma("pool", dst, wv[:, :, j0 * 128:(j0 + nj) * 128], (), [wsl_b[s]])
            return s1, s3

        def load2(su):
            j0, nj = subs[su]
            s2 = wslot()
            dst = wsl[s2][:].rearrange("p (j n) -> p j n", j=4)[:, 0:nj, :]
            dma("pool", dst, w2v[:, j0:j0 + nj, :], (), [wsl_b[s2]])
            return s2

        for (sa, sbb) in groups:
            l13 = {}
            l13[sa] = load13(sa)
            l13[sbb] = load13(sbb)
            for su in (sa, sbb):
                j0, nj = subs[su]
                s1, s3 = l13[su]
                hj0 = 0 if su == sa else 4
                for j in range(nj):
                    for (s, which) in ((s1, 0), (s3, 1)):
                        wv = wsl[s][:].rearrange("p (k n) -> p k n", k=8)
                        for k in range(8):
                            for ti, (c0, n) in enumerate(TILES):
                                mm(PS[ti][:, 0:n], wv[:, k, j * 128:(j + 1) * 128], xbf[:, k, c0:c0 + n],
                                   k == 0, k == 7, [wsl_b[s], xbf_b[k][ti]], [PS_b[ti]])
                        for ti, (c0, n) in enumerate(TILES):
                            fsl = SF[ti][:, 0:n]
                            if which == 0:
                                act(fsl, PS[ti][:, 0:n], AF.Silu, [PS_b[ti]], [ftmp_b[ti]])
                            else:
                                tt(mbuf[:, hj0 + j, c0:c0 + n], fsl, PS[ti][:, 0:n], ALU.mult,
                                   [ftmp_b[ti], PS_b[ti]], [mbuf_b[hj0 + j][ti]])
            s2a = load2(sa)
            s2b = load2(sbb)
            parts = [(s2a, subs[sa][1], 0), (s2b, subs[sbb][1], 4)]
            tot = subs[sa][1] + subs[sbb][1]
            for m in range(8):
                cnt = 0
                for (s2, nj, hj0) in parts:
                    wv = wsl[s2][:].rearrange("p (j n) -> p j n", j=4)
                    for j in range(nj):
                        for ti, (c0, n) in enumerate(TILES):
                            mm(PS[ti][:, 0:n], wv[:, j, m * 128:(m + 1) * 128], mbuf[:, hj0 + j, c0:c0 + n],
                               cnt == 0, cnt == tot - 1, [wsl_b[s2], mbuf_b[hj0 + j][ti]], [PS_b[ti]])
                        cnt += 1
                for ti, (c0, n) in enumerate(TILES):
                    stt(xres[:, m, c0:c0 + n], PS[ti][:, 0:n], 0.5, xres[:, m, c0:c0 + n], ALU.mult, ALU.add,
                        [PS_b[ti], xres_b[m][ti]], [xres_b[m][ti]])

    def layernorm(i_ln, final):
        for ti, (c0, n) in enumerate(TILES):
            pm, pq = 5, 6
            for k in range(8):
                sq_, sqb = SBf[k % 4], SB_b[k % 4]
                act(sq_[:, 0:n], xres[:, k, c0:c0 + n], AF.Square, [xres_b[k][ti]], [sqb])
                mm(PS[pm][:, 0:n], ones_f[:], xres[:, k, c0:c0 + n], k == 0, k == 7, [cst, xres_b[k][ti]], [PS_b[pm]])
                mm(PS[pq][:, 0:n], ones_b[:], sq_[:, 0:n], k == 0, k == 7, [cst, sqb], [PS_b[pq]])
            mean, msq, var, Aa, Bb = SF[0], SF[1], SF[2], SF[3], SF[4]
            act(mean[:, 0:n], PS[pm][:, 0:n], AF.Copy, [PS_b[pm]], [SF_b[0]], scale=1.0 / D)
            act(msq[:, 0:n], PS[pm][:, 0:n], AF.Square, [PS_b[pm]], [SF_b[1]], scale=1.0 / D)
            stt(var[:, 0:n], PS[pq][:, 0:n], 1.0 / D, msq[:, 0:n], ALU.mult, ALU.subtract,
                [PS_b[pq], SF_b[1]], [SF_b[2]])
            act(var[:, 0:n], var[:, 0:n], AF.Ln, [SF_b[2]], [SF_b[2]], bias=LN_EPS)
            act(Aa[:, 0:n], var[:, 0:n], AF.Exp, [SF_b[2]], [SF_b[3]], scale=-0.5)
            stt(Bb[:, 0:n], mean[:, 0:n], -1.0, Aa[:, 0:n], ALU.mult, ALU.mult, [SF_b[0], SF_b[3]], [SF_b[4]])
            for k in range(8):
                z, zb = SF[k % 2], SF_b[k % 2]
                tt(z[:, 0:n], xres[:, k, c0:c0 + n], Aa[:, 0:n], ALU.mult, [xres_b[k][ti], SF_b[3]], [zb])
                tt(z[:, 0:n], z[:, 0:n], Bb[:, 0:n], ALU.add, [zb, SF_b[4]], [zb])
                gi, bi = 16 * i_ln + k, 16 * i_ln + 8 + k
                if final:
                    ts(xres[:, k, c0:c0 + n], z[:, 0:n], pv[:, gi:gi + 1], pv[:, bi:bi + 1], ALU.mult, ALU.add,
                       [zb, pvb], [xres_b[k][ti]])
                    dma("sp", yout[:, k, c0:c0 + n], xres[:, k, c0:c0 + n], [xres_b[k][ti]])
                else:
                    act(xbf[:, k, c0:c0 + n], z[:, 0:n], AF.Identity, [zb, pvb], [xbf_b[k][ti]],
                        bias=pv[:, bi:bi + 1], scale=pv[:, gi:gi + 1])
                    if k % 2 == 0:
                        ts(xres[:, k, c0:c0 + n], z[:, 0:n], pv2[:, gi:gi + 1], pv2[:, bi:bi + 1], ALU.mult, ALU.add,
                           [zb, pvb], [xres_b[k][ti]])
                    else:
                        act(xres[:, k, c0:c0 + n], z[:, 0:n], AF.Identity, [zb, pvb], [xres_b[k][ti]],
                            bias=pv2[:, bi:bi + 1], scale=pv2[:, gi:gi + 1])

    def softplus(dst, src, tmps, r, w):
        t0, t1, t2 = tmps
        act(t0, src, AF.Abs, r, w)
        act(t0, t0, AF.Exp, w, w, scale=-1.0)
        ts(t1, t0, 2.0, None, ALU.add, None, w, w)
        A("dve", lambda h: h.reciprocal(out=t1, in_=t1), w, w)
        tt(t1, t1, t0, ALU.mult, w, w)
        tt(t2, t1, t1, ALU.mult, w, w)
        ts(t0, t2, 1.0 / 11, 1.0 / 9, ALU.mult, ALU.add, w, w)
        for cf in (1.0 / 7, 1.0 / 5, 1.0 / 3, 1.0):
            tt(t0, t0, t2, ALU.mult, w, w)
            ts(t0, t0, cf, None, ALU.add, None, w, w)
        tt(t0, t0, t1, ALU.mult, w, w)
        ts(t1, src, 0.0, None, ALU.max, None, list(r) + list(w), w)
        stt(dst, t0, 2.0, t1, ALU.mult, ALU.add, w, w)

    def layer_params(l):
        dma("sp", pv[:], pvd[l], (), [pvb])
        dma("sp", prow[:], prd[l].partition_broadcast(128), (), [pvb])
        dma("pool", wab_b[:], wabd[l].rearrange("(k p) n -> p k n", p=128), (), [wabb])
        ts(pv2[:], pv[:, 0:48], ALPHA, None, ALU.mult, None, [pvb], [pvb])
        wl = [pvb] + SF_b[0:5]
        tn = [SF[i][:, 0:8] for i in range(5)]
        ts(tn[3], pv[:, 168:176], -1.0, None, ALU.mult, None, [pvb], wl)
        softplus(tn[4], tn[3], (tn[0], tn[1], tn[2]), wl, wl)
        ts(nsp[:, 0:8], tn[4], -8.0, None, ALU.mult, None, wl, wl)
        ts(nsp[:, 8:16], tn[4], -16.0, None, ALU.mult, None, wl, wl)
        act(aexp[:], prow[:, 16:24], AF.Exp, [pvb], [pvb])

    def gate_tables(l):
        pab = 7
        for nb in range(17):
            c0 = nb * 128
            nt = 128 if nb < 16 else NS
            ti = min(nb // 4, 4)
            for k in range(8):
                mm(PS[pab][0:nt, nb * 16:(nb + 1) * 16], xbf[:, k, c0:c0 + nt], wab_b[:, k, :], k == 0, k == 7,
                   [xbf_b[k][ti], wabb], [PS_b[pab]])
        ab_f = SF[4][:, 0:272].rearrange("p (b n) -> p b n", n=16)
        abv = ab_f
        psv = PS[pab][:, 0:272].rearrange("p (b n) -> p b n", n=16)
        brow = prow[:, 0:16].unsqueeze(1).to_broadcast([128, 17, 16])
        tt(abv, psv, brow, ALU.add, [PS_b[pab], pvb], [tbb, SF_b[4]])
        a_v, b_v = ab_f[:, :, 0:8], ab_f[:, :, 8:16]
        T = {k_: v_[:] for k_, v_ in TB.items()}
        for i_ in range(4):
            T["t%d" % i_] = SF[i_][:, 0:136].rearrange("p (b n) -> p b n", n=8)
        wl = [tbb] + SF_b[0:5]
        act(T["beta"], b_v, AF.Sigmoid, wl, [tbb])
        ts(T["nbeta"], T["beta"], -1.0, None, ALU.mult, None, [tbb], [tbb])
        dtb = prow[:, 24:32].unsqueeze(1).to_broadcast([128, 17, 8])
        tt(T["t3"], a_v, dtb, ALU.add, [tbb, pvb], wl)
        softplus(T["g"], T["t3"], (T["t0"], T["t1"], T["t2"]), wl, wl)
        aex = aexp[:].unsqueeze(1).to_broadcast([128, 17, 8])
        stt(T["g"], T["g"], -1.0, aex, ALU.mult, ALU.mult, [tbb, pvb], [tbb])
        gflat = TB["g"][:].rearrange("p b n -> p (b n)")
        gcflat = TB["gc"][:].rearrange("p b n -> p (b n)")
        mm(PS[pab][:, 0:128], maskU[:], gflat[:, 0:128], True, True, [cst, tbb], [PS_b[pab]])
        act(gcflat[:, 0:128], PS[pab][:, 0:128], AF.Copy, [PS_b[pab]], [tbb])
        act(gcflat[:, 128:136], gflat[:, 128:136], AF.Copy, [tbb], [tbb])
        mm(PS[pab][:, 128:256], sel_last[:], gcflat[:, 0:128], True, True, [cst, tbb], [PS_b[pab]])
        act(glb[:], PS[pab][:, 128:256], AF.Copy, [PS_b[pab]], [tbb])
        act(egl[:], glb[:], AF.Exp, [tbb], [tbb])
        tt(ekt[:], glb[:], gcflat[:, 0:128], ALU.subtract, [tbb], [tbb])
        act(ekt[:], ekt[:], AF.Exp, [tbb], [tbb])
        act(T["egc"], T["gc"], AF.Exp, [tbb], [tbb])
        tt(T["bge"], T["beta"], T["egc"], ALU.mult, [tbb], [tbb])
        for nm, src in (("eg", TB["egc"]), ("beta", TB["beta"])):
            sv = src[0:NS, 16, :].unsqueeze(2).to_broadcast([NS, 8, NS])
            iv = ident_f[0:NS, 0:NS].unsqueeze(1).to_broadcast([NS, 8, NS])
            sM = SF[5][0:NS, 0:128]
            tt(sM.rearrange("p (h s) -> p h s", h=8), sv, iv, ALU.mult, [tbb, cst], [srb, SF_b[5]])
            mm(PS[pab][:, 256:384], ones_f[0:NS, :], sM, True, True, [cst, srb, SF_b[5]], [PS_b[pab]])
            act(srow[nm][:], PS[pab][:, 256:384], AF.Copy, [PS_b[pab]], [srb])
        stt(srow["nbe"][:], srow["beta"][:], -1.0, srow["eg"][:], ALU.mult, ALU.mult, [srb], [srb])

    def rbank(X):
        for _ in range(8):
            i = X.banks[X.rot % len(X.banks)]
            X.rot += 1
            if i not in X.held:
                return i
        raise RuntimeError("no free psum bank")

    class Ctx:
        pass

    def mk_ctx(tid):
        X = Ctx()
        X.tid = tid
        X.banks = [0, 1, 2, 3] if tid == 0 else [4, 5, 6, 7]
        X.held = set()
        X.rot = 0
        X.slots = (0, 1) if tid == 0 else (2, 3)
        for nm in ("smb", "gwb", "dgwb", "sstb_b", "ssob", "Sfb", "Sbb", "ksqb", "srq", "wob"):
            setattr(X, nm, Buf())
        X.SF_b = [Buf() for _ in range(6)] if tid else SF_b
        X.SB_b = [Buf() for _ in range(13)] if tid else SB_b
        if tid == 0:
            X.SF = [t[:] for t in SF]
            X.SBf = [t[:] for t in SBf]
            X.hist, X.pco, X.dgw, X.gw_b = hist[:], pco[:], dgw[:], gw_b[:]
            X.sst, X.sstb, X.sso, X.sh0, X.sho = sst[:], sstb[:], sso[:], sh0[:], sho[:]
            X.S_f, X.S_b, X.skq, X.ksq_t, X.srow_qk = S_f[:], S_b[:], skq[:], ksq_t[:], srow["qk"][:]
            X.wo_c = wo_c0[:]
        else:
            mflat = mbuf[:].rearrange("p a b -> p (a b)")
            off = [0]

            def cv(n_, dt=BF16):
                ne = n_ if dt == BF16 else 2 * n_
                a = off[0]
                off[0] += ne + (ne % 2)
                v = mflat[:, a:a + ne]
                return v if dt == BF16 else v.bitcast(F32)
            X.SF = [cv(512, F32) for _ in range(6)]
            X.SBf = [cv(520 if i < 2 else 512) for i in range(13)]
            X.hist = cv(16).rearrange("p (a b) -> p a b", a=4)
            X.pco = cv(12, F32).rearrange("p (a b) -> p a b", a=4)
            X.dgw = cv(512).rearrange("p (a b) -> p a b", a=4)
            X.gw_b = cv(256).rearrange("p (a b) -> p a b", a=2)
            X.sst = cv(192, F32).rearrange("p (a b c) -> p a b c", a=4, b=3)
            X.sstb = cv(192).rearrange("p (a b c) -> p a b c", a=4, b=3)
            X.sso = cv(192, F32).rearrange("p (a b c) -> p a b c", a=4, b=3)
            X.sh0, X.sho = cv(NS, F32), cv(NS, F32)
            X.S_f, X.S_b = cv(128, F32), cv(128)
            X.skq = cv(2 * NS, F32).rearrange("p (a b) -> p a b", a=2)
            X.ksq_t, X.srow_qk = cv(2 * NS, F32), cv(NS, F32)
            X.wo_c = cv(1024)
            assert off[0] <= 8 * TOK, off[0]
        return X

    CTX = [mk_ctx(0), mk_ctx(1)]
    t1_bufs = CTX[1].SF_b + CTX[1].SB_b + [getattr(CTX[1], nm) for nm in
                                           ("smb", "gwb", "dgwb", "sstb_b", "ssob", "Sfb", "Sbb", "ksqb", "srq", "wob")]

    def chunk_body(l, c, X):
        T = TB
        sA, sB = X.slots
        wv_d = wind[l, c].rearrange("(k p) n -> p k n", p=128)
        for s, b0 in ((sA, 0), (sB, 4)):
            dst = wsl[s][:].rearrange("p (k n) -> p k n", k=8)
            dma("pool", dst, wv_d[:, :, b0 * 128:(b0 + 4) * 128], (), [wsl_b[s]])
        dma("pool", X.gw_b[:], gwd[l, :, c].rearrange("g p n -> p g n"), (), [X.gwb])
        dma("pool", X.wo_c[:], wod[l, c * 128:(c + 1) * 128, :], (), [X.wob])
        wA = wsl[sA][:].rearrange("p (k n) -> p k n", k=8)
        wB = wsl[sB][:].rearrange("p (k n) -> p k n", k=8)

        def wchunk(k, blk):
            if blk < 4:
                return wA[:, k, blk * 128:(blk + 1) * 128], wsl_b[sA]
            return wB[:, k, (blk - 4) * 128:(blk - 3) * 128], wsl_b[sB]

        cw_base = (112 + c, 176 + c, 176 + 8 + c, 176 + 16 + c)
        dma("sp", X.sst[:, 0], src0d[l, c], (), [X.sstb_b])
        for q_ in range(3):
            dma("sp", X.sst[:, 1 + q_], sgc0d[l, q_ * 8 + c], (), [X.sstb_b])
        dma("sp", X.sh0[:], sh0d[l, c], (), [X.sstb_b])
        act(X.sstb[:], X.sst[:], AF.Copy, [X.sstb_b], [X.sstb_b])
        mset(hcar[:, c:c + 1], 0.0, [X.smb])
        mset(X.hist[:], 0.0, [X.smb])
        mset(X.S_f[:], 0.0, [X.Sfb])
        mset(X.S_b[:], 0.0, [X.Sbb])

        for ti, (c0, n) in enumerate(TILES):
            samp = ti == 4

            if samp:
                sbl = []
                for tix in range(7, 13):
                    for half in (0, 1):
                        sbl.append((X.SBf[tix][:, half * 256:(half + 1) * 256].bitcast(F32), X.SB_b[tix]))
                for tix in range(0, 7):
                    sbl.append((X.SBf[tix][:, 256:512].bitcast(F32), X.SB_b[tix]))
                for q_ in range(1, 4):
                    sbl.append((X.SF[2][0:NS, q_ * 128:(q_ + 1) * 128], X.SF_b[2]))
                sbb_ = []
                for (_, par) in sbl:
                    cb = Buf()
                    cb.r = list(par.r) + ([par.w] if par.w is not None else [])
                    sbb_.append(cb)
                for s_ in range(NS):
                    dma("sp", sbl[s_][0], sS0d[l, s_, c], (), [sbb_[s_]])

            def proj(blk):
                pb = rbank(X)
                for k in range(8):
                    wv_, wb_ = wchunk(k, blk)
                    mm(PS[pb][:, 0:n], wv_, xbf[:, k, c0:c0 + n], k == 0, k == 7, [wb_, xbf_b[k][ti]], [PS_b[pb]])
                return pb

            def bias(blk):
                col = 48 + c * 8 + blk
                return pv[:, col:col + 1]

            def conv(blk, hb, prex_i):
                pb = proj(blk)
                prex, pxb = X.SBf[prex_i], X.SB_b[prex_i]
                for j in range(4):
                    col = cw_base[hb] + j * (8 if hb == 0 else 24)
                    ts(X.dgw[:, j, :], ident_f[:], pv[:, col:col + 1], None, ALU.mult, None, [cst, pvb], [X.dgwb])
                if not samp:
                    cp(prex[:, 0:3], X.hist[:, hb, 0:3], [X.smb], [pxb])
                    act(prex[:, 3:3 + n], PS[pb][:, 0:n], AF.Identity, [PS_b[pb], pvb], [pxb], bias=bias(blk))
                    if ti < 3:
                        cp(X.hist[:, hb, 0:3], prex[:, n:n + 3], [pxb], [X.smb])
                    else:
                        act(X.pco[:, hb, :], PS[pb][:, n - 3:n], AF.Identity, [PS_b[pb], pvb], [X.smb], bias=bias(blk))
                    pc = rbank(X)
                    for j in range(4):
                        mm(PS[pc][:, 0:n], X.dgw[:, j, :], prex[:, j:j + n], j == 0, j == 3, [X.dgwb, pxb], [PS_b[pc]])
                else:
                    act(prex[:, 0:n], PS[pb][:, 0:n], AF.Identity, [PS_b[pb], pvb], [pxb], bias=bias(blk))
                    act(X.sso[:, hb, 2, :], PS[pb][:, 0:n], AF.Identity, [PS_b[pb], pvb], [X.ssob], bias=bias(blk))
                    cp(X.sso[:, hb, 0:2, :], X.sst[:, hb, 1:3, :], [X.sstb_b], [X.ssob])
                    pc = rbank(X)
                    for j in range(3):
                        mm(PS[pc][:, 0:n], X.dgw[:, j, :], X.sstb[:, hb, j, :], j == 0, False, [X.dgwb, X.sstb_b], [PS_b[pc]])
                    mm(PS[pc][:, 0:n], X.dgw[:, 3, :], prex[:, 0:n], False, True, [X.dgwb, pxb], [PS_b[pc]])
                return pc

            pc = conv(0, 0, 0)
            xc, xcb = X.SF[0], X.SF_b[0]
            act(xc[:, 0:n], PS[pc][:, 0:n], AF.Identity, [PS_b[pc], pvb], [xcb], bias=pv[:, 144 + c:145 + c])
            xcbf, xcbfb = X.SBf[1], X.SB_b[1]
            act(xcbf[:, 0:n], PS[pc][:, 0:n], AF.Identity, [PS_b[pc], pvb], [xcbfb], bias=pv[:, 144 + c:145 + c])
            pr_, pi_ = rbank(X), rbank(X)
            mm(PS[pr_][:, 0:n], X.gw_b[:, 0, :], xcbf[:, 0:n], True, True, [X.gwb, xcbfb], [PS_b[pr_]])
            mm(PS[pi_][:, 0:n], X.gw_b[:, 1, :], xcbf[:, 0:n], True, True, [X.gwb, xcbfb], [PS_b[pi_]])
            rr, rrb = X.SF[1], X.SF_b[1]
            ig, igb = X.SF[2], X.SF_b[2]
            act(rr[:, 0:n], PS[pr_][:, 0:n], AF.Sigmoid, [PS_b[pr_], pvb], [rrb], bias=pv[:, 152 + c:153 + c])
            act(ig[:, 0:n], PS[pi_][:, 0:n], AF.Sigmoid, [PS_b[pi_], pvb], [igb], bias=pv[:, 160 + c:161 + c])
            aa, aab = X.SF[3], X.SF_b[3]
            act(aa[:, 0:n], rr[:, 0:n], AF.Exp, [rrb, pvb], [aab], scale=nsp[:, c:c + 1])
            act(rr[:, 0:n], rr[:, 0:n], AF.Exp, [rrb, pvb], [rrb], scale=nsp[:, 8 + c:9 + c])
            act(rr[:, 0:n], rr[:, 0:n], AF.Ln, [rrb], [rrb], bias=1.0, scale=-1.0)
            act(rr[:, 0:n], rr[:, 0:n], AF.Exp, [rrb], [rrb], scale=0.5)
            tt(ig[:, 0:n], ig[:, 0:n], xc[:, 0:n], ALU.mult, [igb, xcb], [igb])
            tt(ig[:, 0:n], ig[:, 0:n], rr[:, 0:n], ALU.mult, [igb, rrb], [igb])
            hs, hsb = X.SF[0], X.SF_b[0]
            if not samp:
                scan(hs[:, 0:n], aa[:, 0:n], ig[:, 0:n], hcar[:, c:c + 1], [aab, igb, X.smb], [hsb])
                cp(hcar[:, c:c + 1], hs[:, n - 1:n], [hsb], [X.smb])
                if ti == 3:
                    dma("sp", o_prc[l, c], X.pco[:, 0, :], [X.smb])
            else:
                tt(hs[:, 0:n], aa[:, 0:n], X.sh0[:], ALU.mult, [aab, X.sstb_b], [hsb])
                tt(hs[:, 0:n], hs[:, 0:n], ig[:, 0:n], ALU.add, [hsb, igb], [hsb])
                cp(X.sho[:], hs[:, 0:n], [hsb], [X.ssob])
                dma("sp", o_sh[l, c], X.sho[:], [X.ssob])
                dma("sp", o_src[l, c], X.sso[:, 0], [X.ssob])
            pb = proj(1)
            yy, yyb = X.SF[1], X.SF_b[1]
            act(yy[:, 0:n], PS[pb][:, 0:n], AF.Identity, [PS_b[pb], pvb], [yyb], bias=bias(1))
            y2, y2b = X.SF[2], X.SF_b[2]
            tt(y2[:, 0:n], yy[:, 0:n], yy[:, 0:n], ALU.mult, [yyb], [y2b])
            ts(y2[:, 0:n], y2[:, 0:n], 0.044715, 1.0, ALU.mult, ALU.add, [y2b], [y2b])
            tt(y2[:, 0:n], y2[:, 0:n], yy[:, 0:n], ALU.mult, [y2b, yyb], [y2b])
            act(y2[:, 0:n], y2[:, 0:n], AF.Sigmoid, [y2b], [y2b], scale=1.5957691216057308)
            tt(yy[:, 0:n], yy[:, 0:n], y2[:, 0:n], ALU.mult, [yyb, y2b], [yyb])
            tt(hs[:, 0:n], hs[:, 0:n], yy[:, 0:n], ALU.mult, [hsb, yyb], [hsb])
            pb = proj(2)
            act(y2[:, 0:n], PS[pb][:, 0:n], AF.Sigmoid, [PS_b[pb], pvb], [y2b], bias=bias(2))
            ma, mab = X.SF[5], X.SF_b[5]
            tt(ma[:, 0:n], hs[:, 0:n], y2[:, 0:n], ALU.mult, [hsb, y2b], [mab])

            h_ = c
            pc = conv(3, 1, 0)
            qf, qfb = X.SF[0], X.SF_b[0]
            act(qf[:, 0:n], PS[pc][:, 0:n], AF.Sigmoid, [PS_b[pc]], [qfb])
            tt(qf[:, 0:n], qf[:, 0:n], PS[pc][:, 0:n], ALU.mult, [qfb, PS_b[pc]], [qfb])
            pc = conv(4, 2, 1)
            kf, kfb = X.SF[1], X.SF_b[1]
            act(kf[:, 0:n], PS[pc][:, 0:n], AF.Sigmoid, [PS_b[pc]], [kfb])
            tt(kf[:, 0:n], kf[:, 0:n], PS[pc][:, 0:n], ALU.mult, [kfb, PS_b[pc]], [kfb])
            pc = conv(5, 3, 0)
            vT, vTb = X.SBf[2], X.SB_b[2]
            vsg, vsgb = X.SF[2], X.SF_b[2]
            act(vsg[:, 0:n], PS[pc][:, 0:n], AF.Sigmoid, [PS_b[pc]], [vsgb])
            tt(vT[:, 0:n], vsg[:, 0:n], PS[pc][:, 0:n], ALU.mult, [vsgb, PS_b[pc]], [vTb])
            if samp:
                for q_ in range(3):
                    dma("sp", o_sgc[l, q_ * 8 + c], X.sso[:, 1 + q_], [X.ssob])
            elif ti == 3:
                for q_ in range(3):
                    dma("sp", o_pgc[l, q_ * 8 + c], X.pco[:, 1 + q_, :], [X.smb])
            sqs, sqsb = X.SBf[3], X.SB_b[3]
            rs, rsb = X.SF[2], X.SF_b[2]
            qT, qTb = X.SBf[4], X.SB_b[4]
            kT, kTb = X.SBf[5], X.SB_b[5]
            for (src, srcb, dstT, dstTb, scl) in ((qf, qfb, qT, qTb, 128 ** -0.5), (kf, kfb, kT, kTb, 1.0)):
                act(sqs[:, 0:n], src[:, 0:n], AF.Square, [srcb], [sqsb])
                pq = rbank(X)
                mm(PS[pq][:, 0:n], ones_b[:], sqs[:, 0:n], True, True, [cst, sqsb], [PS_b[pq]])
                act(rs[:, 0:n], PS[pq][:, 0:n], AF.Ln, [PS_b[pq]], [rsb], bias=RMS_EPS)
                act(rs[:, 0:n], rs[:, 0:n], AF.Exp, [rsb], [rsb], scale=-0.5)
                if samp:
                    stt(src[:, 0:n], src[:, 0:n], scl, rs[:, 0:n], ALU.mult, ALU.mult, [srcb, rsb], [srcb])
                else:
                    stt(dstT[:, 0:n], src[:, 0:n], scl, rs[:, 0:n], ALU.mult, ALU.mult, [srcb, rsb], [dstTb])

            if samp:
                cols = slice(h_ * NS, (h_ + 1) * NS)
                cp(X.skq[:, 0, :], kf[:, 0:n], [kfb], [X.srq])
                cp(X.skq[:, 1, :], qf[:, 0:n], [qfb], [X.srq])
                qk_, qkb = X.SF[3], X.SF_b[3]
                tt(qk_[:, 0:n], qf[:, 0:n], kf[:, 0:n], ALU.mult, [qfb, kfb], [qkb])
                pq = rbank(X)
                mm(PS[pq][:, 0:n], ones_f[:], qk_[:, 0:n], True, True, [cst, qkb], [PS_b[pq]])
                act(X.srow_qk[:], PS[pq][:, 0:n], AF.Copy, [PS_b[pq]], [X.srq])
                pt = rbank(X)
                tr(PS[pt][0:NS, 0:128], kf[:, 0:n], ident_f[:], [kfb, cst], [PS_b[pt]])
                kpm, kpmb = X.SF[4], X.SF_b[4]
                act(kpm[0:NS, 0:128], PS[pt][0:NS, 0:128], AF.Copy, [PS_b[pt]], [kpmb])
                pks = X.banks[0]
                X.held = {pks}
                vfs, vfsb = X.SF[1], X.SF_b[1]
                act(vfs[:, 0:n], vT[:, 0:n], AF.Copy, [vTb], [vfsb])
                for s_ in range(NS):
                    mm(PS[pks][:, 2 * s_:2 * s_ + 2], sbl[s_][0], X.skq[:, :, s_], True, True,
                       [sbb_[s_], X.srq], [PS_b[pks]])
                act(X.ksq_t[:], PS[pks][:, 0:2 * NS], AF.Copy, [PS_b[pks]], [X.ksqb])
                ksv = X.ksq_t[:].rearrange("p (s t) -> p t s", t=2)
                X.held = set()
                vn, vnb = X.SF[3], X.SF_b[3]
                tt(vn[:, 0:n], vfs[:, 0:n], srow["beta"][:, cols], ALU.mult, [vfsb, srb], [vnb])
                tt(vfs[:, 0:n], ksv[:, 0, :], srow["nbe"][:, cols], ALU.mult, [X.ksqb, srb], [vfsb])
                tt(vn[:, 0:n], vn[:, 0:n], vfs[:, 0:n], ALU.add, [vnb, vfsb], [vnb])
                oo, oob = X.SF[0], X.SF_b[0]
                tt(oo[:, 0:n], ksv[:, 1, :], srow["eg"][:, cols], ALU.mult, [X.ksqb, srb], [oob])
                tt(vfs[:, 0:n], vn[:, 0:n], X.srow_qk[:], ALU.mult, [vnb, X.srq], [vfsb])
                tt(oo[:, 0:n], oo[:, 0:n], vfs[:, 0:n], ALU.add, [oob, vfsb], [oob])
                pt = rbank(X)
                tr(PS[pt][0:NS, 0:128], vn[:, 0:n], ident_f[:], [vnb, cst], [PS_b[pt]])
                vpm, vpmb = X.SF[1], X.SF_b[1]
                act(vpm[0:NS, 0:128], PS[pt][0:NS, 0:128], AF.Copy, [PS_b[pt]], [vpmb])
                for s_ in range(NS):
                    p_ = h_ * NS + s_
                    vm, vmb = sbl[19 + s_ % 3][0], sbb_[19 + s_ % 3]
                    ts(vm, vpm[0:NS, 0:128], ident_f[0:NS, s_:s_ + 1], None, ALU.mult, None, [vpmb, cst], [vmb])
                    po = rbank(X)
                    mm(PS[po][:, 0:128], kpm[0:NS, 0:128], vm, True, True, [kpmb, vmb], [PS_b[po]])
                    so_, sob_ = sbl[16 + s_ % 3][0], sbb_[16 + s_ % 3]
                    stt(so_, sbl[s_][0], srow["eg"][:, p_:p_ + 1], PS[po][:, 0:128], ALU.mult, ALU.add,
                        [sbb_[s_], srb, PS_b[po]], [sob_])
                    dma("sp", o_sS[l, s_, h_], so_, [sob_])
                for (_, par), cb in zip(sbl, sbb_):
                    par.r.extend(cb.r)
                    if cb.w is not None:
                        par.r.append(cb.w)
                po_hold = None
                of, ofb = oo, oob
            else:
                nb0 = ti * 4
                pg, pO, pA, pB = X.banks[0], X.banks[0], X.banks[1], X.banks[2]
                X.held = {pg}
                for nn in range(4):
                    dg_, dgb = X.SF[3], X.SF_b[3]
                    ts(dg_[:, 0:128], ident_f[:], T["gc"][:, nb0 + nn, h_:h_ + 1], None, ALU.mult, None,
                       [cst, tbb], [dgb])
                    mm(PS[pg][:, nn * 128:(nn + 1) * 128], ones_f[:], dg_[:, 0:128], True, True, [cst, dgb], [PS_b[pg]])
                egr, egrb = X.SF[3], X.SF_b[3]
                act(egr[:, 0:n], PS[pg][:, 0:n], AF.Exp, [PS_b[pg]], [egrb])
                qdT, qdTb = X.SBf[6], X.SB_b[6]
                tt(qdT[:, 0:n], qT[:, 0:n], egr[:, 0:n], ALU.mult, [qTb, egrb], [qdTb])
                pk_ = rbank(X)
                pkb = PS[pk_][:].bitcast(BF16)
                for nn in range(4):
                    tr(pkb[:, nn * 128:(nn + 1) * 128], kT[:, nn * 128:(nn + 1) * 128], ident_b[:], [kTb, cst], [PS_b[pk_]])
                kbg, kbgb = X.SBf[7], X.SB_b[7]
                ktm, ktmb = X.SBf[8], X.SB_b[8]
                pk3 = pkb[:, 0:512].rearrange("p (a d) -> p a d", a=4)
                bge_v = T["bge"][:, nb0:nb0 + 4, h_:h_ + 1].to_broadcast([128, 4, 128])
                ekt_v = ekt[:].rearrange("p (b e) -> p b e", e=8)[:, nb0:nb0 + 4, h_:h_ + 1].to_broadcast([128, 4, 128])
                tt(kbg[:, 0:512].rearrange("p (a d) -> p a d", a=4), pk3, bge_v, ALU.mult, [PS_b[pk_], tbb], [kbgb])
                tt(ktm[:, 0:512].rearrange("p (a d) -> p a d", a=4), pk3, ekt_v, ALU.mult, [PS_b[pk_], tbb], [ktmb])
                pv_ = rbank(X)
                pvb_ = PS[pv_][:].bitcast(BF16)
                for nn in range(4):
                    tr(pvb_[:, nn * 128:(nn + 1) * 128], vT[:, nn * 128:(nn + 1) * 128], ident_b[:], [vTb, cst], [PS_b[pv_]])
                vbt, vbtb = X.SBf[9], X.SB_b[9]
                be_v = T["beta"][:, nb0:nb0 + 4, h_:h_ + 1].to_broadcast([128, 4, 128])
                tt(vbt[:, 0:512].rearrange("p (a d) -> p a d", a=4), pvb_[:, 0:512].rearrange("p (a d) -> p a d", a=4),
                   be_v, ALU.mult, [PS_b[pv_], tbb], [vbtb])
                X.held = {pg, pA, pB}
                for nn in range(4):
                    sl = slice(nn * 128, (nn + 1) * 128)
                    mm(PS[pA][:, sl], kT[:, sl], kT[:, sl], True, True, [kTb], [PS_b[pA]])
                    mm(PS[pB][:, sl], kT[:, sl], qT[:, sl], True, True, [kTb, qTb], [PS_b[pB]])
                Dm, Dmb = X.SF[2], X.SF_b[2]
                DTm, DTmb = X.SF[3], X.SF_b[3]
                for nn in range(4):
                    sl = slice(nn * 128, (nn + 1) * 128)
                    gcc = T["gc"][:, nb0 + nn, h_:h_ + 1]
                    stt(Dm[:, sl], PS[pg][:, sl], gcc, maskSL[:], ALU.subtract, ALU.max, [PS_b[pg], tbb, cst], [Dmb])
                    stt(DTm[:, sl], PS[pg][:, sl], gcc, maskUn[:], ALU.subtract, ALU.min, [PS_b[pg], tbb, cst], [DTmb])
                act(Dm[:, 0:n], Dm[:, 0:n], AF.Exp, [Dmb], [Dmb], scale=-1.0)
                act(DTm[:, 0:n], DTm[:, 0:n], AF.Exp, [DTmb], [DTmb])
                m3 = lambda t_: t_[:, 0:512].rearrange("p (a d) -> p a d", a=4)
                X0, X0b = X.SBf[10], X.SB_b[10]
                for nn in range(4):
                    sl = slice(nn * 128, (nn + 1) * 128)
                    stt(X0[:, sl], PS[pA][:, sl], T["nbeta"][:, nb0 + nn, h_:h_ + 1], Dm[:, sl], ALU.mult, ALU.mult,
                        [PS_b[pA], tbb, Dmb], [X0b])
                AIT, AITb = X.SBf[4], X.SB_b[4]
                tt(AIT[:, 0:n], PS[pB][:, 0:n], DTm[:, 0:n], ALU.mult, [PS_b[pB], DTmb], [AITb])
                X.held = set()
                px = rbank(X)
                pxb_ = PS[px][:].bitcast(BF16)
                for nn in range(4):
                    tr(pxb_[:, nn * 128:(nn + 1) * 128], X0[:, nn * 128:(nn + 1) * 128], ident_b[:], [X0b, cst], [PS_b[px]])
                XT0, XT0b = X.SBf[11], X.SB_b[11]
                act(XT0[:, 0:n], pxb_[:, 0:n], AF.Copy, [PS_b[px]], [XT0b])
                RT, RTb = X.SBf[12], X.SB_b[12]
                Rm, Rmb = X.SBf[5], X.SB_b[5]
                U16 = mybir.dt.uint16
                bd3 = BD16[:].unsqueeze(1).to_broadcast([128, 4, 128])
                id3 = ident_b[:].unsqueeze(1).to_broadcast([128, 4, 128])
                Pt = [(X.SBf[0], X.SB_b[0]), (X.SBf[1], X.SB_b[1]), (X.SBf[2], X.SB_b[2]), (X.SBf[3], X.SB_b[3])]
                (Pc, Pcb), (PTc, PTcb) = Pt[0], Pt[1]
                tt(m3(Pc), m3(X0), bd3, ALU.mult, [X0b, cst], [Pcb])
                tt(m3(PTc), m3(XT0), bd3, ALU.mult, [XT0b, cst], [PTcb])
                tt(m3(Rm), m3(Pc), id3, ALU.add, [Pcb, cst], [Rmb])
                tt(m3(RT), m3(PTc), id3, ALU.add, [PTcb, cst], [RTb])

                def mm4(pbank, lhs, rhs, r_):
                    for nn in range(4):
                        sl = slice(nn * 128, (nn + 1) * 128)
                        mm(PS[pbank][:, sl], lhs[:, sl], rhs[:, sl], True, True, r_, [PS_b[pbank]])

                cur = 0
                for lvl in range(1, 4):
                    (Pn, Pnb), (PTn, PTnb) = Pt[2 - cur], Pt[3 - cur]
                    p1, p2 = rbank(X), rbank(X)
                    mm4(p1, PTc, Pc, [PTcb, Pcb])
                    mm4(p2, Pc, PTc, [PTcb, Pcb])
                    act(Pn[:, 0:n], PS[p1][:, 0:n], AF.Copy, [PS_b[p1]], [Pnb])
                    act(PTn[:, 0:n], PS[p2][:, 0:n], AF.Copy, [PS_b[p2]], [PTnb])
                    p3, p4 = rbank(X), rbank(X)
                    mm4(p3, PTn, Rm, [PTnb, Rmb])
                    mm4(p4, Pn, RT, [Pnb, RTb])
                    tt(Rm[:, 0:n], Rm[:, 0:n], PS[p3][:, 0:n], ALU.add, [Rmb, PS_b[p3]], [Rmb])
                    tt(RT[:, 0:n], RT[:, 0:n], PS[p4][:, 0:n], ALU.add, [RTb, PS_b[p4]], [RTb])
                    Pc, Pcb, PTc, PTcb = Pn, Pnb, PTn, PTnb
                    cur = 2 - cur
                (Wt, Wtb), (Ut, Utb) = Pt[0], Pt[1]
                for m_ in (32, 64, 128):
                    om = OFFM[m_][:].bitcast(U16).unsqueeze(1).to_broadcast([128, 4, 128])
                    pu = rbank(X)
                    mm4(pu, X0, RT, [X0b, RTb])
                    if m_ < 128:
                        pw_ = rbank(X)
                        mm4(pw_, XT0, Rm, [XT0b, Rmb])
                        act(Wt[:, 0:n], PS[pw_][:, 0:n], AF.Copy, [PS_b[pw_]], [Wtb])
                    act(Ut[:, 0:n], PS[pu][:, 0:n], AF.Copy, [PS_b[pu]], [Utb])
                    pv2_ = rbank(X)
                    mm4(pv2_, Rm, Ut, [Rmb, Utb])
                    if m_ < 128:
                        pz = rbank(X)
                        mm4(pz, RT, Wt, [RTb, Wtb])
                        A("dve", lambda h, om=om, pz=pz: h.copy_predicated(out=m3(Rm), mask=om, data=m3(PS[pz])),
                          [PS_b[pz], cst], [Rmb], dur=700.0)
                    A("dve", lambda h, om=om, pv2_=pv2_: h.copy_predicated(out=m3(RT), mask=om, data=m3(PS[pv2_])),
                      [PS_b[pv2_], cst], [RTb], dur=700.0)
                pw = rbank(X)
                for nn in range(4):
                    sl = slice(nn * 128, (nn + 1) * 128)
                    mm(PS[pw][:, sl], kbg[:, sl], RT[:, sl], True, True, [kbgb, RTb], [PS_b[pw]])
                wnT, wnTb = X.SBf[3], X.SB_b[3]
                act(wnT[:, 0:n], PS[pw][:, 0:n], AF.Copy, [PS_b[pw]], [wnTb], scale=-1.0)
                X.held = {pO}
                for nn in range(4):
                    sl = slice(nn * 128, (nn + 1) * 128)
                    pn_ = rbank(X)
                    mm(PS[pn_][:, 0:128], RT[:, sl], vbt[:, sl], True, False, [RTb, vbtb], [PS_b[pn_]])
                    mm(PS[pn_][:, 0:128], wnT[:, sl], X.S_b[:], False, True, [wnTb, X.Sbb], [PS_b[pn_]])
                    vnw, vnwb = X.SBf[2][:, (nn % 2) * 128:(nn % 2) * 128 + 128], X.SB_b[2]
                    act(vnw[:, 0:128], PS[pn_][:, 0:128], AF.Copy, [PS_b[pn_]], [vnwb])
                    mm(PS[pO][:, sl], X.S_b[:], qdT[:, sl], True, False, [X.Sbb, qdTb], [PS_b[pO]])
                    mm(PS[pO][:, sl], vnw[:, 0:128], AIT[:, sl], False, True, [vnwb, AITb], [PS_b[pO]])
                    ps_ = rbank(X)
                    mm(PS[ps_][:, 0:128], ktm[:, sl], vnw[:, 0:128], True, True, [ktmb, vnwb], [PS_b[ps_]])
                    eglc = egl[:, (nb0 + nn) * 8 + h_:(nb0 + nn) * 8 + h_ + 1]
                    stt(X.S_f[:], X.S_f[:], eglc, PS[ps_][:, 0:128], ALU.mult, ALU.add, [X.Sfb, tbb, PS_b[ps_]], [X.Sfb])
                    act(X.S_b[:], X.S_f[:], AF.Copy, [X.Sfb], [X.Sbb])
                if ti == 3:
                    dma("sp", o_pS[l, h_], X.S_f[:], [X.Sfb])
                of, ofb = X.SF[0], X.SF_b[0]
                act(of[:, 0:n], PS[pO][:, 0:n], AF.Copy, [PS_b[pO]], [ofb])
                X.held = set()

            sq2, sq2b = X.SBf[3], X.SB_b[3]
            act(sq2[:, 0:n], of[:, 0:n], AF.Square, [ofb], [sq2b])
            pq = rbank(X)
            mm(PS[pq][:, 0:n], ones_b[:], sq2[:, 0:n], True, True, [cst, sq2b], [PS_b[pq]])
            rs2, rs2b = X.SF[1], X.SF_b[1]
            act(rs2[:, 0:n], PS[pq][:, 0:n], AF.Ln, [PS_b[pq]], [rs2b], bias=RMS_EPS, scale=1.0 / 128)
            act(rs2[:, 0:n], rs2[:, 0:n], AF.Exp, [rs2b], [rs2b], scale=-0.5)
            stt(of[:, 0:n], of[:, 0:n], pv[:, 272:273], rs2[:, 0:n], ALU.mult, ALU.mult, [ofb, pvb, rs2b], [ofb])
            pb = proj(6)
            zs, zsb = X.SF[2], X.SF_b[2]
            act(zs[:, 0:n], PS[pb][:, 0:n], AF.Sigmoid, [PS_b[pb], pvb], [zsb], bias=bias(6))
            stt(zs[:, 0:n], PS[pb][:, 0:n], bias(6), zs[:, 0:n], ALU.add, ALU.mult, [PS_b[pb], pvb, zsb], [zsb])
            tt(of[:, 0:n], of[:, 0:n], zs[:, 0:n], ALU.mult, [ofb, zsb], [ofb])
            pb = proj(7)
            act(zs[:, 0:n], PS[pb][:, 0:n], AF.Sigmoid, [PS_b[pb], pvb], [zsb], bias=bias(7))
            tt(of[:, 0:n], of[:, 0:n], zs[:, 0:n], ALU.mult, [ofb, zsb], [ofb])
            mt, mtb = X.SBf[6], X.SB_b[6]
            tt(mt[:, 0:n], of[:, 0:n], ma[:, 0:n], ALU.add, [ofb, mab], [mtb])
            for m in range(8):
                pb = rbank(X)
                mm(PS[pb][:, 0:n], X.wo_c[:, m * 128:(m + 1) * 128], mt[:, 0:n], True, True, [X.wob, mtb], [PS_b[pb]])
                tt(xres[:, m, c0:c0 + n], PS[pb][:, 0:n], xres[:, m, c0:c0 + n], ALU.add,
                   [PS_b[pb], xres_b[m][ti]], [xres_b[m][ti]])


    def mixer(l):
        gate_tables(l)
        nops0 = len(P.all)
        prior = []
        for row in mbuf_b:
            for b in row:
                prior.extend(b.r)
                if b.w is not None:
                    prior.append(b.w)
        for b in t1_bufs:
            b.w = None
            b.r = list(prior)
        for pair in range(4):
            chunk_body(l, 2 * pair, CTX[0])
            chunk_body(l, 2 * pair + 1, CTX[1])
        dma("sp", o_ph[l], hcar[:], [CTX[0].smb, CTX[1].smb])
        if l == 0:
            print("mixer ops per (chunk,tile):", (len(P.all) - nops0) / 40.0)
        post = []
        for b in t1_bufs:
            post.extend(b.r)
            if b.w is not None:
                post.append(b.w)
        for row in mbuf_b:
            for b in row:
                b.r = list(b.r) + post

    for l in range(n_layers):
        layer_params(l)
        ffn(2 * l)
        layernorm(0, False)
        mixer(l)
        layernorm(1, False)
        ffn(2 * l + 1)
        layernorm(2, l == n_layers - 1)
    P.emit(scheduled=SCHED)
    print("ops:", len(P.all), "est total us:", getattr(P, "est_total", 0) / 1e3)
    st.close()
    return nc


def _prep_shared(inp):
    f = lambda a: np.ascontiguousarray(np.asarray(a, dtype=np.float32))
    w1 = f(np.stack([inp["ffn1_w1"], inp["ffn2_w1"]], 1).reshape(2 * L, D, DFF))
    w3 = f(np.stack([inp["ffn1_w3"], inp["ffn2_w3"]], 1).reshape(2 * L, D, DFF))
    w2 = f(np.stack([inp["ffn1_w2"], inp["ffn2_w2"]], 1).reshape(2 * L, DFF, D))
    w_in = np.asarray(inp["w_in"], np.float32)
    b_in = np.asarray(inp["b_in"], np.float32)
    offs = [0, 1024, 6160, 2048, 3072, 4096, 5120, 7184]
    win = np.empty((L, 8, D, 1024), np.float32)
    for c in range(8):
        for bi, o in enumerate(offs):
            win[:, c, :, bi * 128:(bi + 1) * 128] = w_in[:, :, o + c * 128:o + (c + 1) * 128]
    wab = f(w_in[:, :, 6144:6160])
    gatew = np.zeros((L, 2, 8, 128, 128), np.float32)
    for gi, nm in enumerate(("rg_wr", "rg_wi")):
        w = np.asarray(inp[nm], np.float32)
        for c in range(8):
            gatew[:, gi, c, 0:64, 0:64] = w[:, 2 * c]
            gatew[:, gi, c, 64:128, 64:128] = w[:, 2 * c + 1]
    pvec = np.empty((L, 128, NPV), np.float32)
    fm = lambda v: np.asarray(v, np.float32).reshape(L, -1, 128).transpose(0, 2, 1)
    for i, (g, b) in enumerate((("ln1_g", "ln1_b"), ("ln2_g", "ln2_b"), ("ln3_g", "ln3_b"))):
        pvec[:, :, 16 * i:16 * i + 8] = fm(inp[g])
        pvec[:, :, 16 * i + 8:16 * i + 16] = fm(inp[b])
    for c in range(8):
        for bi, o in enumerate(offs):
            pvec[:, :, 48 + c * 8 + bi] = b_in[:, o + c * 128:o + (c + 1) * 128]
    rcw = np.asarray(inp["rg_conv_w"], np.float32)
    for j in range(4):
        pvec[:, :, 112 + j * 8:112 + (j + 1) * 8] = fm(rcw[:, j])
    pvec[:, :, 144:152] = fm(inp["rg_conv_b"])
    pvec[:, :, 152:160] = fm(inp["rg_br"])
    pvec[:, :, 160:168] = fm(inp["rg_bi"])
    pvec[:, :, 168:176] = fm(inp["rg_lambda"])
    gcw = np.asarray(inp["gdn_conv_w"], np.float32)
    for j in range(4):
        pvec[:, :, 176 + j * 24:176 + (j + 1) * 24] = fm(gcw[:, j])
    pvec[:, :, 272] = np.asarray(inp["gdn_norm_w"], np.float32)
    prow = np.concatenate([b_in[:, 6144:6160], np.asarray(inp["gdn_a_log"], np.float32),
                           np.asarray(inp["gdn_dt_bias"], np.float32)], axis=1).reshape(L, 1, 32)
    return {"w1": w1, "w3": w3, "w2": w2, "win": win, "wab": wab, "wo": f(inp["w_o"]), "gatew": gatew,
            "pvec": f(pvec), "prow": f(prow)}


def _prep_core(inp, i):
    xp = np.asarray(inp["x_prompt"], np.float32)[i]
    xs = np.asarray(inp["x_sample"], np.float32)[i * NS:(i + 1) * NS, 0]
    x = np.concatenate([xp, xs], 0)
    xin = np.ascontiguousarray(x.T.reshape(8, 128, TOK).transpose(1, 0, 2))
    sl = slice(i * NS, (i + 1) * NS)
    h0 = np.asarray(inp["state_rglru_h"], np.float32)[:, sl]
    sh0 = np.ascontiguousarray(h0.transpose(0, 2, 1).reshape(L, 8, 128, NS))
    rc = np.asarray(inp["state_rglru_conv"], np.float32)[:, sl]
    src0 = np.ascontiguousarray(rc.transpose(0, 3, 2, 1).reshape(L, 8, 128, 3, NS))
    gc = np.asarray(inp["state_gdn_conv"], np.float32)[:, sl]
    sgc0 = np.ascontiguousarray(gc.transpose(0, 3, 2, 1).reshape(L, 24, 128, 3, NS))
    sS0 = np.ascontiguousarray(np.asarray(inp["state_gdn_S"], np.float32)[:, sl])
    return {"xin": xin, "sh0": sh0, "src0": src0, "sgc0": sgc0, "sS0": sS0}


_NC_CACHE = {}


def kernel(**inp):
    if "nc" not in _NC_CACHE:
        _NC_CACHE["nc"] = build_program()
    nc = _NC_CACHE["nc"]
    shared = _prep_shared(inp)
    in_maps = []
    for i in range(8):
        m = dict(shared)
        m.update(_prep_core(inp, i))
        in_maps.append(m)
    res = run_bass_kernel_spmd(nc, in_maps, core_ids=list(range(8)))
    R = res.results
    Bp = 8
    y_prompt = np.empty((Bp, NP, D), np.float32)
    y_sample = np.empty((8 * NS, 1, D), np.float32)
    p_h = np.empty((L, Bp, D), np.float32)
    p_rc = np.empty((L, Bp, 3, D), np.float32)
    p_S = np.empty((L, Bp, 8, 128, 128), np.float32)
    p_gc = np.empty((L, Bp, 3, 3072), np.float32)
    s_h = np.empty((L, 8 * NS, D), np.float32)
    s_rc = np.empty((L, 8 * NS, 3, D), np.float32)
    s_S = np.empty((L, 8 * NS, 8, 128, 128), np.float32)
    s_gc = np.empty((L, 8 * NS, 3, 3072), np.float32)
    for i in range(8):
        r = R[i]
        y = np.asarray(r["yout"]).transpose(1, 0, 2).reshape(D, TOK).T
        y_prompt[i] = y[:NP]
        sl = slice(i * NS, (i + 1) * NS)
        y_sample[sl, 0] = y[NP:]
        p_h[:, i] = np.asarray(r["o_ph"]).transpose(0, 2, 1).reshape(L, D)
        p_rc[:, i] = np.asarray(r["o_prc"]).transpose(0, 3, 1, 2).reshape(L, 3, D)
        p_S[:, i] = np.asarray(r["o_pS"])
        p_gc[:, i] = np.asarray(r["o_pgc"]).transpose(0, 3, 1, 2).reshape(L, 3, 3072)
        s_h[:, sl] = np.asarray(r["o_sh"]).transpose(0, 3, 1, 2).reshape(L, NS, D)
        s_rc[:, sl] = np.asarray(r["o_src"]).transpose(0, 4, 3, 1, 2).reshape(L, NS, 3, D)
        s_S[:, sl] = np.asarray(r["o_sS"])
        s_gc[:, sl] = np.asarray(r["o_sgc"]).transpose(0, 4, 3, 1, 2).reshape(L, NS, 3, 3072)
    return (y_prompt, y_sample, p_h, p_rc, p_S, p_gc, s_h, s_rc, s_S, s_gc)
```

```python
import math
from contextlib import ExitStack
import numpy as np
import concourse.bass as bass
import concourse.mybir as mybir
from concourse.bass_utils import run_bass_kernel_spmd

F32 = mybir.dt.float32
BF16 = mybir.dt.bfloat16
AF = mybir.ActivationFunctionType
ALU = mybir.AluOpType
AX = mybir.AxisListType

L = 4
D = 1024
NP = 2048
NS = 16
TOK = NP + NS
DFF = 2816
NJ = DFF // 128
ALPHA = (2 * L) ** 0.25
LN_EPS = 1e-5
RMS_EPS = 1e-6
TILES = [(0, 512), (512, 512), (1024, 512), (1536, 512), (2048, 16)]
NPV = 273
ENGS = ("pe", "act", "dve", "pool", "sp")


class Buf:
    __slots__ = ("w", "r", "excl")

    def __init__(self, excl=False):
        self.w = None
        self.r = []
        self.excl = excl


class Prog:
    def __init__(self, nc):
        self.nc = nc
        self.all = []
        self.n_ch = {"sp": 28, "pool": 12}

    def _collect(self, eng, reads, writes, is_dma):
        deps = {}
        for b in reads:
            ev = b.w
            if ev is None:
                continue
            ee = self.all[ev]
            need = not (eng == "pe" and ee[0] == "pe" and ee[1] == "c")
            deps[ev] = deps.get(ev, False) or need
        for b in writes:
            evs = list(b.r)
            if b.w is not None:
                evs.append(b.w)
            for ev in evs:
                ee = self.all[ev]
                same = (ee[0] == eng and ee[1] == "c" and not is_dma)
                deps[ev] = deps.get(ev, False) or (not same)
        return list(deps.items())

    def _add(self, eng, kind, fn, reads, writes, dur, fam):
        ex = [b for b in reads if b.excl and b not in writes]
        if ex:
            writes = list(writes) + ex
        deps = self._collect(eng, reads, writes, kind == "d")
        idx = len(self.all)
        self.all.append([eng, kind, fn, deps, dur, fam])
        for b in reads:
            b.r.append(idx)
        for b in writes:
            b.w = idx
            b.r = []
        return idx

    def op(self, eng, fn, reads=(), writes=(), dur=300.0, fam=None):
        return self._add(eng, "c", fn, reads, writes, dur, fam)

    def dma(self, eng, fn, reads=(), writes=(), dur=3000.0):
        return self._add(eng, "d", fn, reads, writes, dur, None)

    def schedule(self):
        import heapq
        ops = self.all
        n = len(ops)
        succ = [[] for _ in range(n)]
        npred = [0] * n
        last_q = {}
        for i, o in enumerate(ops):
            if o[0] in QUEUE_INORDER:
                p_ = last_q.get(o[0])
                if p_ is not None and all(d != p_ for d, _ in o[3]):
                    o[3].append((p_, False) if ops[p_][1] == "c" and o[1] == "c" else (p_, None))
                last_q[o[0]] = i
        for i, o in enumerate(ops):
            npred[i] = len(o[3])
            for d, _ in o[3]:
                succ[d].append(i)
        issue_only = {}
        for j_, o in enumerate(ops):
            for d, need in o[3]:
                if need is None:
                    issue_only[(d, j_)] = True
        ready_t = [0.0] * n
        fin = [0.0] * n
        eng_free = {e: 0.0 for e in ENGS}
        eng_fam = {e: None for e in ENGS}
        order = {e: [] for e in ENGS}
        future = {e: [] for e in ENGS}
        avail = {e: [] for e in ENGS}
        for i in range(n):
            if npred[i] == 0:
                heapq.heappush(future[ops[i][0]], (0.0, i))
        done = 0
        XLAT = 250.0
        while done < n:
            best = None
            for e in ENGS:
                t = eng_free[e]
                fut, av = future[e], avail[e]
                while fut and fut[0][0] <= t:
                    heapq.heappush(av, heapq.heappop(fut)[1])
                if av:
                    c_ = (t, av[0], e, True)
                elif fut:
                    c_ = (fut[0][0], fut[0][1], e, False)
                else:
                    continue
                if best is None or (c_[0], c_[1]) < (best[0], best[1]):
                    best = c_
            st, i, e, from_av = best
            if from_av:
                av = avail[e]
                if e == "act" and eng_fam[e] is not None and len(av) > 1 and ops[i][5] not in (None, eng_fam[e]):
                    cands = heapq.nsmallest(12, av)
                    alt = [c for c in cands if ops[c][5] in (None, eng_fam[e]) and c - i < 400]
                    if alt:
                        i = alt[0]
                        av.remove(i)
                        heapq.heapify(av)
                    else:
                        heapq.heappop(av)
                else:
                    heapq.heappop(av)
            else:
                heapq.heappop(future[e])
            o = ops[i]
            dur = o[4]
            if o[1] == "c":
                if o[5] is not None and eng_fam[e] is not None and eng_fam[e] != o[5]:
                    st += 1300.0
                if o[5] is not None:
                    eng_fam[e] = o[5]
                eng_free[e] = st + dur
                fin[i] = st + dur
            else:
                eng_free[e] = st + 120.0
                fin[i] = st + dur
            order[e].append(i)
            done += 1
            for j in succ[i]:
                oj = ops[j]
                lat = XLAT if (oj[0] != e or o[1] == "d") else 60.0
                t = fin[i] + lat
                if issue_only.get((i, j)):
                    t = st
                if t > ready_t[j]:
                    ready_t[j] = t
                npred[j] -= 1
                if npred[j] == 0:
                    heapq.heappush(future[oj[0]], (ready_t[j], j))
        self.order = order
        self.est_total = max(eng_free.values())

    def emit(self, scheduled=True):
        nc = self.nc
        ops = self.all
        if scheduled:
            self.schedule()
            order = self.order
        else:
            order = {e: [i for i, o in enumerate(ops) if o[0] == e] for e in ENGS}
        pos = {}
        for e in ENGS:
            for p_, i in enumerate(order[e]):
                pos[i] = p_
        ch_of, ch_val = {}, {}
        ch_next = {e: 0 for e in self.n_ch}
        ch_count = {}
        ch_prev = {}
        for e in self.n_ch:
            for i in order[e]:
                if ops[i][1] != "d":
                    continue
                c = ch_next[e]
                ch_next[e] = (c + 1) % self.n_ch[e]
                prev = ch_count.get((e, c), 0)
                ch_prev[i] = (c, prev)
                ch_count[(e, c)] = prev + 16
                ch_of[i] = c
                ch_val[i] = prev + 16
        marked = {e: set() for e in ENGS}
        waits_of = {}
        for e in ENGS:
            waited = {}
            for i in order[e]:
                o = ops[i]
                wl = []
                for d, need in o[3]:
                    od = ops[d]
                    if need is None:
                        assert od[0] == e and pos[d] < pos[i]
                        continue
                    if od[1] == "d":
                        key, val = ("d", od[0], ch_of[d]), ch_val[d]
                    else:
                        if not need:
                            assert od[0] == e and pos[d] < pos[i]
                            continue
                        key, val = ("e", od[0]), pos[d]
                    if val <= waited.get(key, -1):
                        continue
                    waited[key] = val
                    wl.append((key, val, d))
                    if key[0] == "e":
                        marked[od[0]].add(d)
                if o[1] == "d":
                    c, prev = ch_prev[i]
                    key = ("d", e, c)
                    if prev and prev > waited.get(key, -1):
                        waited[key] = prev
                        wl.append((key, prev, None))
                waits_of[i] = wl
        cnt = {}
        for e in ENGS:
            m = sorted(marked[e], key=lambda i: pos[i])
            cnt[e] = {i: k + 1 for k, i in enumerate(m)}
        with ExitStack() as st:
            esem = {e: st.enter_context(nc.semaphore("s_" + e)) for e in ENGS}
            dsem = {}
            for e, n_ in self.n_ch.items():
                for c in range(n_):
                    dsem[(e, c)] = st.enter_context(nc.semaphore("d_%s%d" % (e, c)))
            block = st.enter_context(nc.Block())
            n_ch = self.n_ch

            def run(e, h):
                for i in order[e]:
                    o = ops[i]
                    for key, val, d in waits_of[i]:
                        if key[0] == "e":
                            h.wait_ge(esem[key[1]], cnt[key[1]][d])
                        else:
                            h.wait_ge(dsem[(key[1], key[2])], val)
                    ins = o[2](h)
                    if o[1] == "d":
                        ins.then_inc(dsem[(e, ch_of[i])], 16)
                    elif i in cnt[e]:
                        ins.then_inc(esem[e], 1)
                if e in n_ch:
                    for c in range(n_ch[e]):
                        v = ch_count.get((e, c), 0)
                        if v:
                            h.wait_ge(dsem[(e, c)], v)

            @block.tensor
            def _(h):
                run("pe", h)

            @block.scalar
            def _(h):
                run("act", h)

            @block.vector
            def _(h):
                run("dve", h)

            @block.gpsimd
            def _(h):
                run("pool", h)

            @block.sync
            def _(h):
                run("sp", h)


SCHED = True
QUEUE_INORDER = ()


def build_program(n_layers=L, debug=None):
    nc = bass.Bass("TRN2", target_bir_lowering=False)
    st = ExitStack()

    def din(name, shape):
        return nc.dram_tensor(name, list(shape), F32, kind="ExternalInput").ap()

    def dout(name, shape):
        return nc.dram_tensor(name, list(shape), F32, kind="ExternalOutput").ap()

    xin = din("xin", [128, 8, TOK])
    w1d = din("w1", [2 * L, D, DFF])
    w3d = din("w3", [2 * L, D, DFF])
    w2d = din("w2", [2 * L, DFF, D])
    wind = din("win", [L, 8, D, 1024])
    wabd = din("wab", [L, D, 16])
    wod = din("wo", [L, D, D])
    gwd = din("gatew", [L, 2, 8, 128, 128])
    pvd = din("pvec", [L, 128, NPV])
    prd = din("prow", [L, 1, 32])
    sh0d = din("sh0", [L, 8, 128, NS])
    src0d = din("src0", [L, 8, 128, 3, NS])
    sgc0d = din("sgc0", [L, 24, 128, 3, NS])
    sS0d = din("sS0", [L, NS, 8, 128, 128])
    yout = dout("yout", [128, 8, TOK])
    o_ph = dout("o_ph", [L, 128, 8])
    o_prc = dout("o_prc", [L, 8, 128, 3])
    o_pS = dout("o_pS", [L, 8, 128, 128])
    o_pgc = dout("o_pgc", [L, 24, 128, 3])
    o_sh = dout("o_sh", [L, 8, 128, NS])
    o_src = dout("o_src", [L, 8, 128, 3, NS])
    o_sS = dout("o_sS", [L, NS, 8, 128, 128])
    o_sgc = dout("o_sgc", [L, 24, 128, 3, NS])
    dbg = {}
    if debug:
        for nm, shp in debug.items():
            dbg[nm] = dout("dbg_" + nm, shp)

    def sb(name, shape, dt):
        return st.enter_context(nc.sbuf_tensor(name, list(shape), dt))

    xres = sb("xres", [128, 8, TOK], F32)
    xbf = sb("xbf", [128, 8, TOK], BF16)
    mbuf = sb("mbuf", [128, 8, TOK], BF16)
    wsl = [sb("wsl%d" % i, [128, 4096], BF16) for i in range(4)]
    NSF, NSB = 6, 13
    SF = [sb("sf%d" % i, [128, 512], F32) for i in range(NSF)]
    SBf = [sb("sbf%d" % i, [128, 520 if i < 2 else 512], BF16) for i in range(NSB)]
    wo_c0 = sb("wo_c0", [128, 1024], BF16)
    ident_f = sb("ident_f", [128, 128], F32)
    ident_b = sb("ident_b", [128, 128], BF16)
    ones_f = sb("ones_f", [128, 128], F32)
    ones_b = sb("ones_b", [128, 128], BF16)
    maskU = sb("maskU", [128, 128], F32)
    maskSL = sb("maskSL", [128, 128], F32)
    maskUn = sb("maskUn", [128, 128], F32)
    sel_last = sb("sel_last", [128, 128], F32)
    BD16 = sb("BD16", [128, 128], BF16)
    OFFM = {m_: sb("OFF%d" % m_, [128, 128], BF16) for m_ in (32, 64, 128)}
    pv = sb("pv_t", [128, NPV], F32)
    pv2 = sb("pv2", [128, 48], F32)
    nsp = sb("nsp", [128, 16], F32)
    prow = sb("prow_t", [128, 32], F32)
    aexp = sb("aexp", [128, 8], F32)
    wab_b = sb("wab_b", [128, 8, 16], BF16)
    gw_b = sb("gw_b", [128, 2, 128], BF16)
    TB = {nm: sb("tb_" + nm, [128, 17, 8], F32) for nm in
          ("beta", "g", "gc", "egc", "bge", "nbeta")}
    ksq_t = sb("ksq_t", [128, 2 * NS], F32)
    glb = sb("glb", [128, 128], F32)
    egl = sb("egl", [128, 128], F32)
    ekt = sb("ekt", [128, 128], F32)
    S_f = sb("S_f", [128, 128], F32)
    S_b = sb("S_b", [128, 128], BF16)
    hcar = sb("hcar", [128, 8], F32)
    hist = sb("hist", [128, 4, 4], BF16)
    pco = sb("pco", [128, 4, 3], F32)
    dgw = sb("dgw", [128, 4, 128], BF16)
    sst = sb("sst", [128, 4, 3, NS], F32)
    sstb = sb("sstb", [128, 4, 3, NS], BF16)
    sso = sb("sso", [128, 4, 3, NS], F32)
    sh0 = sb("sh0t", [128, NS], F32)
    sho = sb("shot", [128, NS], F32)
    srow = {nm: sb("srow_" + nm, [128, 128], F32) for nm in ("eg", "beta", "nbe")}
    srow["qk"] = sb("srow_qk", [128, NS], F32)
    skq = sb("skq", [128, 2, NS], F32)
    PS = [st.enter_context(nc.psum_tensor("ps%d" % i, [128, 512], F32)) for i in range(8)]
    print("sbuf bytes remaining/partition:", nc.sbuf_bytes_remaining)

    P = Prog(nc)
    B_ = Buf
    xres_b = [[B_() for _ in TILES] for _ in range(8)]
    xbf_b = [[B_() for _ in TILES] for _ in range(8)]
    mbuf_b = [[B_() for _ in TILES] for _ in range(8)]
    wsl_b = [B_() for _ in range(4)]
    SF_b = [B_() for _ in range(NSF)]
    SB_b = [B_() for _ in range(NSB)]
    PS_b = [B_(True) for _ in range(8)]
    cst = B_()
    pvb = B_()
    tbb = B_()
    gwb = B_()
    wabb = B_()
    dgwb = B_()
    ksqb = B_()
    smb = B_()
    Sfb, Sbb = B_(), B_()
    sstb_b, ssob = B_(), B_()
    Sin_b = [B_() for _ in range(2)]
    Sout_b = [B_() for _ in range(2)]
    srb = B_()

    def fsz(ap):
        sh = ap.shape
        n_ = 1
        for d_ in sh[1:]:
            n_ *= d_
        return n_

    ACT_FAM = {AF.Exp: "el", AF.Ln: "el", AF.Sigmoid: "sg", AF.Silu: "si", AF.Sqrt: "sq"}

    def A(e, fn, r=(), w=(), dur=None, fam=None):
        if dur is None:
            dur = {"pe": 150.0, "act": 400.0, "dve": 400.0, "pool": 600.0}.get(e, 300.0)
        P.op(e, fn, reads=r, writes=w, dur=dur, fam=fam)

    def act(out, in_, func, r, w, bias=None, scale=None):
        kw = {}
        if bias is not None:
            kw["bias"] = bias
        if scale is not None:
            kw["scale"] = scale
        A("act", lambda h: h.activation(out=out, in_=in_, func=func, **kw), r, w,
          dur=220.0 + 0.8 * fsz(out), fam=ACT_FAM.get(func))

    def mm(out, lhsT, rhs, start, stop, r, w):
        n_ = fsz(rhs)
        f32 = 4.0 if rhs.dtype == F32 else 1.0
        A("pe", lambda h: h.matmul(out, lhsT=lhsT, rhs=rhs, start=start, stop=stop), r, w,
          dur=(45.0 + max(64, n_) * 0.5 * f32) if n_ > 128 else (100.0 * f32))

    def tt(out, in0, in1, op, r, w, eng="dve"):
        A(eng, lambda h: h.tensor_tensor(out=out, in0=in0, in1=in1, op=op), r, w,
          dur=(110.0 + 1.1 * fsz(out)) if eng == "dve" else (300.0 + 1.7 * fsz(out)))

    def ts(out, in0, s1, s2, op0, op1, r, w, eng="dve"):
        if s2 is None:
            A(eng, lambda h: h.tensor_scalar(out=out, in0=in0, scalar1=s1, scalar2=None, op0=op0), r, w,
              dur=110.0 + 1.0 * fsz(out))
        else:
            A(eng, lambda h: h.tensor_scalar(out=out, in0=in0, scalar1=s1, scalar2=s2, op0=op0, op1=op1), r, w,
              dur=110.0 + 1.0 * fsz(out))

    def stt(out, in0, scalar, in1, op0, op1, r, w):
        A("dve", lambda h: h.scalar_tensor_tensor(out=out, in0=in0, scalar=scalar, in1=in1, op0=op0, op1=op1), r, w,
          dur=110.0 + 1.1 * fsz(out))

    def dma(eng, out, in_, r=(), w=()):
        P.dma(eng, lambda h: h.dma_start(out=out, in_=in_), reads=r, writes=w,
              dur=2200.0 + fsz(out) * out.shape[0] * 4 / 150.0)

    def cp(out, in_, r, w, eng="dve"):
        A(eng, lambda h: h.tensor_copy(out=out, in_=in_), r, w,
          dur=(100.0 + 1.0 * fsz(out)) if eng == "dve" else (300.0 + 0.9 * fsz(out)))

    def mset(ap, val, w, eng="dve"):
        A(eng, lambda h: h.memset(ap, val), (), w)

    def tr(out, in_, idn, r, w):
        A("pe", lambda h: h.transpose(out, in_, idn), r, w, dur=130.0)

    def scan(out, d0, d1, init, r, w):
        A("dve", lambda h: h.tensor_tensor_scan(out=out, data0=d0, data1=d1, initial=init,
                                                op0=ALU.mult, op1=ALU.add), r, w, dur=110.0 + 2.1 * fsz(out))

    def dbg_dump(nm, ap, r):
        if nm in dbg:
            dma("sp", dbg[nm], ap, r)

    A("pool", lambda h: h.memset(ident_f[:], 1.0), (), [cst])
    A("pool", lambda h: h.affine_select(out=ident_f[:], in_=ident_f[:], pattern=[[1, 128]],
                                        compare_op=ALU.is_equal, fill=0.0, base=0, channel_multiplier=-1), [cst], [cst])
    A("pool", lambda h: h.memset(ones_f[:], 1.0), (), [cst])
    A("pool", lambda h: h.memset(ones_b[:], 1.0), (), [cst])
    A("pool", lambda h: h.memset(maskU[:], 1.0), (), [cst])
    A("pool", lambda h: h.affine_select(out=maskU[:], in_=maskU[:], pattern=[[1, 128]],
                                        compare_op=ALU.is_ge, fill=0.0, base=0, channel_multiplier=-1), [cst], [cst])
    A("pool", lambda h: h.memset(maskSL[:], 0.0), (), [cst])
    A("pool", lambda h: h.affine_select(out=maskSL[:], in_=maskSL[:], pattern=[[-1, 128]],
                                        compare_op=ALU.is_gt, fill=1e30, base=0, channel_multiplier=1), [cst], [cst])
    A("pool", lambda h: h.memset(maskUn[:], 0.0), (), [cst])
    A("pool", lambda h: h.affine_select(out=maskUn[:], in_=maskUn[:], pattern=[[1, 128]],
                                        compare_op=ALU.is_ge, fill=-1e30, base=0, channel_multiplier=-1), [cst], [cst])
    A("pool", lambda h: h.memset(sel_last[:], 1.0), (), [cst])
    A("pool", lambda h: h.affine_select(out=sel_last[:], in_=sel_last[:], pattern=[[0, 128]],
                                        compare_op=ALU.is_equal, fill=0.0, base=-127, channel_multiplier=1), [cst], [cst])
    A("dve", lambda h: h.tensor_copy(out=ident_b[:], in_=ident_f[:]), [cst], [cst])
    def mk_bd(b_, dst_tile, col0):
        nb_ = 128 // b_
        E = SF[5][0:nb_, 0:128]
        A("pool", lambda h: h.memset(E, 1.0), (), [SF_b[5]])
        A("pool", lambda h: h.affine_select(out=E, in_=E, pattern=[[1, 128]], compare_op=ALU.is_ge, fill=0.0,
                                            base=0, channel_multiplier=-b_), [SF_b[5]], [SF_b[5]])
        A("pool", lambda h: h.affine_select(out=E, in_=E, pattern=[[-1, 128]], compare_op=ALU.is_ge, fill=0.0,
                                            base=b_ - 1, channel_multiplier=b_), [SF_b[5]], [SF_b[5]])
        mm(PS[7][:, col0:col0 + 128], E, E, True, True, [SF_b[5]], [PS_b[7]])
    mk_bd(16, None, 0)
    mk_bd(32, None, 128)
    mk_bd(64, None, 256)
    act(BD16[:], PS[7][:, 0:128], AF.Copy, [PS_b[7]], [cst])
    bdf = SF[4]
    act(bdf[:, 0:384], PS[7][:, 0:384], AF.Copy, [PS_b[7]], [SF_b[4]])
    tt(OFFM[32][:], bdf[:, 128:256], bdf[:, 0:128], ALU.subtract, [SF_b[4]], [cst])
    tt(OFFM[64][:], bdf[:, 256:384], bdf[:, 128:256], ALU.subtract, [SF_b[4]], [cst])
    ts(OFFM[128][:], bdf[:, 256:384], -1.0, 1.0, ALU.mult, ALU.add, [SF_b[4]], [cst])

    for k in range(8):
        for ti, (c0, n) in enumerate(TILES):
            dma("sp", xres[:, k, c0:c0 + n], xin[:, k, c0:c0 + n], (), [xres_b[k][ti]])
            act(xbf[:, k, c0:c0 + n], xres[:, k, c0:c0 + n], AF.Copy, [xres_b[k][ti]], [xbf_b[k][ti]])
            ts(xres[:, k, c0:c0 + n], xres[:, k, c0:c0 + n], ALPHA, None, ALU.mult, None,
               [xres_b[k][ti]], [xres_b[k][ti]])

    wctr = [0]

    def wslot():
        i = wctr[0] % 4
        wctr[0] += 1
        return i

    def ffn(fi):
        w1v = w1d[fi].rearrange("(k p) n -> p k n", p=128)
        w3v = w3d[fi].rearrange("(k p) n -> p k n", p=128)
        w2v = w2d[fi].rearrange("(j p) n -> p j n", p=128)
        ftmp_b = SF_b[0:5]
        subs = [(0, 4), (4, 4), (8, 4), (12, 4), (16, 4), (20, 2)]
        groups = [(0, 1), (2, 3), (4, 5)]

        def load13(su):
            j0, nj = subs[su]
            s1, s3 = wslot(), wslot()
            for s, wv in ((s1, w1v), (s3, w3v)):
                dst = wsl[s][:].rearrange("p (k n) -> p k n", k=8)[:, :, 0:nj * 128]
                dma("pool", dst, wv[:, :, j0 * 128:(j0 + nj) * 128], (), [wsl_b[s]])
            return s1, s3

        def load2(su):
            j0, nj = subs[su]
            s2 = wslot()
            dst = wsl[s2][:].rearrange("p (j n) -> p j n", j=4)[:, 0:nj, :]
            dma("pool", dst, w2v[:, j0:j0 + nj, :], (), [wsl_b[s2]])
            return s2

        for (sa, sbb) in groups:
            l13 = {}
            l13[sa] = load13(sa)
            l13[sbb] = load13(sbb)
            for su in (sa, sbb):
                j0, nj = subs[su]
                s1, s3 = l13[su]
                hj0 = 0 if su == sa else 4
                for j in range(nj):
                    for (s, which) in ((s1, 0), (s3, 1)):
                        wv = wsl[s][:].rearrange("p (k n) -> p k n", k=8)
                        for k in range(8):
                            for ti, (c0, n) in enumerate(TILES):
                                mm(PS[ti][:, 0:n], wv[:, k, j * 128:(j + 1) * 128], xbf[:, k, c0:c0 + n],
                                   k == 0, k == 7, [wsl_b[s], xbf_b[k][ti]], [PS_b[ti]])
                        for ti, (c0, n) in enumerate(TILES):
                            fsl = SF[ti][:, 0:n]
                            if which == 0:
                                act(fsl, PS[ti][:, 0:n], AF.Silu, [PS_b[ti]], [ftmp_b[ti]])
                            else:
                                tt(mbuf[:, hj0 + j, c0:c0 + n], fsl, PS[ti][:, 0:n], ALU.mult,
                                   [ftmp_b[ti], PS_b[ti]], [mbuf_b[hj0 + j][ti]])
            s2a = load2(sa)
            s2b = load2(sbb)
            parts = [(s2a, subs[sa][1], 0), (s2b, subs[sbb][1], 4)]
            tot = subs[sa][1] + subs[sbb][1]
            for m in range(8):
                cnt = 0
                for (s2, nj, hj0) in parts:
                    wv = wsl[s2][:].rearrange("p (j n) -> p j n", j=4)
                    for j in range(nj):
                        for ti, (c0, n) in enumerate(TILES):
                            mm(PS[ti][:, 0:n], wv[:, j, m * 128:(m + 1) * 128], mbuf[:, hj0 + j, c0:c0 + n],
                               cnt == 0, cnt == tot - 1, [wsl_b[s2], mbuf_b[hj0 + j][ti]], [PS_b[ti]])
                        cnt += 1
                for ti, (c0, n) in enumerate(TILES):
                    stt(xres[:, m, c0:c0 + n], PS[ti][:, 0:n], 0.5, xres[:, m, c0:c0 + n], ALU.mult, ALU.add,
                        [PS_b[ti], xres_b[m][ti]], [xres_b[m][ti]])

    def layernorm(i_ln, final):
        for ti, (c0, n) in enumerate(TILES):
            pm, pq = 5, 6
            for k in range(8):
                sq_, sqb = SBf[k % 4], SB_b[k % 4]
                act(sq_[:, 0:n], xres[:, k, c0:c0 + n], AF.Square, [xres_b[k][ti]], [sqb])
                mm(PS[pm][:, 0:n], ones_f[:], xres[:, k, c0:c0 + n], k == 0, k == 7, [cst, xres_b[k][ti]], [PS_b[pm]])
                mm(PS[pq][:, 0:n], ones_b[:], sq_[:, 0:n], k == 0, k == 7, [cst, sqb], [PS_b[pq]])
            mean, msq, var, Aa, Bb = SF[0], SF[1], SF[2], SF[3], SF[4]
            act(mean[:, 0:n], PS[pm][:, 0:n], AF.Copy, [PS_b[pm]], [SF_b[0]], scale=1.0 / D)
            act(msq[:, 0:n], PS[pm][:, 0:n], AF.Square, [PS_b[pm]], [SF_b[1]], scale=1.0 / D)
            stt(var[:, 0:n], PS[pq][:, 0:n], 1.0 / D, msq[:, 0:n], ALU.mult, ALU.subtract,
                [PS_b[pq], SF_b[1]], [SF_b[2]])
            act(var[:, 0:n], var[:, 0:n], AF.Ln, [SF_b[2]], [SF_b[2]], bias=LN_EPS)
            act(Aa[:, 0:n], var[:, 0:n], AF.Exp, [SF_b[2]], [SF_b[3]], scale=-0.5)
            stt(Bb[:, 0:n], mean[:, 0:n], -1.0, Aa[:, 0:n], ALU.mult, ALU.mult, [SF_b[0], SF_b[3]], [SF_b[4]])
            for k in range(8):
                z, zb = SF[k % 2], SF_b[k % 2]
                tt(z[:, 0:n], xres[:, k, c0:c0 + n], Aa[:, 0:n], ALU.mult, [xres_b[k][ti], SF_b[3]], [zb])
                tt(z[:, 0:n], z[:, 0:n], Bb[:, 0:n], ALU.add, [zb, SF_b[4]], [zb])
                gi, bi = 16 * i_ln + k, 16 * i_ln + 8 + k
                if final:
                    ts(xres[:, k, c0:c0 + n], z[:, 0:n], pv[:, gi:gi + 1], pv[:, bi:bi + 1], ALU.mult, ALU.add,
                       [zb, pvb], [xres_b[k][ti]])
                    dma("sp", yout[:, k, c0:c0 + n], xres[:, k, c0:c0 + n], [xres_b[k][ti]])
                else:
                    act(xbf[:, k, c0:c0 + n], z[:, 0:n], AF.Identity, [zb, pvb], [xbf_b[k][ti]],
                        bias=pv[:, bi:bi + 1], scale=pv[:, gi:gi + 1])
                    if k % 2 == 0:
                        ts(xres[:, k, c0:c0 + n], z[:, 0:n], pv2[:, gi:gi + 1], pv2[:, bi:bi + 1], ALU.mult, ALU.add,
                           [zb, pvb], [xres_b[k][ti]])
                    else:
                        act(xres[:, k, c0:c0 + n], z[:, 0:n], AF.Identity, [zb, pvb], [xres_b[k][ti]],
                            bias=pv2[:, bi:bi + 1], scale=pv2[:, gi:gi + 1])

    def softplus(dst, src, tmps, r, w):
        t0, t1, t2 = tmps
        act(t0, src, AF.Abs, r, w)
        act(t0, t0, AF.Exp, w, w, scale=-1.0)
        ts(t1, t0, 2.0, None, ALU.add, None, w, w)
        A("dve", lambda h: h.reciprocal(out=t1, in_=t1), w, w)
        tt(t1, t1, t0, ALU.mult, w, w)
        tt(t2, t1, t1, ALU.mult, w, w)
        ts(t0, t2, 1.0 / 11, 1.0 / 9, ALU.mult, ALU.add, w, w)
        for cf in (1.0 / 7, 1.0 / 5, 1.0 / 3, 1.0):
            tt(t0, t0, t2, ALU.mult, w, w)
            ts(t0, t0, cf, None, ALU.add, None, w, w)
        tt(t0, t0, t1, ALU.mult, w, w)
        ts(t1, src, 0.0, None, ALU.max, None, list(r) + list(w), w)
        stt(dst, t0, 2.0, t1, ALU.mult, ALU.add, w, w)

    def layer_params(l):
        dma("sp", pv[:], pvd[l], (), [pvb])
        dma("sp", prow[:], prd[l].partition_broadcast(128), (), [pvb])
        dma("pool", wab_b[:], wabd[l].rearrange("(k p) n -> p k n", p=128), (), [wabb])
        ts(pv2[:], pv[:, 0:48], ALPHA, None, ALU.mult, None, [pvb], [pvb])
        wl = [pvb] + SF_b[0:5]
        tn = [SF[i][:, 0:8] for i in range(5)]
        ts(tn[3], pv[:, 168:176], -1.0, None, ALU.mult, None, [pvb], wl)
        softplus(tn[4], tn[3], (tn[0], tn[1], tn[2]), wl, wl)
        ts(nsp[:, 0:8], tn[4], -8.0, None, ALU.mult, None, wl, wl)
        ts(nsp[:, 8:16], tn[4], -16.0, None, ALU.mult, None, wl, wl)
        act(aexp[:], prow[:, 16:24], AF.Exp, [pvb], [pvb])

    def gate_tables(l):
        pab = 7
        for nb in range(17):
            c0 = nb * 128
            nt = 128 if nb < 16 else NS
            ti = min(nb // 4, 4)
            for k in range(8):
                mm(PS[pab][0:nt, nb * 16:(nb + 1) * 16], xbf[:, k, c0:c0 + nt], wab_b[:, k, :], k == 0, k == 7,
                   [xbf_b[k][ti], wabb], [PS_b[pab]])
        ab_f = SF[4][:, 0:272].rearrange("p (b n) -> p b n", n=16)
        abv = ab_f
        psv = PS[pab][:, 0:272].rearrange("p (b n) -> p b n", n=16)
        brow = prow[:, 0:16].unsqueeze(1).to_broadcast([128, 17, 16])
        tt(abv, psv, brow, ALU.add, [PS_b[pab], pvb], [tbb, SF_b[4]])
        a_v, b_v = ab_f[:, :, 0:8], ab_f[:, :, 8:16]
        T = {k_: v_[:] for k_, v_ in TB.items()}
        for i_ in range(4):
            T["t%d" % i_] = SF[i_][:, 0:136].rearrange("p (b n) -> p b n", n=8)
        wl = [tbb] + SF_b[0:5]
        act(T["beta"], b_v, AF.Sigmoid, wl, [tbb])
        ts(T["nbeta"], T["beta"], -1.0, None, ALU.mult, None, [tbb], [tbb])
        dtb = prow[:, 24:32].unsqueeze(1).to_broadcast([128, 17, 8])
        tt(T["t3"], a_v, dtb, ALU.add, [tbb, pvb], wl)
        softplus(T["g"], T["t3"], (T["t0"], T["t1"], T["t2"]), wl, wl)
        aex = aexp[:].unsqueeze(1).to_broadcast([128, 17, 8])
        stt(T["g"], T["g"], -1.0, aex, ALU.mult, ALU.mult, [tbb, pvb], [tbb])
        gflat = TB["g"][:].rearrange("p b n -> p (b n)")
        gcflat = TB["gc"][:].rearrange("p b n -> p (b n)")
        mm(PS[pab][:, 0:128], maskU[:], gflat[:, 0:128], True, True, [cst, tbb], [PS_b[pab]])
        act(gcflat[:, 0:128], PS[pab][:, 0:128], AF.Copy, [PS_b[pab]], [tbb])
        act(gcflat[:, 128:136], gflat[:, 128:136], AF.Copy, [tbb], [tbb])
        mm(PS[pab][:, 128:256], sel_last[:], gcflat[:, 0:128], True, True, [cst, tbb], [PS_b[pab]])
        act(glb[:], PS[pab][:, 128:256], AF.Copy, [PS_b[pab]], [tbb])
        act(egl[:], glb[:], AF.Exp, [tbb], [tbb])
        tt(ekt[:], glb[:], gcflat[:, 0:128], ALU.subtract, [tbb], [tbb])
        act(ekt[:], ekt[:], AF.Exp, [tbb], [tbb])
        act(T["egc"], T["gc"], AF.Exp, [tbb], [tbb])
        tt(T["bge"], T["beta"], T["egc"], ALU.mult, [tbb], [tbb])
        for nm, src in (("eg", TB["egc"]), ("beta", TB["beta"])):
            sv = src[0:NS, 16, :].unsqueeze(2).to_broadcast([NS, 8, NS])
            iv = ident_f[0:NS, 0:NS].unsqueeze(1).to_broadcast([NS, 8, NS])
            sM = SF[5][0:NS, 0:128]
            tt(sM.rearrange("p (h s) -> p h s", h=8), sv, iv, ALU.mult, [tbb, cst], [srb, SF_b[5]])
            mm(PS[pab][:, 256:384], ones_f[0:NS, :], sM, True, True, [cst, srb, SF_b[5]], [PS_b[pab]])
            act(srow[nm][:], PS[pab][:, 256:384], AF.Copy, [PS_b[pab]], [srb])
        stt(srow["nbe"][:], srow["beta"][:], -1.0, srow["eg"][:], ALU.mult, ALU.mult, [srb], [srb])

    def rbank(X):
        for _ in range(8):
            i = X.banks[X.rot % len(X.banks)]
            X.rot += 1
            if i not in X.held:
                return i
        raise RuntimeError("no free psum bank")

    class Ctx:
        pass

    def mk_ctx(tid):
        X = Ctx()
        X.tid = tid
        X.banks = [0, 1, 2, 3] if tid == 0 else [4, 5, 6, 7]
        X.held = set()
        X.rot = 0
        X.slots = (0, 1) if tid == 0 else (2, 3)
        for nm in ("smb", "gwb", "dgwb", "sstb_b", "ssob", "Sfb", "Sbb", "ksqb", "srq", "wob"):
            setattr(X, nm, Buf())
        X.SF_b = [Buf() for _ in range(6)] if tid else SF_b
        X.SB_b = [Buf() for _ in range(13)] if tid else SB_b
        if tid == 0:
            X.SF = [t[:] for t in SF]
            X.SBf = [t[:] for t in SBf]
            X.hist, X.pco, X.dgw, X.gw_b = hist[:], pco[:], dgw[:], gw_b[:]
            X.sst, X.sstb, X.sso, X.sh0, X.sho = sst[:], sstb[:], sso[:], sh0[:], sho[:]
            X.S_f, X.S_b, X.skq, X.ksq_t, X.srow_qk = S_f[:], S_b[:], skq[:], ksq_t[:], srow["qk"][:]
            X.wo_c = wo_c0[:]
        else:
            mflat = mbuf[:].rearrange("p a b -> p (a b)")
            off = [0]

            def cv(n_, dt=BF16):
                ne = n_ if dt == BF16 else 2 * n_
                a = off[0]
                off[0] += ne + (ne % 2)
                v = mflat[:, a:a + ne]
                return v if dt == BF16 else v.bitcast(F32)
            X.SF = [cv(512, F32) for _ in range(6)]
            X.SBf = [cv(520 if i < 2 else 512) for i in range(13)]
            X.hist = cv(16).rearrange("p (a b) -> p a b", a=4)
            X.pco = cv(12, F32).rearrange("p (a b) -> p a b", a=4)
            X.dgw = cv(512).rearrange("p (a b) -> p a b", a=4)
            X.gw_b = cv(256).rearrange("p (a b) -> p a b", a=2)
            X.sst = cv(192, F32).rearrange("p (a b c) -> p a b c", a=4, b=3)
            X.sstb = cv(192).rearrange("p (a b c) -> p a b c", a=4, b=3)
            X.sso = cv(192, F32).rearrange("p (a b c) -> p a b c", a=4, b=3)
            X.sh0, X.sho = cv(NS, F32), cv(NS, F32)
            X.S_f, X.S_b = cv(128, F32), cv(128)
            X.skq = cv(2 * NS, F32).rearrange("p (a b) -> p a b", a=2)
            X.ksq_t, X.srow_qk = cv(2 * NS, F32), cv(NS, F32)
            X.wo_c = cv(1024)
            assert off[0] <= 8 * TOK, off[0]
        return X

    CTX = [mk_ctx(0), mk_ctx(1)]
    t1_bufs = CTX[1].SF_b + CTX[1].SB_b + [getattr(CTX[1], nm) for nm in
                                           ("smb", "gwb", "dgwb", "sstb_b", "ssob", "Sfb", "Sbb", "ksqb", "srq", "wob")]

    def chunk_body(l, c, X):
        T = TB
        sA, sB = X.slots
        wv_d = wind[l, c].rearrange("(k p) n -> p k n", p=128)
        for s, b0 in ((sA, 0), (sB, 4)):
            dst = wsl[s][:].rearrange("p (k n) -> p k n", k=8)
            dma("pool", dst, wv_d[:, :, b0 * 128:(b0 + 4) * 128], (), [wsl_b[s]])
        dma("pool", X.gw_b[:], gwd[l, :, c].rearrange("g p n -> p g n"), (), [X.gwb])
        dma("pool", X.wo_c[:], wod[l, c * 128:(c + 1) * 128, :], (), [X.wob])
        wA = wsl[sA][:].rearrange("p (k n) -> p k n", k=8)
        wB = wsl[sB][:].rearrange("p (k n) -> p k n", k=8)

        def wchunk(k, blk):
            if blk < 4:
                return wA[:, k, blk * 128:(blk + 1) * 128], wsl_b[sA]
            return wB[:, k, (blk - 4) * 128:(blk - 3) * 128], wsl_b[sB]

        cw_base = (112 + c, 176 + c, 176 + 8 + c, 176 + 16 + c)
        dma("sp", X.sst[:, 0], src0d[l, c], (), [X.sstb_b])
        for q_ in range(3):
            dma("sp", X.sst[:, 1 + q_], sgc0d[l, q_ * 8 + c], (), [X.sstb_b])
        dma("sp", X.sh0[:], sh0d[l, c], (), [X.sstb_b])
        act(X.sstb[:], X.sst[:], AF.Copy, [X.sstb_b], [X.sstb_b])
        mset(hcar[:, c:c + 1], 0.0, [X.smb])
        mset(X.hist[:], 0.0, [X.smb])
        mset(X.S_f[:], 0.0, [X.Sfb])
        mset(X.S_b[:], 0.0, [X.Sbb])

        for ti, (c0, n) in enumerate(TILES):
            samp = ti == 4

            if samp:
                sbl = []
                for tix in range(7, 13):
                    for half in (0, 1):
                        sbl.append((X.SBf[tix][:, half * 256:(half + 1) * 256].bitcast(F32), X.SB_b[tix]))
                for tix in range(0, 7):
                    sbl.append((X.SBf[tix][:, 256:512].bitcast(F32), X.SB_b[tix]))
                for q_ in range(1, 4):
                    sbl.append((X.SF[2][0:NS, q_ * 128:(q_ + 1) * 128], X.SF_b[2]))
                sbb_ = []
                for (_, par) in sbl:
                    cb = Buf()
                    cb.r = list(par.r) + ([par.w] if par.w is not None else [])
                    sbb_.append(cb)
                for s_ in range(NS):
                    dma("sp", sbl[s_][0], sS0d[l, s_, c], (), [sbb_[s_]])

            def proj(blk):
                pb = rbank(X)
                for k in range(8):
                    wv_, wb_ = wchunk(k, blk)
                    mm(PS[pb][:, 0:n], wv_, xbf[:, k, c0:c0 + n], k == 0, k == 7, [wb_, xbf_b[k][ti]], [PS_b[pb]])
                return pb

            def bias(blk):
                col = 48 + c * 8 + blk
                return pv[:, col:col + 1]

            def conv(blk, hb, prex_i):
                pb = proj(blk)
                prex, pxb = X.SBf[prex_i], X.SB_b[prex_i]
                for j in range(4):
                    col = cw_base[hb] + j * (8 if hb == 0 else 24)
                    ts(X.dgw[:, j, :], ident_f[:], pv[:, col:col + 1], None, ALU.mult, None, [cst, pvb], [X.dgwb])
                if not samp:
                    cp(prex[:, 0:3], X.hist[:, hb, 0:3], [X.smb], [pxb])
                    act(prex[:, 3:3 + n], PS[pb][:, 0:n], AF.Identity, [PS_b[pb], pvb], [pxb], bias=bias(blk))
                    if ti < 3:
                        cp(X.hist[:, hb, 0:3], prex[:, n:n + 3], [pxb], [X.smb])
                    else:
                        act(X.pco[:, hb, :], PS[pb][:, n - 3:n], AF.Identity, [PS_b[pb], pvb], [X.smb], bias=bias(blk))
                    pc = rbank(X)
                    for j in range(4):
                        mm(PS[pc][:, 0:n], X.dgw[:, j, :], prex[:, j:j + n], j == 0, j == 3, [X.dgwb, pxb], [PS_b[pc]])
                else:
                    act(prex[:, 0:n], PS[pb][:, 0:n], AF.Identity, [PS_b[pb], pvb], [pxb], bias=bias(blk))
                    act(X.sso[:, hb, 2, :], PS[pb][:, 0:n], AF.Identity, [PS_b[pb], pvb], [X.ssob], bias=bias(blk))
                    cp(X.sso[:, hb, 0:2, :], X.sst[:, hb, 1:3, :], [X.sstb_b], [X.ssob])
                    pc = rbank(X)
                    for j in range(3):
                        mm(PS[pc][:, 0:n], X.dgw[:, j, :], X.sstb[:, hb, j, :], j == 0, False, [X.dgwb, X.sstb_b], [PS_b[pc]])
                    mm(PS[pc][:, 0:n], X.dgw[:, 3, :], prex[:, 0:n], False, True, [X.dgwb, pxb], [PS_b[pc]])
                return pc

            pc = conv(0, 0, 0)
            xc, xcb = X.SF[0], X.SF_b[0]
            act(xc[:, 0:n], PS[pc][:, 0:n], AF.Identity, [PS_b[pc], pvb], [xcb], bias=pv[:, 144 + c:145 + c])
            xcbf, xcbfb = X.SBf[1], X.SB_b[1]
            act(xcbf[:, 0:n], PS[pc][:, 0:n], AF.Identity, [PS_b[pc], pvb], [xcbfb], bias=pv[:, 144 + c:145 + c])
            pr_, pi_ = rbank(X), rbank(X)
            mm(PS[pr_][:, 0:n], X.gw_b[:, 0, :], xcbf[:, 0:n], True, True, [X.gwb, xcbfb], [PS_b[pr_]])
            mm(PS[pi_][:, 0:n], X.gw_b[:, 1, :], xcbf[:, 0:n], True, True, [X.gwb, xcbfb], [PS_b[pi_]])
            rr, rrb = X.SF[1], X.SF_b[1]
            ig, igb = X.SF[2], X.SF_b[2]
            act(rr[:, 0:n], PS[pr_][:, 0:n], AF.Sigmoid, [PS_b[pr_], pvb], [rrb], bias=pv[:, 152 + c:153 + c])
            act(ig[:, 0:n], PS[pi_][:, 0:n], AF.Sigmoid, [PS_b[pi_], pvb], [igb], bias=pv[:, 160 + c:161 + c])
            aa, aab = X.SF[3], X.SF_b[3]
            act(aa[:, 0:n], rr[:, 0:n], AF.Exp, [rrb, pvb], [aab], scale=nsp[:, c:c + 1])
            act(rr[:, 0:n], rr[:, 0:n], AF.Exp, [rrb, pvb], [rrb], scale=nsp[:, 8 + c:9 + c])
            act(rr[:, 0:n], rr[:, 0:n], AF.Ln, [rrb], [rrb], bias=1.0, scale=-1.0)
            act(rr[:, 0:n], rr[:, 0:n], AF.Exp, [rrb], [rrb], scale=0.5)
            tt(ig[:, 0:n], ig[:, 0:n], xc[:, 0:n], ALU.mult, [igb, xcb], [igb])
            tt(ig[:, 0:n], ig[:, 0:n], rr[:, 0:n], ALU.mult, [igb, rrb], [igb])
            hs, hsb = X.SF[0], X.SF_b[0]
            if not samp:
                scan(hs[:, 0:n], aa[:, 0:n], ig[:, 0:n], hcar[:, c:c + 1], [aab, igb, X.smb], [hsb])
                cp(hcar[:, c:c + 1], hs[:, n - 1:n], [hsb], [X.smb])
                if ti == 3:
                    dma("sp", o_prc[l, c], X.pco[:, 0, :], [X.smb])
            else:
                tt(hs[:, 0:n], aa[:, 0:n], X.sh0[:], ALU.mult, [aab, X.sstb_b], [hsb])
                tt(hs[:, 0:n], hs[:, 0:n], ig[:, 0:n], ALU.add, [hsb, igb], [hsb])
                cp(X.sho[:], hs[:, 0:n], [hsb], [X.ssob])
                dma("sp", o_sh[l, c], X.sho[:], [X.ssob])
                dma("sp", o_src[l, c], X.sso[:, 0], [X.ssob])
            pb = proj(1)
            yy, yyb = X.SF[1], X.SF_b[1]
            act(yy[:, 0:n], PS[pb][:, 0:n], AF.Identity, [PS_b[pb], pvb], [yyb], bias=bias(1))
            y2, y2b = X.SF[2], X.SF_b[2]
            tt(y2[:, 0:n], yy[:, 0:n], yy[:, 0:n], ALU.mult, [yyb], [y2b])
            ts(y2[:, 0:n], y2[:, 0:n], 0.044715, 1.0, ALU.mult, ALU.add, [y2b], [y2b])
            tt(y2[:, 0:n], y2[:, 0:n], yy[:, 0:n], ALU.mult, [y2b, yyb], [y2b])
            act(y2[:, 0:n], y2[:, 0:n], AF.Sigmoid, [y2b], [y2b], scale=1.5957691216057308)
            tt(yy[:, 0:n], yy[:, 0:n], y2[:, 0:n], ALU.mult, [yyb, y2b], [yyb])
            tt(hs[:, 0:n], hs[:, 0:n], yy[:, 0:n], ALU.mult, [hsb, yyb], [hsb])
            pb = proj(2)
            act(y2[:, 0:n], PS[pb][:, 0:n], AF.Sigmoid, [PS_b[pb], pvb], [y2b], bias=bias(2))
            ma, mab = X.SF[5], X.SF_b[5]
            tt(ma[:, 0:n], hs[:, 0:n], y2[:, 0:n], ALU.mult, [hsb, y2b], [mab])

            h_ = c
            pc = conv(3, 1, 0)
            qf, qfb = X.SF[0], X.SF_b[0]
            act(qf[:, 0:n], PS[pc][:, 0:n], AF.Sigmoid, [PS_b[pc]], [qfb])
            tt(qf[:, 0:n], qf[:, 0:n], PS[pc][:, 0:n], ALU.mult, [qfb, PS_b[pc]], [qfb])
            pc = conv(4, 2, 1)
            kf, kfb = X.SF[1], X.SF_b[1]
            act(kf[:, 0:n], PS[pc][:, 0:n], AF.Sigmoid, [PS_b[pc]], [kfb])
            tt(kf[:, 0:n], kf[:, 0:n], PS[pc][:, 0:n], ALU.mult, [kfb, PS_b[pc]], [kfb])
            pc = conv(5, 3, 0)
            vT, vTb = X.SBf[2], X.SB_b[2]
            vsg, vsgb = X.SF[2], X.SF_b[2]
            act(vsg[:, 0:n], PS[pc][:, 0:n], AF.Sigmoid, [PS_b[pc]], [vsgb])
            tt(vT[:, 0:n], vsg[:, 0:n], PS[pc][:, 0:n], ALU.mult, [vsgb, PS_b[pc]], [vTb])
            if samp:
                for q_ in range(3):
                    dma("sp", o_sgc[l, q_ * 8 + c], X.sso[:, 1 + q_], [X.ssob])
            elif ti == 3:
                for q_ in range(3):
                    dma("sp", o_pgc[l, q_ * 8 + c], X.pco[:, 1 + q_, :], [X.smb])
            sqs, sqsb = X.SBf[3], X.SB_b[3]
            rs, rsb = X.SF[2], X.SF_b[2]
            qT, qTb = X.SBf[4], X.SB_b[4]
            kT, kTb = X.SBf[5], X.SB_b[5]
            for (src, srcb, dstT, dstTb, scl) in ((qf, qfb, qT, qTb, 128 ** -0.5), (kf, kfb, kT, kTb, 1.0)):
                act(sqs[:, 0:n], src[:, 0:n], AF.Square, [srcb], [sqsb])
                pq = rbank(X)
                mm(PS[pq][:, 0:n], ones_b[:], sqs[:, 0:n], True, True, [cst, sqsb], [PS_b[pq]])
                act(rs[:, 0:n], PS[pq][:, 0:n], AF.Ln, [PS_b[pq]], [rsb], bias=RMS_EPS)
                act(rs[:, 0:n], rs[:, 0:n], AF.Exp, [rsb], [rsb], scale=-0.5)
                if samp:
                    stt(src[:, 0:n], src[:, 0:n], scl, rs[:, 0:n], ALU.mult, ALU.mult, [srcb, rsb], [srcb])
                else:
                    stt(dstT[:, 0:n], src[:, 0:n], scl, rs[:, 0:n], ALU.mult, ALU.mult, [srcb, rsb], [dstTb])

            if samp:
                cols = slice(h_ * NS, (h_ + 1) * NS)
                cp(X.skq[:, 0, :], kf[:, 0:n], [kfb], [X.srq])
                cp(X.skq[:, 1, :], qf[:, 0:n], [qfb], [X.srq])
                qk_, qkb = X.SF[3], X.SF_b[3]
                tt(qk_[:, 0:n], qf[:, 0:n], kf[:, 0:n], ALU.mult, [qfb, kfb], [qkb])
                pq = rbank(X)
                mm(PS[pq][:, 0:n], ones_f[:], qk_[:, 0:n], True, True, [cst, qkb], [PS_b[pq]])
                act(X.srow_qk[:], PS[pq][:, 0:n], AF.Copy, [PS_b[pq]], [X.srq])
                pt = rbank(X)
                tr(PS[pt][0:NS, 0:128], kf[:, 0:n], ident_f[:], [kfb, cst], [PS_b[pt]])
                kpm, kpmb = X.SF[4], X.SF_b[4]
                act(kpm[0:NS, 0:128], PS[pt][0:NS, 0:128], AF.Copy, [PS_b[pt]], [kpmb])
                pks = X.banks[0]
                X.held = {pks}
                vfs, vfsb = X.SF[1], X.SF_b[1]
                act(vfs[:, 0:n], vT[:, 0:n], AF.Copy, [vTb], [vfsb])
                for s_ in range(NS):
                    mm(PS[pks][:, 2 * s_:2 * s_ + 2], sbl[s_][0], X.skq[:, :, s_], True, True,
                       [sbb_[s_], X.srq], [PS_b[pks]])
                act(X.ksq_t[:], PS[pks][:, 0:2 * NS], AF.Copy, [PS_b[pks]], [X.ksqb])
                ksv = X.ksq_t[:].rearrange("p (s t) -> p t s", t=2)
                X.held = set()
                vn, vnb = X.SF[3], X.SF_b[3]
                tt(vn[:, 0:n], vfs[:, 0:n], srow["beta"][:, cols], ALU.mult, [vfsb, srb], [vnb])
                tt(vfs[:, 0:n], ksv[:, 0, :], srow["nbe"][:, cols], ALU.mult, [X.ksqb, srb], [vfsb])
                tt(vn[:, 0:n], vn[:, 0:n], vfs[:, 0:n], ALU.add, [vnb, vfsb], [vnb])
                oo, oob = X.SF[0], X.SF_b[0]
                tt(oo[:, 0:n], ksv[:, 1, :], srow["eg"][:, cols], ALU.mult, [X.ksqb, srb], [oob])
                tt(vfs[:, 0:n], vn[:, 0:n], X.srow_qk[:], ALU.mult, [vnb, X.srq], [vfsb])
                tt(oo[:, 0:n], oo[:, 0:n], vfs[:, 0:n], ALU.add, [oob, vfsb], [oob])
                pt = rbank(X)
                tr(PS[pt][0:NS, 0:128], vn[:, 0:n], ident_f[:], [vnb, cst], [PS_b[pt]])
                vpm, vpmb = X.SF[1], X.SF_b[1]
                act(vpm[0:NS, 0:128], PS[pt][0:NS, 0:128], AF.Copy, [PS_b[pt]], [vpmb])
                for s_ in range(NS):
                    p_ = h_ * NS + s_
                    vm, vmb = sbl[19 + s_ % 3][0], sbb_[19 + s_ % 3]
                    ts(vm, vpm[0:NS, 0:128], ident_f[0:NS, s_:s_ + 1], None, ALU.mult, None, [vpmb, cst], [vmb])
                    po = rbank(X)
                    mm(PS[po][:, 0:128], kpm[0:NS, 0:128], vm, True, True, [kpmb, vmb], [PS_b[po]])
                    so_, sob_ = sbl[16 + s_ % 3][0], sbb_[16 + s_ % 3]
                    stt(so_, sbl[s_][0], srow["eg"][:, p_:p_ + 1], PS[po][:, 0:128], ALU.mult, ALU.add,
                        [sbb_[s_], srb, PS_b[po]], [sob_])
                    dma("sp", o_sS[l, s_, h_], so_, [sob_])
                for (_, par), cb in zip(sbl, sbb_):
                    par.r.extend(cb.r)
                    if cb.w is not None:
                        par.r.append(cb.w)
                po_hold = None
                of, ofb = oo, oob
            else:
                nb0 = ti * 4
                pg, pO, pA, pB = X.banks[0], X.banks[0], X.banks[1], X.banks[2]
                X.held = {pg}
                for nn in range(4):
                    dg_, dgb = X.SF[3], X.SF_b[3]
                    ts(dg_[:, 0:128], ident_f[:], T["gc"][:, nb0 + nn, h_:h_ + 1], None, ALU.mult, None,
                       [cst, tbb], [dgb])
                    mm(PS[pg][:, nn * 128:(nn + 1) * 128], ones_f[:], dg_[:, 0:128], True, True, [cst, dgb], [PS_b[pg]])
                egr, egrb = X.SF[3], X.SF_b[3]
                act(egr[:, 0:n], PS[pg][:, 0:n], AF.Exp, [PS_b[pg]], [egrb])
                qdT, qdTb = X.SBf[6], X.SB_b[6]
                tt(qdT[:, 0:n], qT[:, 0:n], egr[:, 0:n], ALU.mult, [qTb, egrb], [qdTb])
                pk_ = rbank(X)
                pkb = PS[pk_][:].bitcast(BF16)
                for nn in range(4):
                    tr(pkb[:, nn * 128:(nn + 1) * 128], kT[:, nn * 128:(nn + 1) * 128], ident_b[:], [kTb, cst], [PS_b[pk_]])
                kbg, kbgb = X.SBf[7], X.SB_b[7]
                ktm, ktmb = X.SBf[8], X.SB_b[8]
                pk3 = pkb[:, 0:512].rearrange("p (a d) -> p a d", a=4)
                bge_v = T["bge"][:, nb0:nb0 + 4, h_:h_ + 1].to_broadcast([128, 4, 128])
                ekt_v = ekt[:].rearrange("p (b e) -> p b e", e=8)[:, nb0:nb0 + 4, h_:h_ + 1].to_broadcast([128, 4, 128])
                tt(kbg[:, 0:512].rearrange("p (a d) -> p a d", a=4), pk3, bge_v, ALU.mult, [PS_b[pk_], tbb], [kbgb])
                tt(ktm[:, 0:512].rearrange("p (a d) -> p a d", a=4), pk3, ekt_v, ALU.mult, [PS_b[pk_], tbb], [ktmb])
                pv_ = rbank(X)
                pvb_ = PS[pv_][:].bitcast(BF16)
                for nn in range(4):
                    tr(pvb_[:, nn * 128:(nn + 1) * 128], vT[:, nn * 128:(nn + 1) * 128], ident_b[:], [vTb, cst], [PS_b[pv_]])
                vbt, vbtb = X.SBf[9], X.SB_b[9]
                be_v = T["beta"][:, nb0:nb0 + 4, h_:h_ + 1].to_broadcast([128, 4, 128])
                tt(vbt[:, 0:512].rearrange("p (a d) -> p a d", a=4), pvb_[:, 0:512].rearrange("p (a d) -> p a d", a=4),
                   be_v, ALU.mult, [PS_b[pv_], tbb], [vbtb])
                X.held = {pg, pA, pB}
                for nn in range(4):
                    sl = slice(nn * 128, (nn + 1) * 128)
                    mm(PS[pA][:, sl], kT[:, sl], kT[:, sl], True, True, [kTb], [PS_b[pA]])
                    mm(PS[pB][:, sl], kT[:, sl], qT[:, sl], True, True, [kTb, qTb], [PS_b[pB]])
                Dm, Dmb = X.SF[2], X.SF_b[2]
                DTm, DTmb = X.SF[3], X.SF_b[3]
                for nn in range(4):
                    sl = slice(nn * 128, (nn + 1) * 128)
                    gcc = T["gc"][:, nb0 + nn, h_:h_ + 1]
                    stt(Dm[:, sl], PS[pg][:, sl], gcc, maskSL[:], ALU.subtract, ALU.max, [PS_b[pg], tbb, cst], [Dmb])
                    stt(DTm[:, sl], PS[pg][:, sl], gcc, maskUn[:], ALU.subtract, ALU.min, [PS_b[pg], tbb, cst], [DTmb])
                act(Dm[:, 0:n], Dm[:, 0:n], AF.Exp, [Dmb], [Dmb], scale=-1.0)
                act(DTm[:, 0:n], DTm[:, 0:n], AF.Exp, [DTmb], [DTmb])
                m3 = lambda t_: t_[:, 0:512].rearrange("p (a d) -> p a d", a=4)
                X0, X0b = X.SBf[10], X.SB_b[10]
                for nn in range(4):
                    sl = slice(nn * 128, (nn + 1) * 128)
                    stt(X0[:, sl], PS[pA][:, sl], T["nbeta"][:, nb0 + nn, h_:h_ + 1], Dm[:, sl], ALU.mult, ALU.mult,
                        [PS_b[pA], tbb, Dmb], [X0b])
                AIT, AITb = X.SBf[4], X.SB_b[4]
                tt(AIT[:, 0:n], PS[pB][:, 0:n], DTm[:, 0:n], ALU.mult, [PS_b[pB], DTmb], [AITb])
                X.held = set()
                px = rbank(X)
                pxb_ = PS[px][:].bitcast(BF16)
                for nn in range(4):
                    tr(pxb_[:, nn * 128:(nn + 1) * 128], X0[:, nn * 128:(nn + 1) * 128], ident_b[:], [X0b, cst], [PS_b[px]])
                XT0, XT0b = X.SBf[11], X.SB_b[11]
                act(XT0[:, 0:n], pxb_[:, 0:n], AF.Copy, [PS_b[px]], [XT0b])
                RT, RTb = X.SBf[12], X.SB_b[12]
                Rm, Rmb = X.SBf[5], X.SB_b[5]
                U16 = mybir.dt.uint16
                bd3 = BD16[:].unsqueeze(1).to_broadcast([128, 4, 128])
                id3 = ident_b[:].unsqueeze(1).to_broadcast([128, 4, 128])
                Pt = [(X.SBf[0], X.SB_b[0]), (X.SBf[1], X.SB_b[1]), (X.SBf[2], X.SB_b[2]), (X.SBf[3], X.SB_b[3])]
                (Pc, Pcb), (PTc, PTcb) = Pt[0], Pt[1]
                tt(m3(Pc), m3(X0), bd3, ALU.mult, [X0b, cst], [Pcb])
                tt(m3(PTc), m3(XT0), bd3, ALU.mult, [XT0b, cst], [PTcb])
                tt(m3(Rm), m3(Pc), id3, ALU.add, [Pcb, cst], [Rmb])
                tt(m3(RT), m3(PTc), id3, ALU.add, [PTcb, cst], [RTb])

                def mm4(pbank, lhs, rhs, r_):
                    for nn in range(4):
                        sl = slice(nn * 128, (nn + 1) * 128)
                        mm(PS[pbank][:, sl], lhs[:, sl], rhs[:, sl], True, True, r_, [PS_b[pbank]])

                cur = 0
                for lvl in range(1, 4):
                    (Pn, Pnb), (PTn, PTnb) = Pt[2 - cur], Pt[3 - cur]
                    p1, p2 = rbank(X), rbank(X)
                    mm4(p1, PTc, Pc, [PTcb, Pcb])
                    mm4(p2, Pc, PTc, [PTcb, Pcb])
                    act(Pn[:, 0:n], PS[p1][:, 0:n], AF.Copy, [PS_b[p1]], [Pnb])
                    act(PTn[:, 0:n], PS[p2][:, 0:n], AF.Copy, [PS_b[p2]], [PTnb])
                    p3, p4 = rbank(X), rbank(X)
                    mm4(p3, PTn, Rm, [PTnb, Rmb])
                    mm4(p4, Pn, RT, [Pnb, RTb])
                    tt(Rm[:, 0:n], Rm[:, 0:n], PS[p3][:, 0:n], ALU.add, [Rmb, PS_b[p3]], [Rmb])
                    tt(RT[:, 0:n], RT[:, 0:n], PS[p4][:, 0:n], ALU.add, [RTb, PS_b[p4]], [RTb])
                    Pc, Pcb, PTc, PTcb = Pn, Pnb, PTn, PTnb
                    cur = 2 - cur
                (Wt, Wtb), (Ut, Utb) = Pt[0], Pt[1]
                for m_ in (32, 64, 128):
                    om = OFFM[m_][:].bitcast(U16).unsqueeze(1).to_broadcast([128, 4, 128])
                    pu = rbank(X)
                    mm4(pu, X0, RT, [X0b, RTb])
                    if m_ < 128:
                        pw_ = rbank(X)
                        mm4(pw_, XT0, Rm, [XT0b, Rmb])
                        act(Wt[:, 0:n], PS[pw_][:, 0:n], AF.Copy, [PS_b[pw_]], [Wtb])
                    act(Ut[:, 0:n], PS[pu][:, 0:n], AF.Copy, [PS_b[pu]], [Utb])
                    pv2_ = rbank(X)
                    mm4(pv2_, Rm, Ut, [Rmb, Utb])
                    if m_ < 128:
                        pz = rbank(X)
                        mm4(pz, RT, Wt, [RTb, Wtb])
                        A("dve", lambda h, om=om, pz=pz: h.copy_predicated(out=m3(Rm), mask=om, data=m3(PS[pz])),
                          [PS_b[pz], cst], [Rmb], dur=700.0)
                    A("dve", lambda h, om=om, pv2_=pv2_: h.copy_predicated(out=m3(RT), mask=om, data=m3(PS[pv2_])),
                      [PS_b[pv2_], cst], [RTb], dur=700.0)
                pw = rbank(X)
                for nn in range(4):
                    sl = slice(nn * 128, (nn + 1) * 128)
                    mm(PS[pw][:, sl], kbg[:, sl], RT[:, sl], True, True, [kbgb, RTb], [PS_b[pw]])
                wnT, wnTb = X.SBf[3], X.SB_b[3]
                act(wnT[:, 0:n], PS[pw][:, 0:n], AF.Copy, [PS_b[pw]], [wnTb], scale=-1.0)
                X.held = {pO}
                for nn in range(4):
                    sl = slice(nn * 128, (nn + 1) * 128)
                    pn_ = rbank(X)
                    mm(PS[pn_][:, 0:128], RT[:, sl], vbt[:, sl], True, False, [RTb, vbtb], [PS_b[pn_]])
                    mm(PS[pn_][:, 0:128], wnT[:, sl], X.S_b[:], False, True, [wnTb, X.Sbb], [PS_b[pn_]])
                    vnw, vnwb = X.SBf[2][:, (nn % 2) * 128:(nn % 2) * 128 + 128], X.SB_b[2]
                    act(vnw[:, 0:128], PS[pn_][:, 0:128], AF.Copy, [PS_b[pn_]], [vnwb])
                    mm(PS[pO][:, sl], X.S_b[:], qdT[:, sl], True, False, [X.Sbb, qdTb], [PS_b[pO]])
                    mm(PS[pO][:, sl], vnw[:, 0:128], AIT[:, sl], False, True, [vnwb, AITb], [PS_b[pO]])
                    ps_ = rbank(X)
                    mm(PS[ps_][:, 0:128], ktm[:, sl], vnw[:, 0:128], True, True, [ktmb, vnwb], [PS_b[ps_]])
                    eglc = egl[:, (nb0 + nn) * 8 + h_:(nb0 + nn) * 8 + h_ + 1]
                    stt(X.S_f[:], X.S_f[:], eglc, PS[ps_][:, 0:128], ALU.mult, ALU.add, [X.Sfb, tbb, PS_b[ps_]], [X.Sfb])
                    act(X.S_b[:], X.S_f[:], AF.Copy, [X.Sfb], [X.Sbb])
                if ti == 3:
                    dma("sp", o_pS[l, h_], X.S_f[:], [X.Sfb])
                of, ofb = X.SF[0], X.SF_b[0]
                act(of[:, 0:n], PS[pO][:, 0:n], AF.Copy, [PS_b[pO]], [ofb])
                X.held = set()

            sq2, sq2b = X.SBf[3], X.SB_b[3]
            act(sq2[:, 0:n], of[:, 0:n], AF.Square, [ofb], [sq2b])
            pq = rbank(X)
            mm(PS[pq][:, 0:n], ones_b[:], sq2[:, 0:n], True, True, [cst, sq2b], [PS_b[pq]])
            rs2, rs2b = X.SF[1], X.SF_b[1]
            act(rs2[:, 0:n], PS[pq][:, 0:n], AF.Ln, [PS_b[pq]], [rs2b], bias=RMS_EPS, scale=1.0 / 128)
            act(rs2[:, 0:n], rs2[:, 0:n], AF.Exp, [rs2b], [rs2b], scale=-0.5)
            stt(of[:, 0:n], of[:, 0:n], pv[:, 272:273], rs2[:, 0:n], ALU.mult, ALU.mult, [ofb, pvb, rs2b], [ofb])
            pb = proj(6)
            zs, zsb = X.SF[2], X.SF_b[2]
            act(zs[:, 0:n], PS[pb][:, 0:n], AF.Sigmoid, [PS_b[pb], pvb], [zsb], bias=bias(6))
            stt(zs[:, 0:n], PS[pb][:, 0:n], bias(6), zs[:, 0:n], ALU.add, ALU.mult, [PS_b[pb], pvb, zsb], [zsb])
            tt(of[:, 0:n], of[:, 0:n], zs[:, 0:n], ALU.mult, [ofb, zsb], [ofb])
            pb = proj(7)
            act(zs[:, 0:n], PS[pb][:, 0:n], AF.Sigmoid, [PS_b[pb], pvb], [zsb], bias=bias(7))
            tt(of[:, 0:n], of[:, 0:n], zs[:, 0:n], ALU.mult, [ofb, zsb], [ofb])
            mt, mtb = X.SBf[6], X.SB_b[6]
            tt(mt[:, 0:n], of[:, 0:n], ma[:, 0:n], ALU.add, [ofb, mab], [mtb])
            if X.tid == 1:
                XA = CTX[0]
                for m in range(8):
                    pb = rbank(X)
                    mm(PS[pb][:, 0:n], XA.wo_c[:, m * 128:(m + 1) * 128], XA.SBf[6][:, 0:n], True, False,
                       [XA.wob, XA.SB_b[6]], [PS_b[pb]])
                    mm(PS[pb][:, 0:n], X.wo_c[:, m * 128:(m + 1) * 128], mt[:, 0:n], False, True, [X.wob, mtb], [PS_b[pb]])
                    tt(xres[:, m, c0:c0 + n], PS[pb][:, 0:n], xres[:, m, c0:c0 + n], ALU.add,
                       [PS_b[pb], xres_b[m][ti]], [xres_b[m][ti]])
            yield


    def mixer(l):
        gate_tables(l)
        nops0 = len(P.all)
        prior = []
        for row in mbuf_b:
            for b in row:
                prior.extend(b.r)
                if b.w is not None:
                    prior.append(b.w)
        for b in t1_bufs:
            b.w = None
            b.r = list(prior)
        for pair in range(4):
            gens = [chunk_body(l, 2 * pair, CTX[0]), chunk_body(l, 2 * pair + 1, CTX[1])]
            live = [True, True]
            while any(live):
                for t_ in range(2):
                    if live[t_]:
                        try:
                            next(gens[t_])
                        except StopIteration:
                            live[t_] = False
        dma("sp", o_ph[l], hcar[:], [CTX[0].smb, CTX[1].smb])
        if l == 0:
            print("mixer ops per (chunk,tile):", (len(P.all) - nops0) / 40.0)
        post = []
        for b in t1_bufs:
            post.extend(b.r)
            if b.w is not None:
                post.append(b.w)
        for row in mbuf_b:
            for b in row:
                b.r = list(b.r) + post

    for l in range(n_layers):
        layer_params(l)
        ffn(2 * l)
        layernorm(0, False)
        mixer(l)
        layernorm(1, False)
        ffn(2 * l + 1)
        layernorm(2, l == n_layers - 1)
    P.emit(scheduled=SCHED)
    print("ops:", len(P.all), "est total us:", getattr(P, "est_total", 0) / 1e3)
    st.close()
    return nc


def _prep_shared(inp):
    f = lambda a: np.ascontiguousarray(np.asarray(a, dtype=np.float32))
    w1 = f(np.stack([inp["ffn1_w1"], inp["ffn2_w1"]], 1).reshape(2 * L, D, DFF))
    w3 = f(np.stack([inp["ffn1_w3"], inp["ffn2_w3"]], 1).reshape(2 * L, D, DFF))
    w2 = f(np.stack([inp["ffn1_w2"], inp["ffn2_w2"]], 1).reshape(2 * L, DFF, D))
    w_in = np.asarray(inp["w_in"], np.float32)
    b_in = np.asarray(inp["b_in"], np.float32)
    offs = [0, 1024, 6160, 2048, 3072, 4096, 5120, 7184]
    win = np.empty((L, 8, D, 1024), np.float32)
    for c in range(8):
        for bi, o in enumerate(offs):
            win[:, c, :, bi * 128:(bi + 1) * 128] = w_in[:, :, o + c * 128:o + (c + 1) * 128]
    wab = f(w_in[:, :, 6144:6160])
    gatew = np.zeros((L, 2, 8, 128, 128), np.float32)
    for gi, nm in enumerate(("rg_wr", "rg_wi")):
        w = np.asarray(inp[nm], np.float32)
        for c in range(8):
            gatew[:, gi, c, 0:64, 0:64] = w[:, 2 * c]
            gatew[:, gi, c, 64:128, 64:128] = w[:, 2 * c + 1]
    pvec = np.empty((L, 128, NPV), np.float32)
    fm = lambda v: np.asarray(v, np.float32).reshape(L, -1, 128).transpose(0, 2, 1)
    for i, (g, b) in enumerate((("ln1_g", "ln1_b"), ("ln2_g", "ln2_b"), ("ln3_g", "ln3_b"))):
        pvec[:, :, 16 * i:16 * i + 8] = fm(inp[g])
        pvec[:, :, 16 * i + 8:16 * i + 16] = fm(inp[b])
    for c in range(8):
        for bi, o in enumerate(offs):
            pvec[:, :, 48 + c * 8 + bi] = b_in[:, o + c * 128:o + (c + 1) * 128]
    rcw = np.asarray(inp["rg_conv_w"], np.float32)
    for j in range(4):
        pvec[:, :, 112 + j * 8:112 + (j + 1) * 8] = fm(rcw[:, j])
    pvec[:, :, 144:152] = fm(inp["rg_conv_b"])
    pvec[:, :, 152:160] = fm(inp["rg_br"])
    pvec[:, :, 160:168] = fm(inp["rg_bi"])
    pvec[:, :, 168:176] = fm(inp["rg_lambda"])
    gcw = np.asarray(inp["gdn_conv_w"], np.float32)
    for j in range(4):
        pvec[:, :, 176 + j * 24:176 + (j + 1) * 24] = fm(gcw[:, j])
    pvec[:, :, 272] = np.asarray(inp["gdn_norm_w"], np.float32)
    prow = np.concatenate([b_in[:, 6144:6160], np.asarray(inp["gdn_a_log"], np.float32),
                           np.asarray(inp["gdn_dt_bias"], np.float32)], axis=1).reshape(L, 1, 32)
    return {"w1": w1, "w3": w3, "w2": w2, "win": win, "wab": wab, "wo": f(inp["w_o"]), "gatew": gatew,
            "pvec": f(pvec), "prow": f(prow)}


def _prep_core(inp, i):
    xp = np.asarray(inp["x_prompt"], np.float32)[i]
    xs = np.asarray(inp["x_sample"], np.float32)[i * NS:(i + 1) * NS, 0]
    x = np.concatenate([xp, xs], 0)
    xin = np.ascontiguousarray(x.T.reshape(8, 128, TOK).transpose(1, 0, 2))
    sl = slice(i * NS, (i + 1) * NS)
    h0 = np.asarray(inp["state_rglru_h"], np.float32)[:, sl]
    sh0 = np.ascontiguousarray(h0.transpose(0, 2, 1).reshape(L, 8, 128, NS))
    rc = np.asarray(inp["state_rglru_conv"], np.float32)[:, sl]
    src0 = np.ascontiguousarray(rc.transpose(0, 3, 2, 1).reshape(L, 8, 128, 3, NS))
    gc = np.asarray(inp["state_gdn_conv"], np.float32)[:, sl]
    sgc0 = np.ascontiguousarray(gc.transpose(0, 3, 2, 1).reshape(L, 24, 128, 3, NS))
    sS0 = np.ascontiguousarray(np.asarray(inp["state_gdn_S"], np.float32)[:, sl])
    return {"xin": xin, "sh0": sh0, "src0": src0, "sgc0": sgc0, "sS0": sS0}


_NC_CACHE = {}


def kernel(**inp):
    if "nc" not in _NC_CACHE:
        _NC_CACHE["nc"] = build_program()
    nc = _NC_CACHE["nc"]
    shared = _prep_shared(inp)
    in_maps = []
    for i in range(8):
        m = dict(shared)
        m.update(_prep_core(inp, i))
        in_maps.append(m)
    res = run_bass_kernel_spmd(nc, in_maps, core_ids=list(range(8)))
    R = res.results
    Bp = 8
    y_prompt = np.empty((Bp, NP, D), np.float32)
    y_sample = np.empty((8 * NS, 1, D), np.float32)
    p_h = np.empty((L, Bp, D), np.float32)
    p_rc = np.empty((L, Bp, 3, D), np.float32)
    p_S = np.empty((L, Bp, 8, 128, 128), np.float32)
    p_gc = np.empty((L, Bp, 3, 3072), np.float32)
    s_h = np.empty((L, 8 * NS, D), np.float32)
    s_rc = np.empty((L, 8 * NS, 3, D), np.float32)
    s_S = np.empty((L, 8 * NS, 8, 128, 128), np.float32)
    s_gc = np.empty((L, 8 * NS, 3, 3072), np.float32)
    for i in range(8):
        r = R[i]
        y = np.asarray(r["yout"]).transpose(1, 0, 2).reshape(D, TOK).T
        y_prompt[i] = y[:NP]
        sl = slice(i * NS, (i + 1) * NS)
        y_sample[sl, 0] = y[NP:]
        p_h[:, i] = np.asarray(r["o_ph"]).transpose(0, 2, 1).reshape(L, D)
        p_rc[:, i] = np.asarray(r["o_prc"]).transpose(0, 3, 1, 2).reshape(L, 3, D)
        p_S[:, i] = np.asarray(r["o_pS"])
        p_gc[:, i] = np.asarray(r["o_pgc"]).transpose(0, 3, 1, 2).reshape(L, 3, 3072)
        s_h[:, sl] = np.asarray(r["o_sh"]).transpose(0, 3, 1, 2).reshape(L, NS, D)
        s_rc[:, sl] = np.asarray(r["o_src"]).transpose(0, 4, 3, 1, 2).reshape(L, NS, 3, D)
        s_S[:, sl] = np.asarray(r["o_sS"])
        s_gc[:, sl] = np.asarray(r["o_sgc"]).transpose(0, 4, 3, 1, 2).reshape(L, NS, 3, 3072)
    return (y_prompt, y_sample, p_h, p_rc, p_S, p_gc, s_h, s_rc, s_S, s_gc)
```

```python
import math
from contextlib import ExitStack
import numpy as np
import concourse.bass as bass
import concourse.mybir as mybir
from concourse.bass_utils import run_bass_kernel_spmd

F32 = mybir.dt.float32
BF16 = mybir.dt.bfloat16
AF = mybir.ActivationFunctionType
ALU = mybir.AluOpType
AX = mybir.AxisListType

L = 4
D = 1024
NP = 2048
NS = 16
TOK = NP + NS
DFF = 2816
NJ = DFF // 128
ALPHA = (2 * L) ** 0.25
LN_EPS = 1e-5
RMS_EPS = 1e-6
TILES = [(0, 512), (512, 512), (1024, 512), (1536, 512), (2048, 16)]
NPV = 273
ENGS = ("pe", "act", "dve", "pool", "sp")


class Buf:
    __slots__ = ("w", "r", "excl")

    def __init__(self, excl=False):
        self.w = None
        self.r = []
        self.excl = excl


class Prog:
    def __init__(self, nc):
        self.nc = nc
        self.all = []
        self.n_ch = {"sp": 28, "pool": 12}

    def _collect(self, eng, reads, writes, is_dma):
        deps = {}
        for b in reads:
            ev = b.w
            if ev is None:
                continue
            ee = self.all[ev]
            need = not (eng == "pe" and ee[0] == "pe" and ee[1] == "c")
            deps[ev] = deps.get(ev, False) or need
        for b in writes:
            evs = list(b.r)
            if b.w is not None:
                evs.append(b.w)
            for ev in evs:
                ee = self.all[ev]
                same = (ee[0] == eng and ee[1] == "c" and not is_dma)
                deps[ev] = deps.get(ev, False) or (not same)
        return list(deps.items())

    def _add(self, eng, kind, fn, reads, writes, dur, fam):
        ex = [b for b in reads if b.excl and b not in writes]
        if ex:
            writes = list(writes) + ex
        deps = self._collect(eng, reads, writes, kind == "d")
        idx = len(self.all)
        self.all.append([eng, kind, fn, deps, dur, fam])
        for b in reads:
            b.r.append(idx)
        for b in writes:
            b.w = idx
            b.r = []
        return idx

    def op(self, eng, fn, reads=(), writes=(), dur=300.0, fam=None):
        return self._add(eng, "c", fn, reads, writes, dur, fam)

    def dma(self, eng, fn, reads=(), writes=(), dur=3000.0):
        return self._add(eng, "d", fn, reads, writes, dur, None)

    def schedule(self):
        import heapq
        ops = self.all
        n = len(ops)
        succ = [[] for _ in range(n)]
        npred = [0] * n
        last_q = {}
        for i, o in enumerate(ops):
            if o[0] in QUEUE_INORDER:
                p_ = last_q.get(o[0])
                if p_ is not None and all(d != p_ for d, _ in o[3]):
                    o[3].append((p_, False) if ops[p_][1] == "c" and o[1] == "c" else (p_, None))
                last_q[o[0]] = i
        for i, o in enumerate(ops):
            npred[i] = len(o[3])
            for d, _ in o[3]:
                succ[d].append(i)
        issue_only = {}
        for j_, o in enumerate(ops):
            for d, need in o[3]:
                if need is None:
                    issue_only[(d, j_)] = True
        ready_t = [0.0] * n
        fin = [0.0] * n
        eng_free = {e: 0.0 for e in ENGS}
        eng_fam = {e: None for e in ENGS}
        order = {e: [] for e in ENGS}
        future = {e: [] for e in ENGS}
        avail = {e: [] for e in ENGS}
        for i in range(n):
            if npred[i] == 0:
                heapq.heappush(future[ops[i][0]], (0.0, i))
        done = 0
        XLAT = 250.0
        while done < n:
            best = None
            for e in ENGS:
                t = eng_free[e]
                fut, av = future[e], avail[e]
                while fut and fut[0][0] <= t:
                    heapq.heappush(av, heapq.heappop(fut)[1])
                if av:
                    c_ = (t, av[0], e, True)
                elif fut:
                    c_ = (fut[0][0], fut[0][1], e, False)
                else:
                    continue
                if best is None or (c_[0], c_[1]) < (best[0], best[1]):
                    best = c_
            st, i, e, from_av = best
            if from_av:
                av = avail[e]
                if e == "act" and eng_fam[e] is not None and len(av) > 1 and ops[i][5] not in (None, eng_fam[e]):
                    cands = heapq.nsmallest(12, av)
                    alt = [c for c in cands if ops[c][5] in (None, eng_fam[e]) and c - i < 400]
                    if alt:
                        i = alt[0]
                        av.remove(i)
                        heapq.heapify(av)
                    else:
                        heapq.heappop(av)
                else:
                    heapq.heappop(av)
            else:
                heapq.heappop(future[e])
            o = ops[i]
            dur = o[4]
            if o[1] == "c":
                if o[5] is not None and eng_fam[e] is not None and eng_fam[e] != o[5]:
                    st += 1300.0
                if o[5] is not None:
                    eng_fam[e] = o[5]
                eng_free[e] = st + dur
                fin[i] = st + dur
            else:
                eng_free[e] = st + 120.0
                fin[i] = st + dur
            order[e].append(i)
            done += 1
            for j in succ[i]:
                oj = ops[j]
                lat = XLAT if (oj[0] != e or o[1] == "d") else 60.0
                t = fin[i] + lat
                if issue_only.get((i, j)):
                    t = st
                if t > ready_t[j]:
                    ready_t[j] = t
                npred[j] -= 1
                if npred[j] == 0:
                    heapq.heappush(future[oj[0]], (ready_t[j], j))
        self.order = order
        self.est_total = max(eng_free.values())

    def emit(self, scheduled=True):
        nc = self.nc
        ops = self.all
        if scheduled:
            self.schedule()
            order = self.order
        else:
            order = {e: [i for i, o in enumerate(ops) if o[0] == e] for e in ENGS}
        pos = {}
        for e in ENGS:
            for p_, i in enumerate(order[e]):
                pos[i] = p_
        ch_of, ch_val = {}, {}
        ch_next = {e: 0 for e in self.n_ch}
        ch_count = {}
        ch_prev = {}
        for e in self.n_ch:
            for i in order[e]:
                if ops[i][1] != "d":
                    continue
                c = ch_next[e]
                ch_next[e] = (c + 1) % self.n_ch[e]
                prev = ch_count.get((e, c), 0)
                ch_prev[i] = (c, prev)
                ch_count[(e, c)] = prev + 16
                ch_of[i] = c
                ch_val[i] = prev + 16
        marked = {e: set() for e in ENGS}
        waits_of = {}
        for e in ENGS:
            waited = {}
            for i in order[e]:
                o = ops[i]
                wl = []
                for d, need in o[3]:
                    od = ops[d]
                    if need is None:
                        assert od[0] == e and pos[d] < pos[i]
                        continue
                    if od[1] == "d":
                        key, val = ("d", od[0], ch_of[d]), ch_val[d]
                    else:
                        if not need:
                            assert od[0] == e and pos[d] < pos[i]
                            continue
                        key, val = ("e", od[0]), pos[d]
                    if val <= waited.get(key, -1):
                        continue
                    waited[key] = val
                    wl.append((key, val, d))
                    if key[0] == "e":
                        marked[od[0]].add(d)
                if o[1] == "d":
                    c, prev = ch_prev[i]
                    key = ("d", e, c)
                    if prev and prev > waited.get(key, -1):
                        waited[key] = prev
                        wl.append((key, prev, None))
                waits_of[i] = wl
        cnt = {}
        for e in ENGS:
            m = sorted(marked[e], key=lambda i: pos[i])
            cnt[e] = {i: k + 1 for k, i in enumerate(m)}
        with ExitStack() as st:
            esem = {e: st.enter_context(nc.semaphore("s_" + e)) for e in ENGS}
            dsem = {}
            for e, n_ in self.n_ch.items():
                for c in range(n_):
                    dsem[(e, c)] = st.enter_context(nc.semaphore("d_%s%d" % (e, c)))
            block = st.enter_context(nc.Block())
            n_ch = self.n_ch

            def run(e, h):
                for i in order[e]:
                    o = ops[i]
                    for key, val, d in waits_of[i]:
                        if key[0] == "e":
                            h.wait_ge(esem[key[1]], cnt[key[1]][d])
                        else:
                            h.wait_ge(dsem[(key[1], key[2])], val)
                    ins = o[2](h)
                    if o[1] == "d":
                        ins.then_inc(dsem[(e, ch_of[i])], 16)
                    elif i in cnt[e]:
                        ins.then_inc(esem[e], 1)
                if e in n_ch:
                    for c in range(n_ch[e]):
                        v = ch_count.get((e, c), 0)
                        if v:
                            h.wait_ge(dsem[(e, c)], v)

            @block.tensor
            def _(h):
                run("pe", h)

            @block.scalar
            def _(h):
                run("act", h)

            @block.vector
            def _(h):
                run("dve", h)

            @block.gpsimd
            def _(h):
                run("pool", h)

            @block.sync
            def _(h):
                run("sp", h)


SCHED = True
QUEUE_INORDER = ()


def build_program(n_layers=L, debug=None):
    nc = bass.Bass("TRN2", target_bir_lowering=False)
    st = ExitStack()

    def din(name, shape):
        return nc.dram_tensor(name, list(shape), F32, kind="ExternalInput").ap()

    def dout(name, shape):
        return nc.dram_tensor(name, list(shape), F32, kind="ExternalOutput").ap()

    xin = din("xin", [128, 8, TOK])
    w1d = din("w1", [2 * L, D, DFF])
    w3d = din("w3", [2 * L, D, DFF])
    w2d = din("w2", [2 * L, DFF, D])
    wind = din("win", [L, 8, D, 1024])
    wabd = din("wab", [L, D, 16])
    wod = din("wo", [L, D, D])
    gwd = din("gatew", [L, 2, 8, 128, 128])
    pvd = din("pvec", [L, 128, NPV])
    prd = din("prow", [L, 1, 32])
    sh0d = din("sh0", [L, 8, 128, NS])
    src0d = din("src0", [L, 8, 128, 3, NS])
    sgc0d = din("sgc0", [L, 24, 128, 3, NS])
    sS0d = din("sS0", [L, NS, 8, 128, 128])
    yout = dout("yout", [128, 8, TOK])
    o_ph = dout("o_ph", [L, 128, 8])
    o_prc = dout("o_prc", [L, 8, 128, 3])
    o_pS = dout("o_pS", [L, 8, 128, 128])
    o_pgc = dout("o_pgc", [L, 24, 128, 3])
    o_sh = dout("o_sh", [L, 8, 128, NS])
    o_src = dout("o_src", [L, 8, 128, 3, NS])
    o_sS = dout("o_sS", [L, NS, 8, 128, 128])
    o_sgc = dout("o_sgc", [L, 24, 128, 3, NS])
    dbg = {}
    if debug:
        for nm, shp in debug.items():
            dbg[nm] = dout("dbg_" + nm, shp)

    def sb(name, shape, dt):
        return st.enter_context(nc.sbuf_tensor(name, list(shape), dt))

    xres = sb("xres", [128, 8, TOK], F32)
    xbf = sb("xbf", [128, 8, TOK], BF16)
    mbuf = sb("mbuf", [128, 8, TOK], BF16)
    wsl = [sb("wsl%d" % i, [128, 4096], BF16) for i in range(4)]
    NSF, NSB = 6, 13
    SF = [sb("sf%d" % i, [128, 512], F32) for i in range(NSF)]
    SBf = [sb("sbf%d" % i, [128, 520 if i < 2 else 512], BF16) for i in range(NSB)]
    wo_c0 = sb("wo_c0", [128, 1024], BF16)
    ident_f = sb("ident_f", [128, 128], F32)
    ident_b = sb("ident_b", [128, 128], BF16)
    ones_f = sb("ones_f", [128, 128], F32)
    ones_b = sb("ones_b", [128, 128], BF16)
    maskU = sb("maskU", [128, 128], F32)
    maskSL = sb("maskSL", [128, 128], F32)
    maskUn = sb("maskUn", [128, 128], F32)
    sel_last = sb("sel_last", [128, 128], F32)
    BD16 = sb("BD16", [128, 128], BF16)
    OFFM = {m_: sb("OFF%d" % m_, [128, 128], BF16) for m_ in (32, 64, 128)}
    pv = sb("pv_t", [128, NPV], F32)
    pv2 = sb("pv2", [128, 48], F32)
    nsp = sb("nsp", [128, 16], F32)
    prow = sb("prow_t", [128, 32], F32)
    aexp = sb("aexp", [128, 8], F32)
    wab_b = sb("wab_b", [128, 8, 16], BF16)
    gw_b = sb("gw_b", [128, 2, 128], BF16)
    TB = {nm: sb("tb_" + nm, [128, 17, 8], F32) for nm in
          ("beta", "g", "gc", "egc", "bge", "nbeta")}
    ksq_t = sb("ksq_t", [128, 2 * NS], F32)
    glb = sb("glb", [128, 128], F32)
    egl = sb("egl", [128, 128], F32)
    ekt = sb("ekt", [128, 128], F32)
    S_f = sb("S_f", [128, 128], F32)
    S_b = sb("S_b", [128, 128], BF16)
    hcar = sb("hcar", [128, 8], F32)
    hist = sb("hist", [128, 4, 4], BF16)
    pco = sb("pco", [128, 4, 3], F32)
    dgw = sb("dgw", [128, 4, 128], BF16)
    sst = sb("sst", [128, 4, 3, NS], F32)
    sstb = sb("sstb", [128, 4, 3, NS], BF16)
    sso = sb("sso", [128, 4, 3, NS], F32)
    sh0 = sb("sh0t", [128, NS], F32)
    sho = sb("shot", [128, NS], F32)
    srow = {nm: sb("srow_" + nm, [128, 128], F32) for nm in ("eg", "beta", "nbe")}
    srow["qk"] = sb("srow_qk", [128, NS], F32)
    skq = sb("skq", [128, 2, NS], F32)
    PS = [st.enter_context(nc.psum_tensor("ps%d" % i, [128, 512], F32)) for i in range(8)]
    print("sbuf bytes remaining/partition:", nc.sbuf_bytes_remaining)

    P = Prog(nc)
    B_ = Buf
    xres_b = [[B_() for _ in TILES] for _ in range(8)]
    xbf_b = [[B_() for _ in TILES] for _ in range(8)]
    mbuf_b = [[B_() for _ in TILES] for _ in range(8)]
    wsl_b = [B_() for _ in range(4)]
    SF_b = [B_() for _ in range(NSF)]
    SB_b = [B_() for _ in range(NSB)]
    PS_b = [B_(True) for _ in range(8)]
    cst = B_()
    pvb = B_()
    tbb = B_()
    gwb = B_()
    wabb = B_()
    dgwb = B_()
    ksqb = B_()
    smb = B_()
    Sfb, Sbb = B_(), B_()
    sstb_b, ssob = B_(), B_()
    Sin_b = [B_() for _ in range(2)]
    Sout_b = [B_() for _ in range(2)]
    srb = B_()

    def fsz(ap):
        sh = ap.shape
        n_ = 1
        for d_ in sh[1:]:
            n_ *= d_
        return n_

    ACT_FAM = {AF.Exp: "el", AF.Ln: "el", AF.Sigmoid: "sg", AF.Silu: "si", AF.Sqrt: "sq"}

    def A(e, fn, r=(), w=(), dur=None, fam=None):
        if dur is None:
            dur = {"pe": 150.0, "act": 400.0, "dve": 400.0, "pool": 600.0}.get(e, 300.0)
        P.op(e, fn, reads=r, writes=w, dur=dur, fam=fam)

    def act(out, in_, func, r, w, bias=None, scale=None):
        kw = {}
        if bias is not None:
            kw["bias"] = bias
        if scale is not None:
            kw["scale"] = scale
        A("act", lambda h: h.activation(out=out, in_=in_, func=func, **kw), r, w,
          dur=220.0 + 0.8 * fsz(out), fam=ACT_FAM.get(func))

    def mm(out, lhsT, rhs, start, stop, r, w):
        n_ = fsz(rhs)
        f32 = 4.0 if rhs.dtype == F32 else 1.0
        A("pe", lambda h: h.matmul(out, lhsT=lhsT, rhs=rhs, start=start, stop=stop), r, w,
          dur=(45.0 + max(64, n_) * 0.5 * f32) if n_ > 128 else (100.0 * f32))

    def tt(out, in0, in1, op, r, w, eng="dve"):
        A(eng, lambda h: h.tensor_tensor(out=out, in0=in0, in1=in1, op=op), r, w,
          dur=(110.0 + 1.1 * fsz(out)) if eng == "dve" else (300.0 + 1.7 * fsz(out)))

    def ts(out, in0, s1, s2, op0, op1, r, w, eng="dve"):
        if s2 is None:
            A(eng, lambda h: h.tensor_scalar(out=out, in0=in0, scalar1=s1, scalar2=None, op0=op0), r, w,
              dur=110.0 + 1.0 * fsz(out))
        else:
            A(eng, lambda h: h.tensor_scalar(out=out, in0=in0, scalar1=s1, scalar2=s2, op0=op0, op1=op1), r, w,
              dur=110.0 + 1.0 * fsz(out))

    def stt(out, in0, scalar, in1, op0, op1, r, w):
        A("dve", lambda h: h.scalar_tensor_tensor(out=out, in0=in0, scalar=scalar, in1=in1, op0=op0, op1=op1), r, w,
          dur=110.0 + 1.1 * fsz(out))

    def dma(eng, out, in_, r=(), w=()):
        P.dma(eng, lambda h: h.dma_start(out=out, in_=in_), reads=r, writes=w,
              dur=2200.0 + fsz(out) * out.shape[0] * 4 / 150.0)

    def cp(out, in_, r, w, eng="dve"):
        A(eng, lambda h: h.tensor_copy(out=out, in_=in_), r, w,
          dur=(100.0 + 1.0 * fsz(out)) if eng == "dve" else (300.0 + 0.9 * fsz(out)))

    def mset(ap, val, w, eng="dve"):
        A(eng, lambda h: h.memset(ap, val), (), w)

    def tr(out, in_, idn, r, w):
        A("pe", lambda h: h.transpose(out, in_, idn), r, w, dur=130.0)

    def scan(out, d0, d1, init, r, w):
        A("dve", lambda h: h.tensor_tensor_scan(out=out, data0=d0, data1=d1, initial=init,
                                                op0=ALU.mult, op1=ALU.add), r, w, dur=110.0 + 2.1 * fsz(out))

    def dbg_dump(nm, ap, r):
        if nm in dbg:
            dma("sp", dbg[nm], ap, r)

    A("pool", lambda h: h.memset(ident_f[:], 1.0), (), [cst])
    A("pool", lambda h: h.affine_select(out=ident_f[:], in_=ident_f[:], pattern=[[1, 128]],
                                        compare_op=ALU.is_equal, fill=0.0, base=0, channel_multiplier=-1), [cst], [cst])
    A("pool", lambda h: h.memset(ones_f[:], 1.0), (), [cst])
    A("pool", lambda h: h.memset(ones_b[:], 1.0), (), [cst])
    A("pool", lambda h: h.memset(maskU[:], 1.0), (), [cst])
    A("pool", lambda h: h.affine_select(out=maskU[:], in_=maskU[:], pattern=[[1, 128]],
                                        compare_op=ALU.is_ge, fill=0.0, base=0, channel_multiplier=-1), [cst], [cst])
    A("pool", lambda h: h.memset(maskSL[:], 0.0), (), [cst])
    A("pool", lambda h: h.affine_select(out=maskSL[:], in_=maskSL[:], pattern=[[-1, 128]],
                                        compare_op=ALU.is_gt, fill=1e30, base=0, channel_multiplier=1), [cst], [cst])
    A("pool", lambda h: h.memset(maskUn[:], 0.0), (), [cst])
    A("pool", lambda h: h.affine_select(out=maskUn[:], in_=maskUn[:], pattern=[[1, 128]],
                                        compare_op=ALU.is_ge, fill=-1e30, base=0, channel_multiplier=-1), [cst], [cst])
    A("pool", lambda h: h.memset(sel_last[:], 1.0), (), [cst])
    A("pool", lambda h: h.affine_select(out=sel_last[:], in_=sel_last[:], pattern=[[0, 128]],
                                        compare_op=ALU.is_equal, fill=0.0, base=-127, channel_multiplier=1), [cst], [cst])
    A("dve", lambda h: h.tensor_copy(out=ident_b[:], in_=ident_f[:]), [cst], [cst])
    def mk_bd(b_, dst_tile, col0):
        nb_ = 128 // b_
        E = SF[5][0:nb_, 0:128]
        A("pool", lambda h: h.memset(E, 1.0), (), [SF_b[5]])
        A("pool", lambda h: h.affine_select(out=E, in_=E, pattern=[[1, 128]], compare_op=ALU.is_ge, fill=0.0,
                                            base=0, channel_multiplier=-b_), [SF_b[5]], [SF_b[5]])
        A("pool", lambda h: h.affine_select(out=E, in_=E, pattern=[[-1, 128]], compare_op=ALU.is_ge, fill=0.0,
                                            base=b_ - 1, channel_multiplier=b_), [SF_b[5]], [SF_b[5]])
        mm(PS[7][:, col0:col0 + 128], E, E, True, True, [SF_b[5]], [PS_b[7]])
    mk_bd(16, None, 0)
    mk_bd(32, None, 128)
    mk_bd(64, None, 256)
    act(BD16[:], PS[7][:, 0:128], AF.Copy, [PS_b[7]], [cst])
    bdf = SF[4]
    act(bdf[:, 0:384], PS[7][:, 0:384], AF.Copy, [PS_b[7]], [SF_b[4]])
    tt(OFFM[32][:], bdf[:, 128:256], bdf[:, 0:128], ALU.subtract, [SF_b[4]], [cst])
    tt(OFFM[64][:], bdf[:, 256:384], bdf[:, 128:256], ALU.subtract, [SF_b[4]], [cst])
    ts(OFFM[128][:], bdf[:, 256:384], -1.0, 1.0, ALU.mult, ALU.add, [SF_b[4]], [cst])

    for k in range(8):
        for ti, (c0, n) in enumerate(TILES):
            dma("sp", xres[:, k, c0:c0 + n], xin[:, k, c0:c0 + n], (), [xres_b[k][ti]])
            act(xbf[:, k, c0:c0 + n], xres[:, k, c0:c0 + n], AF.Copy, [xres_b[k][ti]], [xbf_b[k][ti]])
            ts(xres[:, k, c0:c0 + n], xres[:, k, c0:c0 + n], ALPHA, None, ALU.mult, None,
               [xres_b[k][ti]], [xres_b[k][ti]])

    wctr = [0]

    def wslot():
        i = wctr[0] % 4
        wctr[0] += 1
        return i

    def ffn(fi):
        w1v = w1d[fi].rearrange("(k p) n -> p k n", p=128)
        w3v = w3d[fi].rearrange("(k p) n -> p k n", p=128)
        w2v = w2d[fi].rearrange("(j p) n -> p j n", p=128)
        ftmp_b = SF_b[0:5]
        FT = [(0, 413), (413, 413), (826, 413), (1239, 413), (1652, 412)]

        def ov(bl, k_, c0_, n_):
            return [bl[k_][t_] for t_, (a_, w_) in enumerate(FT) if a_ < c0_ + n_ and c0_ < a_ + w_]
        subs = [(0, 4), (4, 4), (8, 4), (12, 4), (16, 4), (20, 2)]
        groups = [(0, 1), (2, 3), (4, 5)]

        def load13(su):
            j0, nj = subs[su]
            s1, s3 = wslot(), wslot()
            for s, wv in ((s1, w1v), (s3, w3v)):
                dst = wsl[s][:].rearrange("p (k n) -> p k n", k=8)[:, :, 0:nj * 128]
                dma("pool", dst, wv[:, :, j0 * 128:(j0 + nj) * 128], (), [wsl_b[s]])
            return s1, s3

        def load2(su):
            j0, nj = subs[su]
            s2 = wslot()
            dst = wsl[s2][:].rearrange("p (j n) -> p j n", j=4)[:, 0:nj, :]
            dma("pool", dst, w2v[:, j0:j0 + nj, :], (), [wsl_b[s2]])
            return s2

        for (sa, sbb) in groups:
            l13 = {}
            l13[sa] = load13(sa)
            l13[sbb] = load13(sbb)
            for su in (sa, sbb):
                j0, nj = subs[su]
                s1, s3 = l13[su]
                hj0 = 0 if su == sa else 4
                for j in range(nj):
                    for (s, which) in ((s1, 0), (s3, 1)):
                        wv = wsl[s][:].rearrange("p (k n) -> p k n", k=8)
                        for k in range(8):
                            for ti, (c0, n) in enumerate(FT):
                                mm(PS[ti][:, 0:n], wv[:, k, j * 128:(j + 1) * 128], xbf[:, k, c0:c0 + n],
                                   k == 0, k == 7, [wsl_b[s]] + ov(xbf_b, k, c0, n), [PS_b[ti]])
                        for ti, (c0, n) in enumerate(FT):
                            fsl = SF[ti][:, 0:n]
                            if which == 0:
                                act(fsl, PS[ti][:, 0:n], AF.Silu, [PS_b[ti]], [ftmp_b[ti]])
                            else:
                                tt(mbuf[:, hj0 + j, c0:c0 + n], fsl, PS[ti][:, 0:n], ALU.mult,
                                   [ftmp_b[ti], PS_b[ti]], [mbuf_b[hj0 + j][ti]])
            s2a = load2(sa)
            s2b = load2(sbb)
            parts = [(s2a, subs[sa][1], 0), (s2b, subs[sbb][1], 4)]
            tot = subs[sa][1] + subs[sbb][1]
            for m in range(8):
                cnt = 0
                for (s2, nj, hj0) in parts:
                    wv = wsl[s2][:].rearrange("p (j n) -> p j n", j=4)
                    for j in range(nj):
                        for ti, (c0, n) in enumerate(FT):
                            mm(PS[ti][:, 0:n], wv[:, j, m * 128:(m + 1) * 128], mbuf[:, hj0 + j, c0:c0 + n],
                               cnt == 0, cnt == tot - 1, [wsl_b[s2], mbuf_b[hj0 + j][ti]], [PS_b[ti]])
                        cnt += 1
                for ti, (c0, n) in enumerate(FT):
                    stt(xres[:, m, c0:c0 + n], PS[ti][:, 0:n], 0.5, xres[:, m, c0:c0 + n], ALU.mult, ALU.add,
                        [PS_b[ti]] + ov(xres_b, m, c0, n), ov(xres_b, m, c0, n))

    def layernorm(i_ln, final):
        for ti, (c0, n) in enumerate(TILES):
            pm, pq = 5, 6
            for k in range(8):
                sq_, sqb = SBf[k % 4], SB_b[k % 4]
                act(sq_[:, 0:n], xres[:, k, c0:c0 + n], AF.Square, [xres_b[k][ti]], [sqb])
                mm(PS[pm][:, 0:n], ones_f[:], xres[:, k, c0:c0 + n], k == 0, k == 7, [cst, xres_b[k][ti]], [PS_b[pm]])
                mm(PS[pq][:, 0:n], ones_b[:], sq_[:, 0:n], k == 0, k == 7, [cst, sqb], [PS_b[pq]])
            mean, msq, var, Aa, Bb = SF[0], SF[1], SF[2], SF[3], SF[4]
            act(mean[:, 0:n], PS[pm][:, 0:n], AF.Copy, [PS_b[pm]], [SF_b[0]], scale=1.0 / D)
            act(msq[:, 0:n], PS[pm][:, 0:n], AF.Square, [PS_b[pm]], [SF_b[1]], scale=1.0 / D)
            stt(var[:, 0:n], PS[pq][:, 0:n], 1.0 / D, msq[:, 0:n], ALU.mult, ALU.subtract,
                [PS_b[pq], SF_b[1]], [SF_b[2]])
            act(var[:, 0:n], var[:, 0:n], AF.Ln, [SF_b[2]], [SF_b[2]], bias=LN_EPS)
            act(Aa[:, 0:n], var[:, 0:n], AF.Exp, [SF_b[2]], [SF_b[3]], scale=-0.5)
            stt(Bb[:, 0:n], mean[:, 0:n], -1.0, Aa[:, 0:n], ALU.mult, ALU.mult, [SF_b[0], SF_b[3]], [SF_b[4]])
            for k in range(8):
                z, zb = SF[k % 2], SF_b[k % 2]
                tt(z[:, 0:n], xres[:, k, c0:c0 + n], Aa[:, 0:n], ALU.mult, [xres_b[k][ti], SF_b[3]], [zb])
                tt(z[:, 0:n], z[:, 0:n], Bb[:, 0:n], ALU.add, [zb, SF_b[4]], [zb])
                gi, bi = 16 * i_ln + k, 16 * i_ln + 8 + k
                if final:
                    ts(xres[:, k, c0:c0 + n], z[:, 0:n], pv[:, gi:gi + 1], pv[:, bi:bi + 1], ALU.mult, ALU.add,
                       [zb, pvb], [xres_b[k][ti]])
                    dma("sp", yout[:, k, c0:c0 + n], xres[:, k, c0:c0 + n], [xres_b[k][ti]])
                else:
                    act(xbf[:, k, c0:c0 + n], z[:, 0:n], AF.Identity, [zb, pvb], [xbf_b[k][ti]],
                        bias=pv[:, bi:bi + 1], scale=pv[:, gi:gi + 1])
                    if k % 2 == 0:
                        ts(xres[:, k, c0:c0 + n], z[:, 0:n], pv2[:, gi:gi + 1], pv2[:, bi:bi + 1], ALU.mult, ALU.add,
                           [zb, pvb], [xres_b[k][ti]])
                    else:
                        act(xres[:, k, c0:c0 + n], z[:, 0:n], AF.Identity, [zb, pvb], [xres_b[k][ti]],
                            bias=pv2[:, bi:bi + 1], scale=pv2[:, gi:gi + 1])

    def softplus(dst, src, tmps, r, w):
        t0, t1, t2 = tmps
        act(t0, src, AF.Abs, r, w)
        act(t0, t0, AF.Exp, w, w, scale=-1.0)
        ts(t1, t0, 2.0, None, ALU.add, None, w, w)
        A("dve", lambda h: h.reciprocal(out=t1, in_=t1), w, w)
        tt(t1, t1, t0, ALU.mult, w, w)
        tt(t2, t1, t1, ALU.mult, w, w)
        ts(t0, t2, 1.0 / 11, 1.0 / 9, ALU.mult, ALU.add, w, w)
        for cf in (1.0 / 7, 1.0 / 5, 1.0 / 3, 1.0):
            tt(t0, t0, t2, ALU.mult, w, w)
            ts(t0, t0, cf, None, ALU.add, None, w, w)
        tt(t0, t0, t1, ALU.mult, w, w)
        ts(t1, src, 0.0, None, ALU.max, None, list(r) + list(w), w)
        stt(dst, t0, 2.0, t1, ALU.mult, ALU.add, w, w)

    def layer_params(l):
        dma("sp", pv[:], pvd[l], (), [pvb])
        dma("sp", prow[:], prd[l].partition_broadcast(128), (), [pvb])
        dma("pool", wab_b[:], wabd[l].rearrange("(k p) n -> p k n", p=128), (), [wabb])
        ts(pv2[:], pv[:, 0:48], ALPHA, None, ALU.mult, None, [pvb], [pvb])
        wl = [pvb] + SF_b[0:5]
        tn = [SF[i][:, 0:8] for i in range(5)]
        ts(tn[3], pv[:, 168:176], -1.0, None, ALU.mult, None, [pvb], wl)
        softplus(tn[4], tn[3], (tn[0], tn[1], tn[2]), wl, wl)
        ts(nsp[:, 0:8], tn[4], -8.0, None, ALU.mult, None, wl, wl)
        ts(nsp[:, 8:16], tn[4], -16.0, None, ALU.mult, None, wl, wl)
        act(aexp[:], prow[:, 16:24], AF.Exp, [pvb], [pvb])

    def gate_tables(l):
        pab = 7
        for nb in range(17):
            c0 = nb * 128
            nt = 128 if nb < 16 else NS
            ti = min(nb // 4, 4)
            for k in range(8):
                mm(PS[pab][0:nt, nb * 16:(nb + 1) * 16], xbf[:, k, c0:c0 + nt], wab_b[:, k, :], k == 0, k == 7,
                   [xbf_b[k][ti], wabb], [PS_b[pab]])
        ab_f = SF[4][:, 0:272].rearrange("p (b n) -> p b n", n=16)
        abv = ab_f
        psv = PS[pab][:, 0:272].rearrange("p (b n) -> p b n", n=16)
        brow = prow[:, 0:16].unsqueeze(1).to_broadcast([128, 17, 16])
        tt(abv, psv, brow, ALU.add, [PS_b[pab], pvb], [tbb, SF_b[4]])
        a_v, b_v = ab_f[:, :, 0:8], ab_f[:, :, 8:16]
        T = {k_: v_[:] for k_, v_ in TB.items()}
        for i_ in range(4):
            T["t%d" % i_] = SF[i_][:, 0:136].rearrange("p (b n) -> p b n", n=8)
        wl = [tbb] + SF_b[0:5]
        act(T["beta"], b_v, AF.Sigmoid, wl, [tbb])
        ts(T["nbeta"], T["beta"], -1.0, None, ALU.mult, None, [tbb], [tbb])
        dtb = prow[:, 24:32].unsqueeze(1).to_broadcast([128, 17, 8])
        tt(T["t3"], a_v, dtb, ALU.add, [tbb, pvb], wl)
        softplus(T["g"], T["t3"], (T["t0"], T["t1"], T["t2"]), wl, wl)
        aex = aexp[:].unsqueeze(1).to_broadcast([128, 17, 8])
        stt(T["g"], T["g"], -1.0, aex, ALU.mult, ALU.mult, [tbb, pvb], [tbb])
        gflat = TB["g"][:].rearrange("p b n -> p (b n)")
        gcflat = TB["gc"][:].rearrange("p b n -> p (b n)")
        mm(PS[pab][:, 0:128], maskU[:], gflat[:, 0:128], True, True, [cst, tbb], [PS_b[pab]])
        act(gcflat[:, 0:128], PS[pab][:, 0:128], AF.Copy, [PS_b[pab]], [tbb])
        act(gcflat[:, 128:136], gflat[:, 128:136], AF.Copy, [tbb], [tbb])
        mm(PS[pab][:, 128:256], sel_last[:], gcflat[:, 0:128], True, True, [cst, tbb], [PS_b[pab]])
        act(glb[:], PS[pab][:, 128:256], AF.Copy, [PS_b[pab]], [tbb])
        act(egl[:], glb[:], AF.Exp, [tbb], [tbb])
        tt(ekt[:], glb[:], gcflat[:, 0:128], ALU.subtract, [tbb], [tbb])
        act(ekt[:], ekt[:], AF.Exp, [tbb], [tbb])
        act(T["egc"], T["gc"], AF.Exp, [tbb], [tbb])
        tt(T["bge"], T["beta"], T["egc"], ALU.mult, [tbb], [tbb])
        for nm, src in (("eg", TB["egc"]), ("beta", TB["beta"])):
            sv = src[0:NS, 16, :].unsqueeze(2).to_broadcast([NS, 8, NS])
            iv = ident_f[0:NS, 0:NS].unsqueeze(1).to_broadcast([NS, 8, NS])
            sM = SF[5][0:NS, 0:128]
            tt(sM.rearrange("p (h s) -> p h s", h=8), sv, iv, ALU.mult, [tbb, cst], [srb, SF_b[5]])
            mm(PS[pab][:, 256:384], ones_f[0:NS, :], sM, True, True, [cst, srb, SF_b[5]], [PS_b[pab]])
            act(srow[nm][:], PS[pab][:, 256:384], AF.Copy, [PS_b[pab]], [srb])
        stt(srow["nbe"][:], srow["beta"][:], -1.0, srow["eg"][:], ALU.mult, ALU.mult, [srb], [srb])

    def rbank(X):
        for _ in range(8):
            i = X.banks[X.rot % len(X.banks)]
            X.rot += 1
            if i not in X.held:
                return i
        raise RuntimeError("no free psum bank")

    class Ctx:
        pass

    def mk_ctx(tid):
        X = Ctx()
        X.tid = tid
        X.banks = [0, 1, 2, 3] if tid == 0 else [4, 5, 6, 7]
        X.held = set()
        X.rot = 0
        X.slots = (0, 1) if tid == 0 else (2, 3)
        for nm in ("smb", "gwb", "dgwb", "sstb_b", "ssob", "Sfb", "Sbb", "ksqb", "srq", "wob"):
            setattr(X, nm, Buf())
        X.SF_b = [Buf() for _ in range(6)] if tid else SF_b
        X.SB_b = [Buf() for _ in range(13)] if tid else SB_b
        if tid == 0:
            X.SF = [t[:] for t in SF]
            X.SBf = [t[:] for t in SBf]
            X.hist, X.pco, X.dgw, X.gw_b = hist[:], pco[:], dgw[:], gw_b[:]
            X.sst, X.sstb, X.sso, X.sh0, X.sho = sst[:], sstb[:], sso[:], sh0[:], sho[:]
            X.S_f, X.S_b, X.skq, X.ksq_t, X.srow_qk = S_f[:], S_b[:], skq[:], ksq_t[:], srow["qk"][:]
            X.wo_c = wo_c0[:]
        else:
            mflat = mbuf[:].rearrange("p a b -> p (a b)")
            off = [0]

            def cv(n_, dt=BF16):
                ne = n_ if dt == BF16 else 2 * n_
                a = off[0]
                off[0] += ne + (ne % 2)
                v = mflat[:, a:a + ne]
                return v if dt == BF16 else v.bitcast(F32)
            X.SF = [cv(512, F32) for _ in range(6)]
            X.SBf = [cv(520 if i < 2 else 512) for i in range(13)]
            X.hist = cv(16).rearrange("p (a b) -> p a b", a=4)
            X.pco = cv(12, F32).rearrange("p (a b) -> p a b", a=4)
            X.dgw = cv(512).rearrange("p (a b) -> p a b", a=4)
            X.gw_b = cv(256).rearrange("p (a b) -> p a b", a=2)
            X.sst = cv(192, F32).rearrange("p (a b c) -> p a b c", a=4, b=3)
            X.sstb = cv(192).rearrange("p (a b c) -> p a b c", a=4, b=3)
            X.sso = cv(192, F32).rearrange("p (a b c) -> p a b c", a=4, b=3)
            X.sh0, X.sho = cv(NS, F32), cv(NS, F32)
            X.S_f, X.S_b = cv(128, F32), cv(128)
            X.skq = cv(2 * NS, F32).rearrange("p (a b) -> p a b", a=2)
            X.ksq_t, X.srow_qk = cv(2 * NS, F32), cv(NS, F32)
            X.wo_c = cv(1024)
            assert off[0] <= 8 * TOK, off[0]
        return X

    CTX = [mk_ctx(0), mk_ctx(1)]
    t1_bufs = CTX[1].SF_b + CTX[1].SB_b + [getattr(CTX[1], nm) for nm in
                                           ("smb", "gwb", "dgwb", "sstb_b", "ssob", "Sfb", "Sbb", "ksqb", "srq", "wob")]

    def chunk_body(l, c, X):
        T = TB
        sA, sB = X.slots
        wv_d = wind[l, c].rearrange("(k p) n -> p k n", p=128)
        for s, b0 in ((sA, 0), (sB, 4)):
            dst = wsl[s][:].rearrange("p (k n) -> p k n", k=8)
            dma("pool", dst, wv_d[:, :, b0 * 128:(b0 + 4) * 128], (), [wsl_b[s]])
        dma("pool", X.gw_b[:], gwd[l, :, c].rearrange("g p n -> p g n"), (), [X.gwb])
        dma("pool", X.wo_c[:], wod[l, c * 128:(c + 1) * 128, :], (), [X.wob])
        wA = wsl[sA][:].rearrange("p (k n) -> p k n", k=8)
        wB = wsl[sB][:].rearrange("p (k n) -> p k n", k=8)

        def wchunk(k, blk):
            if blk < 4:
                return wA[:, k, blk * 128:(blk + 1) * 128], wsl_b[sA]
            return wB[:, k, (blk - 4) * 128:(blk - 3) * 128], wsl_b[sB]

        cw_base = (112 + c, 176 + c, 176 + 8 + c, 176 + 16 + c)
        dma("sp", X.sst[:, 0], src0d[l, c], (), [X.sstb_b])
        for q_ in range(3):
            dma("sp", X.sst[:, 1 + q_], sgc0d[l, q_ * 8 + c], (), [X.sstb_b])
        dma("sp", X.sh0[:], sh0d[l, c], (), [X.sstb_b])
        act(X.sstb[:], X.sst[:], AF.Copy, [X.sstb_b], [X.sstb_b])
        mset(hcar[:, c:c + 1], 0.0, [X.smb])
        mset(X.hist[:], 0.0, [X.smb])
        mset(X.S_f[:], 0.0, [X.Sfb])
        mset(X.S_b[:], 0.0, [X.Sbb])

        for ti, (c0, n) in enumerate(TILES):
            samp = ti == 4

            if samp:
                sbl = []
                for tix in range(7, 13):
                    for half in (0, 1):
                        sbl.append((X.SBf[tix][:, half * 256:(half + 1) * 256].bitcast(F32), X.SB_b[tix]))
                for tix in range(0, 7):
                    sbl.append((X.SBf[tix][:, 256:512].bitcast(F32), X.SB_b[tix]))
                for q_ in range(1, 4):
                    sbl.append((X.SF[2][0:NS, q_ * 128:(q_ + 1) * 128], X.SF_b[2]))
                sbb_ = []
                for (_, par) in sbl:
                    cb = Buf()
                    cb.r = list(par.r) + ([par.w] if par.w is not None else [])
                    sbb_.append(cb)
                for s_ in range(NS):
                    dma("sp", sbl[s_][0], sS0d[l, s_, c], (), [sbb_[s_]])

            def proj(blk):
                pb = rbank(X)
                for k in range(8):
                    wv_, wb_ = wchunk(k, blk)
                    mm(PS[pb][:, 0:n], wv_, xbf[:, k, c0:c0 + n], k == 0, k == 7, [wb_, xbf_b[k][ti]], [PS_b[pb]])
                return pb

            def bias(blk):
                col = 48 + c * 8 + blk
                return pv[:, col:col + 1]

            def conv(blk, hb, prex_i):
                pb = proj(blk)
                prex, pxb = X.SBf[prex_i], X.SB_b[prex_i]
                for j in range(4):
                    col = cw_base[hb] + j * (8 if hb == 0 else 24)
                    ts(X.dgw[:, j, :], ident_f[:], pv[:, col:col + 1], None, ALU.mult, None, [cst, pvb], [X.dgwb])
                if not samp:
                    cp(prex[:, 0:3], X.hist[:, hb, 0:3], [X.smb], [pxb])
                    act(prex[:, 3:3 + n], PS[pb][:, 0:n], AF.Identity, [PS_b[pb], pvb], [pxb], bias=bias(blk))
                    if ti < 3:
                        cp(X.hist[:, hb, 0:3], prex[:, n:n + 3], [pxb], [X.smb])
                    else:
                        act(X.pco[:, hb, :], PS[pb][:, n - 3:n], AF.Identity, [PS_b[pb], pvb], [X.smb], bias=bias(blk))
                    pc = rbank(X)
                    for j in range(4):
                        mm(PS[pc][:, 0:n], X.dgw[:, j, :], prex[:, j:j + n], j == 0, j == 3, [X.dgwb, pxb], [PS_b[pc]])
                else:
                    act(prex[:, 0:n], PS[pb][:, 0:n], AF.Identity, [PS_b[pb], pvb], [pxb], bias=bias(blk))
                    act(X.sso[:, hb, 2, :], PS[pb][:, 0:n], AF.Identity, [PS_b[pb], pvb], [X.ssob], bias=bias(blk))
                    cp(X.sso[:, hb, 0:2, :], X.sst[:, hb, 1:3, :], [X.sstb_b], [X.ssob])
                    pc = rbank(X)
                    for j in range(3):
                        mm(PS[pc][:, 0:n], X.dgw[:, j, :], X.sstb[:, hb, j, :], j == 0, False, [X.dgwb, X.sstb_b], [PS_b[pc]])
                    mm(PS[pc][:, 0:n], X.dgw[:, 3, :], prex[:, 0:n], False, True, [X.dgwb, pxb], [PS_b[pc]])
                return pc

            pc = conv(0, 0, 0)
            xc, xcb = X.SF[0], X.SF_b[0]
            act(xc[:, 0:n], PS[pc][:, 0:n], AF.Identity, [PS_b[pc], pvb], [xcb], bias=pv[:, 144 + c:145 + c])
            xcbf, xcbfb = X.SBf[1], X.SB_b[1]
            act(xcbf[:, 0:n], PS[pc][:, 0:n], AF.Identity, [PS_b[pc], pvb], [xcbfb], bias=pv[:, 144 + c:145 + c])
            pr_, pi_ = rbank(X), rbank(X)
            mm(PS[pr_][:, 0:n], X.gw_b[:, 0, :], xcbf[:, 0:n], True, True, [X.gwb, xcbfb], [PS_b[pr_]])
            mm(PS[pi_][:, 0:n], X.gw_b[:, 1, :], xcbf[:, 0:n], True, True, [X.gwb, xcbfb], [PS_b[pi_]])
            rr, rrb = X.SF[1], X.SF_b[1]
            ig, igb = X.SF[2], X.SF_b[2]
            act(rr[:, 0:n], PS[pr_][:, 0:n], AF.Sigmoid, [PS_b[pr_], pvb], [rrb], bias=pv[:, 152 + c:153 + c])
            act(ig[:, 0:n], PS[pi_][:, 0:n], AF.Sigmoid, [PS_b[pi_], pvb], [igb], bias=pv[:, 160 + c:161 + c])
            aa, aab = X.SF[3], X.SF_b[3]
            act(aa[:, 0:n], rr[:, 0:n], AF.Exp, [rrb, pvb], [aab], scale=nsp[:, c:c + 1])
            act(rr[:, 0:n], rr[:, 0:n], AF.Exp, [rrb, pvb], [rrb], scale=nsp[:, 8 + c:9 + c])
            act(rr[:, 0:n], rr[:, 0:n], AF.Ln, [rrb], [rrb], bias=1.0, scale=-1.0)
            act(rr[:, 0:n], rr[:, 0:n], AF.Exp, [rrb], [rrb], scale=0.5)
            tt(ig[:, 0:n], ig[:, 0:n], xc[:, 0:n], ALU.mult, [igb, xcb], [igb])
            tt(ig[:, 0:n], ig[:, 0:n], rr[:, 0:n], ALU.mult, [igb, rrb], [igb])
            hs, hsb = X.SF[0], X.SF_b[0]
            if not samp:
                scan(hs[:, 0:n], aa[:, 0:n], ig[:, 0:n], hcar[:, c:c + 1], [aab, igb, X.smb], [hsb])
                cp(hcar[:, c:c + 1], hs[:, n - 1:n], [hsb], [X.smb])
                if ti == 3:
                    dma("sp", o_prc[l, c], X.pco[:, 0, :], [X.smb])
            else:
                tt(hs[:, 0:n], aa[:, 0:n], X.sh0[:], ALU.mult, [aab, X.sstb_b], [hsb])
                tt(hs[:, 0:n], hs[:, 0:n], ig[:, 0:n], ALU.add, [hsb, igb], [hsb])
                cp(X.sho[:], hs[:, 0:n], [hsb], [X.ssob])
                dma("sp", o_sh[l, c], X.sho[:], [X.ssob])
                dma("sp", o_src[l, c], X.sso[:, 0], [X.ssob])
            pb = proj(1)
            yy, yyb = X.SF[1], X.SF_b[1]
            act(yy[:, 0:n], PS[pb][:, 0:n], AF.Identity, [PS_b[pb], pvb], [yyb], bias=bias(1))
            y2, y2b = X.SF[2], X.SF_b[2]
            tt(y2[:, 0:n], yy[:, 0:n], yy[:, 0:n], ALU.mult, [yyb], [y2b])
            ts(y2[:, 0:n], y2[:, 0:n], 0.044715, 1.0, ALU.mult, ALU.add, [y2b], [y2b])
            tt(y2[:, 0:n], y2[:, 0:n], yy[:, 0:n], ALU.mult, [y2b, yyb], [y2b])
            act(y2[:, 0:n], y2[:, 0:n], AF.Sigmoid, [y2b], [y2b], scale=1.5957691216057308)
            tt(yy[:, 0:n], yy[:, 0:n], y2[:, 0:n], ALU.mult, [yyb, y2b], [yyb])
            tt(hs[:, 0:n], hs[:, 0:n], yy[:, 0:n], ALU.mult, [hsb, yyb], [hsb])
            pb = proj(2)
            act(y2[:, 0:n], PS[pb][:, 0:n], AF.Sigmoid, [PS_b[pb], pvb], [y2b], bias=bias(2))
            ma, mab = X.SF[5], X.SF_b[5]
            tt(ma[:, 0:n], hs[:, 0:n], y2[:, 0:n], ALU.mult, [hsb, y2b], [mab])

            h_ = c
            pc = conv(3, 1, 0)
            qf, qfb = X.SF[0], X.SF_b[0]
            act(qf[:, 0:n], PS[pc][:, 0:n], AF.Sigmoid, [PS_b[pc]], [qfb])
            tt(qf[:, 0:n], qf[:, 0:n], PS[pc][:, 0:n], ALU.mult, [qfb, PS_b[pc]], [qfb])
            pc = conv(4, 2, 1)
            kf, kfb = X.SF[1], X.SF_b[1]
            act(kf[:, 0:n], PS[pc][:, 0:n], AF.Sigmoid, [PS_b[pc]], [kfb])
            tt(kf[:, 0:n], kf[:, 0:n], PS[pc][:, 0:n], ALU.mult, [kfb, PS_b[pc]], [kfb])
            pc = conv(5, 3, 0)
            vT, vTb = X.SBf[2], X.SB_b[2]
            vsg, vsgb = X.SF[2], X.SF_b[2]
            act(vsg[:, 0:n], PS[pc][:, 0:n], AF.Sigmoid, [PS_b[pc]], [vsgb])
            tt(vT[:, 0:n], vsg[:, 0:n], PS[pc][:, 0:n], ALU.mult, [vsgb, PS_b[pc]], [vTb])
            if samp:
                for q_ in range(3):
                    dma("sp", o_sgc[l, q_ * 8 + c], X.sso[:, 1 + q_], [X.ssob])
            elif ti == 3:
                for q_ in range(3):
                    dma("sp", o_pgc[l, q_ * 8 + c], X.pco[:, 1 + q_, :], [X.smb])
            sqs, sqsb = X.SBf[3], X.SB_b[3]
            rs, rsb = X.SF[2], X.SF_b[2]
            qT, qTb = X.SBf[4], X.SB_b[4]
            kT, kTb = X.SBf[5], X.SB_b[5]
            for (src, srcb, dstT, dstTb, scl) in ((qf, qfb, qT, qTb, 128 ** -0.5), (kf, kfb, kT, kTb, 1.0)):
                act(sqs[:, 0:n], src[:, 0:n], AF.Square, [srcb], [sqsb])
                pq = rbank(X)
                mm(PS[pq][:, 0:n], ones_b[:], sqs[:, 0:n], True, True, [cst, sqsb], [PS_b[pq]])
                act(rs[:, 0:n], PS[pq][:, 0:n], AF.Ln, [PS_b[pq]], [rsb], bias=RMS_EPS)
                act(rs[:, 0:n], rs[:, 0:n], AF.Exp, [rsb], [rsb], scale=-0.5)
                if samp:
                    stt(src[:, 0:n], src[:, 0:n], scl, rs[:, 0:n], ALU.mult, ALU.mult, [srcb, rsb], [srcb])
                else:
                    stt(dstT[:, 0:n], src[:, 0:n], scl, rs[:, 0:n], ALU.mult, ALU.mult, [srcb, rsb], [dstTb])

            if samp:
                cols = slice(h_ * NS, (h_ + 1) * NS)
                cp(X.skq[:, 0, :], kf[:, 0:n], [kfb], [X.srq])
                cp(X.skq[:, 1, :], qf[:, 0:n], [qfb], [X.srq])
                qk_, qkb = X.SF[3], X.SF_b[3]
                tt(qk_[:, 0:n], qf[:, 0:n], kf[:, 0:n], ALU.mult, [qfb, kfb], [qkb])
                pq = rbank(X)
                mm(PS[pq][:, 0:n], ones_f[:], qk_[:, 0:n], True, True, [cst, qkb], [PS_b[pq]])
                act(X.srow_qk[:], PS[pq][:, 0:n], AF.Copy, [PS_b[pq]], [X.srq])
                pt = rbank(X)
                tr(PS[pt][0:NS, 0:128], kf[:, 0:n], ident_f[:], [kfb, cst], [PS_b[pt]])
                kpm, kpmb = X.SF[4], X.SF_b[4]
                act(kpm[0:NS, 0:128], PS[pt][0:NS, 0:128], AF.Copy, [PS_b[pt]], [kpmb])
                pks = X.banks[0]
                X.held = {pks}
                vfs, vfsb = X.SF[1], X.SF_b[1]
                act(vfs[:, 0:n], vT[:, 0:n], AF.Copy, [vTb], [vfsb])
                for s_ in range(NS):
                    mm(PS[pks][:, 2 * s_:2 * s_ + 2], sbl[s_][0], X.skq[:, :, s_], True, True,
                       [sbb_[s_], X.srq], [PS_b[pks]])
                act(X.ksq_t[:], PS[pks][:, 0:2 * NS], AF.Copy, [PS_b[pks]], [X.ksqb])
                ksv = X.ksq_t[:].rearrange("p (s t) -> p t s", t=2)
                X.held = set()
                vn, vnb = X.SF[3], X.SF_b[3]
                tt(vn[:, 0:n], vfs[:, 0:n], srow["beta"][:, cols], ALU.mult, [vfsb, srb], [vnb])
                tt(vfs[:, 0:n], ksv[:, 0, :], srow["nbe"][:, cols], ALU.mult, [X.ksqb, srb], [vfsb])
                tt(vn[:, 0:n], vn[:, 0:n], vfs[:, 0:n], ALU.add, [vnb, vfsb], [vnb])
                oo, oob = X.SF[0], X.SF_b[0]
                tt(oo[:, 0:n], ksv[:, 1, :], srow["eg"][:, cols], ALU.mult, [X.ksqb, srb], [oob])
                tt(vfs[:, 0:n], vn[:, 0:n], X.srow_qk[:], ALU.mult, [vnb, X.srq], [vfsb])
                tt(oo[:, 0:n], oo[:, 0:n], vfs[:, 0:n], ALU.add, [oob, vfsb], [oob])
                pt = rbank(X)
                tr(PS[pt][0:NS, 0:128], vn[:, 0:n], ident_f[:], [vnb, cst], [PS_b[pt]])
                vpm, vpmb = X.SF[1], X.SF_b[1]
                act(vpm[0:NS, 0:128], PS[pt][0:NS, 0:128], AF.Copy, [PS_b[pt]], [vpmb])
                for s_ in range(NS):
                    p_ = h_ * NS + s_
                    vm, vmb = sbl[19 + s_ % 3][0], sbb_[19 + s_ % 3]
                    ts(vm, vpm[0:NS, 0:128], ident_f[0:NS, s_:s_ + 1], None, ALU.mult, None, [vpmb, cst], [vmb])
                    po = rbank(X)
                    mm(PS[po][:, 0:128], kpm[0:NS, 0:128], vm, True, True, [kpmb, vmb], [PS_b[po]])
                    so_, sob_ = sbl[16 + s_ % 3][0], sbb_[16 + s_ % 3]
                    stt(so_, sbl[s_][0], srow["eg"][:, p_:p_ + 1], PS[po][:, 0:128], ALU.mult, ALU.add,
                        [sbb_[s_], srb, PS_b[po]], [sob_])
                    dma("sp", o_sS[l, s_, h_], so_, [sob_])
                for (_, par), cb in zip(sbl, sbb_):
                    par.r.extend(cb.r)
                    if cb.w is not None:
                        par.r.append(cb.w)
                po_hold = None
                of, ofb = oo, oob
            else:
                nb0 = ti * 4
                pg, pO, pA, pB = X.banks[0], X.banks[0], X.banks[1], X.banks[2]
                X.held = {pg}
                for nn in range(4):
                    dg_, dgb = X.SF[3], X.SF_b[3]
                    ts(dg_[:, 0:128], ident_f[:], T["gc"][:, nb0 + nn, h_:h_ + 1], None, ALU.mult, None,
                       [cst, tbb], [dgb])
                    mm(PS[pg][:, nn * 128:(nn + 1) * 128], ones_f[:], dg_[:, 0:128], True, True, [cst, dgb], [PS_b[pg]])
                egr, egrb = X.SF[3], X.SF_b[3]
                act(egr[:, 0:n], PS[pg][:, 0:n], AF.Exp, [PS_b[pg]], [egrb])
                qdT, qdTb = X.SBf[6], X.SB_b[6]
                tt(qdT[:, 0:n], qT[:, 0:n], egr[:, 0:n], ALU.mult, [qTb, egrb], [qdTb])
                pk_ = rbank(X)
                pkb = PS[pk_][:].bitcast(BF16)
                for nn in range(4):
                    tr(pkb[:, nn * 128:(nn + 1) * 128], kT[:, nn * 128:(nn + 1) * 128], ident_b[:], [kTb, cst], [PS_b[pk_]])
                kbg, kbgb = X.SBf[7], X.SB_b[7]
                ktm, ktmb = X.SBf[8], X.SB_b[8]
                pk3 = pkb[:, 0:512].rearrange("p (a d) -> p a d", a=4)
                bge_v = T["bge"][:, nb0:nb0 + 4, h_:h_ + 1].to_broadcast([128, 4, 128])
                ekt_v = ekt[:].rearrange("p (b e) -> p b e", e=8)[:, nb0:nb0 + 4, h_:h_ + 1].to_broadcast([128, 4, 128])
                tt(kbg[:, 0:512].rearrange("p (a d) -> p a d", a=4), pk3, bge_v, ALU.mult, [PS_b[pk_], tbb], [kbgb])
                tt(ktm[:, 0:512].rearrange("p (a d) -> p a d", a=4), pk3, ekt_v, ALU.mult, [PS_b[pk_], tbb], [ktmb])
                pv_ = rbank(X)
                pvb_ = PS[pv_][:].bitcast(BF16)
                for nn in range(4):
                    tr(pvb_[:, nn * 128:(nn + 1) * 128], vT[:, nn * 128:(nn + 1) * 128], ident_b[:], [vTb, cst], [PS_b[pv_]])
                vbt, vbtb = X.SBf[9], X.SB_b[9]
                be_v = T["beta"][:, nb0:nb0 + 4, h_:h_ + 1].to_broadcast([128, 4, 128])
                tt(vbt[:, 0:512].rearrange("p (a d) -> p a d", a=4), pvb_[:, 0:512].rearrange("p (a d) -> p a d", a=4),
                   be_v, ALU.mult, [PS_b[pv_], tbb], [vbtb])
                X.held = {pg, pA, pB}
                for nn in range(4):
                    sl = slice(nn * 128, (nn + 1) * 128)
                    mm(PS[pA][:, sl], kT[:, sl], kT[:, sl], True, True, [kTb], [PS_b[pA]])
                    mm(PS[pB][:, sl], kT[:, sl], qT[:, sl], True, True, [kTb, qTb], [PS_b[pB]])
                Dm, Dmb = X.SF[2], X.SF_b[2]
                DTm, DTmb = X.SF[3], X.SF_b[3]
                for nn in range(4):
                    sl = slice(nn * 128, (nn + 1) * 128)
                    gcc = T["gc"][:, nb0 + nn, h_:h_ + 1]
                    stt(Dm[:, sl], PS[pg][:, sl], gcc, maskSL[:], ALU.subtract, ALU.max, [PS_b[pg], tbb, cst], [Dmb])
                    stt(DTm[:, sl], PS[pg][:, sl], gcc, maskUn[:], ALU.subtract, ALU.min, [PS_b[pg], tbb, cst], [DTmb])
                act(Dm[:, 0:n], Dm[:, 0:n], AF.Exp, [Dmb], [Dmb], scale=-1.0)
                act(DTm[:, 0:n], DTm[:, 0:n], AF.Exp, [DTmb], [DTmb])
                m3 = lambda t_: t_[:, 0:512].rearrange("p (a d) -> p a d", a=4)
                X0, X0b = X.SBf[10], X.SB_b[10]
                for nn in range(4):
                    sl = slice(nn * 128, (nn + 1) * 128)
                    stt(X0[:, sl], PS[pA][:, sl], T["nbeta"][:, nb0 + nn, h_:h_ + 1], Dm[:, sl], ALU.mult, ALU.mult,
                        [PS_b[pA], tbb, Dmb], [X0b])
                AIT, AITb = X.SBf[4], X.SB_b[4]
                tt(AIT[:, 0:n], PS[pB][:, 0:n], DTm[:, 0:n], ALU.mult, [PS_b[pB], DTmb], [AITb])
                X.held = set()
                px = rbank(X)
                pxb_ = PS[px][:].bitcast(BF16)
                for nn in range(4):
                    tr(pxb_[:, nn * 128:(nn + 1) * 128], X0[:, nn * 128:(nn + 1) * 128], ident_b[:], [X0b, cst], [PS_b[px]])
                XT0, XT0b = X.SBf[11], X.SB_b[11]
                act(XT0[:, 0:n], pxb_[:, 0:n], AF.Copy, [PS_b[px]], [XT0b])
                RT, RTb = X.SBf[12], X.SB_b[12]
                Rm, Rmb = X.SBf[5], X.SB_b[5]
                U16 = mybir.dt.uint16
                bd3 = BD16[:].unsqueeze(1).to_broadcast([128, 4, 128])
                id3 = ident_b[:].unsqueeze(1).to_broadcast([128, 4, 128])
                Pt = [(X.SBf[0], X.SB_b[0]), (X.SBf[1], X.SB_b[1]), (X.SBf[2], X.SB_b[2]), (X.SBf[3], X.SB_b[3])]
                (Pc, Pcb), (PTc, PTcb) = Pt[0], Pt[1]
                tt(m3(Pc), m3(X0), bd3, ALU.mult, [X0b, cst], [Pcb])
                tt(m3(PTc), m3(XT0), bd3, ALU.mult, [XT0b, cst], [PTcb])
                tt(m3(Rm), m3(Pc), id3, ALU.add, [Pcb, cst], [Rmb])
                tt(m3(RT), m3(PTc), id3, ALU.add, [PTcb, cst], [RTb])

                def mm4(pbank, lhs, rhs, r_):
                    for nn in range(4):
                        sl = slice(nn * 128, (nn + 1) * 128)
                        mm(PS[pbank][:, sl], lhs[:, sl], rhs[:, sl], True, True, r_, [PS_b[pbank]])

                cur = 0
                for lvl in range(1, 4):
                    (Pn, Pnb), (PTn, PTnb) = Pt[2 - cur], Pt[3 - cur]
                    p1, p2 = rbank(X), rbank(X)
                    mm4(p1, PTc, Pc, [PTcb, Pcb])
                    mm4(p2, Pc, PTc, [PTcb, Pcb])
                    act(Pn[:, 0:n], PS[p1][:, 0:n], AF.Copy, [PS_b[p1]], [Pnb])
                    act(PTn[:, 0:n], PS[p2][:, 0:n], AF.Copy, [PS_b[p2]], [PTnb])
                    p3, p4 = rbank(X), rbank(X)
                    mm4(p3, PTn, Rm, [PTnb, Rmb])
                    mm4(p4, Pn, RT, [Pnb, RTb])
                    tt(Rm[:, 0:n], Rm[:, 0:n], PS[p3][:, 0:n], ALU.add, [Rmb, PS_b[p3]], [Rmb])
                    tt(RT[:, 0:n], RT[:, 0:n], PS[p4][:, 0:n], ALU.add, [RTb, PS_b[p4]], [RTb])
                    Pc, Pcb, PTc, PTcb = Pn, Pnb, PTn, PTnb
                    cur = 2 - cur
                (Wt, Wtb), (Ut, Utb) = Pt[0], Pt[1]
                for m_ in (32, 64, 128):
                    om = OFFM[m_][:].bitcast(U16).unsqueeze(1).to_broadcast([128, 4, 128])
                    pu = rbank(X)
                    mm4(pu, X0, RT, [X0b, RTb])
                    if m_ < 128:
                        pw_ = rbank(X)
                        mm4(pw_, XT0, Rm, [XT0b, Rmb])
                        act(Wt[:, 0:n], PS[pw_][:, 0:n], AF.Copy, [PS_b[pw_]], [Wtb])
                    act(Ut[:, 0:n], PS[pu][:, 0:n], AF.Copy, [PS_b[pu]], [Utb])
                    pv2_ = rbank(X)
                    mm4(pv2_, Rm, Ut, [Rmb, Utb])
                    if m_ < 128:
                        pz = rbank(X)
                        mm4(pz, RT, Wt, [RTb, Wtb])
                        A("dve", lambda h, om=om, pz=pz: h.copy_predicated(out=m3(Rm), mask=om, data=m3(PS[pz])),
                          [PS_b[pz], cst], [Rmb], dur=700.0)
                    A("dve", lambda h, om=om, pv2_=pv2_: h.copy_predicated(out=m3(RT), mask=om, data=m3(PS[pv2_])),
                      [PS_b[pv2_], cst], [RTb], dur=700.0)
                pw = rbank(X)
                for nn in range(4):
                    sl = slice(nn * 128, (nn + 1) * 128)
                    mm(PS[pw][:, sl], kbg[:, sl], RT[:, sl], True, True, [kbgb, RTb], [PS_b[pw]])
                wnT, wnTb = X.SBf[3], X.SB_b[3]
                act(wnT[:, 0:n], PS[pw][:, 0:n], AF.Copy, [PS_b[pw]], [wnTb], scale=-1.0)
                X.held = {pO}
                for nn in range(4):
                    sl = slice(nn * 128, (nn + 1) * 128)
                    pn_ = rbank(X)
                    mm(PS[pn_][:, 0:128], RT[:, sl], vbt[:, sl], True, False, [RTb, vbtb], [PS_b[pn_]])
                    mm(PS[pn_][:, 0:128], wnT[:, sl], X.S_b[:], False, True, [wnTb, X.Sbb], [PS_b[pn_]])
                    vnw, vnwb = X.SBf[2][:, (nn % 2) * 128:(nn % 2) * 128 + 128], X.SB_b[2]
                    act(vnw[:, 0:128], PS[pn_][:, 0:128], AF.Copy, [PS_b[pn_]], [vnwb])
                    mm(PS[pO][:, sl], X.S_b[:], qdT[:, sl], True, False, [X.Sbb, qdTb], [PS_b[pO]])
                    mm(PS[pO][:, sl], vnw[:, 0:128], AIT[:, sl], False, True, [vnwb, AITb], [PS_b[pO]])
                    ps_ = rbank(X)
                    mm(PS[ps_][:, 0:128], ktm[:, sl], vnw[:, 0:128], True, True, [ktmb, vnwb], [PS_b[ps_]])
                    eglc = egl[:, (nb0 + nn) * 8 + h_:(nb0 + nn) * 8 + h_ + 1]
                    stt(X.S_f[:], X.S_f[:], eglc, PS[ps_][:, 0:128], ALU.mult, ALU.add, [X.Sfb, tbb, PS_b[ps_]], [X.Sfb])
                    act(X.S_b[:], X.S_f[:], AF.Copy, [X.Sfb], [X.Sbb])
                if ti == 3:
                    dma("sp", o_pS[l, h_], X.S_f[:], [X.Sfb])
                of, ofb = X.SF[0], X.SF_b[0]
                act(of[:, 0:n], PS[pO][:, 0:n], AF.Copy, [PS_b[pO]], [ofb])
                X.held = set()

            sq2, sq2b = X.SBf[3], X.SB_b[3]
            act(sq2[:, 0:n], of[:, 0:n], AF.Square, [ofb], [sq2b])
            pq = rbank(X)
            mm(PS[pq][:, 0:n], ones_b[:], sq2[:, 0:n], True, True, [cst, sq2b], [PS_b[pq]])
            rs2, rs2b = X.SF[1], X.SF_b[1]
            act(rs2[:, 0:n], PS[pq][:, 0:n], AF.Ln, [PS_b[pq]], [rs2b], bias=RMS_EPS, scale=1.0 / 128)
            act(rs2[:, 0:n], rs2[:, 0:n], AF.Exp, [rs2b], [rs2b], scale=-0.5)
            stt(of[:, 0:n], of[:, 0:n], pv[:, 272:273], rs2[:, 0:n], ALU.mult, ALU.mult, [ofb, pvb, rs2b], [ofb])
            pb = proj(6)
            zs, zsb = X.SF[2], X.SF_b[2]
            act(zs[:, 0:n], PS[pb][:, 0:n], AF.Sigmoid, [PS_b[pb], pvb], [zsb], bias=bias(6))
            stt(zs[:, 0:n], PS[pb][:, 0:n], bias(6), zs[:, 0:n], ALU.add, ALU.mult, [PS_b[pb], pvb, zsb], [zsb])
            tt(of[:, 0:n], of[:, 0:n], zs[:, 0:n], ALU.mult, [ofb, zsb], [ofb])
            pb = proj(7)
            act(zs[:, 0:n], PS[pb][:, 0:n], AF.Sigmoid, [PS_b[pb], pvb], [zsb], bias=bias(7))
            tt(of[:, 0:n], of[:, 0:n], zs[:, 0:n], ALU.mult, [ofb, zsb], [ofb])
            mt, mtb = X.SBf[6], X.SB_b[6]
            tt(mt[:, 0:n], of[:, 0:n], ma[:, 0:n], ALU.add, [ofb, mab], [mtb])
            if X.tid == 1:
                XA = CTX[0]
                for m in range(8):
                    pb = rbank(X)
                    mm(PS[pb][:, 0:n], XA.wo_c[:, m * 128:(m + 1) * 128], XA.SBf[6][:, 0:n], True, False,
                       [XA.wob, XA.SB_b[6]], [PS_b[pb]])
                    mm(PS[pb][:, 0:n], X.wo_c[:, m * 128:(m + 1) * 128], mt[:, 0:n], False, True, [X.wob, mtb], [PS_b[pb]])
                    tt(xres[:, m, c0:c0 + n], PS[pb][:, 0:n], xres[:, m, c0:c0 + n], ALU.add,
                       [PS_b[pb], xres_b[m][ti]], [xres_b[m][ti]])
            yield


    def mixer(l):
        gate_tables(l)
        nops0 = len(P.all)
        prior = []
        for row in mbuf_b:
            for b in row:
                prior.extend(b.r)
                if b.w is not None:
                    prior.append(b.w)
        for b in t1_bufs:
            b.w = None
            b.r = list(prior)
        for pair in range(4):
            gens = [chunk_body(l, 2 * pair, CTX[0]), chunk_body(l, 2 * pair + 1, CTX[1])]
            live = [True, True]
            while any(live):
                for t_ in range(2):
                    if live[t_]:
                        try:
                            next(gens[t_])
                        except StopIteration:
                            live[t_] = False
        dma("sp", o_ph[l], hcar[:], [CTX[0].smb, CTX[1].smb])
        if l == 0:
            print("mixer ops per (chunk,tile):", (len(P.all) - nops0) / 40.0)
        post = []
        for b in t1_bufs:
            post.extend(b.r)
            if b.w is not None:
                post.append(b.w)
        for row in mbuf_b:
            for b in row:
                b.r = list(b.r) + post

    for l in range(n_layers):
        layer_params(l)
        ffn(2 * l)
        layernorm(0, False)
        mixer(l)
        layernorm(1, False)
        ffn(2 * l + 1)
        layernorm(2, l == n_layers - 1)
    P.emit(scheduled=SCHED)
    print("ops:", len(P.all), "est total us:", getattr(P, "est_total", 0) / 1e3)
    st.close()
    return nc


def _prep_shared(inp):
    f = lambda a: np.ascontiguousarray(np.asarray(a, dtype=np.float32))
    w1 = f(np.stack([inp["ffn1_w1"], inp["ffn2_w1"]], 1).reshape(2 * L, D, DFF))
    w3 = f(np.stack([inp["ffn1_w3"], inp["ffn2_w3"]], 1).reshape(2 * L, D, DFF))
    w2 = f(np.stack([inp["ffn1_w2"], inp["ffn2_w2"]], 1).reshape(2 * L, DFF, D))
    w_in = np.asarray(inp["w_in"], np.float32)
    b_in = np.asarray(inp["b_in"], np.float32)
    offs = [0, 1024, 6160, 2048, 3072, 4096, 5120, 7184]
    win = np.empty((L, 8, D, 1024), np.float32)
    for c in range(8):
        for bi, o in enumerate(offs):
            win[:, c, :, bi * 128:(bi + 1) * 128] = w_in[:, :, o + c * 128:o + (c + 1) * 128]
    wab = f(w_in[:, :, 6144:6160])
    gatew = np.zeros((L, 2, 8, 128, 128), np.float32)
    for gi, nm in enumerate(("rg_wr", "rg_wi")):
        w = np.asarray(inp[nm], np.float32)
        for c in range(8):
            gatew[:, gi, c, 0:64, 0:64] = w[:, 2 * c]
            gatew[:, gi, c, 64:128, 64:128] = w[:, 2 * c + 1]
    pvec = np.empty((L, 128, NPV), np.float32)
    fm = lambda v: np.asarray(v, np.float32).reshape(L, -1, 128).transpose(0, 2, 1)
    for i, (g, b) in enumerate((("ln1_g", "ln1_b"), ("ln2_g", "ln2_b"), ("ln3_g", "ln3_b"))):
        pvec[:, :, 16 * i:16 * i + 8] = fm(inp[g])
        pvec[:, :, 16 * i + 8:16 * i + 16] = fm(inp[b])
    for c in range(8):
        for bi, o in enumerate(offs):
            pvec[:, :, 48 + c * 8 + bi] = b_in[:, o + c * 128:o + (c + 1) * 128]
    rcw = np.asarray(inp["rg_conv_w"], np.float32)
    for j in range(4):
        pvec[:, :, 112 + j * 8:112 + (j + 1) * 8] = fm(rcw[:, j])
    pvec[:, :, 144:152] = fm(inp["rg_conv_b"])
    pvec[:, :, 152:160] = fm(inp["rg_br"])
    pvec[:, :, 160:168] = fm(inp["rg_bi"])
    pvec[:, :, 168:176] = fm(inp["rg_lambda"])
    gcw = np.asarray(inp["gdn_conv_w"], np.float32)
    for j in range(4):
        pvec[:, :, 176 + j * 24:176 + (j + 1) * 24] = fm(gcw[:, j])
    pvec[:, :, 272] = np.asarray(inp["gdn_norm_w"], np.float32)
    prow = np.concatenate([b_in[:, 6144:6160], np.asarray(inp["gdn_a_log"], np.float32),
                           np.asarray(inp["gdn_dt_bias"], np.float32)], axis=1).reshape(L, 1, 32)
    return {"w1": w1, "w3": w3, "w2": w2, "win": win, "wab": wab, "wo": f(inp["w_o"]), "gatew": gatew,
            "pvec": f(pvec), "prow": f(prow)}


def _prep_core(inp, i):
    xp = np.asarray(inp["x_prompt"], np.float32)[i]
    xs = np.asarray(inp["x_sample"], np.float32)[i * NS:(i + 1) * NS, 0]
    x = np.concatenate([xp, xs], 0)
    xin = np.ascontiguousarray(x.T.reshape(8, 128, TOK).transpose(1, 0, 2))
    sl = slice(i * NS, (i + 1) * NS)
    h0 = np.asarray(inp["state_rglru_h"], np.float32)[:, sl]
    sh0 = np.ascontiguousarray(h0.transpose(0, 2, 1).reshape(L, 8, 128, NS))
    rc = np.asarray(inp["state_rglru_conv"], np.float32)[:, sl]
    src0 = np.ascontiguousarray(rc.transpose(0, 3, 2, 1).reshape(L, 8, 128, 3, NS))
    gc = np.asarray(inp["state_gdn_conv"], np.float32)[:, sl]
    sgc0 = np.ascontiguousarray(gc.transpose(0, 3, 2, 1).reshape(L, 24, 128, 3, NS))
    sS0 = np.ascontiguousarray(np.asarray(inp["state_gdn_S"], np.float32)[:, sl])
    return {"xin": xin, "sh0": sh0, "src0": src0, "sgc0": sgc0, "sS0": sS0}


_NC_CACHE = {}


def kernel(**inp):
    if "nc" not in _NC_CACHE:
        _NC_CACHE["nc"] = build_program()
    nc = _NC_CACHE["nc"]
    shared = _prep_shared(inp)
    in_maps = []
    for i in range(8):
        m = dict(shared)
        m.update(_prep_core(inp, i))
        in_maps.append(m)
    res = run_bass_kernel_spmd(nc, in_maps, core_ids=list(range(8)))
    R = res.results
    Bp = 8
    y_prompt = np.empty((Bp, NP, D), np.float32)
    y_sample = np.empty((8 * NS, 1, D), np.float32)
    p_h = np.empty((L, Bp, D), np.float32)
    p_rc = np.empty((L, Bp, 3, D), np.float32)
    p_S = np.empty((L, Bp, 8, 128, 128), np.float32)
    p_gc = np.empty((L, Bp, 3, 3072), np.float32)
    s_h = np.empty((L, 8 * NS, D), np.float32)
    s_rc = np.empty((L, 8 * NS, 3, D), np.float32)
    s_S = np.empty((L, 8 * NS, 8, 128, 128), np.float32)
    s_gc = np.empty((L, 8 * NS, 3, 3072), np.float32)
    for i in range(8):
        r = R[i]
        y = np.asarray(r["yout"]).transpose(1, 0, 2).reshape(D, TOK).T
        y_prompt[i] = y[:NP]
        sl = slice(i * NS, (i + 1) * NS)
        y_sample[sl, 0] = y[NP:]
        p_h[:, i] = np.asarray(r["o_ph"]).transpose(0, 2, 1).reshape(L, D)
        p_rc[:, i] = np.asarray(r["o_prc"]).transpose(0, 3, 1, 2).reshape(L, 3, D)
        p_S[:, i] = np.asarray(r["o_pS"])
        p_gc[:, i] = np.asarray(r["o_pgc"]).transpose(0, 3, 1, 2).reshape(L, 3, 3072)
        s_h[:, sl] = np.asarray(r["o_sh"]).transpose(0, 3, 1, 2).reshape(L, NS, D)
        s_rc[:, sl] = np.asarray(r["o_src"]).transpose(0, 4, 3, 1, 2).reshape(L, NS, 3, D)
        s_S[:, sl] = np.asarray(r["o_sS"])
        s_gc[:, sl] = np.asarray(r["o_sgc"]).transpose(0, 4, 3, 1, 2).reshape(L, NS, 3, 3072)
    return (y_prompt, y_sample, p_h, p_rc, p_S, p_gc, s_h, s_rc, s_S, s_gc)
```

```python
import math
from contextlib import ExitStack
import numpy as np
import concourse.bass as bass
import concourse.mybir as mybir
from concourse.bass_utils import run_bass_kernel_spmd

F32 = mybir.dt.float32
BF16 = mybir.dt.bfloat16
AF = mybir.ActivationFunctionType
ALU = mybir.AluOpType
AX = mybir.AxisListType

L = 4
D = 1024
NP = 2048
NS = 16
TOK = NP + NS
DFF = 2816
NJ = DFF // 128
ALPHA = (2 * L) ** 0.25
LN_EPS = 1e-5
RMS_EPS = 1e-6
TILES = [(0, 512), (512, 512), (1024, 512), (1536, 512), (2048, 16)]
NPV = 273
ENGS = ("pe", "act", "dve", "pool", "sp")


class Buf:
    __slots__ = ("w", "r", "excl")

    def __init__(self, excl=False):
        self.w = None
        self.r = []
        self.excl = excl


class Prog:
    def __init__(self, nc):
        self.nc = nc
        self.all = []
        self.n_ch = {"sp": 28, "pool": 12}

    def _collect(self, eng, reads, writes, is_dma):
        deps = {}
        for b in reads:
            ev = b.w
            if ev is None:
                continue
            ee = self.all[ev]
            need = not (eng == "pe" and ee[0] == "pe" and ee[1] == "c")
            deps[ev] = deps.get(ev, False) or need
        for b in writes:
            evs = list(b.r)
            if b.w is not None:
                evs.append(b.w)
            for ev in evs:
                ee = self.all[ev]
                same = (ee[0] == eng and ee[1] == "c" and not is_dma)
                deps[ev] = deps.get(ev, False) or (not same)
        return list(deps.items())

    def _add(self, eng, kind, fn, reads, writes, dur, fam):
        ex = [b for b in reads if b.excl and b not in writes]
        if ex:
            writes = list(writes) + ex
        deps = self._collect(eng, reads, writes, kind == "d")
        idx = len(self.all)
        self.all.append([eng, kind, fn, deps, dur, fam])
        for b in reads:
            b.r.append(idx)
        for b in writes:
            b.w = idx
            b.r = []
        return idx

    def op(self, eng, fn, reads=(), writes=(), dur=300.0, fam=None):
        return self._add(eng, "c", fn, reads, writes, dur, fam)

    def dma(self, eng, fn, reads=(), writes=(), dur=3000.0):
        return self._add(eng, "d", fn, reads, writes, dur, None)

    def schedule(self):
        import heapq
        ops = self.all
        n = len(ops)
        succ = [[] for _ in range(n)]
        npred = [0] * n
        last_q = {}
        for i, o in enumerate(ops):
            if o[0] in QUEUE_INORDER:
                p_ = last_q.get(o[0])
                if p_ is not None and all(d != p_ for d, _ in o[3]):
                    o[3].append((p_, False) if ops[p_][1] == "c" and o[1] == "c" else (p_, None))
                last_q[o[0]] = i
        for i, o in enumerate(ops):
            npred[i] = len(o[3])
            for d, _ in o[3]:
                succ[d].append(i)
        issue_only = {}
        for j_, o in enumerate(ops):
            for d, need in o[3]:
                if need is None:
                    issue_only[(d, j_)] = True
        ready_t = [0.0] * n
        fin = [0.0] * n
        eng_free = {e: 0.0 for e in ENGS}
        eng_fam = {e: None for e in ENGS}
        order = {e: [] for e in ENGS}
        future = {e: [] for e in ENGS}
        avail = {e: [] for e in ENGS}
        for i in range(n):
            if npred[i] == 0:
                heapq.heappush(future[ops[i][0]], (0.0, i))
        done = 0
        XLAT = 250.0
        while done < n:
            best = None
            for e in ENGS:
                t = eng_free[e]
                fut, av = future[e], avail[e]
                while fut and fut[0][0] <= t:
                    heapq.heappush(av, heapq.heappop(fut)[1])
                if av:
                    c_ = (t, av[0], e, True)
                elif fut:
                    c_ = (fut[0][0], fut[0][1], e, False)
                else:
                    continue
                if best is None or (c_[0], c_[1]) < (best[0], best[1]):
                    best = c_
            st, i, e, from_av = best
            if from_av:
                av = avail[e]
                if e == "act" and eng_fam[e] is not None and len(av) > 1 and ops[i][5] not in (None, eng_fam[e]):
                    cands = heapq.nsmallest(12, av)
                    alt = [c for c in cands if ops[c][5] in (None, eng_fam[e]) and c - i < 400]
                    if alt:
                        i = alt[0]
                        av.remove(i)
                        heapq.heapify(av)
                    else:
                        heapq.heappop(av)
                else:
                    heapq.heappop(av)
            else:
                heapq.heappop(future[e])
            o = ops[i]
            dur = o[4]
            if o[1] == "c":
                if o[5] is not None and eng_fam[e] is not None and eng_fam[e] != o[5]:
                    st += 1300.0
                if o[5] is not None:
                    eng_fam[e] = o[5]
                eng_free[e] = st + dur
                fin[i] = st + dur
            else:
                eng_free[e] = st + 120.0
                fin[i] = st + dur
            order[e].append(i)
            done += 1
            for j in succ[i]:
                oj = ops[j]
                lat = XLAT if (oj[0] != e or o[1] == "d") else 60.0
                t = fin[i] + lat
                if issue_only.get((i, j)):
                    t = st
                if t > ready_t[j]:
                    ready_t[j] = t
                npred[j] -= 1
                if npred[j] == 0:
                    heapq.heappush(future[oj[0]], (ready_t[j], j))
        self.order = order
        self.est_total = max(eng_free.values())

    def emit(self, scheduled=True):
        nc = self.nc
        ops = self.all
        if scheduled:
            self.schedule()
            order = self.order
        else:
            order = {e: [i for i, o in enumerate(ops) if o[0] == e] for e in ENGS}
        pos = {}
        for e in ENGS:
            for p_, i in enumerate(order[e]):
                pos[i] = p_
        ch_of, ch_val = {}, {}
        ch_next = {e: 0 for e in self.n_ch}
        ch_count = {}
        ch_prev = {}
        for e in self.n_ch:
            for i in order[e]:
                if ops[i][1] != "d":
                    continue
                c = ch_next[e]
                ch_next[e] = (c + 1) % self.n_ch[e]
                prev = ch_count.get((e, c), 0)
                ch_prev[i] = (c, prev)
                ch_count[(e, c)] = prev + 16
                ch_of[i] = c
                ch_val[i] = prev + 16
        marked = {e: set() for e in ENGS}
        waits_of = {}
        for e in ENGS:
            waited = {}
            for i in order[e]:
                o = ops[i]
                wl = []
                for d, need in o[3]:
                    od = ops[d]
                    if need is None:
                        assert od[0] == e and pos[d] < pos[i]
                        continue
                    if od[1] == "d":
                        key, val = ("d", od[0], ch_of[d]), ch_val[d]
                    else:
                        if not need:
                            assert od[0] == e and pos[d] < pos[i]
                            continue
                        key, val = ("e", od[0]), pos[d]
                    if val <= waited.get(key, -1):
                        continue
                    waited[key] = val
                    wl.append((key, val, d))
                    if key[0] == "e":
                        marked[od[0]].add(d)
                if o[1] == "d":
                    c, prev = ch_prev[i]
                    key = ("d", e, c)
                    if prev and prev > waited.get(key, -1):
                        waited[key] = prev
                        wl.append((key, prev, None))
                waits_of[i] = wl
        cnt = {}
        for e in ENGS:
            m = sorted(marked[e], key=lambda i: pos[i])
            cnt[e] = {i: k + 1 for k, i in enumerate(m)}
        with ExitStack() as st:
            esem = {e: st.enter_context(nc.semaphore("s_" + e)) for e in ENGS}
            dsem = {}
            for e, n_ in self.n_ch.items():
                for c in range(n_):
                    dsem[(e, c)] = st.enter_context(nc.semaphore("d_%s%d" % (e, c)))
            block = st.enter_context(nc.Block())
            n_ch = self.n_ch

            def run(e, h):
                for i in order[e]:
                    o = ops[i]
                    for key, val, d in waits_of[i]:
                        if key[0] == "e":
                            h.wait_ge(esem[key[1]], cnt[key[1]][d])
                        else:
                            h.wait_ge(dsem[(key[1], key[2])], val)
                    ins = o[2](h)
                    if o[1] == "d":
                        ins.then_inc(dsem[(e, ch_of[i])], 16)
                    elif i in cnt[e]:
                        ins.then_inc(esem[e], 1)
                if e in n_ch:
                    for c in range(n_ch[e]):
                        v = ch_count.get((e, c), 0)
                        if v:
                            h.wait_ge(dsem[(e, c)], v)

            @block.tensor
            def _(h):
                run("pe", h)

            @block.scalar
            def _(h):
                run("act", h)

            @block.vector
            def _(h):
                run("dve", h)

            @block.gpsimd
            def _(h):
                run("pool", h)

            @block.sync
            def _(h):
                run("sp", h)


SCHED = True
QUEUE_INORDER = ()


def build_program(n_layers=L, debug=None):
    nc = bass.Bass("TRN2", target_bir_lowering=False)
    st = ExitStack()

    def din(name, shape):
        return nc.dram_tensor(name, list(shape), F32, kind="ExternalInput").ap()

    def dout(name, shape):
        return nc.dram_tensor(name, list(shape), F32, kind="ExternalOutput").ap()

    xin = din("xin", [128, 8, TOK])
    w1d = din("w1", [2 * L, D, DFF])
    w3d = din("w3", [2 * L, D, DFF])
    w2d = din("w2", [2 * L, DFF, D])
    wind = din("win", [L, 8, D, 1024])
    wabd = din("wab", [L, D, 16])
    wod = din("wo", [L, D, D])
    gwd = din("gatew", [L, 2, 8, 128, 128])
    pvd = din("pvec", [L, 128, NPV])
    prd = din("prow", [L, 1, 32])
    sh0d = din("sh0", [L, 8, 128, NS])
    src0d = din("src0", [L, 8, 128, 3, NS])
    sgc0d = din("sgc0", [L, 24, 128, 3, NS])
    sS0d = din("sS0", [L, NS, 8, 128, 128])
    yout = dout("yout", [128, 8, TOK])
    o_ph = dout("o_ph", [L, 128, 8])
    o_prc = dout("o_prc", [L, 8, 128, 3])
    o_pS = dout("o_pS", [L, 8, 128, 128])
    o_pgc = dout("o_pgc", [L, 24, 128, 3])
    o_sh = dout("o_sh", [L, 8, 128, NS])
    o_src = dout("o_src", [L, 8, 128, 3, NS])
    o_sS = dout("o_sS", [L, NS, 8, 128, 128])
    o_sgc = dout("o_sgc", [L, 24, 128, 3, NS])
    dbg = {}
    if debug:
        for nm, shp in debug.items():
            dbg[nm] = dout("dbg_" + nm, shp)

    def sb(name, shape, dt):
        return st.enter_context(nc.sbuf_tensor(name, list(shape), dt))

    xres = sb("xres", [128, 8, TOK], F32)
    xbf = sb("xbf", [128, 8, TOK], BF16)
    mbuf = sb("mbuf", [128, 8, TOK], BF16)
    wsl = [sb("wsl%d" % i, [128, 4096], BF16) for i in range(4)]
    NSF, NSB = 6, 13
    SF = [sb("sf%d" % i, [128, 512], F32) for i in range(NSF)]
    SBf = [sb("sbf%d" % i, [128, 520 if i < 2 else 512], BF16) for i in range(NSB)]
    wo_c0 = sb("wo_c0", [128, 1024], BF16)
    ident_f = sb("ident_f", [128, 128], F32)
    ident_b = sb("ident_b", [128, 128], BF16)
    ones_f = sb("ones_f", [128, 128], F32)
    ones_b = sb("ones_b", [128, 128], BF16)
    maskU = sb("maskU", [128, 128], F32)
    maskSL = sb("maskSL", [128, 128], F32)
    maskUn = sb("maskUn", [128, 128], F32)
    sel_last = sb("sel_last", [128, 128], F32)
    BD16 = sb("BD16", [128, 128], BF16)
    OFFM = {m_: sb("OFF%d" % m_, [128, 128], BF16) for m_ in (32, 64, 128)}
    pv = sb("pv_t", [128, NPV], F32)
    pv2 = sb("pv2", [128, 48], F32)
    nsp = sb("nsp", [128, 16], F32)
    prow = sb("prow_t", [128, 32], F32)
    aexp = sb("aexp", [128, 8], F32)
    wab_b = sb("wab_b", [128, 8, 16], BF16)
    gw_b = sb("gw_b", [128, 2, 128], BF16)
    TB = {nm: sb("tb_" + nm, [128, 17, 8], F32) for nm in
          ("beta", "g", "gc", "egc", "bge", "nbeta")}
    ksq_t = sb("ksq_t", [128, 2 * NS], F32)
    glb = sb("glb", [128, 128], F32)
    egl = sb("egl", [128, 128], F32)
    ekt = sb("ekt", [128, 128], F32)
    S_f = sb("S_f", [128, 128], F32)
    S_b = sb("S_b", [128, 128], BF16)
    hcar = sb("hcar", [128, 8], F32)
    hist = sb("hist", [128, 4, 4], BF16)
    pco = sb("pco", [128, 4, 3], F32)
    dgw = sb("dgw", [128, 4, 128], BF16)
    sst = sb("sst", [128, 4, 3, NS], F32)
    sstb = sb("sstb", [128, 4, 3, NS], BF16)
    sso = sb("sso", [128, 4, 3, NS], F32)
    sh0 = sb("sh0t", [128, NS], F32)
    sho = sb("shot", [128, NS], F32)
    srow = {nm: sb("srow_" + nm, [128, 128], F32) for nm in ("eg", "beta", "nbe")}
    srow["qk"] = sb("srow_qk", [128, NS], F32)
    skq = sb("skq", [128, 2, NS], F32)
    PS = [st.enter_context(nc.psum_tensor("ps%d" % i, [128, 512], F32)) for i in range(8)]
    print("sbuf bytes remaining/partition:", nc.sbuf_bytes_remaining)

    P = Prog(nc)
    B_ = Buf
    xres_b = [[B_() for _ in TILES] for _ in range(8)]
    xbf_b = [[B_() for _ in TILES] for _ in range(8)]
    mbuf_b = [[B_() for _ in TILES] for _ in range(8)]
    wsl_b = [B_() for _ in range(4)]
    SF_b = [B_() for _ in range(NSF)]
    SB_b = [B_() for _ in range(NSB)]
    PS_b = [B_(True) for _ in range(8)]
    cst = B_()
    pvb = B_()
    tbb = B_()
    gwb = B_()
    wabb = B_()
    dgwb = B_()
    ksqb = B_()
    smb = B_()
    Sfb, Sbb = B_(), B_()
    sstb_b, ssob = B_(), B_()
    Sin_b = [B_() for _ in range(2)]
    Sout_b = [B_() for _ in range(2)]
    srb = B_()

    def fsz(ap):
        sh = ap.shape
        n_ = 1
        for d_ in sh[1:]:
            n_ *= d_
        return n_

    ACT_FAM = {AF.Exp: "el", AF.Ln: "el", AF.Sigmoid: "sg", AF.Silu: "si", AF.Sqrt: "sq"}

    def A(e, fn, r=(), w=(), dur=None, fam=None):
        if dur is None:
            dur = {"pe": 150.0, "act": 400.0, "dve": 400.0, "pool": 600.0}.get(e, 300.0)
        P.op(e, fn, reads=r, writes=w, dur=dur, fam=fam)

    def act(out, in_, func, r, w, bias=None, scale=None):
        kw = {}
        if bias is not None:
            kw["bias"] = bias
        if scale is not None:
            kw["scale"] = scale
        A("act", lambda h: h.activation(out=out, in_=in_, func=func, **kw), r, w,
          dur=220.0 + 0.8 * fsz(out), fam=ACT_FAM.get(func))

    def mm(out, lhsT, rhs, start, stop, r, w):
        n_ = fsz(rhs)
        f32 = 4.0 if rhs.dtype == F32 else 1.0
        A("pe", lambda h: h.matmul(out, lhsT=lhsT, rhs=rhs, start=start, stop=stop), r, w,
          dur=(45.0 + max(64, n_) * 0.5 * f32) if n_ > 128 else (100.0 * f32))

    def tt(out, in0, in1, op, r, w, eng="dve"):
        A(eng, lambda h: h.tensor_tensor(out=out, in0=in0, in1=in1, op=op), r, w,
          dur=(110.0 + 1.1 * fsz(out)) if eng == "dve" else (300.0 + 1.7 * fsz(out)))

    def ts(out, in0, s1, s2, op0, op1, r, w, eng="dve"):
        if s2 is None:
            A(eng, lambda h: h.tensor_scalar(out=out, in0=in0, scalar1=s1, scalar2=None, op0=op0), r, w,
              dur=110.0 + 1.0 * fsz(out))
        else:
            A(eng, lambda h: h.tensor_scalar(out=out, in0=in0, scalar1=s1, scalar2=s2, op0=op0, op1=op1), r, w,
              dur=110.0 + 1.0 * fsz(out))

    def stt(out, in0, scalar, in1, op0, op1, r, w):
        A("dve", lambda h: h.scalar_tensor_tensor(out=out, in0=in0, scalar=scalar, in1=in1, op0=op0, op1=op1), r, w,
          dur=110.0 + 1.1 * fsz(out))

    def dma(eng, out, in_, r=(), w=()):
        P.dma(eng, lambda h: h.dma_start(out=out, in_=in_), reads=r, writes=w,
              dur=2200.0 + fsz(out) * out.shape[0] * 4 / 150.0)

    def cp(out, in_, r, w, eng="dve"):
        A(eng, lambda h: h.tensor_copy(out=out, in_=in_), r, w,
          dur=(100.0 + 1.0 * fsz(out)) if eng == "dve" else (300.0 + 0.9 * fsz(out)))

    def mset(ap, val, w, eng="dve"):
        A(eng, lambda h: h.memset(ap, val), (), w)

    def tr(out, in_, idn, r, w):
        A("pe", lambda h: h.transpose(out, in_, idn), r, w, dur=130.0)

    def scan(out, d0, d1, init, r, w):
        A("dve", lambda h: h.tensor_tensor_scan(out=out, data0=d0, data1=d1, initial=init,
                                                op0=ALU.mult, op1=ALU.add), r, w, dur=110.0 + 2.1 * fsz(out))

    def dbg_dump(nm, ap, r):
        if nm in dbg:
            dma("sp", dbg[nm], ap, r)

    A("pool", lambda h: h.memset(ident_f[:], 1.0), (), [cst])
    A("pool", lambda h: h.affine_select(out=ident_f[:], in_=ident_f[:], pattern=[[1, 128]],
                                        compare_op=ALU.is_equal, fill=0.0, base=0, channel_multiplier=-1), [cst], [cst])
    A("pool", lambda h: h.memset(ones_f[:], 1.0), (), [cst])
    A("pool", lambda h: h.memset(ones_b[:], 1.0), (), [cst])
    A("pool", lambda h: h.memset(maskU[:], 1.0), (), [cst])
    A("pool", lambda h: h.affine_select(out=maskU[:], in_=maskU[:], pattern=[[1, 128]],
                                        compare_op=ALU.is_ge, fill=0.0, base=0, channel_multiplier=-1), [cst], [cst])
    A("pool", lambda h: h.memset(maskSL[:], 0.0), (), [cst])
    A("pool", lambda h: h.affine_select(out=maskSL[:], in_=maskSL[:], pattern=[[-1, 128]],
                                        compare_op=ALU.is_gt, fill=1e30, base=0, channel_multiplier=1), [cst], [cst])
    A("pool", lambda h: h.memset(maskUn[:], 0.0), (), [cst])
    A("pool", lambda h: h.affine_select(out=maskUn[:], in_=maskUn[:], pattern=[[1, 128]],
                                        compare_op=ALU.is_ge, fill=-1e30, base=0, channel_multiplier=-1), [cst], [cst])
    A("pool", lambda h: h.memset(sel_last[:], 1.0), (), [cst])
    A("pool", lambda h: h.affine_select(out=sel_last[:], in_=sel_last[:], pattern=[[0, 128]],
                                        compare_op=ALU.is_equal, fill=0.0, base=-127, channel_multiplier=1), [cst], [cst])
    A("dve", lambda h: h.tensor_copy(out=ident_b[:], in_=ident_f[:]), [cst], [cst])
    def mk_bd(b_, dst_tile, col0):
        nb_ = 128 // b_
        E = SF[5][0:nb_, 0:128]
        A("pool", lambda h: h.memset(E, 1.0), (), [SF_b[5]])
        A("pool", lambda h: h.affine_select(out=E, in_=E, pattern=[[1, 128]], compare_op=ALU.is_ge, fill=0.0,
                                            base=0, channel_multiplier=-b_), [SF_b[5]], [SF_b[5]])
        A("pool", lambda h: h.affine_select(out=E, in_=E, pattern=[[-1, 128]], compare_op=ALU.is_ge, fill=0.0,
                                            base=b_ - 1, channel_multiplier=b_), [SF_b[5]], [SF_b[5]])
        mm(PS[7][:, col0:col0 + 128], E, E, True, True, [SF_b[5]], [PS_b[7]])
    mk_bd(16, None, 0)
    mk_bd(32, None, 128)
    mk_bd(64, None, 256)
    act(BD16[:], PS[7][:, 0:128], AF.Copy, [PS_b[7]], [cst])
    bdf = SF[4]
    act(bdf[:, 0:384], PS[7][:, 0:384], AF.Copy, [PS_b[7]], [SF_b[4]])
    tt(OFFM[32][:], bdf[:, 128:256], bdf[:, 0:128], ALU.subtract, [SF_b[4]], [cst])
    tt(OFFM[64][:], bdf[:, 256:384], bdf[:, 128:256], ALU.subtract, [SF_b[4]], [cst])
    ts(OFFM[128][:], bdf[:, 256:384], -1.0, 1.0, ALU.mult, ALU.add, [SF_b[4]], [cst])

    for k in range(8):
        for ti, (c0, n) in enumerate(TILES):
            dma("sp", xres[:, k, c0:c0 + n], xin[:, k, c0:c0 + n], (), [xres_b[k][ti]])
            act(xbf[:, k, c0:c0 + n], xres[:, k, c0:c0 + n], AF.Copy, [xres_b[k][ti]], [xbf_b[k][ti]])
            ts(xres[:, k, c0:c0 + n], xres[:, k, c0:c0 + n], ALPHA, None, ALU.mult, None,
               [xres_b[k][ti]], [xres_b[k][ti]])

    wctr = [0]

    def wslot():
        i = wctr[0] % 4
        wctr[0] += 1
        return i

    def ffn(fi):
        w1v = w1d[fi].rearrange("(k p) n -> p k n", p=128)
        w3v = w3d[fi].rearrange("(k p) n -> p k n", p=128)
        w2v = w2d[fi].rearrange("(j p) n -> p j n", p=128)
        ftmp_b = SF_b[0:5]
        FT = [(0, 413), (413, 413), (826, 413), (1239, 413), (1652, 412)]

        def ov(bl, k_, c0_, n_):
            return [bl[k_][t_] for t_, (a_, w_) in enumerate(FT) if a_ < c0_ + n_ and c0_ < a_ + w_]
        subs = [(0, 4), (4, 4), (8, 4), (12, 4), (16, 4), (20, 2)]
        groups = [(0, 1), (2, 3), (4, 5)]

        def load13(su):
            j0, nj = subs[su]
            s1, s3 = wslot(), wslot()
            for s, wv in ((s1, w1v), (s3, w3v)):
                dst = wsl[s][:].rearrange("p (k n) -> p k n", k=8)[:, :, 0:nj * 128]
                dma("pool", dst, wv[:, :, j0 * 128:(j0 + nj) * 128], (), [wsl_b[s]])
            return s1, s3

        def load2(su):
            j0, nj = subs[su]
            s2 = wslot()
            dst = wsl[s2][:].rearrange("p (j n) -> p j n", j=4)[:, 0:nj, :]
            dma("pool", dst, w2v[:, j0:j0 + nj, :], (), [wsl_b[s2]])
            return s2

        for (sa, sbb) in groups:
            l13 = {}
            l13[sa] = load13(sa)
            l13[sbb] = load13(sbb)
            for su in (sa, sbb):
                j0, nj = subs[su]
                s1, s3 = l13[su]
                hj0 = 0 if su == sa else 4
                for j in range(nj):
                    for (s, which) in ((s1, 0), (s3, 1)):
                        wv = wsl[s][:].rearrange("p (k n) -> p k n", k=8)
                        for k in range(8):
                            for ti, (c0, n) in enumerate(FT):
                                mm(PS[ti][:, 0:n], wv[:, k, j * 128:(j + 1) * 128], xbf[:, k, c0:c0 + n],
                                   k == 0, k == 7, [wsl_b[s]] + ov(xbf_b, k, c0, n), [PS_b[ti]])
                        for ti, (c0, n) in enumerate(FT):
                            fsl = SF[ti][:, 0:n]
                            if which == 0:
                                act(fsl, PS[ti][:, 0:n], AF.Silu, [PS_b[ti]], [ftmp_b[ti]])
                            else:
                                tt(mbuf[:, hj0 + j, c0:c0 + n], fsl, PS[ti][:, 0:n], ALU.mult,
                                   [ftmp_b[ti], PS_b[ti]], [mbuf_b[hj0 + j][ti]])
            s2a = load2(sa)
            s2b = load2(sbb)
            parts = [(s2a, subs[sa][1], 0), (s2b, subs[sbb][1], 4)]
            tot = subs[sa][1] + subs[sbb][1]
            for m in range(8):
                cnt = 0
                for (s2, nj, hj0) in parts:
                    wv = wsl[s2][:].rearrange("p (j n) -> p j n", j=4)
                    for j in range(nj):
                        for ti, (c0, n) in enumerate(FT):
                            mm(PS[ti][:, 0:n], wv[:, j, m * 128:(m + 1) * 128], mbuf[:, hj0 + j, c0:c0 + n],
                               cnt == 0, cnt == tot - 1, [wsl_b[s2], mbuf_b[hj0 + j][ti]], [PS_b[ti]])
                        cnt += 1
                for ti, (c0, n) in enumerate(FT):
                    stt(xres[:, m, c0:c0 + n], PS[ti][:, 0:n], 0.5, xres[:, m, c0:c0 + n], ALU.mult, ALU.add,
                        [PS_b[ti]] + ov(xres_b, m, c0, n), ov(xres_b, m, c0, n))

    def layernorm(i_ln, final):
        for ti, (c0, n) in enumerate(TILES):
            pm, pq = 5, 6
            for k in range(8):
                sq_, sqb = SBf[k % 4], SB_b[k % 4]
                act(sq_[:, 0:n], xres[:, k, c0:c0 + n], AF.Square, [xres_b[k][ti]], [sqb])
                mm(PS[pm][:, 0:n], ones_f[:], xres[:, k, c0:c0 + n], k == 0, k == 7, [cst, xres_b[k][ti]], [PS_b[pm]])
                mm(PS[pq][:, 0:n], ones_b[:], sq_[:, 0:n], k == 0, k == 7, [cst, sqb], [PS_b[pq]])
            mean, msq, var, Aa, Bb = SF[0], SF[1], SF[2], SF[3], SF[4]
            act(mean[:, 0:n], PS[pm][:, 0:n], AF.Copy, [PS_b[pm]], [SF_b[0]], scale=1.0 / D)
            act(msq[:, 0:n], PS[pm][:, 0:n], AF.Square, [PS_b[pm]], [SF_b[1]], scale=1.0 / D)
            stt(var[:, 0:n], PS[pq][:, 0:n], 1.0 / D, msq[:, 0:n], ALU.mult, ALU.subtract,
                [PS_b[pq], SF_b[1]], [SF_b[2]])
            act(var[:, 0:n], var[:, 0:n], AF.Ln, [SF_b[2]], [SF_b[2]], bias=LN_EPS)
            act(Aa[:, 0:n], var[:, 0:n], AF.Exp, [SF_b[2]], [SF_b[3]], scale=-0.5)
            stt(Bb[:, 0:n], mean[:, 0:n], -1.0, Aa[:, 0:n], ALU.mult, ALU.mult, [SF_b[0], SF_b[3]], [SF_b[4]])
            for k in range(8):
                z, zb = SF[k % 2], SF_b[k % 2]
                tt(z[:, 0:n], xres[:, k, c0:c0 + n], Aa[:, 0:n], ALU.mult, [xres_b[k][ti], SF_b[3]], [zb])
                tt(z[:, 0:n], z[:, 0:n], Bb[:, 0:n], ALU.add, [zb, SF_b[4]], [zb])
                gi, bi = 16 * i_ln + k, 16 * i_ln + 8 + k
                if final:
                    ts(xres[:, k, c0:c0 + n], z[:, 0:n], pv[:, gi:gi + 1], pv[:, bi:bi + 1], ALU.mult, ALU.add,
                       [zb, pvb], [xres_b[k][ti]])
                    dma("sp", yout[:, k, c0:c0 + n], xres[:, k, c0:c0 + n], [xres_b[k][ti]])
                else:
                    act(xbf[:, k, c0:c0 + n], z[:, 0:n], AF.Identity, [zb, pvb], [xbf_b[k][ti]],
                        bias=pv[:, bi:bi + 1], scale=pv[:, gi:gi + 1])
                    if k % 2 == 0:
                        ts(xres[:, k, c0:c0 + n], z[:, 0:n], pv2[:, gi:gi + 1], pv2[:, bi:bi + 1], ALU.mult, ALU.add,
                           [zb, pvb], [xres_b[k][ti]])
                    else:
                        act(xres[:, k, c0:c0 + n], z[:, 0:n], AF.Identity, [zb, pvb], [xres_b[k][ti]],
                            bias=pv2[:, bi:bi + 1], scale=pv2[:, gi:gi + 1])

    def softplus(dst, src, tmps, r, w):
        t0, t1, t2 = tmps
        act(t0, src, AF.Abs, r, w)
        act(t0, t0, AF.Exp, w, w, scale=-1.0)
        ts(t1, t0, 2.0, None, ALU.add, None, w, w)
        A("dve", lambda h: h.reciprocal(out=t1, in_=t1), w, w)
        tt(t1, t1, t0, ALU.mult, w, w)
        tt(t2, t1, t1, ALU.mult, w, w)
        ts(t0, t2, 1.0 / 11, 1.0 / 9, ALU.mult, ALU.add, w, w)
        for cf in (1.0 / 7, 1.0 / 5, 1.0 / 3, 1.0):
            tt(t0, t0, t2, ALU.mult, w, w)
            ts(t0, t0, cf, None, ALU.add, None, w, w)
        tt(t0, t0, t1, ALU.mult, w, w)
        ts(t1, src, 0.0, None, ALU.max, None, list(r) + list(w), w)
        stt(dst, t0, 2.0, t1, ALU.mult, ALU.add, w, w)

    def layer_params(l):
        dma("sp", pv[:], pvd[l], (), [pvb])
        dma("sp", prow[:], prd[l].partition_broadcast(128), (), [pvb])
        dma("pool", wab_b[:], wabd[l].rearrange("(k p) n -> p k n", p=128), (), [wabb])
        ts(pv2[:], pv[:, 0:48], ALPHA, None, ALU.mult, None, [pvb], [pvb])
        wl = [pvb] + SF_b[0:5]
        tn = [SF[i][:, 0:8] for i in range(5)]
        ts(tn[3], pv[:, 168:176], -1.0, None, ALU.mult, None, [pvb], wl)
        softplus(tn[4], tn[3], (tn[0], tn[1], tn[2]), wl, wl)
        ts(nsp[:, 0:8], tn[4], -8.0, None, ALU.mult, None, wl, wl)
        ts(nsp[:, 8:16], tn[4], -16.0, None, ALU.mult, None, wl, wl)
        act(aexp[:], prow[:, 16:24], AF.Exp, [pvb], [pvb])

    def gate_tables(l):
        pab = 7
        for nb in range(17):
            c0 = nb * 128
            nt = 128 if nb < 16 else NS
            ti = min(nb // 4, 4)
            for k in range(8):
                mm(PS[pab][0:nt, nb * 16:(nb + 1) * 16], xbf[:, k, c0:c0 + nt], wab_b[:, k, :], k == 0, k == 7,
                   [xbf_b[k][ti], wabb], [PS_b[pab]])
        ab_f = SF[4][:, 0:272].rearrange("p (b n) -> p b n", n=16)
        abv = ab_f
        psv = PS[pab][:, 0:272].rearrange("p (b n) -> p b n", n=16)
        brow = prow[:, 0:16].unsqueeze(1).to_broadcast([128, 17, 16])
        tt(abv, psv, brow, ALU.add, [PS_b[pab], pvb], [tbb, SF_b[4]])
        a_v, b_v = ab_f[:, :, 0:8], ab_f[:, :, 8:16]
        T = {k_: v_[:] for k_, v_ in TB.items()}
        for i_ in range(4):
            T["t%d" % i_] = SF[i_][:, 0:136].rearrange("p (b n) -> p b n", n=8)
        wl = [tbb] + SF_b[0:5]
        act(T["beta"], b_v, AF.Sigmoid, wl, [tbb])
        ts(T["nbeta"], T["beta"], -1.0, None, ALU.mult, None, [tbb], [tbb])
        dtb = prow[:, 24:32].unsqueeze(1).to_broadcast([128, 17, 8])
        tt(T["t3"], a_v, dtb, ALU.add, [tbb, pvb], wl)
        softplus(T["g"], T["t3"], (T["t0"], T["t1"], T["t2"]), wl, wl)
        aex = aexp[:].unsqueeze(1).to_broadcast([128, 17, 8])
        stt(T["g"], T["g"], -1.0, aex, ALU.mult, ALU.mult, [tbb, pvb], [tbb])
        gflat = TB["g"][:].rearrange("p b n -> p (b n)")
        gcflat = TB["gc"][:].rearrange("p b n -> p (b n)")
        mm(PS[pab][:, 0:128], maskU[:], gflat[:, 0:128], True, True, [cst, tbb], [PS_b[pab]])
        act(gcflat[:, 0:128], PS[pab][:, 0:128], AF.Copy, [PS_b[pab]], [tbb])
        act(gcflat[:, 128:136], gflat[:, 128:136], AF.Copy, [tbb], [tbb])
        mm(PS[pab][:, 128:256], sel_last[:], gcflat[:, 0:128], True, True, [cst, tbb], [PS_b[pab]])
        act(glb[:], PS[pab][:, 128:256], AF.Copy, [PS_b[pab]], [tbb])
        act(egl[:], glb[:], AF.Exp, [tbb], [tbb])
        tt(ekt[:], glb[:], gcflat[:, 0:128], ALU.subtract, [tbb], [tbb])
        act(ekt[:], ekt[:], AF.Exp, [tbb], [tbb])
        act(T["egc"], T["gc"], AF.Exp, [tbb], [tbb])
        tt(T["bge"], T["beta"], T["egc"], ALU.mult, [tbb], [tbb])
        for nm, src in (("eg", TB["egc"]), ("beta", TB["beta"])):
            sv = src[0:NS, 16, :].unsqueeze(2).to_broadcast([NS, 8, NS])
            iv = ident_f[0:NS, 0:NS].unsqueeze(1).to_broadcast([NS, 8, NS])
            sM = SF[5][0:NS, 0:128]
            tt(sM.rearrange("p (h s) -> p h s", h=8), sv, iv, ALU.mult, [tbb, cst], [srb, SF_b[5]])
            mm(PS[pab][:, 256:384], ones_f[0:NS, :], sM, True, True, [cst, srb, SF_b[5]], [PS_b[pab]])
            act(srow[nm][:], PS[pab][:, 256:384], AF.Copy, [PS_b[pab]], [srb])
        stt(srow["nbe"][:], srow["beta"][:], -1.0, srow["eg"][:], ALU.mult, ALU.mult, [srb], [srb])

    def rbank(X):
        for _ in range(8):
            i = X.banks[X.rot % len(X.banks)]
            X.rot += 1
            if i not in X.held:
                return i
        raise RuntimeError("no free psum bank")

    class Ctx:
        pass

    def mk_ctx(tid):
        X = Ctx()
        X.tid = tid
        X.banks = [0, 1, 2, 3] if tid == 0 else [4, 5, 6, 7]
        X.held = set()
        X.rot = 0
        X.slots = (0, 1) if tid == 0 else (2, 3)
        for nm in ("smb", "gwb", "dgwb", "sstb_b", "ssob", "Sfb", "Sbb", "ksqb", "srq", "wob"):
            setattr(X, nm, Buf())
        X.SF_b = [Buf() for _ in range(6)] if tid else SF_b
        X.SB_b = [Buf() for _ in range(13)] if tid else SB_b
        if tid == 0:
            X.SF = [t[:] for t in SF]
            X.SBf = [t[:] for t in SBf]
            X.hist, X.pco, X.dgw, X.gw_b = hist[:], pco[:], dgw[:], gw_b[:]
            X.sst, X.sstb, X.sso, X.sh0, X.sho = sst[:], sstb[:], sso[:], sh0[:], sho[:]
            X.S_f, X.S_b, X.skq, X.ksq_t, X.srow_qk = S_f[:], S_b[:], skq[:], ksq_t[:], srow["qk"][:]
            X.wo_c = wo_c0[:]
        else:
            mflat = mbuf[:].rearrange("p a b -> p (a b)")
            off = [0]

            def cv(n_, dt=BF16):
                ne = n_ if dt == BF16 else 2 * n_
                a = off[0]
                off[0] += ne + (ne % 2)
                v = mflat[:, a:a + ne]
                return v if dt == BF16 else v.bitcast(F32)
            X.SF = [cv(512, F32) for _ in range(6)]
            X.SBf = [cv(520 if i < 2 else 512) for i in range(13)]
            X.hist = cv(16).rearrange("p (a b) -> p a b", a=4)
            X.pco = cv(12, F32).rearrange("p (a b) -> p a b", a=4)
            X.dgw = cv(512).rearrange("p (a b) -> p a b", a=4)
            X.gw_b = cv(256).rearrange("p (a b) -> p a b", a=2)
            X.sst = cv(192, F32).rearrange("p (a b c) -> p a b c", a=4, b=3)
            X.sstb = cv(192).rearrange("p (a b c) -> p a b c", a=4, b=3)
            X.sso = cv(192, F32).rearrange("p (a b c) -> p a b c", a=4, b=3)
            X.sh0, X.sho = cv(NS, F32), cv(NS, F32)
            X.S_f, X.S_b = cv(128, F32), cv(128)
            X.skq = cv(2 * NS, F32).rearrange("p (a b) -> p a b", a=2)
            X.ksq_t, X.srow_qk = cv(2 * NS, F32), cv(NS, F32)
            X.wo_c = cv(1024)
            assert off[0] <= 8 * TOK, off[0]
        return X

    CTX = [mk_ctx(0), mk_ctx(1)]
    t1_bufs = CTX[1].SF_b + CTX[1].SB_b + [getattr(CTX[1], nm) for nm in
                                           ("smb", "gwb", "dgwb", "sstb_b", "ssob", "Sfb", "Sbb", "ksqb", "srq", "wob")]

    def chunk_body(l, c, X):
        T = TB
        sA, sB = X.slots
        wv_d = wind[l, c].rearrange("(k p) n -> p k n", p=128)
        for s, b0 in ((sA, 0), (sB, 4)):
            dst = wsl[s][:].rearrange("p (k n) -> p k n", k=8)
            dma("pool", dst, wv_d[:, :, b0 * 128:(b0 + 4) * 128], (), [wsl_b[s]])
        dma("pool", X.gw_b[:], gwd[l, :, c].rearrange("g p n -> p g n"), (), [X.gwb])
        dma("pool", X.wo_c[:], wod[l, c * 128:(c + 1) * 128, :], (), [X.wob])
        wA = wsl[sA][:].rearrange("p (k n) -> p k n", k=8)
        wB = wsl[sB][:].rearrange("p (k n) -> p k n", k=8)

        def wchunk(k, blk):
            if blk < 4:
                return wA[:, k, blk * 128:(blk + 1) * 128], wsl_b[sA]
            return wB[:, k, (blk - 4) * 128:(blk - 3) * 128], wsl_b[sB]

        cw_base = (112 + c, 176 + c, 176 + 8 + c, 176 + 16 + c)
        dma("sp", X.sst[:, 0], src0d[l, c], (), [X.sstb_b])
        for q_ in range(3):
            dma("sp", X.sst[:, 1 + q_], sgc0d[l, q_ * 8 + c], (), [X.sstb_b])
        dma("sp", X.sh0[:], sh0d[l, c], (), [X.sstb_b])
        act(X.sstb[:], X.sst[:], AF.Copy, [X.sstb_b], [X.sstb_b])
        mset(hcar[:, c:c + 1], 0.0, [X.smb])
        mset(X.hist[:], 0.0, [X.smb])
        mset(X.S_f[:], 0.0, [X.Sfb])
        mset(X.S_b[:], 0.0, [X.Sbb])

        for ti, (c0, n) in enumerate(TILES):
            samp = ti == 4

            if samp:
                sbl = []
                for tix in range(7, 13):
                    for half in (0, 1):
                        sbl.append((X.SBf[tix][:, half * 256:(half + 1) * 256].bitcast(F32), X.SB_b[tix]))
                for tix in range(0, 7):
                    sbl.append((X.SBf[tix][:, 256:512].bitcast(F32), X.SB_b[tix]))
                for q_ in range(1, 4):
                    sbl.append((X.SF[2][0:NS, q_ * 128:(q_ + 1) * 128], X.SF_b[2]))
                sbb_ = []
                for (_, par) in sbl:
                    cb = Buf()
                    cb.r = list(par.r) + ([par.w] if par.w is not None else [])
                    sbb_.append(cb)
                for s_ in range(NS):
                    dma("sp", sbl[s_][0], sS0d[l, s_, c], (), [sbb_[s_]])

            def proj(blk):
                pb = rbank(X)
                for k in range(8):
                    wv_, wb_ = wchunk(k, blk)
                    mm(PS[pb][:, 0:n], wv_, xbf[:, k, c0:c0 + n], k == 0, k == 7, [wb_, xbf_b[k][ti]], [PS_b[pb]])
                return pb

            def bias(blk):
                col = 48 + c * 8 + blk
                return pv[:, col:col + 1]

            def conv(blk, hb, prex_i):
                pb = proj(blk)
                prex, pxb = X.SBf[prex_i], X.SB_b[prex_i]
                for j in range(4):
                    col = cw_base[hb] + j * (8 if hb == 0 else 24)
                    ts(X.dgw[:, j, :], ident_f[:], pv[:, col:col + 1], None, ALU.mult, None, [cst, pvb], [X.dgwb])
                if not samp:
                    cp(prex[:, 0:3], X.hist[:, hb, 0:3], [X.smb], [pxb])
                    act(prex[:, 3:3 + n], PS[pb][:, 0:n], AF.Identity, [PS_b[pb], pvb], [pxb], bias=bias(blk))
                    if ti < 3:
                        cp(X.hist[:, hb, 0:3], prex[:, n:n + 3], [pxb], [X.smb])
                    else:
                        act(X.pco[:, hb, :], PS[pb][:, n - 3:n], AF.Identity, [PS_b[pb], pvb], [X.smb], bias=bias(blk))
                    pc = rbank(X)
                    for j in range(4):
                        mm(PS[pc][:, 0:n], X.dgw[:, j, :], prex[:, j:j + n], j == 0, j == 3, [X.dgwb, pxb], [PS_b[pc]])
                else:
                    act(prex[:, 0:n], PS[pb][:, 0:n], AF.Identity, [PS_b[pb], pvb], [pxb], bias=bias(blk))
                    act(X.sso[:, hb, 2, :], PS[pb][:, 0:n], AF.Identity, [PS_b[pb], pvb], [X.ssob], bias=bias(blk))
                    cp(X.sso[:, hb, 0:2, :], X.sst[:, hb, 1:3, :], [X.sstb_b], [X.ssob])
                    pc = rbank(X)
                    for j in range(3):
                        mm(PS[pc][:, 0:n], X.dgw[:, j, :], X.sstb[:, hb, j, :], j == 0, False, [X.dgwb, X.sstb_b], [PS_b[pc]])
                    mm(PS[pc][:, 0:n], X.dgw[:, 3, :], prex[:, 0:n], False, True, [X.dgwb, pxb], [PS_b[pc]])
                return pc

            pc = conv(0, 0, 0)
            xc, xcb = X.SF[0], X.SF_b[0]
            act(xc[:, 0:n], PS[pc][:, 0:n], AF.Identity, [PS_b[pc], pvb], [xcb], bias=pv[:, 144 + c:145 + c])
            xcbf, xcbfb = X.SBf[1], X.SB_b[1]
            act(xcbf[:, 0:n], PS[pc][:, 0:n], AF.Identity, [PS_b[pc], pvb], [xcbfb], bias=pv[:, 144 + c:145 + c])
            pr_, pi_ = rbank(X), rbank(X)
            mm(PS[pr_][:, 0:n], X.gw_b[:, 0, :], xcbf[:, 0:n], True, True, [X.gwb, xcbfb], [PS_b[pr_]])
            mm(PS[pi_][:, 0:n], X.gw_b[:, 1, :], xcbf[:, 0:n], True, True, [X.gwb, xcbfb], [PS_b[pi_]])
            rr, rrb = X.SF[1], X.SF_b[1]
            ig, igb = X.SF[2], X.SF_b[2]
            act(rr[:, 0:n], PS[pr_][:, 0:n], AF.Sigmoid, [PS_b[pr_], pvb], [rrb], bias=pv[:, 152 + c:153 + c])
            act(ig[:, 0:n], PS[pi_][:, 0:n], AF.Sigmoid, [PS_b[pi_], pvb], [igb], bias=pv[:, 160 + c:161 + c])
            aa, aab = X.SF[3], X.SF_b[3]
            act(aa[:, 0:n], rr[:, 0:n], AF.Exp, [rrb, pvb], [aab], scale=nsp[:, c:c + 1])
            act(rr[:, 0:n], rr[:, 0:n], AF.Exp, [rrb, pvb], [rrb], scale=nsp[:, 8 + c:9 + c])
            act(rr[:, 0:n], rr[:, 0:n], AF.Ln, [rrb], [rrb], bias=1.0, scale=-1.0)
            act(rr[:, 0:n], rr[:, 0:n], AF.Exp, [rrb], [rrb], scale=0.5)
            tt(ig[:, 0:n], ig[:, 0:n], xc[:, 0:n], ALU.mult, [igb, xcb], [igb])
            tt(ig[:, 0:n], ig[:, 0:n], rr[:, 0:n], ALU.mult, [igb, rrb], [igb])
            hs, hsb = X.SF[0], X.SF_b[0]
            if not samp:
                scan(hs[:, 0:n], aa[:, 0:n], ig[:, 0:n], hcar[:, c:c + 1], [aab, igb, X.smb], [hsb])
                cp(hcar[:, c:c + 1], hs[:, n - 1:n], [hsb], [X.smb])
                if ti == 3:
                    dma("sp", o_prc[l, c], X.pco[:, 0, :], [X.smb])
            else:
                tt(hs[:, 0:n], aa[:, 0:n], X.sh0[:], ALU.mult, [aab, X.sstb_b], [hsb])
                tt(hs[:, 0:n], hs[:, 0:n], ig[:, 0:n], ALU.add, [hsb, igb], [hsb])
                cp(X.sho[:], hs[:, 0:n], [hsb], [X.ssob])
                dma("sp", o_sh[l, c], X.sho[:], [X.ssob])
                dma("sp", o_src[l, c], X.sso[:, 0], [X.ssob])
            pb = proj(1)
            yy, yyb = X.SF[1], X.SF_b[1]
            act(yy[:, 0:n], PS[pb][:, 0:n], AF.Identity, [PS_b[pb], pvb], [yyb], bias=bias(1))
            y2, y2b = X.SF[2], X.SF_b[2]
            tt(y2[:, 0:n], yy[:, 0:n], yy[:, 0:n], ALU.mult, [yyb], [y2b])
            ts(y2[:, 0:n], y2[:, 0:n], 0.044715, 1.0, ALU.mult, ALU.add, [y2b], [y2b])
            tt(y2[:, 0:n], y2[:, 0:n], yy[:, 0:n], ALU.mult, [y2b, yyb], [y2b])
            act(y2[:, 0:n], y2[:, 0:n], AF.Sigmoid, [y2b], [y2b], scale=1.5957691216057308)
            tt(yy[:, 0:n], yy[:, 0:n], y2[:, 0:n], ALU.mult, [yyb, y2b], [yyb])
            tt(hs[:, 0:n], hs[:, 0:n], yy[:, 0:n], ALU.mult, [hsb, yyb], [hsb])
            pb = proj(2)
            act(y2[:, 0:n], PS[pb][:, 0:n], AF.Sigmoid, [PS_b[pb], pvb], [y2b], bias=bias(2))
            ma, mab = X.SF[5], X.SF_b[5]
            tt(ma[:, 0:n], hs[:, 0:n], y2[:, 0:n], ALU.mult, [hsb, y2b], [mab])

            h_ = c
            if not samp:
                zg, zgb = X.SF[4], X.SF_b[4]
                pb = proj(6)
                act(zg[:, 0:n], PS[pb][:, 0:n], AF.Sigmoid, [PS_b[pb], pvb], [zgb], bias=bias(6))
                stt(zg[:, 0:n], PS[pb][:, 0:n], bias(6), zg[:, 0:n], ALU.add, ALU.mult, [PS_b[pb], pvb, zgb], [zgb])
                pb = proj(7)
                gt_, gtb_ = X.SF[3], X.SF_b[3]
                act(gt_[:, 0:n], PS[pb][:, 0:n], AF.Sigmoid, [PS_b[pb], pvb], [gtb_], bias=bias(7))
                tt(zg[:, 0:n], zg[:, 0:n], gt_[:, 0:n], ALU.mult, [zgb, gtb_], [zgb])
            pc = conv(3, 1, 0)
            qf, qfb = X.SF[0], X.SF_b[0]
            act(qf[:, 0:n], PS[pc][:, 0:n], AF.Sigmoid, [PS_b[pc]], [qfb])
            tt(qf[:, 0:n], qf[:, 0:n], PS[pc][:, 0:n], ALU.mult, [qfb, PS_b[pc]], [qfb])
            pc = conv(4, 2, 1)
            kf, kfb = X.SF[1], X.SF_b[1]
            act(kf[:, 0:n], PS[pc][:, 0:n], AF.Sigmoid, [PS_b[pc]], [kfb])
            tt(kf[:, 0:n], kf[:, 0:n], PS[pc][:, 0:n], ALU.mult, [kfb, PS_b[pc]], [kfb])
            pc = conv(5, 3, 0)
            vT, vTb = X.SBf[2], X.SB_b[2]
            vsg, vsgb = X.SF[2], X.SF_b[2]
            act(vsg[:, 0:n], PS[pc][:, 0:n], AF.Sigmoid, [PS_b[pc]], [vsgb])
            tt(vT[:, 0:n], vsg[:, 0:n], PS[pc][:, 0:n], ALU.mult, [vsgb, PS_b[pc]], [vTb])
            if samp:
                for q_ in range(3):
                    dma("sp", o_sgc[l, q_ * 8 + c], X.sso[:, 1 + q_], [X.ssob])
            elif ti == 3:
                for q_ in range(3):
                    dma("sp", o_pgc[l, q_ * 8 + c], X.pco[:, 1 + q_, :], [X.smb])
            sqs, sqsb = X.SBf[3], X.SB_b[3]
            rs, rsb = X.SF[2], X.SF_b[2]
            qT, qTb = X.SBf[4], X.SB_b[4]
            kT, kTb = X.SBf[5], X.SB_b[5]
            for (src, srcb, dstT, dstTb, scl) in ((qf, qfb, qT, qTb, 128 ** -0.5), (kf, kfb, kT, kTb, 1.0)):
                act(sqs[:, 0:n], src[:, 0:n], AF.Square, [srcb], [sqsb])
                pq = rbank(X)
                mm(PS[pq][:, 0:n], ones_b[:], sqs[:, 0:n], True, True, [cst, sqsb], [PS_b[pq]])
                act(rs[:, 0:n], PS[pq][:, 0:n], AF.Ln, [PS_b[pq]], [rsb], bias=RMS_EPS)
                act(rs[:, 0:n], rs[:, 0:n], AF.Exp, [rsb], [rsb], scale=-0.5)
                if samp:
                    stt(src[:, 0:n], src[:, 0:n], scl, rs[:, 0:n], ALU.mult, ALU.mult, [srcb, rsb], [srcb])
                else:
                    stt(dstT[:, 0:n], src[:, 0:n], scl, rs[:, 0:n], ALU.mult, ALU.mult, [srcb, rsb], [dstTb])

            if samp:
                cols = slice(h_ * NS, (h_ + 1) * NS)
                cp(X.skq[:, 0, :], kf[:, 0:n], [kfb], [X.srq])
                cp(X.skq[:, 1, :], qf[:, 0:n], [qfb], [X.srq])
                qk_, qkb = X.SF[3], X.SF_b[3]
                tt(qk_[:, 0:n], qf[:, 0:n], kf[:, 0:n], ALU.mult, [qfb, kfb], [qkb])
                pq = rbank(X)
                mm(PS[pq][:, 0:n], ones_f[:], qk_[:, 0:n], True, True, [cst, qkb], [PS_b[pq]])
                act(X.srow_qk[:], PS[pq][:, 0:n], AF.Copy, [PS_b[pq]], [X.srq])
                pt = rbank(X)
                tr(PS[pt][0:NS, 0:128], kf[:, 0:n], ident_f[:], [kfb, cst], [PS_b[pt]])
                kpm, kpmb = X.SF[4], X.SF_b[4]
                act(kpm[0:NS, 0:128], PS[pt][0:NS, 0:128], AF.Copy, [PS_b[pt]], [kpmb])
                pks = X.banks[0]
                X.held = {pks}
                vfs, vfsb = X.SF[1], X.SF_b[1]
                act(vfs[:, 0:n], vT[:, 0:n], AF.Copy, [vTb], [vfsb])
                for s_ in range(NS):
                    mm(PS[pks][:, 2 * s_:2 * s_ + 2], sbl[s_][0], X.skq[:, :, s_], True, True,
                       [sbb_[s_], X.srq], [PS_b[pks]])
                act(X.ksq_t[:], PS[pks][:, 0:2 * NS], AF.Copy, [PS_b[pks]], [X.ksqb])
                ksv = X.ksq_t[:].rearrange("p (s t) -> p t s", t=2)
                X.held = set()
                vn, vnb = X.SF[3], X.SF_b[3]
                tt(vn[:, 0:n], vfs[:, 0:n], srow["beta"][:, cols], ALU.mult, [vfsb, srb], [vnb])
                tt(vfs[:, 0:n], ksv[:, 0, :], srow["nbe"][:, cols], ALU.mult, [X.ksqb, srb], [vfsb])
                tt(vn[:, 0:n], vn[:, 0:n], vfs[:, 0:n], ALU.add, [vnb, vfsb], [vnb])
                oo, oob = X.SF[0], X.SF_b[0]
                tt(oo[:, 0:n], ksv[:, 1, :], srow["eg"][:, cols], ALU.mult, [X.ksqb, srb], [oob])
                tt(vfs[:, 0:n], vn[:, 0:n], X.srow_qk[:], ALU.mult, [vnb, X.srq], [vfsb])
                tt(oo[:, 0:n], oo[:, 0:n], vfs[:, 0:n], ALU.add, [oob, vfsb], [oob])
                pt = rbank(X)
                tr(PS[pt][0:NS, 0:128], vn[:, 0:n], ident_f[:], [vnb, cst], [PS_b[pt]])
                vpm, vpmb = X.SF[1], X.SF_b[1]
                act(vpm[0:NS, 0:128], PS[pt][0:NS, 0:128], AF.Copy, [PS_b[pt]], [vpmb])
                for s_ in range(NS):
                    p_ = h_ * NS + s_
                    vm, vmb = sbl[19 + s_ % 3][0], sbb_[19 + s_ % 3]
                    ts(vm, vpm[0:NS, 0:128], ident_f[0:NS, s_:s_ + 1], None, ALU.mult, None, [vpmb, cst], [vmb])
                    po = rbank(X)
                    mm(PS[po][:, 0:128], kpm[0:NS, 0:128], vm, True, True, [kpmb, vmb], [PS_b[po]])
                    so_, sob_ = sbl[16 + s_ % 3][0], sbb_[16 + s_ % 3]
                    stt(so_, sbl[s_][0], srow["eg"][:, p_:p_ + 1], PS[po][:, 0:128], ALU.mult, ALU.add,
                        [sbb_[s_], srb, PS_b[po]], [sob_])
                    dma("sp", o_sS[l, s_, h_], so_, [sob_])
                for (_, par), cb in zip(sbl, sbb_):
                    par.r.extend(cb.r)
                    if cb.w is not None:
                        par.r.append(cb.w)
                po_hold = None
                of, ofb = oo, oob
            else:
                nb0 = ti * 4
                pg, pO, pA, pB = X.banks[0], X.banks[0], X.banks[1], X.banks[2]
                X.held = {pg}
                for nn in range(4):
                    dg_, dgb = X.SF[3], X.SF_b[3]
                    ts(dg_[:, 0:128], ident_f[:], T["gc"][:, nb0 + nn, h_:h_ + 1], None, ALU.mult, None,
                       [cst, tbb], [dgb])
                    mm(PS[pg][:, nn * 128:(nn + 1) * 128], ones_f[:], dg_[:, 0:128], True, True, [cst, dgb], [PS_b[pg]])
                egr, egrb = X.SF[3], X.SF_b[3]
                act(egr[:, 0:n], PS[pg][:, 0:n], AF.Exp, [PS_b[pg]], [egrb])
                qdT, qdTb = X.SBf[6], X.SB_b[6]
                tt(qdT[:, 0:n], qT[:, 0:n], egr[:, 0:n], ALU.mult, [qTb, egrb], [qdTb])
                pk_ = rbank(X)
                pkb = PS[pk_][:].bitcast(BF16)
                for nn in range(4):
                    tr(pkb[:, nn * 128:(nn + 1) * 128], kT[:, nn * 128:(nn + 1) * 128], ident_b[:], [kTb, cst], [PS_b[pk_]])
                kbg, kbgb = X.SBf[7], X.SB_b[7]
                ktm, ktmb = X.SBf[8], X.SB_b[8]
                pk3 = pkb[:, 0:512].rearrange("p (a d) -> p a d", a=4)
                bge_v = T["bge"][:, nb0:nb0 + 4, h_:h_ + 1].to_broadcast([128, 4, 128])
                ekt_v = ekt[:].rearrange("p (b e) -> p b e", e=8)[:, nb0:nb0 + 4, h_:h_ + 1].to_broadcast([128, 4, 128])
                tt(kbg[:, 0:512].rearrange("p (a d) -> p a d", a=4), pk3, bge_v, ALU.mult, [PS_b[pk_], tbb], [kbgb])
                tt(ktm[:, 0:512].rearrange("p (a d) -> p a d", a=4), pk3, ekt_v, ALU.mult, [PS_b[pk_], tbb], [ktmb])
                pv_ = rbank(X)
                pvb_ = PS[pv_][:].bitcast(BF16)
                for nn in range(4):
                    tr(pvb_[:, nn * 128:(nn + 1) * 128], vT[:, nn * 128:(nn + 1) * 128], ident_b[:], [vTb, cst], [PS_b[pv_]])
                vbt, vbtb = X.SBf[9], X.SB_b[9]
                be_v = T["beta"][:, nb0:nb0 + 4, h_:h_ + 1].to_broadcast([128, 4, 128])
                tt(vbt[:, 0:512].rearrange("p (a d) -> p a d", a=4), pvb_[:, 0:512].rearrange("p (a d) -> p a d", a=4),
                   be_v, ALU.mult, [PS_b[pv_], tbb], [vbtb])
                X.held = {pg, pA, pB}
                for nn in range(4):
                    sl = slice(nn * 128, (nn + 1) * 128)
                    mm(PS[pA][:, sl], kT[:, sl], kT[:, sl], True, True, [kTb], [PS_b[pA]])
                    mm(PS[pB][:, sl], kT[:, sl], qT[:, sl], True, True, [kTb, qTb], [PS_b[pB]])
                Dm, Dmb = X.SF[2], X.SF_b[2]
                DTm, DTmb = X.SF[3], X.SF_b[3]
                for nn in range(4):
                    sl = slice(nn * 128, (nn + 1) * 128)
                    gcc = T["gc"][:, nb0 + nn, h_:h_ + 1]
                    stt(Dm[:, sl], PS[pg][:, sl], gcc, maskSL[:], ALU.subtract, ALU.max, [PS_b[pg], tbb, cst], [Dmb])
                    stt(DTm[:, sl], PS[pg][:, sl], gcc, maskUn[:], ALU.subtract, ALU.min, [PS_b[pg], tbb, cst], [DTmb])
                act(Dm[:, 0:n], Dm[:, 0:n], AF.Exp, [Dmb], [Dmb], scale=-1.0)
                act(DTm[:, 0:n], DTm[:, 0:n], AF.Exp, [DTmb], [DTmb])
                m3 = lambda t_: t_[:, 0:512].rearrange("p (a d) -> p a d", a=4)
                X0, X0b = X.SBf[10], X.SB_b[10]
                for nn in range(4):
                    sl = slice(nn * 128, (nn + 1) * 128)
                    stt(X0[:, sl], PS[pA][:, sl], T["nbeta"][:, nb0 + nn, h_:h_ + 1], Dm[:, sl], ALU.mult, ALU.mult,
                        [PS_b[pA], tbb, Dmb], [X0b])
                AIT, AITb = X.SBf[4], X.SB_b[4]
                tt(AIT[:, 0:n], PS[pB][:, 0:n], DTm[:, 0:n], ALU.mult, [PS_b[pB], DTmb], [AITb])
                X.held = set()
                px = rbank(X)
                pxb_ = PS[px][:].bitcast(BF16)
                for nn in range(4):
                    tr(pxb_[:, nn * 128:(nn + 1) * 128], X0[:, nn * 128:(nn + 1) * 128], ident_b[:], [X0b, cst], [PS_b[px]])
                XT0, XT0b = X.SBf[11], X.SB_b[11]
                act(XT0[:, 0:n], pxb_[:, 0:n], AF.Copy, [PS_b[px]], [XT0b])
                RT, RTb = X.SBf[12], X.SB_b[12]
                Rm, Rmb = X.SBf[5], X.SB_b[5]
                U16 = mybir.dt.uint16
                bd3 = BD16[:].unsqueeze(1).to_broadcast([128, 4, 128])
                id3 = ident_b[:].unsqueeze(1).to_broadcast([128, 4, 128])
                Pt = [(X.SBf[0], X.SB_b[0]), (X.SBf[1], X.SB_b[1]), (X.SBf[2], X.SB_b[2]), (X.SBf[3], X.SB_b[3])]
                (Pc, Pcb), (PTc, PTcb) = Pt[0], Pt[1]
                tt(m3(Pc), m3(X0), bd3, ALU.mult, [X0b, cst], [Pcb])
                tt(m3(PTc), m3(XT0), bd3, ALU.mult, [XT0b, cst], [PTcb])
                tt(m3(Rm), m3(Pc), id3, ALU.add, [Pcb, cst], [Rmb])
                tt(m3(RT), m3(PTc), id3, ALU.add, [PTcb, cst], [RTb])

                def mm4(pbank, lhs, rhs, r_):
                    for nn in range(4):
                        sl = slice(nn * 128, (nn + 1) * 128)
                        mm(PS[pbank][:, sl], lhs[:, sl], rhs[:, sl], True, True, r_, [PS_b[pbank]])

                cur = 0
                for lvl in range(1, 4):
                    (Pn, Pnb), (PTn, PTnb) = Pt[2 - cur], Pt[3 - cur]
                    p1, p2 = rbank(X), rbank(X)
                    mm4(p1, PTc, Pc, [PTcb, Pcb])
                    mm4(p2, Pc, PTc, [PTcb, Pcb])
                    act(Pn[:, 0:n], PS[p1][:, 0:n], AF.Copy, [PS_b[p1]], [Pnb])
                    act(PTn[:, 0:n], PS[p2][:, 0:n], AF.Copy, [PS_b[p2]], [PTnb])
                    p3, p4 = rbank(X), rbank(X)
                    mm4(p3, PTn, Rm, [PTnb, Rmb])
                    mm4(p4, Pn, RT, [Pnb, RTb])
                    tt(Rm[:, 0:n], Rm[:, 0:n], PS[p3][:, 0:n], ALU.add, [Rmb, PS_b[p3]], [Rmb])
                    tt(RT[:, 0:n], RT[:, 0:n], PS[p4][:, 0:n], ALU.add, [RTb, PS_b[p4]], [RTb])
                    Pc, Pcb, PTc, PTcb = Pn, Pnb, PTn, PTnb
                    cur = 2 - cur
                (Wt, Wtb), (Ut, Utb) = Pt[0], Pt[1]
                for m_ in (32, 64, 128):
                    om = OFFM[m_][:].bitcast(U16).unsqueeze(1).to_broadcast([128, 4, 128])
                    pu = rbank(X)
                    mm4(pu, X0, RT, [X0b, RTb])
                    if m_ < 128:
                        pw_ = rbank(X)
                        mm4(pw_, XT0, Rm, [XT0b, Rmb])
                        act(Wt[:, 0:n], PS[pw_][:, 0:n], AF.Copy, [PS_b[pw_]], [Wtb])
                    act(Ut[:, 0:n], PS[pu][:, 0:n], AF.Copy, [PS_b[pu]], [Utb])
                    pv2_ = rbank(X)
                    mm4(pv2_, Rm, Ut, [Rmb, Utb])
                    if m_ < 128:
                        pz = rbank(X)
                        mm4(pz, RT, Wt, [RTb, Wtb])
                        A("dve", lambda h, om=om, pz=pz: h.copy_predicated(out=m3(Rm), mask=om, data=m3(PS[pz])),
                          [PS_b[pz], cst], [Rmb], dur=700.0)
                    A("dve", lambda h, om=om, pv2_=pv2_: h.copy_predicated(out=m3(RT), mask=om, data=m3(PS[pv2_])),
                      [PS_b[pv2_], cst], [RTb], dur=700.0)
                pw = rbank(X)
                for nn in range(4):
                    sl = slice(nn * 128, (nn + 1) * 128)
                    mm(PS[pw][:, sl], kbg[:, sl], RT[:, sl], True, True, [kbgb, RTb], [PS_b[pw]])
                wnT, wnTb = X.SBf[3], X.SB_b[3]
                act(wnT[:, 0:n], PS[pw][:, 0:n], AF.Copy, [PS_b[pw]], [wnTb], scale=-1.0)
                X.held = {pO}
                for nn in range(4):
                    sl = slice(nn * 128, (nn + 1) * 128)
                    pn_ = rbank(X)
                    mm(PS[pn_][:, 0:128], RT[:, sl], vbt[:, sl], True, False, [RTb, vbtb], [PS_b[pn_]])
                    mm(PS[pn_][:, 0:128], wnT[:, sl], X.S_b[:], False, True, [wnTb, X.Sbb], [PS_b[pn_]])
                    vnw, vnwb = X.SBf[2][:, (nn % 2) * 128:(nn % 2) * 128 + 128], X.SB_b[2]
                    act(vnw[:, 0:128], PS[pn_][:, 0:128], AF.Copy, [PS_b[pn_]], [vnwb])
                    mm(PS[pO][:, sl], X.S_b[:], qdT[:, sl], True, False, [X.Sbb, qdTb], [PS_b[pO]])
                    mm(PS[pO][:, sl], vnw[:, 0:128], AIT[:, sl], False, True, [vnwb, AITb], [PS_b[pO]])
                    ps_ = rbank(X)
                    mm(PS[ps_][:, 0:128], ktm[:, sl], vnw[:, 0:128], True, True, [ktmb, vnwb], [PS_b[ps_]])
                    eglc = egl[:, (nb0 + nn) * 8 + h_:(nb0 + nn) * 8 + h_ + 1]
                    stt(X.S_f[:], X.S_f[:], eglc, PS[ps_][:, 0:128], ALU.mult, ALU.add, [X.Sfb, tbb, PS_b[ps_]], [X.Sfb])
                    act(X.S_b[:], X.S_f[:], AF.Copy, [X.Sfb], [X.Sbb])
                if ti == 3:
                    dma("sp", o_pS[l, h_], X.S_f[:], [X.Sfb])
                of, ofb = X.SF[0], X.SF_b[0]
                act(of[:, 0:n], PS[pO][:, 0:n], AF.Copy, [PS_b[pO]], [ofb])
                X.held = set()

            sq2, sq2b = X.SBf[3], X.SB_b[3]
            act(sq2[:, 0:n], of[:, 0:n], AF.Square, [ofb], [sq2b])
            pq = rbank(X)
            mm(PS[pq][:, 0:n], ones_b[:], sq2[:, 0:n], True, True, [cst, sq2b], [PS_b[pq]])
            rs2, rs2b = X.SF[1], X.SF_b[1]
            act(rs2[:, 0:n], PS[pq][:, 0:n], AF.Ln, [PS_b[pq]], [rs2b], bias=RMS_EPS, scale=1.0 / 128)
            act(rs2[:, 0:n], rs2[:, 0:n], AF.Exp, [rs2b], [rs2b], scale=-0.5)
            stt(of[:, 0:n], of[:, 0:n], pv[:, 272:273], rs2[:, 0:n], ALU.mult, ALU.mult, [ofb, pvb, rs2b], [ofb])
            if samp:
                pb = proj(6)
                zs, zsb = X.SF[2], X.SF_b[2]
                act(zs[:, 0:n], PS[pb][:, 0:n], AF.Sigmoid, [PS_b[pb], pvb], [zsb], bias=bias(6))
                stt(zs[:, 0:n], PS[pb][:, 0:n], bias(6), zs[:, 0:n], ALU.add, ALU.mult, [PS_b[pb], pvb, zsb], [zsb])
                tt(of[:, 0:n], of[:, 0:n], zs[:, 0:n], ALU.mult, [ofb, zsb], [ofb])
                pb = proj(7)
                act(zs[:, 0:n], PS[pb][:, 0:n], AF.Sigmoid, [PS_b[pb], pvb], [zsb], bias=bias(7))
                tt(of[:, 0:n], of[:, 0:n], zs[:, 0:n], ALU.mult, [ofb, zsb], [ofb])
            else:
                tt(of[:, 0:n], of[:, 0:n], zg[:, 0:n], ALU.mult, [ofb, zgb], [ofb])
            mt, mtb = X.SBf[6], X.SB_b[6]
            tt(mt[:, 0:n], of[:, 0:n], ma[:, 0:n], ALU.add, [ofb, mab], [mtb])
            if X.tid == 1:
                XA = CTX[0]
                for m in range(8):
                    pb = rbank(X)
                    mm(PS[pb][:, 0:n], XA.wo_c[:, m * 128:(m + 1) * 128], XA.SBf[6][:, 0:n], True, False,
                       [XA.wob, XA.SB_b[6]], [PS_b[pb]])
                    mm(PS[pb][:, 0:n], X.wo_c[:, m * 128:(m + 1) * 128], mt[:, 0:n], False, True, [X.wob, mtb], [PS_b[pb]])
                    tt(xres[:, m, c0:c0 + n], PS[pb][:, 0:n], xres[:, m, c0:c0 + n], ALU.add,
                       [PS_b[pb], xres_b[m][ti]], [xres_b[m][ti]])
            yield


    def mixer(l):
        gate_tables(l)
        nops0 = len(P.all)
        prior = []
        for row in mbuf_b:
            for b in row:
                prior.extend(b.r)
                if b.w is not None:
                    prior.append(b.w)
        for b in t1_bufs:
            b.w = None
            b.r = list(prior)
        for pair in range(4):
            gens = [chunk_body(l, 2 * pair, CTX[0]), chunk_body(l, 2 * pair + 1, CTX[1])]
            live = [True, True]
            while any(live):
                for t_ in range(2):
                    if live[t_]:
                        try:
                            next(gens[t_])
                        except StopIteration:
                            live[t_] = False
        dma("sp", o_ph[l], hcar[:], [CTX[0].smb, CTX[1].smb])
        if l == 0:
            print("mixer ops per (chunk,tile):", (len(P.all) - nops0) / 40.0)
        post = []
        for b in t1_bufs:
            post.extend(b.r)
            if b.w is not None:
                post.append(b.w)
        for row in mbuf_b:
            for b in row:
                b.r = list(b.r) + post

    for l in range(n_layers):
        layer_params(l)
        ffn(2 * l)
        layernorm(0, False)
        mixer(l)
        layernorm(1, False)
        ffn(2 * l + 1)
        layernorm(2, l == n_layers - 1)
    P.emit(scheduled=SCHED)
    print("ops:", len(P.all), "est total us:", getattr(P, "est_total", 0) / 1e3)
    st.close()
    return nc


def _prep_shared(inp):
    f = lambda a: np.ascontiguousarray(np.asarray(a, dtype=np.float32))
    w1 = f(np.stack([inp["ffn1_w1"], inp["ffn2_w1"]], 1).reshape(2 * L, D, DFF))
    w3 = f(np.stack([inp["ffn1_w3"], inp["ffn2_w3"]], 1).reshape(2 * L, D, DFF))
    w2 = f(np.stack([inp["ffn1_w2"], inp["ffn2_w2"]], 1).reshape(2 * L, DFF, D))
    w_in = np.asarray(inp["w_in"], np.float32)
    b_in = np.asarray(inp["b_in"], np.float32)
    offs = [0, 1024, 6160, 2048, 3072, 4096, 5120, 7184]
    win = np.empty((L, 8, D, 1024), np.float32)
    for c in range(8):
        for bi, o in enumerate(offs):
            win[:, c, :, bi * 128:(bi + 1) * 128] = w_in[:, :, o + c * 128:o + (c + 1) * 128]
    wab = f(w_in[:, :, 6144:6160])
    gatew = np.zeros((L, 2, 8, 128, 128), np.float32)
    for gi, nm in enumerate(("rg_wr", "rg_wi")):
        w = np.asarray(inp[nm], np.float32)
        for c in range(8):
            gatew[:, gi, c, 0:64, 0:64] = w[:, 2 * c]
            gatew[:, gi, c, 64:128, 64:128] = w[:, 2 * c + 1]
    pvec = np.empty((L, 128, NPV), np.float32)
    fm = lambda v: np.asarray(v, np.float32).reshape(L, -1, 128).transpose(0, 2, 1)
    for i, (g, b) in enumerate((("ln1_g", "ln1_b"), ("ln2_g", "ln2_b"), ("ln3_g", "ln3_b"))):
        pvec[:, :, 16 * i:16 * i + 8] = fm(inp[g])
        pvec[:, :, 16 * i + 8:16 * i + 16] = fm(inp[b])
    for c in range(8):
        for bi, o in enumerate(offs):
            pvec[:, :, 48 + c * 8 + bi] = b_in[:, o + c * 128:o + (c + 1) * 128]
    rcw = np.asarray(inp["rg_conv_w"], np.float32)
    for j in range(4):
        pvec[:, :, 112 + j * 8:112 + (j + 1) * 8] = fm(rcw[:, j])
    pvec[:, :, 144:152] = fm(inp["rg_conv_b"])
    pvec[:, :, 152:160] = fm(inp["rg_br"])
    pvec[:, :, 160:168] = fm(inp["rg_bi"])
    pvec[:, :, 168:176] = fm(inp["rg_lambda"])
    gcw = np.asarray(inp["gdn_conv_w"], np.float32)
    for j in range(4):
        pvec[:, :, 176 + j * 24:176 + (j + 1) * 24] = fm(gcw[:, j])
    pvec[:, :, 272] = np.asarray(inp["gdn_norm_w"], np.float32)
    prow = np.concatenate([b_in[:, 6144:6160], np.asarray(inp["gdn_a_log"], np.float32),
                           np.asarray(inp["gdn_dt_bias"], np.float32)], axis=1).reshape(L, 1, 32)
    return {"w1": w1, "w3": w3, "w2": w2, "win": win, "wab": wab, "wo": f(inp["w_o"]), "gatew": gatew,
            "pvec": f(pvec), "prow": f(prow)}


def _prep_core(inp, i):
    xp = np.asarray(inp["x_prompt"], np.float32)[i]
    xs = np.asarray(inp["x_sample"], np.float32)[i * NS:(i + 1) * NS, 0]
    x = np.concatenate([xp, xs], 0)
    xin = np.ascontiguousarray(x.T.reshape(8, 128, TOK).transpose(1, 0, 2))
    sl = slice(i * NS, (i + 1) * NS)
    h0 = np.asarray(inp["state_rglru_h"], np.float32)[:, sl]
    sh0 = np.ascontiguousarray(h0.transpose(0, 2, 1).reshape(L, 8, 128, NS))
    rc = np.asarray(inp["state_rglru_conv"], np.float32)[:, sl]
    src0 = np.ascontiguousarray(rc.transpose(0, 3, 2, 1).reshape(L, 8, 128, 3, NS))
    gc = np.asarray(inp["state_gdn_conv"], np.float32)[:, sl]
    sgc0 = np.ascontiguousarray(gc.transpose(0, 3, 2, 1).reshape(L, 24, 128, 3, NS))
    sS0 = np.ascontiguousarray(np.asarray(inp["state_gdn_S"], np.float32)[:, sl])
    return {"xin": xin, "sh0": sh0, "src0": src0, "sgc0": sgc0, "sS0": sS0}


_NC_CACHE = {}


def kernel(**inp):
    if "nc" not in _NC_CACHE:
        _NC_CACHE["nc"] = build_program()
    nc = _NC_CACHE["nc"]
    shared = _prep_shared(inp)
    in_maps = []
    for i in range(8):
        m = dict(shared)
        m.update(_prep_core(inp, i))
        in_maps.append(m)
    res = run_bass_kernel_spmd(nc, in_maps, core_ids=list(range(8)))
    R = res.results
    Bp = 8
    y_prompt = np.empty((Bp, NP, D), np.float32)
    y_sample = np.empty((8 * NS, 1, D), np.float32)
    p_h = np.empty((L, Bp, D), np.float32)
    p_rc = np.empty((L, Bp, 3, D), np.float32)
    p_S = np.empty((L, Bp, 8, 128, 128), np.float32)
    p_gc = np.empty((L, Bp, 3, 3072), np.float32)
    s_h = np.empty((L, 8 * NS, D), np.float32)
    s_rc = np.empty((L, 8 * NS, 3, D), np.float32)
    s_S = np.empty((L, 8 * NS, 8, 128, 128), np.float32)
    s_gc = np.empty((L, 8 * NS, 3, 3072), np.float32)
    for i in range(8):
        r = R[i]
        y = np.asarray(r["yout"]).transpose(1, 0, 2).reshape(D, TOK).T
        y_prompt[i] = y[:NP]
        sl = slice(i * NS, (i + 1) * NS)
        y_sample[sl, 0] = y[NP:]
        p_h[:, i] = np.asarray(r["o_ph"]).transpose(0, 2, 1).reshape(L, D)
        p_rc[:, i] = np.asarray(r["o_prc"]).transpose(0, 3, 1, 2).reshape(L, 3, D)
        p_S[:, i] = np.asarray(r["o_pS"])
        p_gc[:, i] = np.asarray(r["o_pgc"]).transpose(0, 3, 1, 2).reshape(L, 3, 3072)
        s_h[:, sl] = np.asarray(r["o_sh"]).transpose(0, 3, 1, 2).reshape(L, NS, D)
        s_rc[:, sl] = np.asarray(r["o_src"]).transpose(0, 4, 3, 1, 2).reshape(L, NS, 3, D)
        s_S[:, sl] = np.asarray(r["o_sS"])
        s_gc[:, sl] = np.asarray(r["o_sgc"]).transpose(0, 4, 3, 1, 2).reshape(L, NS, 3, 3072)
    return (y_prompt, y_sample, p_h, p_rc, p_S, p_gc, s_h, s_rc, s_S, s_gc)
```
